# Optimizing a Trainium2 kernel written in Bass

```python
import math
import jax
import jax.numpy as jnp
from jax import lax
import numpy as np

D_MODEL = 1024
BATCH = 8
SEQ = 4096
DEPTH = 4

GRID_W = 64
CTX_LEN = 256
N_EVEN = (DEPTH + 1) // 2
N_ODD = DEPTH // 2
EPS = 1e-6
CONV_W = 4
LRU_WIDTH = D_MODEL // 2
LRU_BLOCKS = 8
LRU_BLOCK = LRU_WIDTH // LRU_BLOCKS
LRU_C = 8.0
RET_HEADS = 4
RET_DK = D_MODEL // 16
RET_DV = D_MODEL // 8
RET_CHUNK = 128
ROPE_BASE = 10000.0
HG_HEADS = 4
HG_DK = D_MODEL // 8
HG_DV = D_MODEL // 8
GDN_HEADS = 4
GDN_DK = D_MODEL // 8
GDN_DV = D_MODEL // 8
LIN_CHUNK = 64
D_FF = 4 * D_MODEL
EV_SIZES = (LRU_WIDTH, LRU_WIDTH, RET_HEADS * RET_DK, RET_HEADS * RET_DK, RET_HEADS * RET_DV, RET_HEADS * RET_DV)
OD_SIZES = (HG_HEADS * HG_DK, HG_HEADS * HG_DK, HG_HEADS * HG_DK, HG_HEADS * HG_DV, HG_HEADS * HG_DV,
            GDN_HEADS * GDN_DK, GDN_HEADS * GDN_DK, GDN_HEADS * GDN_DV, GDN_HEADS * GDN_DV, 2 * GDN_HEADS, 2 * GDN_HEADS)
EV_IN = sum(EV_SIZES)
OD_IN = sum(OD_SIZES)
MIX_WIDTH = LRU_WIDTH + RET_HEADS * RET_DV
F32 = jnp.float32

kernel_name = 'hybrid_bidir_rglru_retention_hgrn2_gdn'


def _rmsnorm(x, g):
    xf = x.astype(F32)
    y = xf * lax.rsqrt(jnp.mean(xf * xf, axis=-1, keepdims=True) + EPS)
    return (y * g.astype(F32)).astype(x.dtype)


def _modulate(h, shift, scale):
    return h * (1.0 + scale) + shift


def _split(t, sizes):
    return jnp.split(t, np.cumsum(sizes)[:-1].tolist(), axis=-1)


def _heads(t, n_heads):
    return t.reshape(t.shape[:-1] + (n_heads, t.shape[-1] // n_heads))


def _dwconv(t, w):
    left = CONV_W // 2
    return lax.conv_general_dilated(t, w.astype(t.dtype)[:, None, :], window_strides=(1,),
                                    padding=[(left, CONV_W - 1 - left)],
                                    dimension_numbers=('NWC', 'WIO', 'NWC'),
                                    feature_group_count=t.shape[-1])


def _l2norm(t):
    t = t.astype(F32)
    return t * lax.rsqrt(jnp.sum(t * t, axis=-1, keepdims=True) + EPS)


def _head_rms(o):
    o = o.astype(F32)
    o = o * lax.rsqrt(jnp.mean(o * o, axis=-1, keepdims=True) + EPS)
    return o.reshape(o.shape[0], o.shape[1], -1)


def _head_groupnorm(o):
    o = o.astype(F32)
    o = o - jnp.mean(o, axis=-1, keepdims=True)
    return _head_rms(o)


def _scan_order(ctx_part, lat_part, reverse):
    if reverse:
        ctx_part, lat_part = jnp.flip(ctx_part, 1), jnp.flip(lat_part, 1)
    return jnp.concatenate([ctx_part, lat_part], axis=1)


def _natural_order(y, n_ctx, reverse):
    if not reverse:
        return y
    return jnp.concatenate([jnp.flip(y[:, :n_ctx], 1), jnp.flip(y[:, n_ctx:], 1)], axis=1)


def _blk(t, chunk):
    b, l = t.shape[0], t.shape[1]
    t = t.astype(F32).reshape((b, l // chunk, chunk) + t.shape[2:])
    return jnp.moveaxis(t, 3, 2)


def _unblk(t):
    b, n, h, c, d = t.shape
    return jnp.moveaxis(t, 2, 3).reshape(b, n * c, h, d)


def _masked_exp(mask, logits):
    return jnp.where(mask, jnp.exp(jnp.where(mask, logits, 0.0)), 0.0)


def _linear_scan(a, b):
    def comb(l, r):
        return l[0] * r[0], r[0] * l[1] + r[1]
    _, h = lax.associative_scan(comb, (a, b), axis=1)
    return h


def _axial_rope(rows):
    n_freq = RET_DK // 4
    inv = jnp.power(ROPE_BASE, -jnp.arange(n_freq, dtype=F32) / n_freq)
    r = jnp.arange(rows, dtype=F32)
    col = jnp.arange(GRID_W, dtype=F32)
    row_ang = jnp.broadcast_to(r[:, None, None] * inv, (rows, GRID_W, n_freq))
    col_ang = jnp.broadcast_to(col[None, :, None] * inv, (rows, GRID_W, n_freq))
    ang = jnp.concatenate([row_ang, col_ang], axis=-1).reshape(rows * GRID_W, 2 * n_freq)
    return jnp.cos(ang), jnp.sin(ang)


def _rope(x, cos, sin):
    half = x.shape[-1] // 2
    xf = x.astype(F32)
    x1, x2 = xf[..., :half], xf[..., half:]
    c, s = cos[None, :, None, :], sin[None, :, None, :]
    return jnp.concatenate([x1 * c - x2 * s, x1 * s + x2 * c], axis=-1).astype(x.dtype)


def _retention_chunked(q, k, v, log_gamma, chunk):
    qb, kb, vb = _blk(q, chunk), _blk(k, chunk), _blk(v, chunk)
    b, n, h, c, dk = qb.shape
    dv = vb.shape[-1]
    pos = jnp.arange(c, dtype=F32)
    diff = pos[:, None] - pos[None, :]
    lg = log_gamma.astype(F32)
    decay_in = jnp.where(diff >= 0, jnp.exp(lg[:, None, None] * jnp.maximum(diff, 0.0)), 0.0)
    q_dec = jnp.exp(lg[:, None] * (pos + 1.0))
    k_dec = jnp.exp(lg[:, None] * (c - 1.0 - pos))
    c_dec = jnp.exp(lg * c)
    inner = jnp.einsum('bnhid,bnhjd->bnhij', qb, kb) * decay_in
    o_in = jnp.einsum('bnhij,bnhjv->bnhiv', inner, vb)

    def step(state, inp):
        q_c, kd_c, v_c = inp
        o_x = jnp.einsum('bhid,bhdv->bhiv', q_c, state) * q_dec[..., None]
        state = state * c_dec[:, None, None] + jnp.einsum('bhjd,bhjv->bhdv', kd_c, v_c)
        return state, o_x

    xs = (jnp.moveaxis(qb, 1, 0), jnp.moveaxis(kb * k_dec[..., None], 1, 0), jnp.moveaxis(vb, 1, 0))
    _, o_x = lax.scan(step, jnp.zeros((b, h, dk, dv), F32), xs)
    return _unblk(o_in + jnp.moveaxis(o_x, 0, 1))


def _gla_chunked(q, k, v, logf, chunk):
    qb, kb, vb = _blk(q, chunk), _blk(k, chunk), _blk(v, chunk)
    ab = jnp.cumsum(_blk(logf, chunk), axis=3)
    b, n, h, c, dk = qb.shape
    dv = vb.shape[-1]
    causal = jnp.tril(jnp.ones((c, c), dtype=bool))[:, :, None]

    def step(state, inp):
        q_c, k_c, v_c, a_c = inp
        dec = _masked_exp(causal, a_c[:, :, :, None, :] - a_c[:, :, None, :, :])
        scores = jnp.einsum('bhid,bhjd,bhijd->bhij', q_c, k_c, dec)
        a_last = a_c[:, :, -1, :]
        o = (jnp.einsum('bhij,bhjv->bhiv', scores, v_c)
             + jnp.einsum('bhid,bhdv->bhiv', q_c * jnp.exp(a_c), state))
        state = (state * jnp.exp(a_last)[..., None]
                 + jnp.einsum('bhjd,bhjv->bhdv', k_c * jnp.exp(a_last[:, :, None, :] - a_c), v_c))
        return state, o

    xs = tuple(jnp.moveaxis(t, 1, 0) for t in (qb, kb, vb, ab))
    _, o = lax.scan(step, jnp.zeros((b, h, dk, dv), F32), xs)
    return _unblk(jnp.moveaxis(o, 0, 1))


def _gated_delta_chunked(q, k, v, beta, g, chunk):
    qb, kb, vb = _blk(q, chunk), _blk(k, chunk), _blk(v, chunk)
    bb = _blk(beta, chunk)
    gc = jnp.cumsum(_blk(g, chunk), axis=-1)
    b, n, h, c, dk = qb.shape
    dv = vb.shape[-1]
    causal = jnp.tril(jnp.ones((c, c), dtype=bool))
    strict = jnp.tril(jnp.ones((c, c), dtype=bool), k=-1)
    gam = _masked_exp(causal, gc[..., :, None] - gc[..., None, :])
    kbeta = kb * bb[..., None]
    m = jnp.where(strict, jnp.einsum('bnhid,bnhjd->bnhij', kbeta, kb) * gam, 0.0)
    rhs = jnp.concatenate([vb * bb[..., None], kbeta * jnp.exp(gc)[..., None]], axis=-1)
    sol = lax.linalg.triangular_solve(m + jnp.eye(c, dtype=F32), rhs, left_side=True, lower=True,
                                      unit_diagonal=True)
    u, w = sol[..., :dv], sol[..., dv:]
    qk = jnp.einsum('bnhid,bnhjd->bnhij', qb, kb) * gam
    q_dec = qb * jnp.exp(gc)[..., None]
    k_dec = kb * jnp.exp(gc[..., -1:] - gc)[..., None]
    g_last = jnp.exp(gc[..., -1])

    def step(state, inp):
        u_c, w_c, qk_c, qd_c, kd_c, gl_c = inp
        v_new = u_c - jnp.einsum('bhcd,bhdv->bhcv', w_c, state)
        o = jnp.einsum('bhcd,bhdv->bhcv', qd_c, state) + jnp.einsum('bhij,bhjv->bhiv', qk_c, v_new)
        state = state * gl_c[..., None, None] + jnp.einsum('bhcd,bhcv->bhdv', kd_c, v_new)
        return state, o

    xs = tuple(jnp.moveaxis(t, 1, 0) for t in (u, w, qk, q_dec, k_dec, g_last))
    _, o = lax.scan(step, jnp.zeros((b, h, dk, dv), F32), xs)
    return _unblk(jnp.moveaxis(o, 0, 1))


def _even_mixer(hc, hl, w_in, conv_w, conv_b, wa, ba, wx, bx, lam, log_gamma, cos, sin):
    n_ctx = hc.shape[1]
    bsz = hl.shape[0]
    xc, gc, qc, kc, vc, zc = _split(hc @ w_in, EV_SIZES)
    xl, gl, ql, kl, vl, zl = _split(hl @ w_in, EV_SIZES)
    uc = _dwconv(xc, conv_w) + conv_b
    ul = _dwconv(xl, conv_w) + conv_b
    h = 0.0
    for d, rev in enumerate((False, True)):
        u = _scan_order(uc, ul, rev)
        seq_len = u.shape[1]
        ub = u.reshape(bsz, seq_len, LRU_BLOCKS, LRU_BLOCK)
        r = jax.nn.sigmoid((jnp.einsum('blki,kij->blkj', ub, wa[d]).reshape(bsz, seq_len, LRU_WIDTH) + ba[d]).astype(F32))
        i = jax.nn.sigmoid((jnp.einsum('blki,kij->blkj', ub, wx[d]).reshape(bsz, seq_len, LRU_WIDTH) + bx[d]).astype(F32))
        log_a = -LRU_C * jax.nn.softplus(-lam[d].astype(F32)) * r
        inp = jnp.sqrt(-jnp.expm1(2.0 * log_a)) * (i * u.astype(F32))
        h = h + _natural_order(_linear_scan(jnp.exp(log_a), inp), n_ctx, rev)
    y_a = h * jax.nn.gelu(jnp.concatenate([gc, gl], axis=1).astype(F32))
    qk_scale = RET_DK ** -0.5
    qc, kc, vc = _heads(qc, RET_HEADS), _heads(kc, RET_HEADS) * qk_scale, _heads(vc, RET_HEADS)
    ql = _rope(_heads(ql, RET_HEADS), cos, sin)
    kl = _rope(_heads(kl, RET_HEADS), cos, sin) * qk_scale
    vl = _heads(vl, RET_HEADS)
    o = 0.0
    for d, rev in enumerate((False, True)):
        o_d = _retention_chunked(_scan_order(qc, ql, rev), _scan_order(kc, kl, rev),
                                 _scan_order(vc, vl, rev), log_gamma[d], RET_CHUNK)
        o = o + _natural_order(o_d, n_ctx, rev)
    y_b = _head_groupnorm(o) * jax.nn.silu(jnp.concatenate([zc, zl], axis=1).astype(F32))
    return jnp.concatenate([y_a, y_b], axis=-1).astype(hl.dtype)


def _odd_mixer(hc, hl, w_in, lb, conv_w, a_log, dt_bias):
    n_ctx = hc.shape[1]
    (hq_c, ff_c, fb_c, hi_c, hz_c, gq_c, gk_c, gv_c, gz_c, gb_c, ga_c) = _split(hc @ w_in, OD_SIZES)
    (hq_l, ff_l, fb_l, hi_l, hz_l, gq_l, gk_l, gv_l, gz_l, gb_l, ga_l) = _split(hl @ w_in, OD_SIZES)
    q_c, q_l = _heads(jax.nn.silu(hq_c), HG_HEADS), _heads(jax.nn.silu(hq_l), HG_HEADS)
    i_c, i_l = _heads(hi_c, HG_HEADS), _heads(hi_l, HG_HEADS)
    o = 0.0
    for d, rev in enumerate((False, True)):
        fpre = _scan_order(ff_c, ff_l, rev) if d == 0 else _scan_order(fb_c, fb_l, rev)
        lbd = lb[d]
        f = lbd + (1.0 - lbd) * jax.nn.sigmoid(fpre.astype(F32))
        logf = _heads(jnp.log(f), HG_HEADS)
        o_d = _gla_chunked(_scan_order(q_c, q_l, rev), _heads(1.0 - f, HG_HEADS), _scan_order(i_c, i_l, rev),
                           logf, LIN_CHUNK)
        o = o + _natural_order(o_d, n_ctx, rev)
    y_c = _head_rms(o) * jax.nn.silu(jnp.concatenate([hz_c, hz_l], axis=1).astype(F32))
    qkv_c = jax.nn.silu(_dwconv(jnp.concatenate([gq_c, gk_c, gv_c], axis=-1), conv_w))
    qkv_l = jax.nn.silu(_dwconv(jnp.concatenate([gq_l, gk_l, gv_l], axis=-1), conv_w))
    dq_c, dk_c, dv_c = _split(qkv_c, (GDN_HEADS * GDN_DK, GDN_HEADS * GDN_DK, GDN_HEADS * GDN_DV))
    dq_l, dk_l, dv_l = _split(qkv_l, (GDN_HEADS * GDN_DK, GDN_HEADS * GDN_DK, GDN_HEADS * GDN_DV))
    q_scale = GDN_DK ** -0.5
    dq_c, dq_l = _l2norm(_heads(dq_c, GDN_HEADS)) * q_scale, _l2norm(_heads(dq_l, GDN_HEADS)) * q_scale
    dk_c, dk_l = _l2norm(_heads(dk_c, GDN_HEADS)), _l2norm(_heads(dk_l, GDN_HEADS))
    dv_c, dv_l = _heads(dv_c, GDN_HEADS), _heads(dv_l, GDN_HEADS)
    o = 0.0
    for d, rev in enumerate((False, True)):
        sl = slice(d * GDN_HEADS, (d + 1) * GDN_HEADS)
        beta = jax.nn.sigmoid(_scan_order(gb_c[..., sl], gb_l[..., sl], rev).astype(F32))
        a_pre = _scan_order(ga_c[..., sl], ga_l[..., sl], rev).astype(F32)
        g = -jnp.exp(a_log[d].astype(F32)) * jax.nn.softplus(a_pre + dt_bias[d].astype(F32))
        o_d = _gated_delta_chunked(_scan_order(dq_c, dq_l, rev), _scan_order(dk_c, dk_l, rev),
                                   _scan_order(dv_c, dv_l, rev), beta, g, LIN_CHUNK)
        o = o + _natural_order(o_d, n_ctx, rev)
    y_d = _head_rms(o) * jax.nn.silu(jnp.concatenate([gz_c, gz_l], axis=1).astype(F32))
    return jnp.concatenate([y_c, y_d], axis=-1).astype(hl.dtype)


def _sqrelu_mlp(h, w1, w2):
    return jnp.square(jax.nn.relu(h @ w1)) @ w2


def setup_inputs(seed: int = 0) -> dict:
    key = jax.random.key(seed)
    keys = iter(jax.random.split(key, 40))

    def nrm(shape, std):
        return jax.random.normal(next(keys), shape, F32) * std

    def unif(shape, lo, hi):
        return jax.random.uniform(next(keys), shape, F32, lo, hi)

    x = nrm((BATCH, SEQ, D_MODEL), 1.0)
    c = nrm((BATCH, D_MODEL), 1.0)
    ctx = nrm((BATCH, CTX_LEN, D_MODEL), 1.0)
    c_ctx = nrm((D_MODEL,), 1.0)
    ada_w = nrm((DEPTH, D_MODEL, 6 * D_MODEL), 0.5 * D_MODEL ** -0.5)
    ada_b = nrm((DEPTH, 6 * D_MODEL), 0.02)
    norm1_g = 1.0 + nrm((DEPTH, D_MODEL), 0.02)
    norm2_g = 1.0 + nrm((DEPTH, D_MODEL), 0.02)
    mix_w_out = nrm((DEPTH, MIX_WIDTH, D_MODEL), MIX_WIDTH ** -0.5)
    mlp_w1 = nrm((DEPTH, D_MODEL, D_FF), D_MODEL ** -0.5)
    mlp_w2 = nrm((DEPTH, D_FF, D_MODEL), D_FF ** -0.5)
    ev_w_in = nrm((N_EVEN, D_MODEL, EV_IN), D_MODEL ** -0.5)
    lru_conv_w = nrm((N_EVEN, CONV_W, LRU_WIDTH), CONV_W ** -0.5)
    lru_conv_b = nrm((N_EVEN, LRU_WIDTH), 0.02)
    lru_wa = nrm((N_EVEN, 2, LRU_BLOCKS, LRU_BLOCK, LRU_BLOCK), LRU_BLOCK ** -0.5)
    lru_ba = nrm((N_EVEN, 2, LRU_WIDTH), 0.02)
    lru_wx = nrm((N_EVEN, 2, LRU_BLOCKS, LRU_BLOCK, LRU_BLOCK), LRU_BLOCK ** -0.5)
    lru_bx = nrm((N_EVEN, 2, LRU_WIDTH), 0.02)
    a_c = unif((N_EVEN, 2, LRU_WIDTH), 0.9, 0.999) ** (1.0 / LRU_C)
    lru_lambda = jnp.log(a_c) - jnp.log1p(-a_c)
    ret_base = jnp.log1p(-jnp.exp2(-5.0 - jnp.arange(RET_HEADS, dtype=F32)))
    ret_log_gamma = ret_base * jnp.exp(nrm((N_EVEN, 2, RET_HEADS), 0.1))
    od_w_in = nrm((N_ODD, D_MODEL, OD_IN), D_MODEL ** -0.5)
    hg_lb_logits = nrm((2, N_ODD, HG_HEADS * HG_DK), 0.1)
    gdn_conv_w = nrm((N_ODD, CONV_W, GDN_HEADS * (2 * GDN_DK + GDN_DV)), CONV_W ** -0.5)
    gdn_a_log = jnp.log(unif((N_ODD, 2, GDN_HEADS), 1.0, 16.0))
    dt = jnp.exp(unif((N_ODD, 2, GDN_HEADS), math.log(1e-3), math.log(1e-1)))
    gdn_dt_bias = dt + jnp.log(-jnp.expm1(-dt))
    final_g = 1.0 + nrm((D_MODEL,), 0.02)
    return {'x': x, 'c': c, 'ctx': ctx, 'c_ctx': c_ctx, 'ada_w': ada_w, 'ada_b': ada_b,
            'norm1_g': norm1_g, 'norm2_g': norm2_g, 'mix_w_out': mix_w_out, 'mlp_w1': mlp_w1,
            'mlp_w2': mlp_w2, 'ev_w_in': ev_w_in, 'lru_conv_w': lru_conv_w, 'lru_conv_b': lru_conv_b,
            'lru_wa': lru_wa, 'lru_ba': lru_ba, 'lru_wx': lru_wx, 'lru_bx': lru_bx,
            'lru_lambda': lru_lambda, 'ret_log_gamma': ret_log_gamma, 'od_w_in': od_w_in,
            'hg_lb_logits': hg_lb_logits, 'gdn_conv_w': gdn_conv_w, 'gdn_a_log': gdn_a_log,
            'gdn_dt_bias': gdn_dt_bias, 'final_g': final_g}


def reference(x, c, ctx, c_ctx, ada_w, ada_b, norm1_g, norm2_g, mix_w_out, mlp_w1, mlp_w2,
              ev_w_in, lru_conv_w, lru_conv_b, lru_wa, lru_ba, lru_wx, lru_bx, lru_lambda,
              ret_log_gamma, od_w_in, hg_lb_logits, gdn_conv_w, gdn_a_log, gdn_dt_bias, final_g):
    n_ctx = ctx.shape[1]
    rows = x.shape[1] // GRID_W
    cos, sin = _axial_rope(rows)
    sm = jax.nn.softmax(hg_lb_logits.astype(F32), axis=1)
    hg_lb = jnp.cumsum(sm, axis=1) - sm[:, :1]
    s_lat = jax.nn.silu(c)
    s_ctx = jax.nn.silu(c_ctx)[None, :]
    xl, xc = x, ctx
    for layer in range(DEPTH):
        mod_l = jnp.split((s_lat @ ada_w[layer] + ada_b[layer])[:, None, :], 6, axis=-1)
        mod_c = jnp.split((s_ctx @ ada_w[layer] + ada_b[layer])[:, None, :], 6, axis=-1)
        hl = _modulate(_rmsnorm(xl, norm1_g[layer]), mod_l[0], mod_l[1])
        hc = _modulate(_rmsnorm(xc, norm1_g[layer]), mod_c[0], mod_c[1])
        if layer % 2 == 0:
            e = layer // 2
            z = _even_mixer(hc, hl, ev_w_in[e], lru_conv_w[e], lru_conv_b[e], lru_wa[e], lru_ba[e],
                            lru_wx[e], lru_bx[e], lru_lambda[e], ret_log_gamma[e], cos, sin)
        else:
            o = layer // 2
            z = _odd_mixer(hc, hl, od_w_in[o], hg_lb[:, o], gdn_conv_w[o], gdn_a_log[o], gdn_dt_bias[o])
        xl = xl + mod_l[2] * (z[:, n_ctx:] @ mix_w_out[layer])
        hl2 = _modulate(_rmsnorm(xl, norm2_g[layer]), mod_l[3], mod_l[4])
        xl = xl + mod_l[5] * _sqrelu_mlp(hl2, mlp_w1[layer], mlp_w2[layer])
        if layer < DEPTH - 1:
            xc = xc + mod_c[2] * (z[:, :n_ctx] @ mix_w_out[layer])
            hc2 = _modulate(_rmsnorm(xc, norm2_g[layer]), mod_c[3], mod_c[4])
            xc = xc + mod_c[5] * _sqrelu_mlp(hc2, mlp_w1[layer], mlp_w2[layer])
    return _rmsnorm(xl, final_g)
```

```python
import contextlib
import os
import numpy as np
import concourse.bass as bass
import concourse.mybir as mybir
from concourse.bass_utils import run_bass_kernel_spmd

F32 = mybir.dt.float32
BF16 = mybir.dt.bfloat16
AF = mybir.ActivationFunctionType
ALU = mybir.AluOpType
AX = mybir.AxisListType

D = 1024
NCTX = 256
NLAT = 4096
T = NCTX + NLAT
DEPTH = 4
DFF = 4096
EV_IN = 2560
OD_IN = 4624
EPS = 1e-6

ENGS = ("sync", "tensor", "vector", "scalar", "gpsimd")
SEM_WRAP = 20000
DMA_SLOTS = 8
SAME_ENGINE_SYNC = True


class Buf:
    __slots__ = ("w", "r", "x")

    def __init__(self, excl=False):
        self.w = None
        self.r = []
        self.x = excl


class Op:
    __slots__ = ("eng", "fn", "deps", "dma", "signaled", "token")

    def __init__(self, eng, fn, dma):
        self.eng = eng
        self.fn = fn
        self.deps = []
        self.dma = dma
        self.signaled = False
        self.token = None


class TT:
    def __init__(self, t):
        self.t = t
        self.b = Buf()
        self.subs = {}

    def sub(self, key):
        b = self.subs.get(key)
        if b is None:
            b = Buf()
            self.subs[key] = b
        return b

    def __getitem__(self, k):
        return self.t[k]


def _bufs(lst):
    out = []
    for x in lst:
        if isinstance(x, Buf):
            out.append(x)
        elif isinstance(x, TT):
            out.append(x.b)
        else:
            raise TypeError(type(x))
    return out


class Prog:
    def __init__(self, nc):
        self.nc = nc
        self.ops = {e: [] for e in ENGS}
        self.dma_ops = {e: [] for e in ENGS}
        self.nops = 0
        self.barrier_deps = []
        self.barrier_pending = set()

    def barrier(self):
        deps = []
        for e in ENGS:
            for op in reversed(self.ops[e]):
                if not op.dma:
                    deps.append(op)
                    break
            deps += self.dma_ops[e][-DMA_SLOTS:]
        self.barrier_deps = deps
        self.barrier_pending = set(ENGS)

    def add(self, eng, fn, reads=(), writes=(), dma=False):
        op = Op(eng, fn, dma)
        reads = _bufs(reads)
        writes = _bufs(writes)
        xr = [b for b in reads if b.x]
        if xr:
            writes = writes + [b for b in xr if b not in writes]
            reads = [b for b in reads if not b.x]
        deps = []
        for b in reads:
            if b.w is not None:
                deps.append(b.w)
        for b in writes:
            if b.w is not None:
                deps.append(b.w)
            lastc = {}
            for r_ in b.r:
                if r_.dma:
                    deps.append(r_)
                else:
                    lastc[r_.eng] = r_
            deps.extend(lastc.values())
        for b in writes:
            b.w = op
            b.r = []
        for b in reads:
            b.r.append(op)
        if eng in self.barrier_pending:
            self.barrier_pending.discard(eng)
            deps.extend(self.barrier_deps)
        if dma:
            lst = self.dma_ops[eng]
            if len(lst) >= DMA_SLOTS:
                deps.append(lst[len(lst) - DMA_SLOTS])
            lst.append(op)
        seen = set()
        dd = []
        for d in deps:
            if id(d) in seen or d is op:
                continue
            seen.add(id(d))
            if (not d.dma) and (not dma) and d.eng == eng and (eng == "tensor" or not SAME_ENGINE_SYNC):
                continue
            dd.append(d)
            d.signaled = True
        op.deps = dd
        self.ops[eng].append(op)
        self.nops += 1
        return op

    def dma(self, eng, out, in_, reads=(), writes=(), **kw):
        return self.add(eng, lambda e: e.dma_start(out=out, in_=in_, **kw), reads, writes, dma=True)

    def emit(self, final_wait_ops=()):
        nc = self.nc
        nsem = {}
        for e in ENGS:
            cnt = 0
            dcnt = 0
            for op in self.ops[e]:
                if op.dma:
                    op.token = ("d", e, dcnt % DMA_SLOTS, 16 * (dcnt // DMA_SLOTS + 1))
                    dcnt += 1
                elif op.signaled:
                    op.token = ("c", e, cnt // SEM_WRAP, cnt % SEM_WRAP + 1)
                    cnt += 1
            nsem[e] = (cnt + SEM_WRAP - 1) // SEM_WRAP
        with contextlib.ExitStack() as st:
            sems = {}
            for e in ENGS:
                for k in range(nsem[e]):
                    sems[("c", e, k)] = st.enter_context(nc.semaphore(f"c_{e}_{k}"))
                if self.dma_ops[e]:
                    for k in range(DMA_SLOTS):
                        sems[("d", e, k)] = st.enter_context(nc.semaphore(f"d_{e}_{k}"))
            block = st.enter_context(nc.Block())

            def make(e):
                def body(eng):
                    waited = {}
                    for op in self.ops[e]:
                        for d in op.deps:
                            key = d.token[:3]
                            val = d.token[3]
                            if key[0] == "c":
                                kk = (key[0], key[1])
                                cur = waited.get(kk, (-1, 0))
                                if (key[2], val) <= cur:
                                    continue
                                waited[kk] = (key[2], val)
                            else:
                                if waited.get(key, 0) >= val:
                                    continue
                                waited[key] = val
                            eng.wait_ge(sems[key], val)
                        ins = op.fn(eng)
                        if op.token is not None:
                            ins.then_inc(sems[op.token[:3]], 16 if op.dma else 1)
                    if e == "sync":
                        for d in final_wait_ops:
                            eng.wait_ge(sems[d.token[:3]], d.token[3])
                return body

            for e in ENGS:
                if self.ops[e] or (e == "sync" and final_wait_ops):
                    getattr(block, e)(make(e))


class KB:
    def __init__(self, nc, debug=False):
        self.nc = nc
        self.P = Prog(nc)
        self.debug = debug
        self.dram = {}
        self.uid = 0
        self.rr = 0

    def din(self, name, shape, dt=F32):
        t = TT(self.nc.dram_tensor(name, list(shape), dt, kind="ExternalInput").ap())
        self.dram[name] = t
        return t

    def dout(self, name, shape, dt=F32):
        t = TT(self.nc.dram_tensor(name, list(shape), dt, kind="ExternalOutput").ap())
        self.dram[name] = t
        return t

    def dscr(self, name, shape, dt=F32, dbg=False):
        kind = "ExternalOutput" if (self.debug and dbg) else "Internal"
        t = TT(self.nc.dram_tensor(name, list(shape), dt, kind=kind).ap())
        self.dram[name] = t
        return t

    def sb(self, st, shape, dt=F32, name=None):
        self.uid += 1
        return TT(st.enter_context(self.nc.sbuf_tensor(f"{name or 's'}{self.uid}", list(shape), dt)))

    def ps(self, st, shape, dt=F32, name=None):
        self.uid += 1
        nfree = 512 if dt == F32 else 1024
        full = st.enter_context(self.nc.psum_tensor(f"{name or 'p'}{self.uid}", [128, nfree], dt))
        n = 1
        for x in shape[1:]:
            n *= x
        assert n <= nfree
        v = full[0:shape[0], 0:n]
        if len(shape) == 3:
            v = v.rearrange("p (a b) -> p a b", a=shape[1])
        t = TT(v)
        t.b = Buf(excl=True)
        return t

    def psq(self, bank, view):
        t = TT(view)
        t.b = bank.b
        return t

    @contextlib.contextmanager
    def phase(self):
        with contextlib.ExitStack() as st:
            yield st
        self.P.barrier()

    def capture(self, f, *args):
        saved = self.P.add
        lst = []

        def rec(eng, fn, reads=(), writes=(), dma=False):
            lst.append((eng, fn, list(reads), list(writes), dma))
            return None
        self.P.add = rec
        try:
            f(*args)
        finally:
            self.P.add = saved
        return lst

    def emit_interleaved(self, lists):
        idx = [0] * len(lists)
        live = True
        while live:
            live = False
            for k, l in enumerate(lists):
                if idx[k] < len(l):
                    self.P.add(*l[idx[k]])
                    idx[k] += 1
                    live = True

    def op(self, eng, fn, reads=(), writes=()):
        return self.P.add(eng, fn, reads, writes)

    def V(self, fn, reads=(), writes=()):
        return self.P.add("vector", fn, reads, writes)

    def A(self, fn, reads=(), writes=()):
        return self.P.add("scalar", fn, reads, writes)

    def G(self, fn, reads=(), writes=()):
        return self.P.add("gpsimd", fn, reads, writes)

    def PE(self, fn, reads=(), writes=()):
        return self.P.add("tensor", fn, reads, writes)

    def VA(self, fn, reads=(), writes=()):
        self.rr += 1
        return self.P.add("vector" if self.rr % 2 else "scalar", fn, reads, writes)

    def load(self, out, in_, reads=(), writes=(), eng="sync"):
        return self.P.dma(eng, out, in_, reads, writes)

    def store(self, out, in_, reads=(), writes=(), eng="gpsimd"):
        return self.P.dma(eng, out, in_, reads, writes)


def copy_any(out, in_):
    def f(e):
        if hasattr(e, "tensor_copy"):
            return e.tensor_copy(out=out, in_=in_)
        return e.activation(out=out, in_=in_, func=AF.Copy)
    return f


def token_groups(n):
    gs = []
    t = 0
    while t < NCTX:
        m = min(n, NCTX - t)
        gs.append((t, m, 1))
        t += m
    while t < T:
        m = min(n, T - t)
        gs.append((t, m, 0))
        t += m
    return gs


def phase_input_transpose(K, x_b, ctx_b, xT, ident):
    with K.phase() as st:
        idt = K.sb(st, [128, 128], F32, "ident")
        K.load(idt[:], ident[:, :], writes=[idt])
        xin = [K.sb(st, [128, D], F32, "xin") for _ in range(2)]
        stg = [K.sb(st, [128, 8, 128], F32, "xstg") for _ in range(2)]
        pss = [K.ps(st, [128, 4, 128], F32, "ptr") for _ in range(2)]
        for i in range(T // 128):
            xt = xin[i % 2]
            sg = stg[i % 2]
            src = ctx_b.t[i * 128:(i + 1) * 128, :] if i < 2 else x_b.t[(i - 2) * 128:(i - 1) * 128, :]
            K.load(xt[:], src, writes=[xt])
            for kg in range(2):
                ps = pss[kg]
                for j in range(4):
                    k = kg * 4 + j
                    K.PE(lambda e, ps=ps, j=j, k=k, xt=xt: e.transpose(ps[:, j, :], xt[:, k * 128:(k + 1) * 128], idt[:]),
                         [xt, idt], [ps])
                K.VA(copy_any(sg[:, kg * 4:(kg + 1) * 4, :], ps[:]), [ps], [sg])
            K.store(xT.t.rearrange("(k p) t -> p k t", p=128)[:, :, i * 128:(i + 1) * 128], sg[:],
                    reads=[sg], writes=[xT.sub(i // 2 if i < 2 else 1 + (i - 2) // 4)])


def phase_adaln(K, layer, c_b, c_ctx, ada_w, ada_b, norm1_g, norm2_g, mod):
    modT, G1, G2 = mod["modT"], mod["G1"], mod["G2"]
    with K.phase() as st:
        sT = K.sb(st, [128, 2, 8], F32, "sT")
        K.load(sT[:, 0, :], c_b.t[:, :], writes=[sT])
        K.load(sT[:, 1, :], c_ctx.t[:, :], writes=[sT])
        sS = K.sb(st, [128, 2, 8], F32, "sS")
        K.A(lambda e: e.activation(out=sS[:], in_=sT[:], func=AF.Silu), [sT], [sS])
        bT = K.sb(st, [128, 48], F32, "bT")
        K.load(bT[:], ada_b.t[layer], writes=[bT])
        gT = K.sb(st, [128, 2, 8], F32, "gT")
        K.load(gT[:, 0, :], norm1_g.t[layer], writes=[gT])
        K.load(gT[:, 1, :], norm2_g.t[layer], writes=[gT])
        wst = [K.sb(st, [128, 8, 1024], F32, "adaw") for _ in range(2)]
        ps = K.ps(st, [128, 48, 2], F32, "pmod")
        for j in range(6):
            w = wst[j % 2]
            for kh in range(2):
                K.load(w[:, kh * 4:(kh + 1) * 4, :],
                       ada_w.t[layer].rearrange("(k p) n -> p k n", p=128)[:, kh * 4:(kh + 1) * 4, j * 1024:(j + 1) * 1024],
                       writes=[w])
            for f in range(8):
                for k in range(8):
                    K.PE(lambda e, w=w, f=f, k=k, j=j: e.matmul(ps[:, j * 8 + f, :], lhsT=w[:, k, f * 128:(f + 1) * 128],
                                                               rhs=sS[:, :, k], start=(k == 0), stop=(k == 7)),
                         [w, sS], [ps])
        K.V(lambda e: e.tensor_tensor(out=modT[:], in0=ps[:], in1=bT[:].unsqueeze(2).broadcast_to([128, 48, 2]), op=ALU.add),
            [ps, bT], [modT])
        for (G, gi, j) in ((G1, 0, 1), (G2, 1, 4)):
            K.V(lambda e, G=G, gi=gi, j=j: e.scalar_tensor_tensor(
                out=G[:], in0=modT[:, j * 8:(j + 1) * 8, :], scalar=1.0,
                in1=gT[:, gi, :].unsqueeze(2).broadcast_to([128, 8, 2]), op0=ALU.add, op1=ALU.mult),
                [modT, gT], [G])


def load_weight_bf16(K, st, w_ap, R, F, name, stage):
    nk = R // 128
    dst = K.sb(st, [128, nk, F], BF16, name)
    wv = w_ap.rearrange("(k p) n -> p k n", p=128)
    i = 0
    for k0 in range(0, nk, 8):
        kn = min(8, nk - k0)
        SW = stage[0].t.shape[2]
        for c0 in range(0, F, SW):
            cn = min(SW, F - c0)
            sg = stage[i % len(stage)]
            K.load(sg[:, 0:kn, 0:cn], wv[:, k0:k0 + kn, c0:c0 + cn], writes=[sg])
            eng = "gpsimd" if i % 2 == 0 else "vector"
            K.op(eng, lambda e, sg=sg, k0=k0, kn=kn, c0=c0, cn=cn: e.tensor_copy(out=dst[:, k0:k0 + kn, c0:c0 + cn], in_=sg[:, 0:kn, 0:cn]),
                 [sg], [dst])
            i += 1
    return dst


def norm_modulate(K, xg, n, s, G, modT, jshift, ones, sq, psn, rstd, tmp, hT):
    K.A(lambda e: e.activation(out=sq[:, :, 0:n], in_=xg[:, :, 0:n], func=AF.Square), [xg], [sq])
    for k in range(8):
        K.PE(lambda e, k=k: e.matmul(psn[:, 0:n], lhsT=ones[:], rhs=sq[:, k, 0:n], start=(k == 0), stop=(k == 7)), [sq, ones], [psn])
    K.V(lambda e: e.tensor_scalar(out=rstd[:, 0:n], in0=psn[:, 0:n], scalar1=1.0 / D, scalar2=EPS, op0=ALU.mult, op1=ALU.add), [psn], [rstd])
    K.A(lambda e: e.activation(out=rstd[:, 0:n], in_=rstd[:, 0:n], func=AF.Sqrt), [rstd], [rstd])
    K.V(lambda e: e.reciprocal(out=rstd[:, 0:n], in_=rstd[:, 0:n]), [rstd], [rstd])
    for k in range(8):
        K.V(lambda e, k=k: e.scalar_tensor_tensor(out=tmp[:, k, 0:n], in0=xg[:, k, 0:n], scalar=G[:, k, s:s + 1], in1=rstd[:, 0:n],
                                                  op0=ALU.mult, op1=ALU.mult), [xg, G, rstd], [tmp])
        K.A(lambda e, k=k: e.activation(out=hT[:, k, 0:n], in_=tmp[:, k, 0:n], func=AF.Identity,
                                        bias=modT[:, jshift * 8 + k, s:s + 1], scale=1.0), [tmp, modT], [hT])


def phase_norm_inproj(K, xT, w_in_ap, F, projT, mod, ones_d, hdbg=None):
    NG = 512
    with K.phase() as st:
        ones = K.sb(st, [128, 128], F32, "ones")
        K.load(ones[:], ones_d[:, :], writes=[ones])
        stage = [K.sb(st, [128, 8, 256], F32, "wstage") for _ in range(2)]
        W = load_weight_bf16(K, st, w_in_ap, D, F, "win", stage)
        xgs = [K.sb(st, [128, 8, NG], F32, "xg") for _ in range(2)]
        sq = K.sb(st, [128, 8, NG], F32, "sq")
        tmp = K.sb(st, [128, 8, NG], F32, "tmp")
        rstd = K.sb(st, [128, NG], F32, "rstd")
        hTs = [K.sb(st, [128, 8, NG], BF16, "hT") for _ in range(2)]
        outs = [K.sb(st, [128, NG], F32, "pout") for _ in range(3)]
        psn = K.ps(st, [128, NG], F32, "psn")
        pss = [K.ps(st, [128, NG], F32, "psp") for _ in range(4)]
        xv = xT.t.rearrange("(k p) t -> p k t", p=128)
        groups = token_groups(NG)
        nft = (F + 127) // 128
        cntr = {"c": 0}

        def norm_list(gi):
            t0, n, s_ = groups[gi]
            xg, hT = xgs[gi % 2], hTs[gi % 2]
            K.load(xg[:, :, 0:n], xv[:, :, t0:t0 + n], reads=[xT.sub(gi)], writes=[xg])
            norm_modulate(K, xg, n, s_, mod["G1"], mod["modT"], 0, ones, sq, psn, rstd, tmp, hT)
            if hdbg is not None:
                hf = tmp
                K.V(lambda en: en.tensor_copy(out=hf[:, :, 0:n], in_=hT[:, :, 0:n]), [hT], [hf])
                K.store(hdbg.t.rearrange("(k p) t -> p k t", p=128)[:, :, t0:t0 + n], hf[:, :, 0:n], reads=[hf], writes=[hdbg])

        def mm_list(gi):
            t0, n, s_ = groups[gi]
            hT = hTs[gi % 2]
            for f in range(nft):
                m = min(128, F - f * 128)
                ps = pss[cntr["c"] % 4]
                ot = outs[cntr["c"] % 3]
                cntr["c"] += 1
                for k in range(8):
                    K.PE(lambda en, ps=ps, f=f, m=m, k=k: en.matmul(ps[0:m, 0:n], lhsT=W[:, k, f * 128:f * 128 + m], rhs=hT[:, k, 0:n],
                                                                 start=(k == 0), stop=(k == 7)), [W, hT], [ps])
                K.VA(copy_any(ot[0:m, 0:n], ps[0:m, 0:n]), [ps], [ot])
                K.store(projT.t[f * 128:f * 128 + m, t0:t0 + n], ot[0:m, 0:n], reads=[ot], writes=[projT.sub((f, gi))])

        norm_list(0)
        for gi in range(len(groups)):
            lists = [K.capture(mm_list, gi)]
            if gi + 1 < len(groups):
                lists.append(K.capture(norm_list, gi + 1))
            K.emit_interleaved(lists)


NP_GROUPS = token_groups(512)


def region(t0):
    for gi, (a, n, s) in enumerate(NP_GROUPS):
        if a <= t0 < a + n:
            return gi
    raise ValueError


def regions(tt, t0, n):
    return [tt.sub(gi) for gi, (a, m, s) in enumerate(NP_GROUPS) if a < t0 + n and t0 < a + m]


def phase_out_w1(K, xT, zT, aTd, w_out_ap, w1_ap, mod, ones_d, skip_ctx, xmid_dbg=None):
    NG = 256
    modT = mod["modT"]
    with K.phase() as st:
        ones = K.sb(st, [128, 128], F32, "ones")
        K.load(ones[:], ones_d[:, :], writes=[ones])
        stage = [K.sb(st, [128, 8, 512], F32, "wstage") for _ in range(2)]
        Wo = load_weight_bf16(K, st, w_out_ap, D, D, "wo", stage)
        W1 = load_weight_bf16(K, st, w1_ap, D, DFF, "w1", stage)
        zgs = [K.sb(st, [128, 8, NG], F32, "zg") for _ in range(2)]
        xgs = [K.sb(st, [128, 8, NG], F32, "xg") for _ in range(2)]
        zbs = [K.sb(st, [128, 8, NG], BF16, "zb") for _ in range(2)]
        sq = K.sb(st, [128, 8, NG], F32, "sq")
        rstd = K.sb(st, [128, NG], F32, "rstd")
        hTs = [K.sb(st, [128, 8, NG], BF16, "hT") for _ in range(2)]
        rl = [K.sb(st, [128, NG], F32, "rl") for _ in range(2)]
        aTs = [K.sb(st, [128, 4, NG], BF16, "aTs") for _ in range(2)]
        psn = K.ps(st, [128, NG], F32, "psn")
        psA = [K.ps(st, [128, NG], F32, "psA") for _ in range(2)]
        psC = [K.ps(st, [128, NG], F32, "psC") for _ in range(4)]
        xv = xT.t.rearrange("(k p) t -> p k t", p=128)
        zv = zT.t.rearrange("(k p) t -> p k t", p=128)
        av = aTd.t.rearrange("(k p) t -> p k t", p=128)
        groups = [g for g in token_groups(NG) if not (g[2] == 1 and skip_ctx)]
        cA = {"c": 0}
        cC = {"c": 0}

        def stage_ab(i):
            t0, n, s_ = groups[i]
            zg, xg, zb, hT = zgs[i % 2], xgs[i % 2], zbs[i % 2], hTs[i % 2]
            reg = region(t0)
            K.load(zg[:, :, 0:n], zv[:, :, t0:t0 + n], reads=[zT.sub(reg)], writes=[zg])
            K.load(xg[:, :, 0:n], xv[:, :, t0:t0 + n], reads=[xT.sub(reg)], writes=[xg])
            K.G(lambda en: en.tensor_copy(out=zb[:, :, 0:n], in_=zg[:, :, 0:n]), [zg], [zb])
            for f in range(8):
                ps = psA[cA["c"] % 2]
                cA["c"] += 1
                for k in range(8):
                    K.PE(lambda en, ps=ps, f=f, k=k: en.matmul(ps[:, 0:n], lhsT=Wo[:, k, f * 128:(f + 1) * 128], rhs=zb[:, k, 0:n],
                                                            start=(k == 0), stop=(k == 7)), [Wo, zb], [ps])
                K.V(lambda en, ps=ps, f=f: en.scalar_tensor_tensor(
                    out=xg[:, f, 0:n], in0=ps[:, 0:n], scalar=modT[:, 2 * 8 + f, s_:s_ + 1], in1=xg[:, f, 0:n], op0=ALU.mult, op1=ALU.add),
                    [ps, modT, xg], [xg])
            K.store(xv[:, :, t0:t0 + n], xg[:, :, 0:n], reads=[xg], writes=[xT.sub(reg)])
            if xmid_dbg is not None:
                K.store(xmid_dbg.t.rearrange("(k p) t -> p k t", p=128)[:, :, t0:t0 + n], xg[:, :, 0:n], reads=[xg], writes=[xmid_dbg])
            norm_modulate(K, xg, n, s_, mod["G2"], modT, 3, ones, sq, psn, rstd, zg, hT)

        def stage_c(i):
            t0, n, s_ = groups[i]
            hT = hTs[i % 2]
            reg = region(t0)
            for f in range(32):
                ps = psC[cC["c"] % 4]
                r = rl[cC["c"] % 2]
                cC["c"] += 1
                ast = aTs[(f // 4) % 2]
                for k in range(8):
                    K.PE(lambda en, ps=ps, f=f, k=k: en.matmul(ps[:, 0:n], lhsT=W1[:, k, f * 128:(f + 1) * 128], rhs=hT[:, k, 0:n],
                                                            start=(k == 0), stop=(k == 7)), [W1, hT], [ps])
                K.A(lambda en, ps=ps, r=r: en.activation(out=r[:, 0:n], in_=ps[:, 0:n], func=AF.Relu), [ps], [r])
                K.op("gpsimd" if f % 2 else "vector",
                     lambda en, r=r, f=f, ast=ast: en.tensor_tensor(out=ast[:, f % 4, 0:n], in0=r[:, 0:n], in1=r[:, 0:n], op=ALU.mult), [r], [ast])
                if f % 4 == 3:
                    K.store(av[:, f - 3:f + 1, t0:t0 + n], ast[:, :, 0:n], reads=[ast], writes=[aTd.sub(reg)])

        stage_ab(0)
        for i in range(len(groups)):
            lists = [K.capture(stage_c, i)]
            if i + 1 < len(groups):
                lists.append(K.capture(stage_ab, i + 1))
            K.emit_interleaved(lists)


def phase_w2(K, xT, aTd, w2_ap, mod, skip_ctx):
    NG = 512
    modT = mod["modT"]
    with K.phase() as st:
        stage = [K.sb(st, [128, 8, 512], F32, "wstage") for _ in range(2)]
        W2 = load_weight_bf16(K, st, w2_ap, DFF, D, "w2", stage)
        ags = [K.sb(st, [128, 32, NG], BF16, "ag") for _ in range(2)]
        xgs = [K.sb(st, [128, 8, NG], F32, "xg") for _ in range(2)]
        pss = [K.ps(st, [128, NG], F32, "psp") for _ in range(4)]
        xv = xT.t.rearrange("(k p) t -> p k t", p=128)
        av = aTd.t.rearrange("(k p) t -> p k t", p=128)
        cnt = 0
        for gi, (t0, n, s) in enumerate(NP_GROUPS):
            if s == 1 and skip_ctx:
                continue
            ag = ags[gi % 2]
            xg = xgs[gi % 2]
            for q in range(4):
                K.load(ag[:, q * 8:(q + 1) * 8, 0:n], av[:, q * 8:(q + 1) * 8, t0:t0 + n], reads=[aTd.sub(gi)], writes=[ag])
            K.load(xg[:, :, 0:n], xv[:, :, t0:t0 + n], reads=[xT.sub(gi)], writes=[xg])
            for f in range(8):
                ps = pss[cnt % 4]
                cnt += 1
                for k in range(32):
                    K.PE(lambda e, ps=ps, f=f, k=k, n=n, ag=ag: e.matmul(ps[:, 0:n], lhsT=W2[:, k, f * 128:(f + 1) * 128], rhs=ag[:, k, 0:n],
                                                                     start=(k == 0), stop=(k == 31)), [W2, ag], [ps])
                K.V(lambda e, ps=ps, f=f, xg=xg, n=n, s=s: e.scalar_tensor_tensor(
                    out=xg[:, f, 0:n], in0=ps[:, 0:n], scalar=modT[:, 5 * 8 + f, s:s + 1], in1=xg[:, f, 0:n], op0=ALU.mult, op1=ALU.add),
                    [ps, modT, xg], [xg])
            K.store(xv[:, :, t0:t0 + n], xg[:, :, 0:n], reads=[xg], writes=[xT.sub(gi)])


def phase_final(K, xT, final_g, ones_d, ident, out):
    NG = 512
    with K.phase() as st:
        ones = K.sb(st, [128, 128], F32, "ones")
        K.load(ones[:], ones_d[:, :], writes=[ones])
        idt = K.sb(st, [128, 128], F32, "ident")
        K.load(idt[:], ident[:, :], writes=[idt])
        gf = K.sb(st, [128, 8], F32, "gf")
        K.load(gf[:], final_g.t[:, :], writes=[gf])
        xgs = [K.sb(st, [128, 8, NG], F32, "xg") for _ in range(2)]
        sq = K.sb(st, [128, 8, NG], F32, "sq")
        rstd = K.sb(st, [128, NG], F32, "rstd")
        yT = K.sb(st, [128, 8, NG], F32, "yT")
        ots = [K.sb(st, [128, D], F32, "ot") for _ in range(2)]
        psn = K.ps(st, [128, NG], F32, "psn")
        pss = [K.ps(st, [128, 4, 128], F32, "ptr") for _ in range(2)]
        xv = xT.t.rearrange("(k p) t -> p k t", p=128)
        fin = []
        cnt = 0
        for gi, (t0, n, s) in enumerate(token_groups(NG)):
            if s == 1:
                continue
            xg = xgs[gi % 2]
            K.load(xg[:, :, 0:n], xv[:, :, t0:t0 + n], reads=[xT.sub(gi)], writes=[xg])
            K.A(lambda e, xg=xg: e.activation(out=sq[:], in_=xg[:], func=AF.Square), [xg], [sq])
            for k in range(8):
                K.PE(lambda e, k=k: e.matmul(psn[:], lhsT=ones[:], rhs=sq[:, k, :], start=(k == 0), stop=(k == 7)), [sq, ones], [psn])
            K.V(lambda e: e.tensor_scalar(out=rstd[:], in0=psn[:], scalar1=1.0 / D, scalar2=EPS, op0=ALU.mult, op1=ALU.add), [psn], [rstd])
            K.A(lambda e: e.activation(out=rstd[:], in_=rstd[:], func=AF.Sqrt), [rstd], [rstd])
            K.V(lambda e: e.reciprocal(out=rstd[:], in_=rstd[:]), [rstd], [rstd])
            for k in range(8):
                K.V(lambda e, k=k, xg=xg: e.scalar_tensor_tensor(out=yT[:, k, :], in0=xg[:, k, :], scalar=gf[:, k:k + 1], in1=rstd[:],
                                                                op0=ALU.mult, op1=ALU.mult), [xg, gf, rstd], [yT])
            for tt in range(n // 128):
                ot = ots[cnt % 2]
                for kg in range(2):
                    ps = pss[kg]
                    for j in range(4):
                        k = kg * 4 + j
                        K.PE(lambda e, ps=ps, j=j, k=k, tt=tt: e.transpose(ps[:, j, :], yT[:, k, tt * 128:(tt + 1) * 128], idt[:]), [yT, idt], [ps])
                    K.VA(copy_any(ot[:, kg * 512:(kg + 1) * 512], ps[:].rearrange("p a b -> p (a b)")), [ps], [ot])
                r0 = t0 - NCTX + tt * 128
                fin.append(K.store(out.t[r0:r0 + 128, :], ot[:], reads=[ot], writes=[out]))
                cnt += 1
    return fin


def _col(v, nchunk):
    return np.ascontiguousarray(np.asarray(v, np.float32).reshape(nchunk, 128).T)


def build_program(cfg):
    debug = cfg.get("debug", False)
    layers = cfg.get("layers", list(range(DEPTH)))
    nc = bass.Bass("TRN2", target_bir_lowering=False)
    K = KB(nc, debug=debug)
    I = {}
    I["x_b"] = K.din("x_b", [NLAT, D])
    I["ctx_b"] = K.din("ctx_b", [NCTX, D])
    I["c_b"] = K.din("c_b", [128, 8])
    I["c_ctx"] = K.din("c_ctx", [128, 8])
    I["ada_w"] = K.din("ada_w", [DEPTH, D, 6 * D])
    I["ada_b"] = K.din("ada_b", [DEPTH, 128, 48])
    I["norm1_g"] = K.din("norm1_g", [DEPTH, 128, 8])
    I["norm2_g"] = K.din("norm2_g", [DEPTH, 128, 8])
    I["mix_w_out"] = K.din("mix_w_out", [DEPTH, D, D])
    I["mlp_w1"] = K.din("mlp_w1", [DEPTH, D, DFF])
    I["mlp_w2"] = K.din("mlp_w2", [DEPTH, DFF, D])
    I["ev_w_in"] = K.din("ev_w_in", [2, D, EV_IN])
    I["od_w_in"] = K.din("od_w_in", [2, D, OD_IN])
    I["final_g"] = K.din("final_g", [128, 8])
    I["ident"] = K.din("ident", [128, 128])
    I["ones"] = K.din("ones", [128, 128])
    mixer_inputs(K, I)
    out = K.dout("out", [NLAT, D])
    xT = K.dscr("xT", [D, T], F32, dbg=True)
    projT = K.dscr("projT", [OD_IN, T], F32, dbg=True)
    zT = K.dscr("zT", [D, T], F32, dbg=True)
    aTd = K.dscr("aTd", [DFF, T], BF16)
    hdbg = K.dscr("hdbg", [D, T], F32, dbg=True) if debug else None
    xmid = K.dscr("xmid", [D, T], F32, dbg=True) if debug else None
    zin = K.din("zin", [D, T]) if cfg.get("z_from_input") else None
    S = mixer_scratch(K)

    with contextlib.ExitStack() as gst:
        mod = {"modT": K.sb(gst, [128, 48, 2], F32, "modT"), "G1": K.sb(gst, [128, 8, 2], F32, "G1"),
               "G2": K.sb(gst, [128, 8, 2], F32, "G2")}
        phase_input_transpose(K, I["x_b"], I["ctx_b"], xT, I["ident"].t)
        for layer in layers:
            last = (layer == DEPTH - 1)
            phase_adaln(K, layer, I["c_b"], I["c_ctx"], I["ada_w"], I["ada_b"], I["norm1_g"], I["norm2_g"], mod)
            if layer % 2 == 0:
                w_in, F = I["ev_w_in"].t[layer // 2], EV_IN
            else:
                w_in, F = I["od_w_in"].t[layer // 2], OD_IN
            phase_norm_inproj(K, xT, w_in, F, projT, mod, I["ones"].t, hdbg=hdbg if (debug and layer == layers[0]) else None)
            zsrc = zT
            if zin is not None:
                zsrc = zin
            elif layer % 2 == 0:
                even_mixer(K, layer // 2, I, S, projT, zT)
            else:
                odd_mixer(K, layer // 2, I, S, projT, zT)
            phase_out_w1(K, xT, zsrc, aTd, I["mix_w_out"].t[layer], I["mlp_w1"].t[layer], mod, I["ones"].t, skip_ctx=last,
                         xmid_dbg=xmid if (debug and layer == layers[0]) else None)
            phase_w2(K, xT, aTd, I["mlp_w2"].t[layer], mod, skip_ctx=last)
        fin = phase_final(K, xT, I["final_g"], I["ones"].t, I["ident"].t, out)
    K.P.emit(final_wait_ops=fin)
    return nc, K


def mixer_inputs(K, I):
    I["lru_cw"] = K.din("lru_cw", [2, 128, 4, 4])
    I["lru_cb"] = K.din("lru_cb", [2, 128, 4])
    I["lru_ba"] = K.din("lru_ba", [2, 128, 2, 4])
    I["lru_bx"] = K.din("lru_bx", [2, 128, 2, 4])
    I["lru_lam"] = K.din("lru_lam", [2, 128, 2, 4])
    I["lru_wa_bd"] = K.din("lru_wa_bd", [2, 2, 4, 128, 128])
    I["lru_wx_bd"] = K.din("lru_wx_bd", [2, 2, 4, 128, 128])
    I["ret_lg"] = K.din("ret_lg", [2, 128, 8])
    I["pos_cols"] = K.din("pos_cols", [128, 2])
    I["tri_f"] = K.din("tri_f", [128, 128])
    I["tri_b"] = K.din("tri_b", [128, 128])
    I["rope_cos"] = K.din("rope_cos", [128, NLAT])
    I["rope_sin"] = K.din("rope_sin", [128, NLAT])
    I["hg_logits"] = K.din("hg_logits", [128, 2, 2, 4])
    I["gdn_cw"] = K.din("gdn_cw", [2, 128, 12, 4])
    I["gdn_ab"] = K.din("gdn_ab", [2, 16, 2])
    I["gdn_sel"] = K.din("gdn_sel", [16, 16, 128])
    I["gdn_masks"] = K.din("gdn_masks", [4, 128, 128])


def mixer_scratch(K):
    S = {}
    S["O_f"] = K.dscr("O_f", [T, 512], F32, dbg=True)
    S["O_b"] = K.dscr("O_b", [T, 512], F32, dbg=True)
    S["GATES"] = K.dscr("GATES", [3, 16, T], F32, dbg=True)
    return S


def all_regions(tt, rows):
    return [tt.sub((r, gi)) for r in rows for gi in range(len(NP_GROUPS))]


def z_regions(zT):
    return [zT.sub(gi) for gi in range(len(NP_GROUPS))]


SEGS = ((0, NCTX), (NCTX, T))
TT512 = [(t0, min(512, T - t0)) for t0 in range(0, T, 512)]


def lru_phase(K, e, I, projT, zT):
    with K.phase() as st:
        cw = K.sb(st, [128, 4, 4], F32, "cw")
        cb = K.sb(st, [128, 4], F32, "cb")
        ba = K.sb(st, [128, 2, 4], F32, "ba")
        bx = K.sb(st, [128, 2, 4], F32, "bx")
        lam = K.sb(st, [128, 2, 4], F32, "lam")
        cl = K.sb(st, [128, 2, 4], F32, "cl")
        one = K.sb(st, [128, 1], F32, "one")
        K.load(cw[:], I["lru_cw"].t[e], writes=[cw])
        K.load(cb[:], I["lru_cb"].t[e], writes=[cb])
        K.load(ba[:], I["lru_ba"].t[e], writes=[ba])
        K.load(bx[:], I["lru_bx"].t[e], writes=[bx])
        K.load(lam[:], I["lru_lam"].t[e], writes=[lam])
        K.V(lambda en: en.memset(one[:], 1.0), [], [one])
        nba = K.sb(st, [128, 2, 4], F32, "nba")
        nbx = K.sb(st, [128, 2, 4], F32, "nbx")
        K.V(lambda en: en.tensor_scalar(out=nba[:], in0=ba[:], scalar1=-1.0, scalar2=None, op0=ALU.mult), [ba], [nba])
        K.V(lambda en: en.tensor_scalar(out=nbx[:], in0=bx[:], scalar1=-1.0, scalar2=None, op0=ALU.mult), [bx], [nbx])
        K.A(lambda en: en.activation(out=cl[:], in_=lam[:], func=AF.Exp, scale=-1.0), [lam], [cl])
        K.V(lambda en: en.tensor_scalar(out=cl[:], in0=cl[:], scalar1=1.0, scalar2=None, op0=ALU.add), [cl], [cl])
        K.A(lambda en: en.activation(out=cl[:], in_=cl[:], func=AF.Ln), [cl], [cl])
        K.V(lambda en: en.tensor_scalar(out=cl[:], in0=cl[:], scalar1=-8.0, scalar2=None, op0=ALU.mult), [cl], [cl])
        wst = K.sb(st, [128, 128], F32, "bdst")
        BD = {}
        for d in range(2):
            for ct in range(4):
                for nm, key in (("a", "lru_wa_bd"), ("x", "lru_wx_bd")):
                    w = K.sb(st, [128, 128], BF16, "bd")
                    K.load(wst[:], I[key].t[e, d, ct], writes=[wst])
                    K.V(lambda en, w=w: en.tensor_copy(out=w[:], in_=wst[:]), [wst], [w])
                    BD[(nm, d, ct)] = w
        B1 = K.sb(st, [128, T], F32, "B1")
        B2 = K.sb(st, [128, T], F32, "B2")
        B3 = K.sb(st, [128, T], F32, "B3")
        B4 = K.sb(st, [128, T], F32, "B4")
        B5 = K.sb(st, [128, T], F32, "B5")
        B6 = K.sb(st, [128, T], F32, "B6")
        ub = K.sb(st, [128, T], BF16, "ub")
        rt = [K.sb(st, [128, 512], F32, "rt") for _ in range(2)]
        it = [K.sb(st, [128, 512], F32, "it") for _ in range(2)]
        mt = [K.sb(st, [128, 512], F32, "mt") for _ in range(2)]
        psa = [K.ps(st, [128, 512], F32, "psa") for _ in range(2)]
        psx = [K.ps(st, [128, 512], F32, "psx") for _ in range(2)]
        for ct in range(4):
            x, u, Aa, INP, H0, H1 = B1, B2, B3, B4, B5, B6
            K.load(x[:], projT.t[ct * 128:(ct + 1) * 128, :], reads=all_regions(projT, [ct]), writes=[x])
            K.V(lambda en, ct=ct: en.tensor_scalar(out=u[:], in0=x[:], scalar1=cw[:, ct, 2:3], scalar2=cb[:, ct:ct + 1], op0=ALU.mult, op1=ALU.add),
                [x, cw, cb], [u])
            for (s0, s1) in SEGS:
                for (j, off) in ((0, -2), (1, -1), (3, 1)):
                    if off < 0:
                        oa, ob, ia, ib = s0 - off, s1, s0, s1 + off
                    else:
                        oa, ob, ia, ib = s0, s1 - off, s0 + off, s1
                    K.V(lambda en, ct=ct, j=j, oa=oa, ob=ob, ia=ia, ib=ib: en.scalar_tensor_tensor(
                        out=u[:, oa:ob], in0=x[:, ia:ib], scalar=cw[:, ct, j:j + 1], in1=u[:, oa:ob], op0=ALU.mult, op1=ALU.add), [x, u, cw], [u])
            K.A(lambda en: en.activation(out=ub[:], in_=u[:], func=AF.Copy), [u], [ub])
            for d in range(2):
                for ti, (t0, n) in enumerate(TT512):
                    pa, px = psa[ti % 2], psx[ti % 2]
                    r, ii, m = rt[ti % 2], it[ti % 2], mt[ti % 2]
                    K.PE(lambda en, pa=pa, d=d, ct=ct, t0=t0, n=n: en.matmul(pa[:, 0:n], lhsT=BD[("a", d, ct)][:], rhs=ub[:, t0:t0 + n], start=True, stop=True),
                         [BD[("a", d, ct)], ub], [pa])
                    K.PE(lambda en, px=px, d=d, ct=ct, t0=t0, n=n: en.matmul(px[:, 0:n], lhsT=BD[("x", d, ct)][:], rhs=ub[:, t0:t0 + n], start=True, stop=True),
                         [BD[("x", d, ct)], ub], [px])
                    K.A(lambda en, pa=pa, r=r, d=d, ct=ct, n=n: en.activation(out=r[:, 0:n], in_=pa[:, 0:n], func=AF.Exp, bias=nba[:, d, ct:ct + 1], scale=-1.0),
                        [pa, nba], [r])
                    K.G(lambda en, r=r, n=n: en.tensor_scalar(out=r[:, 0:n], in0=r[:, 0:n], scalar1=1.0, scalar2=None, op0=ALU.add), [r], [r])
                    K.V(lambda en, r=r, n=n: en.reciprocal(out=r[:, 0:n], in_=r[:, 0:n]), [r], [r])
                    K.A(lambda en, r=r, d=d, ct=ct, t0=t0, n=n: en.activation(out=Aa[:, t0:t0 + n], in_=r[:, 0:n], func=AF.Exp, scale=cl[:, d, ct:ct + 1]),
                        [r, cl], [Aa])
                    K.A(lambda en, px=px, ii=ii, d=d, ct=ct, n=n: en.activation(out=ii[:, 0:n], in_=px[:, 0:n], func=AF.Exp, bias=nbx[:, d, ct:ct + 1], scale=-1.0),
                        [px, nbx], [ii])
                    K.G(lambda en, ii=ii, n=n: en.tensor_scalar(out=ii[:, 0:n], in0=ii[:, 0:n], scalar1=1.0, scalar2=None, op0=ALU.add), [ii], [ii])
                    K.V(lambda en, ii=ii, n=n: en.reciprocal(out=ii[:, 0:n], in_=ii[:, 0:n]), [ii], [ii])
                    K.G(lambda en, m=m, t0=t0, n=n: en.tensor_tensor(out=m[:, 0:n], in0=Aa[:, t0:t0 + n], in1=Aa[:, t0:t0 + n], op=ALU.mult), [Aa], [m])
                    K.G(lambda en, m=m, n=n: en.tensor_scalar(out=m[:, 0:n], in0=m[:, 0:n], scalar1=-1.0, scalar2=1.0, op0=ALU.mult, op1=ALU.add), [m], [m])
                    K.A(lambda en, m=m, n=n: en.activation(out=m[:, 0:n], in_=m[:, 0:n], func=AF.Ln), [m], [m])
                    K.A(lambda en, m=m, n=n: en.activation(out=m[:, 0:n], in_=m[:, 0:n], func=AF.Exp, scale=0.5), [m], [m])
                    K.V(lambda en, m=m, ii=ii, n=n: en.tensor_tensor(out=m[:, 0:n], in0=m[:, 0:n], in1=ii[:, 0:n], op=ALU.mult), [m, ii], [m])
                    K.V(lambda en, m=m, t0=t0, n=n: en.tensor_tensor(out=INP[:, t0:t0 + n], in0=m[:, 0:n], in1=u[:, t0:t0 + n], op=ALU.mult), [m, u], [INP])
                if d == 0:
                    K.V(lambda en: en.tensor_tensor_scan(out=H0[:], data0=Aa[:], data1=INP[:], initial=0.0, op0=ALU.mult, op1=ALU.add), [Aa, INP], [H0])
                else:
                    K.V(lambda en: en.tensor_tensor_scan(out=H1[:, 0:NCTX][:, ::-1], data0=Aa[:, 0:NCTX][:, ::-1], data1=INP[:, 0:NCTX][:, ::-1],
                                                         initial=0.0, op0=ALU.mult, op1=ALU.add), [Aa, INP], [H1])
                    K.V(lambda en: en.tensor_tensor_scan(out=H1[:, NCTX:T][:, ::-1], data0=Aa[:, NCTX:T][:, ::-1], data1=INP[:, NCTX:T][:, ::-1],
                                                         initial=H1[:, 0:1], op0=ALU.mult, op1=ALU.add), [Aa, INP, H1], [H1])
            K.G(lambda en: en.tensor_tensor(out=H0[:], in0=H0[:], in1=H1[:], op=ALU.add), [H0, H1], [H0])
            g, tq = B1, B3
            K.load(g[:], projT.t[512 + ct * 128:512 + (ct + 1) * 128, :], reads=all_regions(projT, [4 + ct]), writes=[g])
            K.A(lambda en: en.activation(out=tq[:], in_=g[:], func=AF.Square), [g], [tq])
            K.V(lambda en: en.tensor_scalar(out=tq[:], in0=tq[:], scalar1=0.044715, scalar2=1.0, op0=ALU.mult, op1=ALU.add), [tq], [tq])
            K.G(lambda en: en.tensor_tensor(out=tq[:], in0=tq[:], in1=g[:], op=ALU.mult), [tq, g], [tq])
            K.A(lambda en: en.activation(out=tq[:], in_=tq[:], func=AF.Sigmoid, scale=1.5957691216057308), [tq], [tq])
            K.V(lambda en: en.tensor_tensor(out=tq[:], in0=tq[:], in1=g[:], op=ALU.mult), [tq, g], [tq])
            K.V(lambda en: en.tensor_tensor(out=tq[:], in0=tq[:], in1=H0[:], op=ALU.mult), [tq, H0], [tq])
            K.store(zT.t[ct * 128:(ct + 1) * 128, :], tq[:], reads=[tq], writes=z_regions(zT))


def even_mixer(K, e, I, S, projT, zT):
    lru_phase(K, e, I, projT, zT)
    retention_phase(K, e, I, S, projT, zT)


def retention_phase(K, e, I, S, projT, zT):
    O = [S["O_f"], S["O_b"]]
    NCH = T // 128
    with K.phase() as st:
        idf = K.sb(st, [128, 128], F32, "idf")
        idb = K.sb(st, [128, 128], BF16, "idb")
        K.load(idf[:], I["ident"].t[:, :], writes=[idf])
        K.V(lambda en: en.tensor_copy(out=idb[:], in_=idf[:]), [idf], [idb])
        lg = K.sb(st, [128, 8], F32, "lg")
        pos = K.sb(st, [128, 2], F32, "pos")
        K.load(lg[:], I["ret_lg"].t[e], writes=[lg])
        K.load(pos[:], I["pos_cols"].t[:, :], writes=[pos])
        tri = [K.sb(st, [128, 128], F32, "tri") for _ in range(2)]
        K.load(tri[0][:], I["tri_f"].t[:, :], writes=[tri[0]])
        K.load(tri[1][:], I["tri_b"].t[:, :], writes=[tri[1]])
        t8 = K.sb(st, [128, 8], F32, "t8")
        qd8 = K.sb(st, [128, 8], F32, "qd8")
        gi8 = K.sb(st, [128, 8], F32, "gi8")
        cd8 = K.sb(st, [128, 8], F32, "cd8")
        for d in range(2):
            K.V(lambda en, d=d: en.tensor_scalar(out=t8[:, d * 4:(d + 1) * 4], in0=lg[:, d * 4:(d + 1) * 4], scalar1=pos[:, d:d + 1], scalar2=None, op0=ALU.mult),
                [lg, pos], [t8])
        K.A(lambda en: en.activation(out=qd8[:], in_=t8[:], func=AF.Exp), [t8], [qd8])
        K.A(lambda en: en.activation(out=gi8[:], in_=t8[:], func=AF.Exp, scale=-1.0), [t8], [gi8])
        K.A(lambda en: en.activation(out=cd8[:], in_=lg[:], func=AF.Exp, scale=128.0), [lg], [cd8])
        GINV, QDEC, CD = [], [], []
        for d in range(2):
            for (lst, src) in ((GINV, gi8), (QDEC, qd8), (CD, cd8)):
                x = K.sb(st, [128, 4, 128], F32, "mul")
                K.V(lambda en, x=x, src=src, d=d: en.tensor_copy(out=x[:], in_=src[:, d * 4:(d + 1) * 4].unsqueeze(2).broadcast_to([128, 4, 128])), [src], [x])
                lst.append(x)
        qk = [K.sb(st, [128, 2, T], BF16, "qb"), K.sb(st, [128, 2, T], BF16, "kb")]
        X = K.sb(st, [128, T], F32, "ropex")
        SW = K.sb(st, [128, NLAT], F32, "ropesw")
        COS = K.sb(st, [128, NLAT], F32, "cos")
        SIN = K.sb(st, [128, NLAT], F32, "sin")
        K.load(COS[:], I["rope_cos"].t[:, :], writes=[COS])
        K.load(SIN[:], I["rope_sin"].t[:, :], writes=[SIN])
        for which in range(2):
            for p in range(2):
                ft = 8 + which * 2 + p
                K.load(X[:], projT.t[ft * 128:(ft + 1) * 128, :], reads=all_regions(projT, [ft]), writes=[X])
                for bi, (dst, src) in enumerate(((0, 32), (32, 0), (64, 96), (96, 64))):
                    K.op("vector" if bi % 2 == 0 else "scalar", copy_any(SW[dst:dst + 32, :], X[src:src + 32, NCTX:T]), [X], [SW])
                K.V(lambda en: en.tensor_tensor(out=X[:, NCTX:T], in0=X[:, NCTX:T], in1=COS[:], op=ALU.mult), [X, COS], [X])
                K.G(lambda en: en.tensor_tensor(out=SW[:], in0=SW[:], in1=SIN[:], op=ALU.mult), [SW, SIN], [SW])
                K.V(lambda en: en.tensor_tensor(out=X[:, NCTX:T], in0=X[:, NCTX:T], in1=SW[:], op=ALU.add), [X, SW], [X])
                K.A(lambda en, which=which, p=p: en.activation(out=qk[which][:, p, :], in_=X[:], func=AF.Copy, scale=(0.125 if which else 1.0)),
                    [X], [qk[which]])
        qb, kb = qk
        qz = K.sb(st, [128, 4, T], BF16, "qz")
        K.G(lambda en: en.memset(qz[:], 0.0), [], [qz])
        for h in range(4):
            p, b = h // 2, (h % 2) * 64
            K.op("vector" if h % 2 == 0 else "scalar", copy_any(qz[b:b + 64, h, :], qb[b:b + 64, p, :]), [qb], [qz])
        vin = [K.sb(st, [128, 4, 128], F32, "vin") for _ in range(2)]
        VD = [K.sb(st, [128, 4, 128], BF16, "VD") for _ in range(2)]
        ktok = [K.sb(st, [128, 2, 128], BF16, "ktok") for _ in range(2)]
        PT = [K.sb(st, [128, 4, 128], BF16, "PT") for _ in range(2)]
        osb = [K.sb(st, [128, 4, 128], F32, "osb") for _ in range(2)]
        Sf = [K.sb(st, [128, 4, 128], F32, "Sf") for _ in range(2)]
        Sb = [K.sb(st, [128, 4, 128], BF16, "Sb") for _ in range(2)]
        tS = [K.sb(st, [128, 4, 128], F32, "tS") for _ in range(2)]
        ps_v = K.ps(st, [128, 4, 128], F32, "ps_v")
        ps_k = K.ps(st, [128, 2, 128], BF16, "ps_k")
        ps_st = [K.ps(st, [128, 4, 128], F32, "ps_st") for _ in range(2)]
        ps_o = [K.ps(st, [128, 4, 128], F32, "ps_o") for _ in range(2)]
        ps_s = [K.ps(st, [128, 4, 128], F32, "ps_s") for _ in range(2)]
        for d in range(2):
            K.V(lambda en, d=d: en.memset(Sf[d][:], 0.0), [], [Sf[d]])
            K.V(lambda en, d=d: en.memset(Sb[d][:], 0.0), [], [Sb[d]])
        order = [list(range(NCH)), [1, 0] + list(range(NCH - 1, 1, -1))]
        vrows = projT.t[1536:2048, :].rearrange("(h p) t -> p h t", p=128)
        for step in range(NCH):
            for d in range(2):
                n = order[d][step]
                c0 = n * 128
                reg = region(c0)
                K.load(vin[d][:], vrows[:, :, c0:c0 + 128], reads=[projT.sub((12 + h, reg)) for h in range(4)], writes=[vin[d]])
                for h in range(4):
                    K.PE(lambda en, d=d, h=h: en.transpose(ps_v[:, h, :], vin[d][:, h, :], idf[:]), [vin[d], idf], [ps_v])
                K.V(lambda en, d=d: en.tensor_tensor(out=VD[d][:], in0=ps_v[:], in1=GINV[d][:], op=ALU.mult), [ps_v, GINV[d]], [VD[d]])
                for p in range(2):
                    K.PE(lambda en, p=p, c0=c0: en.transpose(ps_k[:, p, :], kb[:, p, c0:c0 + 128], idb[:]), [kb, idb], [ps_k])
                K.A(lambda en, d=d: en.activation(out=ktok[d][:], in_=ps_k[:], func=AF.Copy), [ps_k], [ktok[d]])
                for h in range(4):
                    p, b = h // 2, (h % 2) * 64
                    K.PE(lambda en, d=d, h=h, p=p, b=b, c0=c0: en.matmul(ps_st[d][:, h, :], lhsT=kb[:, p, c0:c0 + 128], rhs=qz[:, h, c0:c0 + 128],
                                                                     start=True, stop=True), [kb, qz], [ps_st[d]])
                K.V(lambda en, d=d: en.tensor_tensor(out=PT[d][:], in0=ps_st[d][:], in1=tri[d][:].unsqueeze(1).broadcast_to([128, 4, 128]), op=ALU.mult),
                    [ps_st[d], tri[d]], [PT[d]])
                for h in range(4):
                    p, b = h // 2, (h % 2) * 64
                    K.PE(lambda en, d=d, h=h: en.matmul(ps_o[d][:, h, :], lhsT=PT[d][:, h, :], rhs=VD[d][:, h, :], start=True, stop=False),
                         [PT[d], VD[d]], [ps_o[d]])
                    K.PE(lambda en, d=d, h=h, p=p, b=b, c0=c0: en.matmul(ps_o[d][:, h, :], lhsT=qz[:, h, c0:c0 + 128], rhs=Sb[d][:, h, :],
                                                                     start=False, stop=True), [qz, Sb[d]], [ps_o[d]])
                K.V(lambda en, d=d: en.tensor_tensor(out=osb[d][:], in0=ps_o[d][:], in1=QDEC[d][:], op=ALU.mult), [ps_o[d], QDEC[d]], [osb[d]])
                K.store(O[d].t[c0:c0 + 128, :], osb[d][:].rearrange("p h v -> p (h v)"), reads=[osb[d]], writes=[O[d].sub(n)])
                for h in range(4):
                    p = h // 2
                    K.PE(lambda en, d=d, h=h, p=p: en.matmul(ps_s[d][:, h, :], lhsT=ktok[d][:, p, :], rhs=VD[d][:, h, :], start=True, stop=True),
                         [ktok[d], VD[d]], [ps_s[d]])
                K.V(lambda en, d=d: en.tensor_tensor(out=tS[d][:], in0=ps_s[d][:], in1=Sf[d][:], op=ALU.add), [ps_s[d], Sf[d]], [tS[d]])
                K.G(lambda en, d=d: en.tensor_tensor(out=Sf[d][:], in0=tS[d][:], in1=CD[d][:], op=ALU.mult), [tS[d], CD[d]], [Sf[d]])
                K.A(lambda en, d=d: en.activation(out=Sb[d][:], in_=Sf[d][:], func=AF.Copy), [Sf[d]], [Sb[d]])
    head_norm_epilogue(K, I, O, projT, zT, gate_ft0=16, out_row0=512, center=True)


def head_norm_epilogue(K, I, O, projT, zT, gate_ft0, out_row0, center):
    NCH = T // 128
    with K.phase() as st:
        idf = K.sb(st, [128, 128], F32, "idf")
        K.load(idf[:], I["ident"].t[:, :], writes=[idf])
        zrows = projT.t[gate_ft0 * 128:(gate_ft0 + 4) * 128, :].rearrange("(h p) t -> p h t", p=128)
        zout = zT.t[out_row0:out_row0 + 512, :].rearrange("(h p) t -> p h t", p=128)
        ofs = [K.sb(st, [128, 4, 128], F32, "of") for _ in range(2)]
        obs = [K.sb(st, [128, 4, 128], F32, "ob") for _ in range(2)]
        zgs = [K.sb(st, [128, 4, 128], F32, "zg") for _ in range(2)]
        ocs = [K.sb(st, [128, 4, 128], F32, "oc") for _ in range(2)]
        sqs = [K.sb(st, [128, 4, 128], F32, "sq") for _ in range(2)]
        st4s = [K.sb(st, [128, 4], F32, "st4") for _ in range(2)]
        rs4s = [K.sb(st, [128, 4], F32, "rs4") for _ in range(2)]
        yo = [K.sb(st, [128, 4, 128], F32, "yo") for _ in range(2)]
        ps_t = [K.ps(st, [128, 4, 128], F32, "ps_t") for _ in range(2)]

        def chunk(n):
            c0 = n * 128
            reg = region(c0)
            of, ob, zg, y, pt = ofs[n % 2], obs[n % 2], zgs[n % 2], yo[n % 2], ps_t[n % 2]
            oc, sq, st4, rs4 = ocs[n % 2], sqs[n % 2], st4s[n % 2], rs4s[n % 2]
            K.load(of[:].rearrange("p h v -> p (h v)"), O[0].t[c0:c0 + 128, :], reads=[O[0].sub(n)], writes=[of])
            K.load(ob[:].rearrange("p h v -> p (h v)"), O[1].t[c0:c0 + 128, :], reads=[O[1].sub(n)], writes=[ob])
            K.load(zg[:], zrows[:, :, c0:c0 + 128], reads=[projT.sub((gate_ft0 + h, reg)) for h in range(4)], writes=[zg])
            if center:
                K.V(lambda en: en.tensor_tensor(out=of[:], in0=of[:], in1=ob[:], op=ALU.add), [of, ob], [of])
                K.V(lambda en: en.tensor_reduce(out=st4[:], in_=of[:], axis=AX.X, op=ALU.add), [of], [st4])
                K.V(lambda en: en.tensor_scalar(out=st4[:], in0=st4[:], scalar1=-1.0 / 128, scalar2=None, op0=ALU.mult), [st4], [st4])
                K.V(lambda en: en.tensor_tensor(out=oc[:], in0=of[:], in1=st4[:].unsqueeze(2).broadcast_to([128, 4, 128]), op=ALU.add), [of, st4], [oc])
            else:
                K.V(lambda en: en.tensor_tensor(out=oc[:], in0=of[:], in1=ob[:], op=ALU.add), [of, ob], [oc])
            K.G(lambda en: en.tensor_tensor(out=sq[:], in0=oc[:], in1=oc[:], op=ALU.mult), [oc], [sq])
            K.V(lambda en: en.tensor_reduce(out=rs4[:], in_=sq[:], axis=AX.X, op=ALU.add), [sq], [rs4])
            K.V(lambda en: en.tensor_scalar(out=rs4[:], in0=rs4[:], scalar1=1.0 / 128, scalar2=EPS, op0=ALU.mult, op1=ALU.add), [rs4], [rs4])
            K.A(lambda en: en.activation(out=rs4[:], in_=rs4[:], func=AF.Sqrt), [rs4], [rs4])
            K.V(lambda en: en.reciprocal(out=rs4[:], in_=rs4[:]), [rs4], [rs4])
            K.G(lambda en: en.tensor_tensor(out=oc[:], in0=oc[:], in1=rs4[:].unsqueeze(2).broadcast_to([128, 4, 128]), op=ALU.mult), [oc, rs4], [oc])
            for h in range(4):
                K.PE(lambda en, h=h: en.transpose(pt[:, h, :], oc[:, h, :], idf[:]), [oc, idf], [pt])
            K.A(lambda en: en.activation(out=zg[:], in_=zg[:], func=AF.Silu), [zg], [zg])
            K.V(lambda en: en.tensor_tensor(out=y[:], in0=pt[:], in1=zg[:], op=ALU.mult), [pt, zg], [y])
            K.store(zout[:, :, c0:c0 + 128], y[:], reads=[y], writes=[zT.sub(reg)])

        for n in range(0, NCH, 2):
            K.emit_interleaved([K.capture(chunk, n), K.capture(chunk, n + 1)])


GC = 32
NGC = T // GC


def gla_phase(K, o, I, S, projT, zT):
    O = [S["O_f"], S["O_b"]]
    with K.phase() as st:
        idf = K.sb(st, [128, 128], F32, "idf")
        idb = K.sb(st, [128, 128], BF16, "idb")
        K.load(idf[:], I["ident"].t[:, :], writes=[idf])
        K.V(lambda en: en.tensor_copy(out=idb[:], in_=idf[:]), [idf], [idb])
        tri = [K.sb(st, [128, 128], F32, "tri") for _ in range(2)]
        K.load(tri[0][:], I["tri_f"].t[:, :], writes=[tri[0]])
        K.load(tri[1][:], I["tri_b"].t[:, :], writes=[tri[1]])
        lgt = K.sb(st, [128, 2, 2, 4], F32, "lgt")
        lb = K.sb(st, [128, 2, 4], F32, "lb")
        oml = K.sb(st, [128, 2, 4], F32, "oml")
        K.load(lgt[:], I["hg_logits"].t[:, :, :, :], writes=[lgt])
        if o == 0:
            K.V(lambda en: en.memset(lb[:], 0.0), [], [lb])
        else:
            K.V(lambda en: en.tensor_tensor(out=lb[:], in0=lgt[:, :, 1, :], in1=lgt[:, :, 0, :], op=ALU.subtract), [lgt], [lb])
            K.A(lambda en: en.activation(out=lb[:], in_=lb[:], func=AF.Sigmoid), [lb], [lb])
        K.V(lambda en: en.tensor_scalar(out=oml[:], in0=lb[:], scalar1=-1.0, scalar2=1.0, op0=ALU.mult, op1=ALU.add), [lb], [oml])
        MASKX = K.sb(st, [128, T + GC], F32, "mask")
        K.G(lambda en: en.memset(MASKX[:], 1.0), [], [MASKX])
        K.G(lambda en: en.memset(MASKX[:, 0::GC], 0.0), [], [MASKX])
        Q = K.sb(st, [128, T], F32, "Q")
        F1 = K.sb(st, [128, T], F32, "F1")
        F2 = K.sb(st, [128, T], F32, "F2")
        F3 = K.sb(st, [128, T], F32, "F3")
        QTs = [[K.sb(st, [128, T], BF16, "QT") for _ in range(2)] for _ in range(2)]
        KTs = [[K.sb(st, [128, T], BF16, "KT") for _ in range(2)] for _ in range(2)]
        Vbs = [K.sb(st, [128, T], BF16, "Vb") for _ in range(2)]
        GLs = [[K.sb(st, [128, NGC], F32, "GL") for _ in range(2)] for _ in range(2)]
        TR = [[K.sb(st, [GC, 2, 128], BF16, "TR") for _ in range(2)] for _ in range(2)]
        PT = [[K.sb(st, [GC, GC], BF16, "PT") for _ in range(2)] for _ in range(2)]
        OS = [[K.sb(st, [GC, 8, 128], F32, "OS") for _ in range(2)] for _ in range(2)]
        Sf = [K.sb(st, [128, 128], F32, "Sf") for _ in range(2)]
        Sb = [K.sb(st, [128, 128], BF16, "Sb") for _ in range(2)]
        ps_tr = [K.ps(st, [GC, 2, 128], BF16, "ps_tr") for _ in range(2)]
        ps_st = [K.ps(st, [GC, GC], F32, "ps_st") for _ in range(2)]
        ps_o = [K.ps(st, [GC, 128], F32, "ps_o") for _ in range(2)]
        ps_s = [K.ps(st, [128, 128], F32, "ps_s") for _ in range(2)]
        nctx_c = NCTX // GC
        order = [list(range(NGC)), list(range(nctx_c - 1, -1, -1)) + list(range(NGC - 1, nctx_c - 1, -1))]

        def prep_head(h):
            QT, KT, Vb, GL = QTs[h % 2], KTs[h % 2], Vbs[h % 2], GLs[h % 2]
            K.load(Q[:], projT.t[h * 128:(h + 1) * 128, :], reads=all_regions(projT, [h]), writes=[Q])
            K.A(lambda en: en.activation(out=Q[:], in_=Q[:], func=AF.Silu), [Q], [Q])
            for d in range(2):
                ft = 4 + 4 * d + h
                K.load(F1[:], projT.t[ft * 128:(ft + 1) * 128, :], reads=all_regions(projT, [ft]), writes=[F1])
                K.A(lambda en: en.activation(out=F1[:], in_=F1[:], func=AF.Sigmoid), [F1], [F1])
                K.V(lambda en, d=d: en.tensor_scalar(out=F1[:], in0=F1[:], scalar1=oml[:, d, h:h + 1], scalar2=lb[:, d, h:h + 1], op0=ALU.mult, op1=ALU.add),
                    [F1, oml, lb], [F1])
                K.A(lambda en: en.activation(out=F2[:], in_=F1[:], func=AF.Ln), [F1], [F2])
                if d == 0:
                    K.V(lambda en: en.tensor_tensor_scan(out=F3[:], data0=MASKX[:, 0:T], data1=F2[:], initial=0.0, op0=ALU.mult, op1=ALU.add), [MASKX, F2], [F3])
                else:
                    K.V(lambda en: en.tensor_tensor_scan(out=F3[:, ::-1], data0=MASKX[:, 1:T + 1][:, ::-1], data1=F2[:, ::-1], initial=0.0,
                                                         op0=ALU.mult, op1=ALU.add), [MASKX, F2], [F3])
                K.A(lambda en: en.activation(out=F2[:], in_=F3[:], func=AF.Exp), [F3], [F2])
                K.V(lambda en, d=d: en.tensor_tensor(out=QT[d][:], in0=Q[:], in1=F2[:], op=ALU.mult), [Q, F2], [QT[d]])
                K.G(lambda en, d=d: en.tensor_copy(out=GL[d][:], in_=F2[:, (GC - 1 if d == 0 else 0)::GC]), [F2], [GL[d]])
                K.A(lambda en: en.activation(out=F2[:], in_=F3[:], func=AF.Exp, scale=-1.0), [F3], [F2])
                K.G(lambda en: en.tensor_scalar(out=F1[:], in0=F1[:], scalar1=-1.0, scalar2=1.0, op0=ALU.mult, op1=ALU.add), [F1], [F1])
                K.G(lambda en, d=d: en.tensor_tensor(out=KT[d][:], in0=F1[:], in1=F2[:], op=ALU.mult), [F1, F2], [KT[d]])
            ft = 12 + h
            K.load(F3[:], projT.t[ft * 128:(ft + 1) * 128, :], reads=all_regions(projT, [ft]), writes=[F3])
            K.G(lambda en: en.tensor_copy(out=Vb[:], in_=F3[:]), [F3], [Vb])

        def intra(d, step, h):
            QT, KT, Vb = QTs[h % 2], KTs[h % 2], Vbs[h % 2]
            n = order[d][step]
            c0 = n * GC
            tr, pt = TR[d][step % 2], PT[d][step % 2]
            K.PE(lambda en: en.transpose(ps_tr[d][:, 0, :], Vb[:, c0:c0 + GC], idb[:]), [Vb, idb], [ps_tr[d]])
            K.PE(lambda en: en.transpose(ps_tr[d][:, 1, :], KT[d][:, c0:c0 + GC], idb[:]), [KT[d], idb], [ps_tr[d]])
            K.A(lambda en: en.activation(out=tr[:], in_=ps_tr[d][:], func=AF.Copy), [ps_tr[d]], [tr])
            K.PE(lambda en: en.matmul(ps_st[d][:], lhsT=KT[d][:, c0:c0 + GC], rhs=QT[d][:, c0:c0 + GC], start=True, stop=True),
                 [KT[d], QT[d]], [ps_st[d]])
            K.V(lambda en: en.tensor_tensor(out=pt[:], in0=ps_st[d][:], in1=tri[d][0:GC, 0:GC], op=ALU.mult), [ps_st[d], tri[d]], [pt])

        def inter(d, step, h):
            QT, GL = QTs[h % 2], GLs[h % 2]
            n = order[d][step]
            c0 = n * GC
            tr, pt = TR[d][step % 2], PT[d][step % 2]
            os_ = OS[d][(step // 8) % 2]
            K.PE(lambda en: en.matmul(ps_o[d][:], lhsT=pt[:], rhs=tr[:, 0, :], start=True, stop=False), [pt, tr], [ps_o[d]])
            K.PE(lambda en: en.matmul(ps_o[d][:], lhsT=QT[d][:, c0:c0 + GC], rhs=Sb[d][:], start=False, stop=True), [QT[d], Sb[d]], [ps_o[d]])
            K.PE(lambda en: en.matmul(ps_s[d][:], lhsT=tr[:, 1, :], rhs=tr[:, 0, :], start=True, stop=True), [tr], [ps_s[d]])
            K.V(lambda en: en.tensor_copy(out=os_[:, n % 8, :], in_=ps_o[d][:]), [ps_o[d]], [os_])
            if step == 0:
                K.V(lambda en: en.tensor_copy(out=Sf[d][:], in_=ps_s[d][:]), [ps_s[d]], [Sf[d]])
            else:
                np_ = order[d][step - 1]
                K.V(lambda en: en.scalar_tensor_tensor(out=Sf[d][:], in0=Sf[d][:], scalar=GL[d][:, np_:np_ + 1], in1=ps_s[d][:],
                                                       op0=ALU.mult, op1=ALU.add), [Sf[d], GL[d], ps_s[d]], [Sf[d]])
            K.G(lambda en: en.tensor_scalar(out=Sb[d][:], in0=Sf[d][:], scalar1=GL[d][:, n:n + 1], scalar2=None, op0=ALU.mult), [Sf[d], GL[d]], [Sb[d]])
            if step % 8 == 7:
                n0 = (n // 8) * 8
                K.store(O[d].t[n0 * GC:(n0 + 8) * GC, h * 128:(h + 1) * 128].rearrange("(c p) v -> p c v", p=GC), os_[:],
                        reads=[os_], writes=[O[d].sub(n0 * GC // 128), O[d].sub(n0 * GC // 128 + 1)])

        prep_head(0)
        for h in range(4):
            nxt = K.capture(prep_head, h + 1) if h + 1 < 4 else []
            stride = (len(nxt) + NGC - 2) // (NGC - 1) if nxt else 0
            for d in range(2):
                K.V(lambda en, d=d: en.memset(Sb[d][:], 0.0), [], [Sb[d]])
            K.emit_interleaved([K.capture(intra, d, 0, h) for d in range(2)])
            for step in range(NGC):
                lists = []
                if step + 1 < NGC:
                    lists += [K.capture(intra, d, step + 1, h) for d in range(2)]
                lists += [K.capture(inter, d, step, h) for d in range(2)]
                if nxt:
                    lists.append(nxt[step * stride:(step + 1) * stride])
                K.emit_interleaved(lists)
            if nxt and NGC * stride < len(nxt):
                K.emit_interleaved([nxt[NGC * stride:]])
    head_norm_epilogue(K, I, O, projT, zT, gate_ft0=16, out_row0=0, center=False)


def odd_mixer(K, o, I, S, projT, zT):
    if not os.environ.get("GDN_DBG"):
        gla_phase(K, o, I, S, projT, zT)
    gdn_phase(K, o, I, S, projT, zT)


def _quarter(bank, q):
    v = bank.t
    if len(v.shape) == 3:
        return v[:, q, :]
    return v[:, q * 128:(q + 1) * 128]


def gdn_phase(K, o, I, S, projT, zT):
    O = [S["O_f"], S["O_b"]]
    GATES = S["GATES"]
    NSC = T // 128
    with K.phase() as st:
        G16 = K.sb(st, [16, T], F32, "G16")
        BETA = K.sb(st, [16, T], F32, "BETA")
        GCf = K.sb(st, [16, T], F32, "GCf")
        GCb = K.sb(st, [16, T], F32, "GCb")
        ab = K.sb(st, [16, 2], F32, "ab")
        nea = K.sb(st, [16, 1], F32, "nea")
        one64 = K.sb(st, [16, 64], F32, "one64")
        K.load(G16[:], projT.t[4608:4624, :], reads=all_regions(projT, [36]), writes=[G16])
        K.load(ab[:], I["gdn_ab"].t[o], writes=[ab])
        K.V(lambda en: en.memset(one64[:], 1.0), [], [one64])
        K.A(lambda en: en.activation(out=nea[:], in_=ab[:, 0:1], func=AF.Exp), [ab], [nea])
        K.V(lambda en: en.tensor_scalar(out=nea[:], in0=nea[:], scalar1=-1.0, scalar2=None, op0=ALU.mult), [nea], [nea])
        K.A(lambda en: en.activation(out=BETA[:], in_=G16[:], func=AF.Sigmoid), [G16], [BETA])
        K.A(lambda en: en.activation(out=G16[:], in_=G16[:], func=AF.Exp, bias=ab[:, 1:2], scale=1.0), [G16, ab], [G16])
        K.V(lambda en: en.tensor_scalar(out=G16[:], in0=G16[:], scalar1=1.0, scalar2=None, op0=ALU.add), [G16], [G16])
        K.A(lambda en: en.activation(out=G16[:], in_=G16[:], func=AF.Ln), [G16], [G16])
        K.V(lambda en: en.tensor_scalar(out=G16[:], in0=G16[:], scalar1=nea[:, 0:1], scalar2=None, op0=ALU.mult), [G16, nea], [G16])
        for c in range(T // 64):
            K.V(lambda en, c=c: en.tensor_tensor_scan(out=GCf[:, c * 64:(c + 1) * 64], data0=one64[:], data1=G16[:, c * 64:(c + 1) * 64],
                                                      initial=0.0, op0=ALU.mult, op1=ALU.add), [G16, one64], [GCf])
            K.V(lambda en, c=c: en.tensor_tensor_scan(out=GCb[:, c * 64:(c + 1) * 64][:, ::-1], data0=one64[:], data1=G16[:, c * 64:(c + 1) * 64][:, ::-1],
                                                      initial=0.0, op0=ALU.mult, op1=ALU.add), [G16, one64], [GCb])
        K.store(GATES.t[0], BETA[:], reads=[BETA], writes=[GATES])
        K.store(GATES.t[1], GCf[:], reads=[GCf], writes=[GATES])
        K.store(GATES.t[2], GCb[:], reads=[GCb], writes=[GATES])
    DBG = int(os.environ.get("GDN_DBG", "99"))
    if DBG == 0:
        return
    with K.phase() as st:
        idf = K.sb(st, [128, 128], F32, "idf")
        idb = K.sb(st, [128, 128], BF16, "idb")
        ones = K.sb(st, [128, 128], F32, "ones")
        K.load(idf[:], I["ident"].t[:, :], writes=[idf])
        K.load(ones[:], I["ones"].t[:, :], writes=[ones])
        K.V(lambda en: en.tensor_copy(out=idb[:], in_=idf[:]), [idf], [idb])
        SEL = K.sb(st, [16, 16, 128], F32, "SEL")
        K.load(SEL[:], I["gdn_sel"].t[:, :, :], writes=[SEL])
        MSK = [K.sb(st, [128, 128], F32, "msk") for _ in range(4)]
        for i in range(4):
            K.load(MSK[i][:], I["gdn_masks"].t[i], writes=[MSK[i]])
        cw = K.sb(st, [128, 12, 4], F32, "gcw")
        K.load(cw[:], I["gdn_cw"].t[o], writes=[cw])
        X = K.sb(st, [128, T], F32, "gX")
        U = K.sb(st, [128, T], F32, "gU")
        SQ = K.sb(st, [128, T], F32, "gSQ")
        NRMs = [[K.sb(st, [128, T], BF16, "gN") for _ in range(3)] for _ in range(2)]
        rs = [K.sb(st, [128, 512], F32, "grs") for _ in range(2)]
        psn = [K.ps(st, [128, 512], F32, "gpsn") for _ in range(2)]
        banks = [K.ps(st, [128, 4, 128], F32, "gbank") for _ in range(5)]
        bfbank = K.ps(st, [128, 4, 128], BF16, "gbfbank")
        qt = [K.psq(banks[i // 4], banks[i // 4].t[:, i % 4, :]) for i in range(8)]
        rb = [[banks[2], banks[3]], [banks[4], psn[0]]]
        rot = [[K.psq(rb[d][i % 2], _quarter(rb[d][i % 2], (i // 2) % 4)) for i in range(8)] for d in range(2)]
        qbf = [K.psq(bfbank, bfbank.t[:, i, :]) for i in range(4)]
        pp = {"i": [0, 0], "d": 0}

        def PQ():
            d_ = pp["d"]
            pp["i"][d_] += 1
            return rot[d_][pp["i"][d_] % 8]

        def mk(shape, dt, name):
            return [[K.sb(st, shape, dt, name) for _ in range(2)] for _ in range(2)]

        gate_in = mk([16, 2, 128], F32, "gin")
        cols = mk([128, 2, 16], F32, "gcols")
        DT_ = mk([128, 128], F32, "gDT")
        ETs = mk([128, 128], F32, "gETs")
        ETi = mk([128, 128], F32, "gETi")
        Nm = mk([128, 128], F32, "gN_")
        Mm = mk([128, 128], F32, "gM_")
        Pm = [mk([128, 128], F32, "gP") for _ in range(2)]
        PTm = [mk([128, 128], F32, "gPT") for _ in range(2)]
        Y = mk([128, 128], F32, "gY")
        Ybf = mk([128, 128], BF16, "gYbf")
        qkT = mk([128, 128], BF16, "gqkT")
        Vb_ = mk([128, 128], BF16, "gVb")
        Kbe = mk([128, 128], BF16, "gKbe")
        cvec = mk([128, 4], F32, "gcvec")
        Ut = mk([128, 128], F32, "gUt")
        nWT = mk([128, 128], BF16, "gnWT")
        kdA = mk([128, 128], BF16, "gkdA")
        kdB = mk([128, 128], BF16, "gkdB")
        egB = mk([128, 128], F32, "gegB")
        qdT = mk([128, 128], BF16, "gqdT")
        VNEW = [K.sb(st, [128, 128], BF16, "gVNEW") for _ in range(2)]
        OSB = mk([128, 128], F32, "gOSB")
        Sf = [K.sb(st, [128, 128], F32, "gSf") for _ in range(2)]
        Sb = [K.sb(st, [128, 128], BF16, "gSb") for _ in range(2)]
        for d in range(2):
            for par in range(2):
                K.G(lambda en, d=d, par=par: en.memset(kdA[d][par][:], 0.0), [], [kdA[d][par]])
                K.G(lambda en, d=d, par=par: en.memset(kdB[d][par][:], 0.0), [], [kdB[d][par]])
        order = [list(range(NSC)), [1, 0] + list(range(NSC - 1, 1, -1))]
        lastc = [(63, 127), (0, 64)]
        def prep(d, h, step):
            QN, KN, VN = NRMs[h % 2]
            pp["d"] = d
            par = step % 2
            sc = order[d][step]
            c0 = sc * 128
            rg, rb = 8 + d * 4 + h, d * 4 + h
            gin, cl_, dt_, ets, eti = gate_in[d][par], cols[d][par], DT_[d][par], ETs[d][par], ETi[d][par]
            N_, M_, y, ybf = Nm[d][par], Mm[d][par], Y[d][par], Ybf[d][par]
            cv = cvec[d][par]
            K.load(gin[:, 0, :], GATES.t[0, :, c0:c0 + 128], reads=[GATES], writes=[gin])
            K.load(gin[:, 1, :], GATES.t[1 + d, :, c0:c0 + 128], reads=[GATES], writes=[gin])
            p_gc, p_b, p_c, p_kk, p_qk = PQ(), PQ(), PQ(), PQ(), PQ()
            K.PE(lambda en: en.matmul(p_gc[:], lhsT=SEL[:, rg, :], rhs=gin[:, 1, :], start=True, stop=True), [SEL, gin], [p_gc])
            K.PE(lambda en: en.matmul(p_b[:], lhsT=SEL[:, rb, :], rhs=gin[:, 0, :], start=True, stop=True), [SEL, gin], [p_b])
            K.PE(lambda en: en.transpose(p_c[:, 0:16], gin[:, 0, :], idf[0:16, 0:16]), [gin, idf], [p_c])
            K.PE(lambda en: en.transpose(p_c[:, 16:32], gin[:, 1, :], idf[0:16, 0:16]), [gin, idf], [p_c])
            K.A(lambda en: en.activation(out=cl_[:].rearrange("p a b -> p (a b)"), in_=p_c[:, 0:32], func=AF.Copy), [p_c], [cl_])
            bcol, gcol = cl_[:, 0, rb:rb + 1], cl_[:, 1, rg:rg + 1]
            K.PE(lambda en: en.matmul(p_kk[:], lhsT=KN[:, c0:c0 + 128], rhs=KN[:, c0:c0 + 128], start=True, stop=True), [KN], [p_kk])
            K.PE(lambda en: en.matmul(p_qk[:], lhsT=KN[:, c0:c0 + 128], rhs=QN[:, c0:c0 + 128], start=True, stop=True), [KN, QN], [p_qk])
            p_kt, p_vt = qbf[(2 * d) % 4], qbf[(2 * d + 1) % 4]
            K.PE(lambda en: en.transpose(p_kt[:], KN[:, c0:c0 + 128], idb[:]), [KN, idb], [p_kt])
            K.PE(lambda en: en.transpose(p_vt[:], VN[:, c0:c0 + 128], idb[:]), [VN, idb], [p_vt])
            K.V(lambda en: en.tensor_scalar(out=dt_[:], in0=p_gc[:], scalar1=gcol, scalar2=0.0, op0=ALU.subtract, op1=ALU.min), [p_gc, cl_], [dt_])
            K.A(lambda en: en.activation(out=egB[d][par][:], in_=p_gc[:], func=AF.Exp), [p_gc], [egB[d][par]])
            for (r0, col) in ((0, lastc[d][0]), (64, lastc[d][1])):
                K.V(lambda en, r0=r0, col=col: en.tensor_tensor(out=cv[r0:r0 + 64, 2:3], in0=p_gc[r0:r0 + 64, col:col + 1], in1=cl_[r0:r0 + 64, 1, rg:rg + 1],
                                                               op=ALU.subtract), [p_gc, cl_], [cv])
            K.A(lambda en: en.activation(out=cv[:, 3:4], in_=cv[:, 2:3], func=AF.Exp), [cv], [cv])
            K.A(lambda en: en.activation(out=dt_[:], in_=dt_[:], func=AF.Exp), [dt_], [dt_])
            K.G(lambda en: en.tensor_tensor(out=ets[:], in0=dt_[:], in1=MSK[2 * d][:], op=ALU.mult), [dt_, MSK[2 * d]], [ets])
            K.G(lambda en: en.tensor_tensor(out=eti[:], in0=dt_[:], in1=MSK[2 * d + 1][:], op=ALU.mult), [dt_, MSK[2 * d + 1]], [eti])
            K.V(lambda en: en.tensor_tensor(out=ets[:], in0=ets[:], in1=p_b[:], op=ALU.mult), [ets, p_b], [ets])
            K.V(lambda en: en.tensor_tensor(out=N_[:], in0=ets[:], in1=p_kk[:], op=ALU.mult), [ets, p_kk], [N_])
            K.V(lambda en: en.tensor_tensor(out=qkT[d][par][:], in0=eti[:], in1=p_qk[:], op=ALU.mult), [eti, p_qk], [qkT[d][par]])
            p_m = PQ()
            K.PE(lambda en: en.transpose(p_m[:], N_[:], idf[:]), [N_, idf], [p_m])
            K.A(lambda en: en.activation(out=M_[:], in_=p_m[:], func=AF.Copy), [p_m], [M_])
            K.G(lambda en: en.tensor_tensor(out=y[:], in0=idf[:], in1=N_[:], op=ALU.subtract), [idf, N_], [y])
            Pc, PTc = N_, M_
            for lev in range(1, 6):
                Pn, PTn = Pm[lev % 2][d][par], PTm[lev % 2][d][par]
                p1 = PQ()
                K.PE(lambda en, p1=p1, Pc=Pc, PTc=PTc: en.matmul(p1[:], lhsT=Pc[:], rhs=PTc[:], start=True, stop=True), [Pc, PTc], [p1])
                K.A(lambda en, p1=p1, PTn=PTn: en.activation(out=PTn[:], in_=p1[:], func=AF.Copy), [p1], [PTn])
                if lev < 5:
                    p2 = PQ()
                    K.PE(lambda en, p2=p2, Pc=Pc, PTc=PTc: en.matmul(p2[:], lhsT=PTc[:], rhs=Pc[:], start=True, stop=True), [Pc, PTc], [p2])
                    K.A(lambda en, p2=p2, Pn=Pn: en.activation(out=Pn[:], in_=p2[:], func=AF.Copy), [p2], [Pn])
                p3 = PQ()
                K.PE(lambda en, p3=p3, PTn=PTn: en.matmul(p3[:], lhsT=PTn[:], rhs=y[:], start=True, stop=True), [PTn, y], [p3])
                K.V(lambda en, p3=p3: en.tensor_tensor(out=y[:], in0=y[:], in1=p3[:], op=ALU.add), [y, p3], [y])
                Pc, PTc = Pn, PTn
            K.A(lambda en: en.activation(out=ybf[:], in_=y[:], func=AF.Copy), [y], [ybf])
            K.V(lambda en: en.tensor_scalar(out=Vb_[d][par][:], in0=p_vt[:], scalar1=bcol, scalar2=None, op0=ALU.mult), [p_vt, cl_], [Vb_[d][par]])
            K.A(lambda en: en.activation(out=cv[:, 0:1], in_=gcol, func=AF.Exp), [cl_], [cv])
            K.V(lambda en: en.tensor_tensor(out=cv[:, 1:2], in0=cv[:, 0:1], in1=bcol, op=ALU.mult), [cv, cl_], [cv])
            K.V(lambda en: en.tensor_scalar(out=Kbe[d][par][:], in0=p_kt[:], scalar1=cv[:, 1:2], scalar2=None, op0=ALU.mult), [p_kt, cv], [Kbe[d][par]])
            K.V(lambda en: en.tensor_scalar(out=kdA[d][par][0:64, :], in0=p_kt[0:64, :], scalar1=cv[0:64, 3:4], scalar2=None, op0=ALU.mult), [p_kt, cv], [kdA[d][par]])
            K.V(lambda en: en.tensor_scalar(out=kdB[d][par][64:128, :], in0=p_kt[64:128, :], scalar1=cv[64:128, 3:4], scalar2=None, op0=ALU.mult), [p_kt, cv], [kdB[d][par]])
            K.V(lambda en: en.tensor_tensor(out=qdT[d][par][:], in0=QN[:, c0:c0 + 128], in1=egB[d][par][:], op=ALU.mult), [QN, egB[d][par]], [qdT[d][par]])
            p_u, p_w = PQ(), PQ()
            K.PE(lambda en: en.matmul(p_u[:], lhsT=ybf[:], rhs=Vb_[d][par][:], start=True, stop=True), [ybf, Vb_[d][par]], [p_u])
            K.PE(lambda en: en.matmul(p_w[:], lhsT=Kbe[d][par][:], rhs=ybf[:], start=True, stop=True), [ybf, Kbe[d][par]], [p_w])
            K.A(lambda en: en.activation(out=Ut[d][par][:], in_=p_u[:], func=AF.Copy), [p_u], [Ut[d][par]])
            K.A(lambda en: en.activation(out=nWT[d][par][:], in_=p_w[:], func=AF.Copy, scale=-1.0), [p_w], [nWT[d][par]])

        def seq(d, h, step):
            par = step % 2
            sc = order[d][step]
            c0 = sc * 128
            halves = ((0, kdA[d][par]), (64, kdB[d][par]))
            if d == 1:
                halves = halves[::-1]
            p_vn, p_o, p_s = qt[d * 4 + 0], qt[d * 4 + 1], qt[d * 4 + 2]
            for (r0, kd) in halves:
                col = lastc[d][0] if r0 == 0 else lastc[d][1]
                K.PE(lambda en: en.matmul(p_vn[:], lhsT=nWT[d][par][:], rhs=Sb[d][:], start=True, stop=True), [nWT[d][par], Sb[d]], [p_vn])
                K.V(lambda en, r0=r0: en.tensor_tensor(out=VNEW[d][r0:r0 + 64, :], in0=p_vn[r0:r0 + 64, :], in1=Ut[d][par][r0:r0 + 64, :], op=ALU.add),
                    [p_vn, Ut[d][par]], [VNEW[d]])
                K.PE(lambda en: en.matmul(p_o[:], lhsT=qdT[d][par][:], rhs=Sb[d][:], start=True, stop=False), [qdT[d][par], Sb[d]], [p_o])
                K.PE(lambda en: en.matmul(p_o[:], lhsT=qkT[d][par][:], rhs=VNEW[d][:], start=False, stop=True), [qkT[d][par], VNEW[d]], [p_o])
                K.A(lambda en, r0=r0: en.activation(out=OSB[d][par][r0:r0 + 64, :], in_=p_o[r0:r0 + 64, :], func=AF.Copy), [p_o], [OSB[d][par]])
                K.PE(lambda en, kd=kd: en.matmul(p_s[:], lhsT=kd[:], rhs=VNEW[d][:], start=True, stop=True), [kd, VNEW[d]], [p_s])
                K.V(lambda en, col=col: en.scalar_tensor_tensor(out=Sf[d][:], in0=Sf[d][:], scalar=egB[d][par][:, col:col + 1], in1=p_s[:],
                                                               op0=ALU.mult, op1=ALU.add), [Sf[d], egB[d][par], p_s], [Sf[d]])
                K.A(lambda en: en.activation(out=Sb[d][:], in_=Sf[d][:], func=AF.Copy), [Sf[d]], [Sb[d]])
            K.store(O[d].t[c0:c0 + 128, h * 128:(h + 1) * 128], OSB[d][par][:], reads=[OSB[d][par]], writes=[O[d].sub(sc)])

        def head_prep(h):
            NRM = NRMs[h % 2]
            for wi in range(3):
                c = wi * 4 + h
                ft = 20 + c
                K.load(X[:], projT.t[ft * 128:(ft + 1) * 128, :], reads=all_regions(projT, [ft]), writes=[X])
                K.V(lambda en, c=c: en.tensor_scalar(out=U[:], in0=X[:], scalar1=cw[:, c, 2:3], scalar2=None, op0=ALU.mult), [X, cw], [U])
                for (s0, s1) in SEGS:
                    for (j, off) in ((0, -2), (1, -1), (3, 1)):
                        if off < 0:
                            oa, ob, ia, ib = s0 - off, s1, s0, s1 + off
                        else:
                            oa, ob, ia, ib = s0, s1 - off, s0 + off, s1
                        K.V(lambda en, c=c, j=j, oa=oa, ob=ob, ia=ia, ib=ib: en.scalar_tensor_tensor(
                            out=U[:, oa:ob], in0=X[:, ia:ib], scalar=cw[:, c, j:j + 1], in1=U[:, oa:ob], op0=ALU.mult, op1=ALU.add), [X, U, cw], [U])
                K.A(lambda en: en.activation(out=U[:], in_=U[:], func=AF.Silu), [U], [U])
                if wi == 2:
                    K.G(lambda en: en.tensor_copy(out=NRM[2][:], in_=U[:]), [U], [NRM[2]])
                    continue
                K.G(lambda en: en.tensor_tensor(out=SQ[:], in0=U[:], in1=U[:], op=ALU.mult), [U], [SQ])
                for ti, (t0, n) in enumerate(TT512):
                    pn, r = psn[1], rs[ti % 2]
                    K.PE(lambda en, pn=pn, t0=t0, n=n: en.matmul(pn[:, 0:n], lhsT=ones[:], rhs=SQ[:, t0:t0 + n], start=True, stop=True), [ones, SQ], [pn])
                    K.V(lambda en, pn=pn, r=r, n=n: en.tensor_scalar(out=r[:, 0:n], in0=pn[:, 0:n], scalar1=EPS, scalar2=None, op0=ALU.add), [pn], [r])
                    K.A(lambda en, r=r, n=n: en.activation(out=r[:, 0:n], in_=r[:, 0:n], func=AF.Sqrt), [r], [r])
                    K.V(lambda en, r=r, n=n: en.reciprocal(out=r[:, 0:n], in_=r[:, 0:n]), [r], [r])
                    K.V(lambda en, r=r, t0=t0, n=n, wi=wi: en.scalar_tensor_tensor(out=NRM[wi][:, t0:t0 + n], in0=U[:, t0:t0 + n],
                                                                                 scalar=(128.0 ** -0.5 if wi == 0 else 1.0), in1=r[:, 0:n],
                                                                                 op0=ALU.mult, op1=ALU.mult), [U, r], [NRM[wi]])

        NH = 4 if DBG > 10 else 1
        head_prep(0)
        for h in range(NH):
            nxt = K.capture(head_prep, h + 1) if h + 1 < NH else []
            stride = (len(nxt) + NSC - 1) // NSC if nxt else 0
            for d in range(2):
                K.V(lambda en, d=d: en.memset(Sf[d][:], 0.0), [], [Sf[d]])
                K.V(lambda en, d=d: en.memset(Sb[d][:], 0.0), [], [Sb[d]])
                K.V(lambda en, d=d: en.memset(VNEW[d][:], 0.0), [], [VNEW[d]])
            if DBG == 1:
                continue
            K.emit_interleaved([K.capture(prep, d, h, 0) for d in range(2)])
            if DBG == 2:
                continue
            for step in range(NSC if DBG > 10 else 1):
                lists = []
                if step + 1 < NSC:
                    lists += [K.capture(prep, d, h, step + 1) for d in range(2)]
                lists += [K.capture(seq, d, h, step) for d in range(2)]
                if nxt:
                    lists.append(nxt[step * stride:(step + 1) * stride])
                K.emit_interleaved(lists)
    if DBG > 10:
        head_norm_epilogue(K, I, O, projT, zT, gate_ft0=32, out_row0=512, center=False)


def host_inputs(inputs, b):
    f = lambda a: np.ascontiguousarray(np.asarray(a, np.float32))
    m = {
        "x_b": f(inputs["x"][b]), "ctx_b": f(inputs["ctx"][b]),
        "c_b": _col(inputs["c"][b], 8), "c_ctx": _col(inputs["c_ctx"], 8),
        "ada_w": f(inputs["ada_w"]),
        "ada_b": np.stack([_col(inputs["ada_b"][l], 48) for l in range(DEPTH)]),
        "norm1_g": np.stack([_col(inputs["norm1_g"][l], 8) for l in range(DEPTH)]),
        "norm2_g": np.stack([_col(inputs["norm2_g"][l], 8) for l in range(DEPTH)]),
        "mix_w_out": f(inputs["mix_w_out"]), "mlp_w1": f(inputs["mlp_w1"]), "mlp_w2": f(inputs["mlp_w2"]),
        "ev_w_in": f(inputs["ev_w_in"]), "od_w_in": f(inputs["od_w_in"]),
        "final_g": _col(inputs["final_g"], 8),
        "ident": np.eye(128, dtype=np.float32), "ones": np.ones((128, 128), np.float32),
    }
    m.update(host_mixer_inputs(inputs))
    return m


def _bd(w):
    out = np.zeros((2, 4, 128, 128), np.float32)
    for d in range(2):
        for ct in range(4):
            out[d, ct, 0:64, 0:64] = w[d, 2 * ct]
            out[d, ct, 64:128, 64:128] = w[d, 2 * ct + 1]
    return out


def host_mixer_inputs(inputs):
    f = lambda a: np.ascontiguousarray(np.asarray(a, np.float32))
    m = {}
    m["lru_cw"] = f(np.asarray(inputs["lru_conv_w"]).reshape(2, 4, 4, 128).transpose(0, 3, 2, 1))
    m["lru_cb"] = f(np.asarray(inputs["lru_conv_b"]).reshape(2, 4, 128).transpose(0, 2, 1))
    for nm, key in (("lru_ba", "lru_ba"), ("lru_bx", "lru_bx"), ("lru_lam", "lru_lambda")):
        m[nm] = f(np.asarray(inputs[key]).reshape(2, 2, 4, 128).transpose(0, 3, 1, 2))
    m["lru_wa_bd"] = np.stack([_bd(np.asarray(inputs["lru_wa"][e])) for e in range(2)])
    m["lru_wx_bd"] = np.stack([_bd(np.asarray(inputs["lru_wx"][e])) for e in range(2)])
    m["ret_lg"] = f(np.broadcast_to(np.asarray(inputs["ret_log_gamma"]).reshape(2, 1, 8), (2, 128, 8)))
    j = np.arange(128, dtype=np.float32)
    m["pos_cols"] = f(np.stack([j + 1.0, 128.0 - j], 1))
    m["tri_f"] = f((j[:, None] <= j[None, :]).astype(np.float32))
    m["tri_b"] = f((j[:, None] >= j[None, :]).astype(np.float32))
    n_freq = 16
    inv = np.power(np.float32(10000.0), -np.arange(n_freq, dtype=np.float32) / n_freq).astype(np.float32)
    rows = NLAT // 64
    r = np.arange(rows, dtype=np.float32)
    c = np.arange(64, dtype=np.float32)
    row_ang = np.broadcast_to(r[:, None, None] * inv, (rows, 64, n_freq))
    col_ang = np.broadcast_to(c[None, :, None] * inv, (rows, 64, n_freq))
    ang = np.concatenate([row_ang, col_ang], -1).reshape(NLAT, 32).astype(np.float32)
    cosT = np.cos(ang).T.astype(np.float32)
    sinT = np.sin(ang).T.astype(np.float32)
    m["rope_cos"] = f(np.concatenate([cosT, cosT, cosT, cosT], 0))
    m["rope_sin"] = f(np.concatenate([-sinT, sinT, -sinT, sinT], 0))
    m["hg_logits"] = f(np.asarray(inputs["hg_lb_logits"]).reshape(2, 2, 4, 128).transpose(3, 0, 1, 2))
    m["gdn_cw"] = f(np.asarray(inputs["gdn_conv_w"]).reshape(2, 4, 12, 128).transpose(0, 3, 2, 1))
    ab = np.zeros((2, 16, 2), np.float32)
    ab[:, 8:16, 0] = np.asarray(inputs["gdn_a_log"]).reshape(2, 8)
    ab[:, 8:16, 1] = np.asarray(inputs["gdn_dt_bias"]).reshape(2, 8)
    m["gdn_ab"] = ab
    sel = np.zeros((16, 16, 128), np.float32)
    for r in range(16):
        sel[r, r, :] = 1.0
    m["gdn_sel"] = sel
    blk = (j[:, None] // 64) == (j[None, :] // 64)
    m["gdn_masks"] = f(np.stack([(j[:, None] < j[None, :]) & blk, (j[:, None] <= j[None, :]) & blk,
                                 (j[:, None] > j[None, :]) & blk, (j[:, None] >= j[None, :]) & blk]).astype(np.float32))
    return m


def build_mixer_test(kind, idx, F):
    nc = bass.Bass("TRN2", target_bir_lowering=False)
    K = KB(nc, debug=True)
    I = {}
    I["ident"] = K.din("ident", [128, 128])
    I["ones"] = K.din("ones", [128, 128])
    mixer_inputs(K, I)
    projT = K.din("projT", [F, T])
    zT = K.dout("zT", [D, T])
    done = K.dout("done", [128, 128])
    S = mixer_scratch(K)
    if kind == "even":
        even_mixer(K, idx, I, S, projT, zT)
    else:
        odd_mixer(K, idx, I, S, projT, zT)
    K.P.barrier()
    fin = K.P.dma("sync", done.t[:, :], I["ident"].t[:, :])
    K.P.emit(final_wait_ops=[fin])
    return nc, K


_PROGRAM_CACHE = {}


def kernel(**inputs):
    n_cores = 8
    if "full" not in _PROGRAM_CACHE:
        _PROGRAM_CACHE["full"] = build_program({"debug": False})[0]
    nc = _PROGRAM_CACHE["full"]
    shared = None
    in_maps = []
    for b in range(n_cores):
        m = host_inputs(inputs, b)
        if shared is None:
            shared = m
        else:
            for k in list(m.keys()):
                if k not in ("x_b", "ctx_b", "c_b"):
                    m[k] = shared[k]
        in_maps.append(m)
    res = run_bass_kernel_spmd(nc, in_maps, core_ids=list(range(n_cores)))
    out = np.stack([np.asarray(res.results[b]["out"], dtype=np.float32) for b in range(n_cores)], axis=0)
    return out
```

```python
import contextlib
import os
import numpy as np
import concourse.bass as bass
import concourse.mybir as mybir
from concourse.bass_utils import run_bass_kernel_spmd

F32 = mybir.dt.float32
BF16 = mybir.dt.bfloat16
AF = mybir.ActivationFunctionType
ALU = mybir.AluOpType
AX = mybir.AxisListType

D = 1024
NCTX = 256
NLAT = 4096
T = NCTX + NLAT
DEPTH = 4
DFF = 4096
EV_IN = 2560
OD_IN = 4624
EPS = 1e-6

ENGS = ("sync", "tensor", "vector", "scalar", "gpsimd")
SEM_WRAP = 20000
DMA_SLOTS = 8
SAME_ENGINE_SYNC = True


class Buf:
    __slots__ = ("w", "r", "x")

    def __init__(self, excl=False):
        self.w = None
        self.r = []
        self.x = excl


class Op:
    __slots__ = ("eng", "fn", "deps", "dma", "signaled", "token")

    def __init__(self, eng, fn, dma):
        self.eng = eng
        self.fn = fn
        self.deps = []
        self.dma = dma
        self.signaled = False
        self.token = None


class TT:
    def __init__(self, t):
        self.t = t
        self.b = Buf()
        self.subs = {}

    def sub(self, key):
        b = self.subs.get(key)
        if b is None:
            b = Buf()
            self.subs[key] = b
        return b

    def __getitem__(self, k):
        return self.t[k]


def _bufs(lst):
    out = []
    for x in lst:
        if isinstance(x, Buf):
            out.append(x)
        elif isinstance(x, TT):
            out.append(x.b)
        else:
            raise TypeError(type(x))
    return out


class Prog:
    def __init__(self, nc):
        self.nc = nc
        self.ops = {e: [] for e in ENGS}
        self.dma_ops = {e: [] for e in ENGS}
        self.nops = 0
        self.barrier_deps = []
        self.barrier_pending = set()

    def barrier(self):
        deps = []
        for e in ENGS:
            for op in reversed(self.ops[e]):
                if not op.dma:
                    deps.append(op)
                    break
            deps += self.dma_ops[e][-DMA_SLOTS:]
        self.barrier_deps = deps
        self.barrier_pending = set(ENGS)

    def add(self, eng, fn, reads=(), writes=(), dma=False):
        op = Op(eng, fn, dma)
        reads = _bufs(reads)
        writes = _bufs(writes)
        xr = [b for b in reads if b.x]
        if xr:
            writes = writes + [b for b in xr if b not in writes]
            reads = [b for b in reads if not b.x]
        deps = []
        for b in reads:
            if b.w is not None:
                deps.append(b.w)
        for b in writes:
            if b.w is not None:
                deps.append(b.w)
            lastc = {}
            for r_ in b.r:
                if r_.dma:
                    deps.append(r_)
                else:
                    lastc[r_.eng] = r_
            deps.extend(lastc.values())
        for b in writes:
            b.w = op
            b.r = []
        for b in reads:
            b.r.append(op)
        if eng in self.barrier_pending:
            self.barrier_pending.discard(eng)
            deps.extend(self.barrier_deps)
        if dma:
            lst = self.dma_ops[eng]
            if len(lst) >= DMA_SLOTS:
                deps.append(lst[len(lst) - DMA_SLOTS])
            lst.append(op)
        seen = set()
        dd = []
        for d in deps:
            if id(d) in seen or d is op:
                continue
            seen.add(id(d))
            if (not d.dma) and (not dma) and d.eng == eng and (eng == "tensor" or not SAME_ENGINE_SYNC):
                continue
            dd.append(d)
            d.signaled = True
        op.deps = dd
        self.ops[eng].append(op)
        self.nops += 1
        return op

    def dma(self, eng, out, in_, reads=(), writes=(), **kw):
        return self.add(eng, lambda e: e.dma_start(out=out, in_=in_, **kw), reads, writes, dma=True)

    def emit(self, final_wait_ops=()):
        nc = self.nc
        nsem = {}
        for e in ENGS:
            cnt = 0
            dcnt = 0
            for op in self.ops[e]:
                if op.dma:
                    op.token = ("d", e, dcnt % DMA_SLOTS, 16 * (dcnt // DMA_SLOTS + 1))
                    dcnt += 1
                elif op.signaled:
                    op.token = ("c", e, cnt // SEM_WRAP, cnt % SEM_WRAP + 1)
                    cnt += 1
            nsem[e] = (cnt + SEM_WRAP - 1) // SEM_WRAP
        with contextlib.ExitStack() as st:
            sems = {}
            for e in ENGS:
                for k in range(nsem[e]):
                    sems[("c", e, k)] = st.enter_context(nc.semaphore(f"c_{e}_{k}"))
                if self.dma_ops[e]:
                    for k in range(DMA_SLOTS):
                        sems[("d", e, k)] = st.enter_context(nc.semaphore(f"d_{e}_{k}"))
            block = st.enter_context(nc.Block())

            def make(e):
                def body(eng):
                    waited = {}
                    for op in self.ops[e]:
                        for d in op.deps:
                            key = d.token[:3]
                            val = d.token[3]
                            if key[0] == "c":
                                kk = (key[0], key[1])
                                cur = waited.get(kk, (-1, 0))
                                if (key[2], val) <= cur:
                                    continue
                                waited[kk] = (key[2], val)
                            else:
                                if waited.get(key, 0) >= val:
                                    continue
                                waited[key] = val
                            eng.wait_ge(sems[key], val)
                        ins = op.fn(eng)
                        if op.token is not None:
                            ins.then_inc(sems[op.token[:3]], 16 if op.dma else 1)
                    if e == "sync":
                        for d in final_wait_ops:
                            eng.wait_ge(sems[d.token[:3]], d.token[3])
                return body

            for e in ENGS:
                if self.ops[e] or (e == "sync" and final_wait_ops):
                    getattr(block, e)(make(e))


class KB:
    def __init__(self, nc, debug=False):
        self.nc = nc
        self.P = Prog(nc)
        self.debug = debug
        self.dram = {}
        self.uid = 0
        self.rr = 0

    def din(self, name, shape, dt=F32):
        t = TT(self.nc.dram_tensor(name, list(shape), dt, kind="ExternalInput").ap())
        self.dram[name] = t
        return t

    def dout(self, name, shape, dt=F32):
        t = TT(self.nc.dram_tensor(name, list(shape), dt, kind="ExternalOutput").ap())
        self.dram[name] = t
        return t

    def dscr(self, name, shape, dt=F32, dbg=False):
        kind = "ExternalOutput" if (self.debug and dbg) else "Internal"
        t = TT(self.nc.dram_tensor(name, list(shape), dt, kind=kind).ap())
        self.dram[name] = t
        return t

    def sb(self, st, shape, dt=F32, name=None):
        self.uid += 1
        return TT(st.enter_context(self.nc.sbuf_tensor(f"{name or 's'}{self.uid}", list(shape), dt)))

    def ps(self, st, shape, dt=F32, name=None):
        self.uid += 1
        nfree = 512 if dt == F32 else 1024
        full = st.enter_context(self.nc.psum_tensor(f"{name or 'p'}{self.uid}", [128, nfree], dt))
        n = 1
        for x in shape[1:]:
            n *= x
        assert n <= nfree
        v = full[0:shape[0], 0:n]
        if len(shape) == 3:
            v = v.rearrange("p (a b) -> p a b", a=shape[1])
        t = TT(v)
        t.b = Buf(excl=True)
        return t

    def psq(self, bank, view):
        t = TT(view)
        t.b = bank.b
        return t

    @contextlib.contextmanager
    def phase(self):
        with contextlib.ExitStack() as st:
            yield st
        self.P.barrier()

    def capture(self, f, *args):
        saved = self.P.add
        lst = []

        def rec(eng, fn, reads=(), writes=(), dma=False):
            lst.append((eng, fn, list(reads), list(writes), dma))
            return None
        self.P.add = rec
        try:
            f(*args)
        finally:
            self.P.add = saved
        return lst

    def emit_interleaved(self, lists):
        idx = [0] * len(lists)
        live = True
        while live:
            live = False
            for k, l in enumerate(lists):
                if idx[k] < len(l):
                    self.P.add(*l[idx[k]])
                    idx[k] += 1
                    live = True

    def op(self, eng, fn, reads=(), writes=()):
        return self.P.add(eng, fn, reads, writes)

    def V(self, fn, reads=(), writes=()):
        return self.P.add("vector", fn, reads, writes)

    def A(self, fn, reads=(), writes=()):
        return self.P.add("scalar", fn, reads, writes)

    def G(self, fn, reads=(), writes=()):
        return self.P.add("gpsimd", fn, reads, writes)

    def PE(self, fn, reads=(), writes=()):
        return self.P.add("tensor", fn, reads, writes)

    def VA(self, fn, reads=(), writes=()):
        self.rr += 1
        return self.P.add("vector" if self.rr % 2 else "scalar", fn, reads, writes)

    def load(self, out, in_, reads=(), writes=(), eng="sync"):
        return self.P.dma(eng, out, in_, reads, writes)

    def store(self, out, in_, reads=(), writes=(), eng="gpsimd"):
        return self.P.dma(eng, out, in_, reads, writes)


def copy_any(out, in_):
    def f(e):
        if hasattr(e, "tensor_copy"):
            return e.tensor_copy(out=out, in_=in_)
        return e.activation(out=out, in_=in_, func=AF.Copy)
    return f


def token_groups(n):
    gs = []
    t = 0
    while t < NCTX:
        m = min(n, NCTX - t)
        gs.append((t, m, 1))
        t += m
    while t < T:
        m = min(n, T - t)
        gs.append((t, m, 0))
        t += m
    return gs


def phase_input_transpose(K, x_b, ctx_b, xT, ident):
    with K.phase() as st:
        idt = K.sb(st, [128, 128], F32, "ident")
        K.load(idt[:], ident[:, :], writes=[idt])
        xin = [K.sb(st, [128, D], F32, "xin") for _ in range(2)]
        stg = [K.sb(st, [128, 8, 128], F32, "xstg") for _ in range(2)]
        pss = [K.ps(st, [128, 4, 128], F32, "ptr") for _ in range(2)]
        for i in range(T // 128):
            xt = xin[i % 2]
            sg = stg[i % 2]
            src = ctx_b.t[i * 128:(i + 1) * 128, :] if i < 2 else x_b.t[(i - 2) * 128:(i - 1) * 128, :]
            K.load(xt[:], src, writes=[xt])
            for kg in range(2):
                ps = pss[kg]
                for j in range(4):
                    k = kg * 4 + j
                    K.PE(lambda e, ps=ps, j=j, k=k, xt=xt: e.transpose(ps[:, j, :], xt[:, k * 128:(k + 1) * 128], idt[:]),
                         [xt, idt], [ps])
                K.VA(copy_any(sg[:, kg * 4:(kg + 1) * 4, :], ps[:]), [ps], [sg])
            K.store(xT.t.rearrange("(k p) t -> p k t", p=128)[:, :, i * 128:(i + 1) * 128], sg[:],
                    reads=[sg], writes=[xT.sub(i // 2 if i < 2 else 1 + (i - 2) // 4)])


def phase_adaln(K, layer, c_b, c_ctx, ada_w, ada_b, norm1_g, norm2_g, mod):
    modT, G1, G2 = mod["modT"], mod["G1"], mod["G2"]
    with K.phase() as st:
        sT = K.sb(st, [128, 2, 8], F32, "sT")
        K.load(sT[:, 0, :], c_b.t[:, :], writes=[sT])
        K.load(sT[:, 1, :], c_ctx.t[:, :], writes=[sT])
        sS = K.sb(st, [128, 2, 8], F32, "sS")
        K.A(lambda e: e.activation(out=sS[:], in_=sT[:], func=AF.Silu), [sT], [sS])
        bT = K.sb(st, [128, 48], F32, "bT")
        K.load(bT[:], ada_b.t[layer], writes=[bT])
        gT = K.sb(st, [128, 2, 8], F32, "gT")
        K.load(gT[:, 0, :], norm1_g.t[layer], writes=[gT])
        K.load(gT[:, 1, :], norm2_g.t[layer], writes=[gT])
        wst = [K.sb(st, [128, 8, 1024], F32, "adaw") for _ in range(2)]
        ps = K.ps(st, [128, 48, 2], F32, "pmod")
        for j in range(6):
            w = wst[j % 2]
            for kh in range(2):
                K.load(w[:, kh * 4:(kh + 1) * 4, :],
                       ada_w.t[layer].rearrange("(k p) n -> p k n", p=128)[:, kh * 4:(kh + 1) * 4, j * 1024:(j + 1) * 1024],
                       writes=[w])
            for f in range(8):
                for k in range(8):
                    K.PE(lambda e, w=w, f=f, k=k, j=j: e.matmul(ps[:, j * 8 + f, :], lhsT=w[:, k, f * 128:(f + 1) * 128],
                                                               rhs=sS[:, :, k], start=(k == 0), stop=(k == 7)),
                         [w, sS], [ps])
        K.V(lambda e: e.tensor_tensor(out=modT[:], in0=ps[:], in1=bT[:].unsqueeze(2).broadcast_to([128, 48, 2]), op=ALU.add),
            [ps, bT], [modT])
        for (G, gi, j) in ((G1, 0, 1), (G2, 1, 4)):
            K.V(lambda e, G=G, gi=gi, j=j: e.scalar_tensor_tensor(
                out=G[:], in0=modT[:, j * 8:(j + 1) * 8, :], scalar=1.0,
                in1=gT[:, gi, :].unsqueeze(2).broadcast_to([128, 8, 2]), op0=ALU.add, op1=ALU.mult),
                [modT, gT], [G])


def load_weight_bf16(K, st, w_ap, R, F, name, stage):
    nk = R // 128
    dst = K.sb(st, [128, nk, F], BF16, name)
    wv = w_ap.rearrange("(k p) n -> p k n", p=128)
    i = 0
    for k0 in range(0, nk, 8):
        kn = min(8, nk - k0)
        SW = stage[0].t.shape[2]
        for c0 in range(0, F, SW):
            cn = min(SW, F - c0)
            sg = stage[i % len(stage)]
            K.load(sg[:, 0:kn, 0:cn], wv[:, k0:k0 + kn, c0:c0 + cn], writes=[sg])
            eng = "gpsimd" if i % 2 == 0 else "vector"
            K.op(eng, lambda e, sg=sg, k0=k0, kn=kn, c0=c0, cn=cn: e.tensor_copy(out=dst[:, k0:k0 + kn, c0:c0 + cn], in_=sg[:, 0:kn, 0:cn]),
                 [sg], [dst])
            i += 1
    return dst


def norm_modulate(K, xg, n, s, G, modT, jshift, ones, sq, psn, rstd, tmp, hT):
    K.A(lambda e: e.activation(out=sq[:, :, 0:n], in_=xg[:, :, 0:n], func=AF.Square), [xg], [sq])
    for k in range(8):
        K.PE(lambda e, k=k: e.matmul(psn[:, 0:n], lhsT=ones[:], rhs=sq[:, k, 0:n], start=(k == 0), stop=(k == 7)), [sq, ones], [psn])
    K.V(lambda e: e.tensor_scalar(out=rstd[:, 0:n], in0=psn[:, 0:n], scalar1=1.0 / D, scalar2=EPS, op0=ALU.mult, op1=ALU.add), [psn], [rstd])
    K.A(lambda e: e.activation(out=rstd[:, 0:n], in_=rstd[:, 0:n], func=AF.Sqrt), [rstd], [rstd])
    K.V(lambda e: e.reciprocal(out=rstd[:, 0:n], in_=rstd[:, 0:n]), [rstd], [rstd])
    for k in range(8):
        K.V(lambda e, k=k: e.scalar_tensor_tensor(out=tmp[:, k, 0:n], in0=xg[:, k, 0:n], scalar=G[:, k, s:s + 1], in1=rstd[:, 0:n],
                                                  op0=ALU.mult, op1=ALU.mult), [xg, G, rstd], [tmp])
        K.A(lambda e, k=k: e.activation(out=hT[:, k, 0:n], in_=tmp[:, k, 0:n], func=AF.Identity,
                                        bias=modT[:, jshift * 8 + k, s:s + 1], scale=1.0), [tmp, modT], [hT])


def phase_norm_inproj(K, xT, w_in_ap, F, projT, mod, ones_d, hdbg=None):
    NG = 512
    with K.phase() as st:
        ones = K.sb(st, [128, 128], F32, "ones")
        K.load(ones[:], ones_d[:, :], writes=[ones])
        stage = [K.sb(st, [128, 8, 256], F32, "wstage") for _ in range(2)]
        W = load_weight_bf16(K, st, w_in_ap, D, F, "win", stage)
        xgs = [K.sb(st, [128, 8, NG], F32, "xg") for _ in range(2)]
        sq = K.sb(st, [128, 8, NG], F32, "sq")
        tmp = K.sb(st, [128, 8, NG], F32, "tmp")
        rstd = K.sb(st, [128, NG], F32, "rstd")
        hTs = [K.sb(st, [128, 8, NG], BF16, "hT") for _ in range(2)]
        outs = [K.sb(st, [128, NG], F32, "pout") for _ in range(3)]
        psn = K.ps(st, [128, NG], F32, "psn")
        pss = [K.ps(st, [128, NG], F32, "psp") for _ in range(4)]
        xv = xT.t.rearrange("(k p) t -> p k t", p=128)
        groups = token_groups(NG)
        nft = (F + 127) // 128
        cntr = {"c": 0}

        def norm_list(gi):
            t0, n, s_ = groups[gi]
            xg, hT = xgs[gi % 2], hTs[gi % 2]
            K.load(xg[:, :, 0:n], xv[:, :, t0:t0 + n], reads=[xT.sub(gi)], writes=[xg])
            norm_modulate(K, xg, n, s_, mod["G1"], mod["modT"], 0, ones, sq, psn, rstd, tmp, hT)
            if hdbg is not None:
                hf = tmp
                K.V(lambda en: en.tensor_copy(out=hf[:, :, 0:n], in_=hT[:, :, 0:n]), [hT], [hf])
                K.store(hdbg.t.rearrange("(k p) t -> p k t", p=128)[:, :, t0:t0 + n], hf[:, :, 0:n], reads=[hf], writes=[hdbg])

        def mm_list(gi):
            t0, n, s_ = groups[gi]
            hT = hTs[gi % 2]
            for f in range(nft):
                m = min(128, F - f * 128)
                ps = pss[cntr["c"] % 4]
                ot = outs[cntr["c"] % 3]
                cntr["c"] += 1
                for k in range(8):
                    K.PE(lambda en, ps=ps, f=f, m=m, k=k: en.matmul(ps[0:m, 0:n], lhsT=W[:, k, f * 128:f * 128 + m], rhs=hT[:, k, 0:n],
                                                                 start=(k == 0), stop=(k == 7)), [W, hT], [ps])
                K.VA(copy_any(ot[0:m, 0:n], ps[0:m, 0:n]), [ps], [ot])
                K.store(projT.t[f * 128:f * 128 + m, t0:t0 + n], ot[0:m, 0:n], reads=[ot], writes=[projT.sub((f, gi))])

        norm_list(0)
        for gi in range(len(groups)):
            lists = [K.capture(mm_list, gi)]
            if gi + 1 < len(groups):
                lists.append(K.capture(norm_list, gi + 1))
            K.emit_interleaved(lists)


NP_GROUPS = token_groups(512)


def region(t0):
    for gi, (a, n, s) in enumerate(NP_GROUPS):
        if a <= t0 < a + n:
            return gi
    raise ValueError


def regions(tt, t0, n):
    return [tt.sub(gi) for gi, (a, m, s) in enumerate(NP_GROUPS) if a < t0 + n and t0 < a + m]


def phase_out_w1(K, xT, zT, aTd, w_out_ap, w1_ap, mod, ones_d, skip_ctx, xmid_dbg=None):
    NG = 256
    modT = mod["modT"]
    with K.phase() as st:
        ones = K.sb(st, [128, 128], F32, "ones")
        K.load(ones[:], ones_d[:, :], writes=[ones])
        stage = [K.sb(st, [128, 8, 512], F32, "wstage") for _ in range(2)]
        Wo = load_weight_bf16(K, st, w_out_ap, D, D, "wo", stage)
        W1 = load_weight_bf16(K, st, w1_ap, D, DFF, "w1", stage)
        zgs = [K.sb(st, [128, 8, NG], F32, "zg") for _ in range(2)]
        xgs = [K.sb(st, [128, 8, NG], F32, "xg") for _ in range(2)]
        zbs = [K.sb(st, [128, 8, NG], BF16, "zb") for _ in range(2)]
        sq = K.sb(st, [128, 8, NG], F32, "sq")
        rstd = K.sb(st, [128, NG], F32, "rstd")
        hTs = [K.sb(st, [128, 8, NG], BF16, "hT") for _ in range(2)]
        rl = [K.sb(st, [128, NG], F32, "rl") for _ in range(2)]
        aTs = [K.sb(st, [128, 4, NG], BF16, "aTs") for _ in range(2)]
        psn = K.ps(st, [128, NG], F32, "psn")
        psA = [K.ps(st, [128, NG], F32, "psA") for _ in range(2)]
        psC = [K.ps(st, [128, NG], F32, "psC") for _ in range(4)]
        xv = xT.t.rearrange("(k p) t -> p k t", p=128)
        zv = zT.t.rearrange("(k p) t -> p k t", p=128)
        av = aTd.t.rearrange("(k p) t -> p k t", p=128)
        groups = [g for g in token_groups(NG) if not (g[2] == 1 and skip_ctx)]
        cA = {"c": 0}
        cC = {"c": 0}

        def stage_ab(i):
            t0, n, s_ = groups[i]
            zg, xg, zb, hT = zgs[i % 2], xgs[i % 2], zbs[i % 2], hTs[i % 2]
            reg = region(t0)
            K.load(zg[:, :, 0:n], zv[:, :, t0:t0 + n], reads=[zT.sub(reg)], writes=[zg])
            K.load(xg[:, :, 0:n], xv[:, :, t0:t0 + n], reads=[xT.sub(reg)], writes=[xg])
            K.G(lambda en: en.tensor_copy(out=zb[:, :, 0:n], in_=zg[:, :, 0:n]), [zg], [zb])
            for f in range(8):
                ps = psA[cA["c"] % 2]
                cA["c"] += 1
                for k in range(8):
                    K.PE(lambda en, ps=ps, f=f, k=k: en.matmul(ps[:, 0:n], lhsT=Wo[:, k, f * 128:(f + 1) * 128], rhs=zb[:, k, 0:n],
                                                            start=(k == 0), stop=(k == 7)), [Wo, zb], [ps])
                K.V(lambda en, ps=ps, f=f: en.scalar_tensor_tensor(
                    out=xg[:, f, 0:n], in0=ps[:, 0:n], scalar=modT[:, 2 * 8 + f, s_:s_ + 1], in1=xg[:, f, 0:n], op0=ALU.mult, op1=ALU.add),
                    [ps, modT, xg], [xg])
            K.store(xv[:, :, t0:t0 + n], xg[:, :, 0:n], reads=[xg], writes=[xT.sub(reg)])
            if xmid_dbg is not None:
                K.store(xmid_dbg.t.rearrange("(k p) t -> p k t", p=128)[:, :, t0:t0 + n], xg[:, :, 0:n], reads=[xg], writes=[xmid_dbg])
            norm_modulate(K, xg, n, s_, mod["G2"], modT, 3, ones, sq, psn, rstd, zg, hT)

        def stage_c(i):
            t0, n, s_ = groups[i]
            hT = hTs[i % 2]
            reg = region(t0)
            for f in range(32):
                ps = psC[cC["c"] % 4]
                r = rl[cC["c"] % 2]
                cC["c"] += 1
                ast = aTs[(f // 4) % 2]
                for k in range(8):
                    K.PE(lambda en, ps=ps, f=f, k=k: en.matmul(ps[:, 0:n], lhsT=W1[:, k, f * 128:(f + 1) * 128], rhs=hT[:, k, 0:n],
                                                            start=(k == 0), stop=(k == 7)), [W1, hT], [ps])
                K.A(lambda en, ps=ps, r=r: en.activation(out=r[:, 0:n], in_=ps[:, 0:n], func=AF.Relu), [ps], [r])
                K.op("gpsimd" if f % 2 else "vector",
                     lambda en, r=r, f=f, ast=ast: en.tensor_tensor(out=ast[:, f % 4, 0:n], in0=r[:, 0:n], in1=r[:, 0:n], op=ALU.mult), [r], [ast])
                if f % 4 == 3:
                    K.store(av[:, f - 3:f + 1, t0:t0 + n], ast[:, :, 0:n], reads=[ast], writes=[aTd.sub(reg)])

        stage_ab(0)
        for i in range(len(groups)):
            lists = [K.capture(stage_c, i)]
            if i + 1 < len(groups):
                lists.append(K.capture(stage_ab, i + 1))
            K.emit_interleaved(lists)


def phase_w2(K, xT, aTd, w2_ap, mod, skip_ctx):
    NG = 512
    modT = mod["modT"]
    with K.phase() as st:
        stage = [K.sb(st, [128, 8, 512], F32, "wstage") for _ in range(2)]
        W2 = load_weight_bf16(K, st, w2_ap, DFF, D, "w2", stage)
        ags = [K.sb(st, [128, 32, NG], BF16, "ag") for _ in range(2)]
        xgs = [K.sb(st, [128, 8, NG], F32, "xg") for _ in range(2)]
        pss = [K.ps(st, [128, NG], F32, "psp") for _ in range(4)]
        xv = xT.t.rearrange("(k p) t -> p k t", p=128)
        av = aTd.t.rearrange("(k p) t -> p k t", p=128)
        cnt = 0
        for gi, (t0, n, s) in enumerate(NP_GROUPS):
            if s == 1 and skip_ctx:
                continue
            ag = ags[gi % 2]
            xg = xgs[gi % 2]
            for q in range(4):
                K.load(ag[:, q * 8:(q + 1) * 8, 0:n], av[:, q * 8:(q + 1) * 8, t0:t0 + n], reads=[aTd.sub(gi)], writes=[ag])
            K.load(xg[:, :, 0:n], xv[:, :, t0:t0 + n], reads=[xT.sub(gi)], writes=[xg])
            for f in range(8):
                ps = pss[cnt % 4]
                cnt += 1
                for k in range(32):
                    K.PE(lambda e, ps=ps, f=f, k=k, n=n, ag=ag: e.matmul(ps[:, 0:n], lhsT=W2[:, k, f * 128:(f + 1) * 128], rhs=ag[:, k, 0:n],
                                                                     start=(k == 0), stop=(k == 31)), [W2, ag], [ps])
                K.V(lambda e, ps=ps, f=f, xg=xg, n=n, s=s: e.scalar_tensor_tensor(
                    out=xg[:, f, 0:n], in0=ps[:, 0:n], scalar=modT[:, 5 * 8 + f, s:s + 1], in1=xg[:, f, 0:n], op0=ALU.mult, op1=ALU.add),
                    [ps, modT, xg], [xg])
            K.store(xv[:, :, t0:t0 + n], xg[:, :, 0:n], reads=[xg], writes=[xT.sub(gi)])


def phase_final(K, xT, final_g, ones_d, ident, out):
    NG = 512
    with K.phase() as st:
        ones = K.sb(st, [128, 128], F32, "ones")
        K.load(ones[:], ones_d[:, :], writes=[ones])
        idt = K.sb(st, [128, 128], F32, "ident")
        K.load(idt[:], ident[:, :], writes=[idt])
        gf = K.sb(st, [128, 8], F32, "gf")
        K.load(gf[:], final_g.t[:, :], writes=[gf])
        xgs = [K.sb(st, [128, 8, NG], F32, "xg") for _ in range(2)]
        sq = K.sb(st, [128, 8, NG], F32, "sq")
        rstd = K.sb(st, [128, NG], F32, "rstd")
        yT = K.sb(st, [128, 8, NG], F32, "yT")
        ots = [K.sb(st, [128, D], F32, "ot") for _ in range(2)]
        psn = K.ps(st, [128, NG], F32, "psn")
        pss = [K.ps(st, [128, 4, 128], F32, "ptr") for _ in range(2)]
        xv = xT.t.rearrange("(k p) t -> p k t", p=128)
        fin = []
        cnt = 0
        for gi, (t0, n, s) in enumerate(token_groups(NG)):
            if s == 1:
                continue
            xg = xgs[gi % 2]
            K.load(xg[:, :, 0:n], xv[:, :, t0:t0 + n], reads=[xT.sub(gi)], writes=[xg])
            K.A(lambda e, xg=xg: e.activation(out=sq[:], in_=xg[:], func=AF.Square), [xg], [sq])
            for k in range(8):
                K.PE(lambda e, k=k: e.matmul(psn[:], lhsT=ones[:], rhs=sq[:, k, :], start=(k == 0), stop=(k == 7)), [sq, ones], [psn])
            K.V(lambda e: e.tensor_scalar(out=rstd[:], in0=psn[:], scalar1=1.0 / D, scalar2=EPS, op0=ALU.mult, op1=ALU.add), [psn], [rstd])
            K.A(lambda e: e.activation(out=rstd[:], in_=rstd[:], func=AF.Sqrt), [rstd], [rstd])
            K.V(lambda e: e.reciprocal(out=rstd[:], in_=rstd[:]), [rstd], [rstd])
            for k in range(8):
                K.V(lambda e, k=k, xg=xg: e.scalar_tensor_tensor(out=yT[:, k, :], in0=xg[:, k, :], scalar=gf[:, k:k + 1], in1=rstd[:],
                                                                op0=ALU.mult, op1=ALU.mult), [xg, gf, rstd], [yT])
            for tt in range(n // 128):
                ot = ots[cnt % 2]
                for kg in range(2):
                    ps = pss[kg]
                    for j in range(4):
                        k = kg * 4 + j
                        K.PE(lambda e, ps=ps, j=j, k=k, tt=tt: e.transpose(ps[:, j, :], yT[:, k, tt * 128:(tt + 1) * 128], idt[:]), [yT, idt], [ps])
                    K.VA(copy_any(ot[:, kg * 512:(kg + 1) * 512], ps[:].rearrange("p a b -> p (a b)")), [ps], [ot])
                r0 = t0 - NCTX + tt * 128
                fin.append(K.store(out.t[r0:r0 + 128, :], ot[:], reads=[ot], writes=[out]))
                cnt += 1
    return fin


def _col(v, nchunk):
    return np.ascontiguousarray(np.asarray(v, np.float32).reshape(nchunk, 128).T)


def build_program(cfg):
    debug = cfg.get("debug", False)
    layers = cfg.get("layers", list(range(DEPTH)))
    nc = bass.Bass("TRN2", target_bir_lowering=False)
    K = KB(nc, debug=debug)
    I = {}
    I["x_b"] = K.din("x_b", [NLAT, D])
    I["ctx_b"] = K.din("ctx_b", [NCTX, D])
    I["c_b"] = K.din("c_b", [128, 8])
    I["c_ctx"] = K.din("c_ctx", [128, 8])
    I["ada_w"] = K.din("ada_w", [DEPTH, D, 6 * D])
    I["ada_b"] = K.din("ada_b", [DEPTH, 128, 48])
    I["norm1_g"] = K.din("norm1_g", [DEPTH, 128, 8])
    I["norm2_g"] = K.din("norm2_g", [DEPTH, 128, 8])
    I["mix_w_out"] = K.din("mix_w_out", [DEPTH, D, D])
    I["mlp_w1"] = K.din("mlp_w1", [DEPTH, D, DFF])
    I["mlp_w2"] = K.din("mlp_w2", [DEPTH, DFF, D])
    I["ev_w_in"] = K.din("ev_w_in", [2, D, EV_IN])
    I["od_w_in"] = K.din("od_w_in", [2, D, OD_IN])
    I["final_g"] = K.din("final_g", [128, 8])
    I["ident"] = K.din("ident", [128, 128])
    I["ones"] = K.din("ones", [128, 128])
    mixer_inputs(K, I)
    out = K.dout("out", [NLAT, D])
    xT = K.dscr("xT", [D, T], F32, dbg=True)
    projT = K.dscr("projT", [OD_IN, T], F32, dbg=True)
    zT = K.dscr("zT", [D, T], F32, dbg=True)
    aTd = K.dscr("aTd", [DFF, T], BF16)
    hdbg = K.dscr("hdbg", [D, T], F32, dbg=True) if debug else None
    xmid = K.dscr("xmid", [D, T], F32, dbg=True) if debug else None
    zin = K.din("zin", [D, T]) if cfg.get("z_from_input") else None
    S = mixer_scratch(K)

    with contextlib.ExitStack() as gst:
        mod = {"modT": K.sb(gst, [128, 48, 2], F32, "modT"), "G1": K.sb(gst, [128, 8, 2], F32, "G1"),
               "G2": K.sb(gst, [128, 8, 2], F32, "G2")}
        phase_input_transpose(K, I["x_b"], I["ctx_b"], xT, I["ident"].t)
        for layer in layers:
            last = (layer == DEPTH - 1)
            phase_adaln(K, layer, I["c_b"], I["c_ctx"], I["ada_w"], I["ada_b"], I["norm1_g"], I["norm2_g"], mod)
            if layer % 2 == 0:
                w_in, F = I["ev_w_in"].t[layer // 2], EV_IN
            else:
                w_in, F = I["od_w_in"].t[layer // 2], OD_IN
            phase_norm_inproj(K, xT, w_in, F, projT, mod, I["ones"].t, hdbg=hdbg if (debug and layer == layers[0]) else None)
            zsrc = zT
            if zin is not None:
                zsrc = zin
            elif layer % 2 == 0:
                even_mixer(K, layer // 2, I, S, projT, zT)
            else:
                odd_mixer(K, layer // 2, I, S, projT, zT)
            phase_out_w1(K, xT, zsrc, aTd, I["mix_w_out"].t[layer], I["mlp_w1"].t[layer], mod, I["ones"].t, skip_ctx=last,
                         xmid_dbg=xmid if (debug and layer == layers[0]) else None)
            phase_w2(K, xT, aTd, I["mlp_w2"].t[layer], mod, skip_ctx=last)
        fin = phase_final(K, xT, I["final_g"], I["ones"].t, I["ident"].t, out)
    K.P.emit(final_wait_ops=fin)
    return nc, K


def mixer_inputs(K, I):
    I["lru_cw"] = K.din("lru_cw", [2, 128, 4, 4])
    I["lru_cb"] = K.din("lru_cb", [2, 128, 4])
    I["lru_ba"] = K.din("lru_ba", [2, 128, 2, 4])
    I["lru_bx"] = K.din("lru_bx", [2, 128, 2, 4])
    I["lru_lam"] = K.din("lru_lam", [2, 128, 2, 4])
    I["lru_wa_bd"] = K.din("lru_wa_bd", [2, 2, 4, 128, 128])
    I["lru_wx_bd"] = K.din("lru_wx_bd", [2, 2, 4, 128, 128])
    I["ret_lg"] = K.din("ret_lg", [2, 128, 8])
    I["pos_cols"] = K.din("pos_cols", [128, 2])
    I["tri_f"] = K.din("tri_f", [128, 128])
    I["tri_b"] = K.din("tri_b", [128, 128])
    I["rope_cos"] = K.din("rope_cos", [128, NLAT])
    I["rope_sin"] = K.din("rope_sin", [128, NLAT])
    I["hg_logits"] = K.din("hg_logits", [128, 2, 2, 4])
    I["gdn_cw"] = K.din("gdn_cw", [2, 128, 12, 4])
    I["gdn_ab"] = K.din("gdn_ab", [2, 16, 2])
    I["gdn_sel"] = K.din("gdn_sel", [16, 16, 128])
    I["gdn_masks"] = K.din("gdn_masks", [4, 128, 128])


def mixer_scratch(K):
    S = {}
    S["O_f"] = K.dscr("O_f", [T, 512], F32, dbg=True)
    S["O_b"] = K.dscr("O_b", [T, 512], F32, dbg=True)
    S["GATES"] = K.dscr("GATES", [3, 16, T], F32, dbg=True)
    return S


def all_regions(tt, rows):
    return [tt.sub((r, gi)) for r in rows for gi in range(len(NP_GROUPS))]


def z_regions(zT):
    return [zT.sub(gi) for gi in range(len(NP_GROUPS))]


SEGS = ((0, NCTX), (NCTX, T))
TT512 = [(t0, min(512, T - t0)) for t0 in range(0, T, 512)]


def lru_phase(K, e, I, projT, zT):
    with K.phase() as st:
        cw = K.sb(st, [128, 4, 4], F32, "cw")
        cb = K.sb(st, [128, 4], F32, "cb")
        ba = K.sb(st, [128, 2, 4], F32, "ba")
        bx = K.sb(st, [128, 2, 4], F32, "bx")
        lam = K.sb(st, [128, 2, 4], F32, "lam")
        cl = K.sb(st, [128, 2, 4], F32, "cl")
        one = K.sb(st, [128, 1], F32, "one")
        K.load(cw[:], I["lru_cw"].t[e], writes=[cw])
        K.load(cb[:], I["lru_cb"].t[e], writes=[cb])
        K.load(ba[:], I["lru_ba"].t[e], writes=[ba])
        K.load(bx[:], I["lru_bx"].t[e], writes=[bx])
        K.load(lam[:], I["lru_lam"].t[e], writes=[lam])
        K.V(lambda en: en.memset(one[:], 1.0), [], [one])
        nba = K.sb(st, [128, 2, 4], F32, "nba")
        nbx = K.sb(st, [128, 2, 4], F32, "nbx")
        K.V(lambda en: en.tensor_scalar(out=nba[:], in0=ba[:], scalar1=-1.0, scalar2=None, op0=ALU.mult), [ba], [nba])
        K.V(lambda en: en.tensor_scalar(out=nbx[:], in0=bx[:], scalar1=-1.0, scalar2=None, op0=ALU.mult), [bx], [nbx])
        K.A(lambda en: en.activation(out=cl[:], in_=lam[:], func=AF.Exp, scale=-1.0), [lam], [cl])
        K.V(lambda en: en.tensor_scalar(out=cl[:], in0=cl[:], scalar1=1.0, scalar2=None, op0=ALU.add), [cl], [cl])
        K.A(lambda en: en.activation(out=cl[:], in_=cl[:], func=AF.Ln), [cl], [cl])
        K.V(lambda en: en.tensor_scalar(out=cl[:], in0=cl[:], scalar1=-8.0, scalar2=None, op0=ALU.mult), [cl], [cl])
        wst = K.sb(st, [128, 128], F32, "bdst")
        BD = {}
        for d in range(2):
            for ct in range(4):
                for nm, key in (("a", "lru_wa_bd"), ("x", "lru_wx_bd")):
                    w = K.sb(st, [128, 128], BF16, "bd")
                    K.load(wst[:], I[key].t[e, d, ct], writes=[wst])
                    K.V(lambda en, w=w: en.tensor_copy(out=w[:], in_=wst[:]), [wst], [w])
                    BD[(nm, d, ct)] = w
        B1 = K.sb(st, [128, T], F32, "B1")
        B2 = K.sb(st, [128, T], F32, "B2")
        B3 = K.sb(st, [128, T], F32, "B3")
        B4 = K.sb(st, [128, T], F32, "B4")
        B5 = K.sb(st, [128, T], F32, "B5")
        B6 = K.sb(st, [128, T], F32, "B6")
        ub = K.sb(st, [128, T], BF16, "ub")
        rt = [K.sb(st, [128, 512], F32, "rt") for _ in range(2)]
        it = [K.sb(st, [128, 512], F32, "it") for _ in range(2)]
        mt = [K.sb(st, [128, 512], F32, "mt") for _ in range(2)]
        psa = [K.ps(st, [128, 512], F32, "psa") for _ in range(2)]
        psx = [K.ps(st, [128, 512], F32, "psx") for _ in range(2)]
        for ct in range(4):
            x, u, Aa, INP, H0, H1 = B1, B2, B3, B4, B5, B6
            K.load(x[:], projT.t[ct * 128:(ct + 1) * 128, :], reads=all_regions(projT, [ct]), writes=[x])
            K.V(lambda en, ct=ct: en.tensor_scalar(out=u[:], in0=x[:], scalar1=cw[:, ct, 2:3], scalar2=cb[:, ct:ct + 1], op0=ALU.mult, op1=ALU.add),
                [x, cw, cb], [u])
            for (s0, s1) in SEGS:
                for (j, off) in ((0, -2), (1, -1), (3, 1)):
                    if off < 0:
                        oa, ob, ia, ib = s0 - off, s1, s0, s1 + off
                    else:
                        oa, ob, ia, ib = s0, s1 - off, s0 + off, s1
                    K.V(lambda en, ct=ct, j=j, oa=oa, ob=ob, ia=ia, ib=ib: en.scalar_tensor_tensor(
                        out=u[:, oa:ob], in0=x[:, ia:ib], scalar=cw[:, ct, j:j + 1], in1=u[:, oa:ob], op0=ALU.mult, op1=ALU.add), [x, u, cw], [u])
            K.A(lambda en: en.activation(out=ub[:], in_=u[:], func=AF.Copy), [u], [ub])
            for d in range(2):
                for ti, (t0, n) in enumerate(TT512):
                    pa, px = psa[ti % 2], psx[ti % 2]
                    r, ii, m = rt[ti % 2], it[ti % 2], mt[ti % 2]
                    K.PE(lambda en, pa=pa, d=d, ct=ct, t0=t0, n=n: en.matmul(pa[:, 0:n], lhsT=BD[("a", d, ct)][:], rhs=ub[:, t0:t0 + n], start=True, stop=True),
                         [BD[("a", d, ct)], ub], [pa])
                    K.PE(lambda en, px=px, d=d, ct=ct, t0=t0, n=n: en.matmul(px[:, 0:n], lhsT=BD[("x", d, ct)][:], rhs=ub[:, t0:t0 + n], start=True, stop=True),
                         [BD[("x", d, ct)], ub], [px])
                    K.A(lambda en, pa=pa, r=r, d=d, ct=ct, n=n: en.activation(out=r[:, 0:n], in_=pa[:, 0:n], func=AF.Exp, bias=nba[:, d, ct:ct + 1], scale=-1.0),
                        [pa, nba], [r])
                    K.V(lambda en, r=r, n=n: en.tensor_scalar(out=r[:, 0:n], in0=r[:, 0:n], scalar1=1.0, scalar2=None, op0=ALU.add), [r], [r])
                    K.V(lambda en, r=r, n=n: en.reciprocal(out=r[:, 0:n], in_=r[:, 0:n]), [r], [r])
                    K.A(lambda en, r=r, d=d, ct=ct, t0=t0, n=n: en.activation(out=Aa[:, t0:t0 + n], in_=r[:, 0:n], func=AF.Exp, scale=cl[:, d, ct:ct + 1]),
                        [r, cl], [Aa])
                    K.A(lambda en, px=px, ii=ii, d=d, ct=ct, n=n: en.activation(out=ii[:, 0:n], in_=px[:, 0:n], func=AF.Exp, bias=nbx[:, d, ct:ct + 1], scale=-1.0),
                        [px, nbx], [ii])
                    K.V(lambda en, ii=ii, n=n: en.tensor_scalar(out=ii[:, 0:n], in0=ii[:, 0:n], scalar1=1.0, scalar2=None, op0=ALU.add), [ii], [ii])
                    K.V(lambda en, ii=ii, n=n: en.reciprocal(out=ii[:, 0:n], in_=ii[:, 0:n]), [ii], [ii])
                    K.V(lambda en, m=m, t0=t0, n=n: en.scalar_tensor_tensor(out=m[:, 0:n], in0=Aa[:, t0:t0 + n], scalar=-1.0, in1=Aa[:, t0:t0 + n],
                                                                            op0=ALU.mult, op1=ALU.mult), [Aa], [m])
                    K.A(lambda en, m=m, n=n: en.activation(out=m[:, 0:n], in_=m[:, 0:n], func=AF.Ln, bias=one[:, 0:1], scale=1.0), [m, one], [m])
                    K.A(lambda en, m=m, n=n: en.activation(out=m[:, 0:n], in_=m[:, 0:n], func=AF.Exp, scale=0.5), [m], [m])
                    K.V(lambda en, m=m, ii=ii, n=n: en.tensor_tensor(out=m[:, 0:n], in0=m[:, 0:n], in1=ii[:, 0:n], op=ALU.mult), [m, ii], [m])
                    K.V(lambda en, m=m, t0=t0, n=n: en.tensor_tensor(out=INP[:, t0:t0 + n], in0=m[:, 0:n], in1=u[:, t0:t0 + n], op=ALU.mult), [m, u], [INP])
                if d == 0:
                    K.V(lambda en: en.tensor_tensor_scan(out=H0[:], data0=Aa[:], data1=INP[:], initial=0.0, op0=ALU.mult, op1=ALU.add), [Aa, INP], [H0])
                else:
                    K.V(lambda en: en.tensor_tensor_scan(out=H1[:, 0:NCTX][:, ::-1], data0=Aa[:, 0:NCTX][:, ::-1], data1=INP[:, 0:NCTX][:, ::-1],
                                                         initial=0.0, op0=ALU.mult, op1=ALU.add), [Aa, INP], [H1])
                    K.V(lambda en: en.tensor_tensor_scan(out=H1[:, NCTX:T][:, ::-1], data0=Aa[:, NCTX:T][:, ::-1], data1=INP[:, NCTX:T][:, ::-1],
                                                         initial=H1[:, 0:1], op0=ALU.mult, op1=ALU.add), [Aa, INP, H1], [H1])
            K.G(lambda en: en.tensor_tensor(out=H0[:], in0=H0[:], in1=H1[:], op=ALU.add), [H0, H1], [H0])
            g, tq = B1, B3
            K.load(g[:], projT.t[512 + ct * 128:512 + (ct + 1) * 128, :], reads=all_regions(projT, [4 + ct]), writes=[g])
            K.A(lambda en: en.activation(out=tq[:], in_=g[:], func=AF.Square), [g], [tq])
            K.V(lambda en: en.tensor_scalar(out=tq[:], in0=tq[:], scalar1=0.044715, scalar2=1.0, op0=ALU.mult, op1=ALU.add), [tq], [tq])
            K.G(lambda en: en.tensor_tensor(out=tq[:], in0=tq[:], in1=g[:], op=ALU.mult), [tq, g], [tq])
            K.A(lambda en: en.activation(out=tq[:], in_=tq[:], func=AF.Sigmoid, scale=1.5957691216057308), [tq], [tq])
            K.V(lambda en: en.tensor_tensor(out=tq[:], in0=tq[:], in1=g[:], op=ALU.mult), [tq, g], [tq])
            K.V(lambda en: en.tensor_tensor(out=tq[:], in0=tq[:], in1=H0[:], op=ALU.mult), [tq, H0], [tq])
            K.store(zT.t[ct * 128:(ct + 1) * 128, :], tq[:], reads=[tq], writes=z_regions(zT))


def even_mixer(K, e, I, S, projT, zT):
    lru_phase(K, e, I, projT, zT)
    retention_phase(K, e, I, S, projT, zT)


def retention_phase(K, e, I, S, projT, zT):
    O = [S["O_f"], S["O_b"]]
    NCH = T // 128
    with K.phase() as st:
        idf = K.sb(st, [128, 128], F32, "idf")
        idb = K.sb(st, [128, 128], BF16, "idb")
        K.load(idf[:], I["ident"].t[:, :], writes=[idf])
        K.V(lambda en: en.tensor_copy(out=idb[:], in_=idf[:]), [idf], [idb])
        lg = K.sb(st, [128, 8], F32, "lg")
        pos = K.sb(st, [128, 2], F32, "pos")
        K.load(lg[:], I["ret_lg"].t[e], writes=[lg])
        K.load(pos[:], I["pos_cols"].t[:, :], writes=[pos])
        tri = [K.sb(st, [128, 128], F32, "tri") for _ in range(2)]
        K.load(tri[0][:], I["tri_f"].t[:, :], writes=[tri[0]])
        K.load(tri[1][:], I["tri_b"].t[:, :], writes=[tri[1]])
        t8 = K.sb(st, [128, 8], F32, "t8")
        qd8 = K.sb(st, [128, 8], F32, "qd8")
        gi8 = K.sb(st, [128, 8], F32, "gi8")
        cd8 = K.sb(st, [128, 8], F32, "cd8")
        for d in range(2):
            K.V(lambda en, d=d: en.tensor_scalar(out=t8[:, d * 4:(d + 1) * 4], in0=lg[:, d * 4:(d + 1) * 4], scalar1=pos[:, d:d + 1], scalar2=None, op0=ALU.mult),
                [lg, pos], [t8])
        K.A(lambda en: en.activation(out=qd8[:], in_=t8[:], func=AF.Exp), [t8], [qd8])
        K.A(lambda en: en.activation(out=gi8[:], in_=t8[:], func=AF.Exp, scale=-1.0), [t8], [gi8])
        K.A(lambda en: en.activation(out=cd8[:], in_=lg[:], func=AF.Exp, scale=128.0), [lg], [cd8])
        GINV, QDEC, CD = [], [], []
        for d in range(2):
            for (lst, src) in ((GINV, gi8), (QDEC, qd8), (CD, cd8)):
                x = K.sb(st, [128, 4, 128], F32, "mul")
                K.V(lambda en, x=x, src=src, d=d: en.tensor_copy(out=x[:], in_=src[:, d * 4:(d + 1) * 4].unsqueeze(2).broadcast_to([128, 4, 128])), [src], [x])
                lst.append(x)
        qk = [K.sb(st, [128, 2, T], BF16, "qb"), K.sb(st, [128, 2, T], BF16, "kb")]
        X = K.sb(st, [128, T], F32, "ropex")
        SW = K.sb(st, [128, NLAT], F32, "ropesw")
        COS = K.sb(st, [128, NLAT], F32, "cos")
        SIN = K.sb(st, [128, NLAT], F32, "sin")
        K.load(COS[:], I["rope_cos"].t[:, :], writes=[COS])
        K.load(SIN[:], I["rope_sin"].t[:, :], writes=[SIN])
        for which in range(2):
            for p in range(2):
                ft = 8 + which * 2 + p
                K.load(X[:], projT.t[ft * 128:(ft + 1) * 128, :], reads=all_regions(projT, [ft]), writes=[X])
                for bi, (dst, src) in enumerate(((0, 32), (32, 0), (64, 96), (96, 64))):
                    K.op("vector" if bi % 2 == 0 else "scalar", copy_any(SW[dst:dst + 32, :], X[src:src + 32, NCTX:T]), [X], [SW])
                K.V(lambda en: en.tensor_tensor(out=X[:, NCTX:T], in0=X[:, NCTX:T], in1=COS[:], op=ALU.mult), [X, COS], [X])
                K.G(lambda en: en.tensor_tensor(out=SW[:], in0=SW[:], in1=SIN[:], op=ALU.mult), [SW, SIN], [SW])
                K.V(lambda en: en.tensor_tensor(out=X[:, NCTX:T], in0=X[:, NCTX:T], in1=SW[:], op=ALU.add), [X, SW], [X])
                K.A(lambda en, which=which, p=p: en.activation(out=qk[which][:, p, :], in_=X[:], func=AF.Copy, scale=(0.125 if which else 1.0)),
                    [X], [qk[which]])
        qb, kb = qk
        qz = K.sb(st, [128, 4, T], BF16, "qz")
        K.G(lambda en: en.memset(qz[:], 0.0), [], [qz])
        for h in range(4):
            p, b = h // 2, (h % 2) * 64
            K.op("vector" if h % 2 == 0 else "scalar", copy_any(qz[b:b + 64, h, :], qb[b:b + 64, p, :]), [qb], [qz])
        vin = [K.sb(st, [128, 4, 128], F32, "vin") for _ in range(2)]
        VD = [K.sb(st, [128, 4, 128], BF16, "VD") for _ in range(2)]
        ktok = [K.sb(st, [128, 2, 128], BF16, "ktok") for _ in range(2)]
        PT = [K.sb(st, [128, 4, 128], BF16, "PT") for _ in range(2)]
        osb = [K.sb(st, [128, 4, 128], F32, "osb") for _ in range(2)]
        Sf = [K.sb(st, [128, 4, 128], F32, "Sf") for _ in range(2)]
        Sb = [K.sb(st, [128, 4, 128], BF16, "Sb") for _ in range(2)]
        tS = [K.sb(st, [128, 4, 128], F32, "tS") for _ in range(2)]
        ps_v = K.ps(st, [128, 4, 128], F32, "ps_v")
        ps_k = K.ps(st, [128, 2, 128], BF16, "ps_k")
        ps_st = [K.ps(st, [128, 4, 128], F32, "ps_st") for _ in range(2)]
        ps_o = [K.ps(st, [128, 4, 128], F32, "ps_o") for _ in range(2)]
        ps_s = [K.ps(st, [128, 4, 128], F32, "ps_s") for _ in range(2)]
        for d in range(2):
            K.V(lambda en, d=d: en.memset(Sf[d][:], 0.0), [], [Sf[d]])
            K.V(lambda en, d=d: en.memset(Sb[d][:], 0.0), [], [Sb[d]])
        order = [list(range(NCH)), [1, 0] + list(range(NCH - 1, 1, -1))]
        vrows = projT.t[1536:2048, :].rearrange("(h p) t -> p h t", p=128)
        for step in range(NCH):
            for d in range(2):
                n = order[d][step]
                c0 = n * 128
                reg = region(c0)
                K.load(vin[d][:], vrows[:, :, c0:c0 + 128], reads=[projT.sub((12 + h, reg)) for h in range(4)], writes=[vin[d]])
                for h in range(4):
                    K.PE(lambda en, d=d, h=h: en.transpose(ps_v[:, h, :], vin[d][:, h, :], idf[:]), [vin[d], idf], [ps_v])
                K.V(lambda en, d=d: en.tensor_tensor(out=VD[d][:], in0=ps_v[:], in1=GINV[d][:], op=ALU.mult), [ps_v, GINV[d]], [VD[d]])
                for p in range(2):
                    K.PE(lambda en, p=p, c0=c0: en.transpose(ps_k[:, p, :], kb[:, p, c0:c0 + 128], idb[:]), [kb, idb], [ps_k])
                K.A(lambda en, d=d: en.activation(out=ktok[d][:], in_=ps_k[:], func=AF.Copy), [ps_k], [ktok[d]])
                for h in range(4):
                    p, b = h // 2, (h % 2) * 64
                    K.PE(lambda en, d=d, h=h, p=p, b=b, c0=c0: en.matmul(ps_st[d][:, h, :], lhsT=kb[:, p, c0:c0 + 128], rhs=qz[:, h, c0:c0 + 128],
                                                                     start=True, stop=True), [kb, qz], [ps_st[d]])
                K.V(lambda en, d=d: en.tensor_tensor(out=PT[d][:], in0=ps_st[d][:], in1=tri[d][:].unsqueeze(1).broadcast_to([128, 4, 128]), op=ALU.mult),
                    [ps_st[d], tri[d]], [PT[d]])
                for h in range(4):
                    p, b = h // 2, (h % 2) * 64
                    K.PE(lambda en, d=d, h=h: en.matmul(ps_o[d][:, h, :], lhsT=PT[d][:, h, :], rhs=VD[d][:, h, :], start=True, stop=False),
                         [PT[d], VD[d]], [ps_o[d]])
                    K.PE(lambda en, d=d, h=h, p=p, b=b, c0=c0: en.matmul(ps_o[d][:, h, :], lhsT=qz[:, h, c0:c0 + 128], rhs=Sb[d][:, h, :],
                                                                     start=False, stop=True), [qz, Sb[d]], [ps_o[d]])
                K.V(lambda en, d=d: en.tensor_tensor(out=osb[d][:], in0=ps_o[d][:], in1=QDEC[d][:], op=ALU.mult), [ps_o[d], QDEC[d]], [osb[d]])
                K.store(O[d].t[c0:c0 + 128, :], osb[d][:].rearrange("p h v -> p (h v)"), reads=[osb[d]], writes=[O[d].sub(n)])
                for h in range(4):
                    p = h // 2
                    K.PE(lambda en, d=d, h=h, p=p: en.matmul(ps_s[d][:, h, :], lhsT=ktok[d][:, p, :], rhs=VD[d][:, h, :], start=True, stop=True),
                         [ktok[d], VD[d]], [ps_s[d]])
                K.V(lambda en, d=d: en.tensor_tensor(out=tS[d][:], in0=ps_s[d][:], in1=Sf[d][:], op=ALU.add), [ps_s[d], Sf[d]], [tS[d]])
                K.G(lambda en, d=d: en.tensor_tensor(out=Sf[d][:], in0=tS[d][:], in1=CD[d][:], op=ALU.mult), [tS[d], CD[d]], [Sf[d]])
                K.A(lambda en, d=d: en.activation(out=Sb[d][:], in_=Sf[d][:], func=AF.Copy), [Sf[d]], [Sb[d]])
    head_norm_epilogue(K, I, O, projT, zT, gate_ft0=16, out_row0=512, center=True)


def head_norm_epilogue(K, I, O, projT, zT, gate_ft0, out_row0, center):
    NCH = T // 128
    with K.phase() as st:
        idf = K.sb(st, [128, 128], F32, "idf")
        K.load(idf[:], I["ident"].t[:, :], writes=[idf])
        zrows = projT.t[gate_ft0 * 128:(gate_ft0 + 4) * 128, :].rearrange("(h p) t -> p h t", p=128)
        zout = zT.t[out_row0:out_row0 + 512, :].rearrange("(h p) t -> p h t", p=128)
        ofs = [K.sb(st, [128, 4, 128], F32, "of") for _ in range(2)]
        obs = [K.sb(st, [128, 4, 128], F32, "ob") for _ in range(2)]
        zgs = [K.sb(st, [128, 4, 128], F32, "zg") for _ in range(2)]
        ocs = [K.sb(st, [128, 4, 128], F32, "oc") for _ in range(2)]
        sqs = [K.sb(st, [128, 4, 128], F32, "sq") for _ in range(2)]
        st4s = [K.sb(st, [128, 4], F32, "st4") for _ in range(2)]
        rs4s = [K.sb(st, [128, 4], F32, "rs4") for _ in range(2)]
        yo = [K.sb(st, [128, 4, 128], F32, "yo") for _ in range(2)]
        ps_t = [K.ps(st, [128, 4, 128], F32, "ps_t") for _ in range(2)]

        def chunk(n):
            c0 = n * 128
            reg = region(c0)
            of, ob, zg, y, pt = ofs[n % 2], obs[n % 2], zgs[n % 2], yo[n % 2], ps_t[n % 2]
            oc, sq, st4, rs4 = ocs[n % 2], sqs[n % 2], st4s[n % 2], rs4s[n % 2]
            K.load(of[:].rearrange("p h v -> p (h v)"), O[0].t[c0:c0 + 128, :], reads=[O[0].sub(n)], writes=[of])
            K.load(ob[:].rearrange("p h v -> p (h v)"), O[1].t[c0:c0 + 128, :], reads=[O[1].sub(n)], writes=[ob])
            K.load(zg[:], zrows[:, :, c0:c0 + 128], reads=[projT.sub((gate_ft0 + h, reg)) for h in range(4)], writes=[zg])
            if center:
                K.V(lambda en: en.tensor_tensor(out=of[:], in0=of[:], in1=ob[:], op=ALU.add), [of, ob], [of])
                K.V(lambda en: en.tensor_reduce(out=st4[:], in_=of[:], axis=AX.X, op=ALU.add), [of], [st4])
                K.V(lambda en: en.tensor_scalar(out=st4[:], in0=st4[:], scalar1=-1.0 / 128, scalar2=None, op0=ALU.mult), [st4], [st4])
                K.V(lambda en: en.tensor_tensor(out=oc[:], in0=of[:], in1=st4[:].unsqueeze(2).broadcast_to([128, 4, 128]), op=ALU.add), [of, st4], [oc])
            else:
                K.V(lambda en: en.tensor_tensor(out=oc[:], in0=of[:], in1=ob[:], op=ALU.add), [of, ob], [oc])
            K.G(lambda en: en.tensor_tensor(out=sq[:], in0=oc[:], in1=oc[:], op=ALU.mult), [oc], [sq])
            K.V(lambda en: en.tensor_reduce(out=rs4[:], in_=sq[:], axis=AX.X, op=ALU.add), [sq], [rs4])
            K.V(lambda en: en.tensor_scalar(out=rs4[:], in0=rs4[:], scalar1=1.0 / 128, scalar2=EPS, op0=ALU.mult, op1=ALU.add), [rs4], [rs4])
            K.A(lambda en: en.activation(out=rs4[:], in_=rs4[:], func=AF.Sqrt), [rs4], [rs4])
            K.V(lambda en: en.reciprocal(out=rs4[:], in_=rs4[:]), [rs4], [rs4])
            K.V(lambda en: en.tensor_tensor(out=oc[:], in0=oc[:], in1=rs4[:].unsqueeze(2).broadcast_to([128, 4, 128]), op=ALU.mult), [oc, rs4], [oc])
            for h in range(4):
                K.PE(lambda en, h=h: en.transpose(pt[:, h, :], oc[:, h, :], idf[:]), [oc, idf], [pt])
            K.A(lambda en: en.activation(out=zg[:], in_=zg[:], func=AF.Silu), [zg], [zg])
            K.V(lambda en: en.tensor_tensor(out=y[:], in0=pt[:], in1=zg[:], op=ALU.mult), [pt, zg], [y])
            K.store(zout[:, :, c0:c0 + 128], y[:], reads=[y], writes=[zT.sub(reg)])

        for n in range(0, NCH, 2):
            K.emit_interleaved([K.capture(chunk, n), K.capture(chunk, n + 1)])


GC = 32
NGC = T // GC


def gla_phase(K, o, I, S, projT, zT):
    O = [S["O_f"], S["O_b"]]
    with K.phase() as st:
        idf = K.sb(st, [128, 128], F32, "idf")
        idb = K.sb(st, [128, 128], BF16, "idb")
        K.load(idf[:], I["ident"].t[:, :], writes=[idf])
        K.V(lambda en: en.tensor_copy(out=idb[:], in_=idf[:]), [idf], [idb])
        tri = [K.sb(st, [128, 128], F32, "tri") for _ in range(2)]
        K.load(tri[0][:], I["tri_f"].t[:, :], writes=[tri[0]])
        K.load(tri[1][:], I["tri_b"].t[:, :], writes=[tri[1]])
        lgt = K.sb(st, [128, 2, 2, 4], F32, "lgt")
        lb = K.sb(st, [128, 2, 4], F32, "lb")
        oml = K.sb(st, [128, 2, 4], F32, "oml")
        K.load(lgt[:], I["hg_logits"].t[:, :, :, :], writes=[lgt])
        if o == 0:
            K.V(lambda en: en.memset(lb[:], 0.0), [], [lb])
        else:
            K.V(lambda en: en.tensor_tensor(out=lb[:], in0=lgt[:, :, 1, :], in1=lgt[:, :, 0, :], op=ALU.subtract), [lgt], [lb])
            K.A(lambda en: en.activation(out=lb[:], in_=lb[:], func=AF.Sigmoid), [lb], [lb])
        K.V(lambda en: en.tensor_scalar(out=oml[:], in0=lb[:], scalar1=-1.0, scalar2=1.0, op0=ALU.mult, op1=ALU.add), [lb], [oml])
        MASKX = K.sb(st, [128, T + GC], F32, "mask")
        K.G(lambda en: en.memset(MASKX[:], 1.0), [], [MASKX])
        K.G(lambda en: en.memset(MASKX[:, 0::GC], 0.0), [], [MASKX])
        Q = K.sb(st, [128, T], F32, "Q")
        F1 = K.sb(st, [128, T], F32, "F1")
        F2 = K.sb(st, [128, T], F32, "F2")
        F3 = K.sb(st, [128, T], F32, "F3")
        QTs = [[K.sb(st, [128, T], BF16, "QT") for _ in range(2)] for _ in range(2)]
        KTs = [[K.sb(st, [128, T], BF16, "KT") for _ in range(2)] for _ in range(2)]
        Vbs = [K.sb(st, [128, T], BF16, "Vb") for _ in range(2)]
        GLs = [[K.sb(st, [128, NGC], F32, "GL") for _ in range(2)] for _ in range(2)]
        TR = [[K.sb(st, [GC, 2, 128], BF16, "TR") for _ in range(2)] for _ in range(2)]
        PT = [[K.sb(st, [GC, GC], BF16, "PT") for _ in range(2)] for _ in range(2)]
        OS = [[K.sb(st, [GC, 8, 128], F32, "OS") for _ in range(2)] for _ in range(2)]
        Sf = [K.sb(st, [128, 128], F32, "Sf") for _ in range(2)]
        Sb = [K.sb(st, [128, 128], BF16, "Sb") for _ in range(2)]
        ps_tr = [K.ps(st, [GC, 2, 128], BF16, "ps_tr") for _ in range(2)]
        ps_st = [K.ps(st, [GC, GC], F32, "ps_st") for _ in range(2)]
        ps_o = [K.ps(st, [GC, 128], F32, "ps_o") for _ in range(2)]
        ps_s = [K.ps(st, [128, 128], F32, "ps_s") for _ in range(2)]
        nctx_c = NCTX // GC
        order = [list(range(NGC)), list(range(nctx_c - 1, -1, -1)) + list(range(NGC - 1, nctx_c - 1, -1))]

        def prep_head(h):
            QT, KT, Vb, GL = QTs[h % 2], KTs[h % 2], Vbs[h % 2], GLs[h % 2]
            K.load(Q[:], projT.t[h * 128:(h + 1) * 128, :], reads=all_regions(projT, [h]), writes=[Q])
            K.A(lambda en: en.activation(out=Q[:], in_=Q[:], func=AF.Silu), [Q], [Q])
            for d in range(2):
                ft = 4 + 4 * d + h
                K.load(F1[:], projT.t[ft * 128:(ft + 1) * 128, :], reads=all_regions(projT, [ft]), writes=[F1])
                K.A(lambda en: en.activation(out=F1[:], in_=F1[:], func=AF.Sigmoid), [F1], [F1])
                K.V(lambda en, d=d: en.tensor_scalar(out=F1[:], in0=F1[:], scalar1=oml[:, d, h:h + 1], scalar2=lb[:, d, h:h + 1], op0=ALU.mult, op1=ALU.add),
                    [F1, oml, lb], [F1])
                K.A(lambda en: en.activation(out=F2[:], in_=F1[:], func=AF.Ln), [F1], [F2])
                if d == 0:
                    K.V(lambda en: en.tensor_tensor_scan(out=F3[:], data0=MASKX[:, 0:T], data1=F2[:], initial=0.0, op0=ALU.mult, op1=ALU.add), [MASKX, F2], [F3])
                else:
                    K.V(lambda en: en.tensor_tensor_scan(out=F3[:, ::-1], data0=MASKX[:, 1:T + 1][:, ::-1], data1=F2[:, ::-1], initial=0.0,
                                                         op0=ALU.mult, op1=ALU.add), [MASKX, F2], [F3])
                K.A(lambda en: en.activation(out=F2[:], in_=F3[:], func=AF.Exp), [F3], [F2])
                K.V(lambda en, d=d: en.tensor_tensor(out=QT[d][:], in0=Q[:], in1=F2[:], op=ALU.mult), [Q, F2], [QT[d]])
                K.G(lambda en, d=d: en.tensor_copy(out=GL[d][:], in_=F2[:, (GC - 1 if d == 0 else 0)::GC]), [F2], [GL[d]])
                K.A(lambda en: en.activation(out=F2[:], in_=F3[:], func=AF.Exp, scale=-1.0), [F3], [F2])
                K.G(lambda en: en.tensor_scalar(out=F1[:], in0=F1[:], scalar1=-1.0, scalar2=1.0, op0=ALU.mult, op1=ALU.add), [F1], [F1])
                K.G(lambda en, d=d: en.tensor_tensor(out=KT[d][:], in0=F1[:], in1=F2[:], op=ALU.mult), [F1, F2], [KT[d]])
            ft = 12 + h
            K.load(F3[:], projT.t[ft * 128:(ft + 1) * 128, :], reads=all_regions(projT, [ft]), writes=[F3])
            K.G(lambda en: en.tensor_copy(out=Vb[:], in_=F3[:]), [F3], [Vb])

        def intra(d, step, h):
            QT, KT, Vb = QTs[h % 2], KTs[h % 2], Vbs[h % 2]
            n = order[d][step]
            c0 = n * GC
            tr, pt = TR[d][step % 2], PT[d][step % 2]
            K.PE(lambda en: en.transpose(ps_tr[d][:, 0, :], Vb[:, c0:c0 + GC], idb[:]), [Vb, idb], [ps_tr[d]])
            K.PE(lambda en: en.transpose(ps_tr[d][:, 1, :], KT[d][:, c0:c0 + GC], idb[:]), [KT[d], idb], [ps_tr[d]])
            K.A(lambda en: en.activation(out=tr[:], in_=ps_tr[d][:], func=AF.Copy), [ps_tr[d]], [tr])
            K.PE(lambda en: en.matmul(ps_st[d][:], lhsT=KT[d][:, c0:c0 + GC], rhs=QT[d][:, c0:c0 + GC], start=True, stop=True),
                 [KT[d], QT[d]], [ps_st[d]])
            K.V(lambda en: en.tensor_tensor(out=pt[:], in0=ps_st[d][:], in1=tri[d][0:GC, 0:GC], op=ALU.mult), [ps_st[d], tri[d]], [pt])

        def inter(d, step, h):
            QT, GL = QTs[h % 2], GLs[h % 2]
            n = order[d][step]
            c0 = n * GC
            tr, pt = TR[d][step % 2], PT[d][step % 2]
            os_ = OS[d][(step // 8) % 2]
            K.PE(lambda en: en.matmul(ps_o[d][:], lhsT=pt[:], rhs=tr[:, 0, :], start=True, stop=False), [pt, tr], [ps_o[d]])
            K.PE(lambda en: en.matmul(ps_o[d][:], lhsT=QT[d][:, c0:c0 + GC], rhs=Sb[d][:], start=False, stop=True), [QT[d], Sb[d]], [ps_o[d]])
            K.PE(lambda en: en.matmul(ps_s[d][:], lhsT=tr[:, 1, :], rhs=tr[:, 0, :], start=True, stop=True), [tr], [ps_s[d]])
            K.V(lambda en: en.tensor_copy(out=os_[:, n % 8, :], in_=ps_o[d][:]), [ps_o[d]], [os_])
            if step == 0:
                K.V(lambda en: en.tensor_copy(out=Sf[d][:], in_=ps_s[d][:]), [ps_s[d]], [Sf[d]])
            else:
                np_ = order[d][step - 1]
                K.V(lambda en: en.scalar_tensor_tensor(out=Sf[d][:], in0=Sf[d][:], scalar=GL[d][:, np_:np_ + 1], in1=ps_s[d][:],
                                                       op0=ALU.mult, op1=ALU.add), [Sf[d], GL[d], ps_s[d]], [Sf[d]])
            K.A(lambda en: en.activation(out=Sb[d][:], in_=Sf[d][:], func=AF.Copy, scale=GL[d][:, n:n + 1]), [Sf[d], GL[d]], [Sb[d]])
            if step % 8 == 7:
                n0 = (n // 8) * 8
                K.store(O[d].t[n0 * GC:(n0 + 8) * GC, h * 128:(h + 1) * 128].rearrange("(c p) v -> p c v", p=GC), os_[:],
                        reads=[os_], writes=[O[d].sub(n0 * GC // 128), O[d].sub(n0 * GC // 128 + 1)])

        prep_head(0)
        for h in range(4):
            nxt = K.capture(prep_head, h + 1) if h + 1 < 4 else []
            stride = (len(nxt) + NGC - 2) // (NGC - 1) if nxt else 0
            for d in range(2):
                K.V(lambda en, d=d: en.memset(Sb[d][:], 0.0), [], [Sb[d]])
            K.emit_interleaved([K.capture(intra, d, 0, h) for d in range(2)])
            for step in range(NGC):
                lists = []
                if step + 1 < NGC:
                    lists += [K.capture(intra, d, step + 1, h) for d in range(2)]
                lists += [K.capture(inter, d, step, h) for d in range(2)]
                if nxt:
                    lists.append(nxt[step * stride:(step + 1) * stride])
                K.emit_interleaved(lists)
            if nxt and NGC * stride < len(nxt):
                K.emit_interleaved([nxt[NGC * stride:]])
    head_norm_epilogue(K, I, O, projT, zT, gate_ft0=16, out_row0=0, center=False)


def odd_mixer(K, o, I, S, projT, zT):
    if not os.environ.get("GDN_DBG"):
        gla_phase(K, o, I, S, projT, zT)
    gdn_phase(K, o, I, S, projT, zT)


def _quarter(bank, q):
    v = bank.t
    if len(v.shape) == 3:
        return v[:, q, :]
    return v[:, q * 128:(q + 1) * 128]


def gdn_phase(K, o, I, S, projT, zT):
    O = [S["O_f"], S["O_b"]]
    GATES = S["GATES"]
    NSC = T // 128
    with K.phase() as st:
        G16 = K.sb(st, [16, T], F32, "G16")
        BETA = K.sb(st, [16, T], F32, "BETA")
        GCf = K.sb(st, [16, T], F32, "GCf")
        GCb = K.sb(st, [16, T], F32, "GCb")
        ab = K.sb(st, [16, 2], F32, "ab")
        nea = K.sb(st, [16, 1], F32, "nea")
        one64 = K.sb(st, [16, 64], F32, "one64")
        K.load(G16[:], projT.t[4608:4624, :], reads=all_regions(projT, [36]), writes=[G16])
        K.load(ab[:], I["gdn_ab"].t[o], writes=[ab])
        K.V(lambda en: en.memset(one64[:], 1.0), [], [one64])
        K.A(lambda en: en.activation(out=nea[:], in_=ab[:, 0:1], func=AF.Exp), [ab], [nea])
        K.V(lambda en: en.tensor_scalar(out=nea[:], in0=nea[:], scalar1=-1.0, scalar2=None, op0=ALU.mult), [nea], [nea])
        K.A(lambda en: en.activation(out=BETA[:], in_=G16[:], func=AF.Sigmoid), [G16], [BETA])
        K.A(lambda en: en.activation(out=G16[:], in_=G16[:], func=AF.Exp, bias=ab[:, 1:2], scale=1.0), [G16, ab], [G16])
        K.V(lambda en: en.tensor_scalar(out=G16[:], in0=G16[:], scalar1=1.0, scalar2=None, op0=ALU.add), [G16], [G16])
        K.A(lambda en: en.activation(out=G16[:], in_=G16[:], func=AF.Ln), [G16], [G16])
        K.V(lambda en: en.tensor_scalar(out=G16[:], in0=G16[:], scalar1=nea[:, 0:1], scalar2=None, op0=ALU.mult), [G16, nea], [G16])
        for c in range(T // 64):
            K.V(lambda en, c=c: en.tensor_tensor_scan(out=GCf[:, c * 64:(c + 1) * 64], data0=one64[:], data1=G16[:, c * 64:(c + 1) * 64],
                                                      initial=0.0, op0=ALU.mult, op1=ALU.add), [G16, one64], [GCf])
            K.V(lambda en, c=c: en.tensor_tensor_scan(out=GCb[:, c * 64:(c + 1) * 64][:, ::-1], data0=one64[:], data1=G16[:, c * 64:(c + 1) * 64][:, ::-1],
                                                      initial=0.0, op0=ALU.mult, op1=ALU.add), [G16, one64], [GCb])
        K.store(GATES.t[0], BETA[:], reads=[BETA], writes=[GATES])
        K.store(GATES.t[1], GCf[:], reads=[GCf], writes=[GATES])
        K.store(GATES.t[2], GCb[:], reads=[GCb], writes=[GATES])
    DBG = int(os.environ.get("GDN_DBG", "99"))
    if DBG == 0:
        return
    with K.phase() as st:
        idf = K.sb(st, [128, 128], F32, "idf")
        idb = K.sb(st, [128, 128], BF16, "idb")
        ones = K.sb(st, [128, 128], F32, "ones")
        K.load(idf[:], I["ident"].t[:, :], writes=[idf])
        K.load(ones[:], I["ones"].t[:, :], writes=[ones])
        K.V(lambda en: en.tensor_copy(out=idb[:], in_=idf[:]), [idf], [idb])
        SEL = K.sb(st, [16, 16, 128], F32, "SEL")
        K.load(SEL[:], I["gdn_sel"].t[:, :, :], writes=[SEL])
        MSK = [K.sb(st, [128, 128], F32, "msk") for _ in range(4)]
        for i in range(4):
            K.load(MSK[i][:], I["gdn_masks"].t[i], writes=[MSK[i]])
        cw = K.sb(st, [128, 12, 4], F32, "gcw")
        K.load(cw[:], I["gdn_cw"].t[o], writes=[cw])
        X = K.sb(st, [128, T], F32, "gX")
        U = K.sb(st, [128, T], F32, "gU")
        SQ = K.sb(st, [128, T], F32, "gSQ")
        NRMs = [[K.sb(st, [128, T], BF16, "gN") for _ in range(3)] for _ in range(2)]
        rs = [K.sb(st, [128, 512], F32, "grs") for _ in range(2)]
        psn = [K.ps(st, [128, 512], F32, "gpsn") for _ in range(2)]
        banks = [K.ps(st, [128, 4, 128], F32, "gbank") for _ in range(5)]
        bfbank = K.ps(st, [128, 4, 128], BF16, "gbfbank")
        qt = [K.psq(banks[i // 4], banks[i // 4].t[:, i % 4, :]) for i in range(8)]
        rb = [[banks[2], banks[3]], [banks[4], psn[0]]]
        rot = [[K.psq(rb[d][i % 2], _quarter(rb[d][i % 2], (i // 2) % 4)) for i in range(8)] for d in range(2)]
        qbf = [K.psq(bfbank, bfbank.t[:, i, :]) for i in range(4)]
        pp = {"i": [0, 0], "d": 0}

        def PQ():
            d_ = pp["d"]
            pp["i"][d_] += 1
            return rot[d_][pp["i"][d_] % 8]

        def mk(shape, dt, name):
            return [[K.sb(st, shape, dt, name) for _ in range(2)] for _ in range(2)]

        gate_in = mk([16, 2, 128], F32, "gin")
        cols = mk([128, 2, 16], F32, "gcols")
        DT_ = mk([128, 128], F32, "gDT")
        ETs = mk([128, 128], F32, "gETs")
        ETi = mk([128, 128], F32, "gETi")
        Nm = mk([128, 128], F32, "gN_")
        Mm = mk([128, 128], F32, "gM_")
        Pm = [mk([128, 128], F32, "gP") for _ in range(2)]
        PTm = [mk([128, 128], F32, "gPT") for _ in range(2)]
        Y = mk([128, 128], F32, "gY")
        Ybf = mk([128, 128], BF16, "gYbf")
        qkT = mk([128, 128], BF16, "gqkT")
        Vb_ = mk([128, 128], BF16, "gVb")
        Kbe = mk([128, 128], BF16, "gKbe")
        cvec = mk([128, 4], F32, "gcvec")
        Ut = mk([128, 128], F32, "gUt")
        nWT = mk([128, 128], BF16, "gnWT")
        kdA = mk([128, 128], BF16, "gkdA")
        kdB = mk([128, 128], BF16, "gkdB")
        egB = mk([128, 128], F32, "gegB")
        qdT = mk([128, 128], BF16, "gqdT")
        VNEW = [K.sb(st, [128, 128], BF16, "gVNEW") for _ in range(2)]
        OSB = mk([128, 128], F32, "gOSB")
        Sf = [K.sb(st, [128, 128], F32, "gSf") for _ in range(2)]
        Sb = [K.sb(st, [128, 128], BF16, "gSb") for _ in range(2)]
        for d in range(2):
            for par in range(2):
                K.G(lambda en, d=d, par=par: en.memset(kdA[d][par][:], 0.0), [], [kdA[d][par]])
                K.G(lambda en, d=d, par=par: en.memset(kdB[d][par][:], 0.0), [], [kdB[d][par]])
        order = [list(range(NSC)), [1, 0] + list(range(NSC - 1, 1, -1))]
        lastc = [(63, 127), (0, 64)]
        def prep(d, h, step):
            QN, KN, VN = NRMs[h % 2]
            pp["d"] = d
            par = step % 2
            sc = order[d][step]
            c0 = sc * 128
            rg, rb = 8 + d * 4 + h, d * 4 + h
            gin, cl_, dt_, ets, eti = gate_in[d][par], cols[d][par], DT_[d][par], ETs[d][par], ETi[d][par]
            N_, M_, y, ybf = Nm[d][par], Mm[d][par], Y[d][par], Ybf[d][par]
            cv = cvec[d][par]
            K.load(gin[:, 0, :], GATES.t[0, :, c0:c0 + 128], reads=[GATES], writes=[gin])
            K.load(gin[:, 1, :], GATES.t[1 + d, :, c0:c0 + 128], reads=[GATES], writes=[gin])
            p_gc, p_b, p_c, p_kk, p_qk = PQ(), PQ(), PQ(), PQ(), PQ()
            K.PE(lambda en: en.matmul(p_gc[:], lhsT=SEL[:, rg, :], rhs=gin[:, 1, :], start=True, stop=True), [SEL, gin], [p_gc])
            K.PE(lambda en: en.matmul(p_b[:], lhsT=SEL[:, rb, :], rhs=gin[:, 0, :], start=True, stop=True), [SEL, gin], [p_b])
            K.PE(lambda en: en.transpose(p_c[:, 0:16], gin[:, 0, :], idf[0:16, 0:16]), [gin, idf], [p_c])
            K.PE(lambda en: en.transpose(p_c[:, 16:32], gin[:, 1, :], idf[0:16, 0:16]), [gin, idf], [p_c])
            K.A(lambda en: en.activation(out=cl_[:].rearrange("p a b -> p (a b)"), in_=p_c[:, 0:32], func=AF.Copy), [p_c], [cl_])
            bcol, gcol = cl_[:, 0, rb:rb + 1], cl_[:, 1, rg:rg + 1]
            K.PE(lambda en: en.matmul(p_kk[:], lhsT=KN[:, c0:c0 + 128], rhs=KN[:, c0:c0 + 128], start=True, stop=True), [KN], [p_kk])
            K.PE(lambda en: en.matmul(p_qk[:], lhsT=KN[:, c0:c0 + 128], rhs=QN[:, c0:c0 + 128], start=True, stop=True), [KN, QN], [p_qk])
            p_kt, p_vt = qbf[(2 * d) % 4], qbf[(2 * d + 1) % 4]
            K.PE(lambda en: en.transpose(p_kt[:], KN[:, c0:c0 + 128], idb[:]), [KN, idb], [p_kt])
            K.PE(lambda en: en.transpose(p_vt[:], VN[:, c0:c0 + 128], idb[:]), [VN, idb], [p_vt])
            K.V(lambda en: en.tensor_scalar(out=dt_[:], in0=p_gc[:], scalar1=gcol, scalar2=0.0, op0=ALU.subtract, op1=ALU.min), [p_gc, cl_], [dt_])
            K.A(lambda en: en.activation(out=egB[d][par][:], in_=p_gc[:], func=AF.Exp), [p_gc], [egB[d][par]])
            for (r0, col) in ((0, lastc[d][0]), (64, lastc[d][1])):
                K.V(lambda en, r0=r0, col=col: en.tensor_tensor(out=cv[r0:r0 + 64, 2:3], in0=p_gc[r0:r0 + 64, col:col + 1], in1=cl_[r0:r0 + 64, 1, rg:rg + 1],
                                                               op=ALU.subtract), [p_gc, cl_], [cv])
            K.A(lambda en: en.activation(out=cv[:, 3:4], in_=cv[:, 2:3], func=AF.Exp), [cv], [cv])
            K.A(lambda en: en.activation(out=dt_[:], in_=dt_[:], func=AF.Exp), [dt_], [dt_])
            K.G(lambda en: en.tensor_tensor(out=ets[:], in0=dt_[:], in1=MSK[2 * d][:], op=ALU.mult), [dt_, MSK[2 * d]], [ets])
            K.G(lambda en: en.tensor_tensor(out=eti[:], in0=dt_[:], in1=MSK[2 * d + 1][:], op=ALU.mult), [dt_, MSK[2 * d + 1]], [eti])
            K.V(lambda en: en.tensor_tensor(out=ets[:], in0=ets[:], in1=p_b[:], op=ALU.mult), [ets, p_b], [ets])
            K.V(lambda en: en.tensor_tensor(out=N_[:], in0=ets[:], in1=p_kk[:], op=ALU.mult), [ets, p_kk], [N_])
            K.V(lambda en: en.tensor_tensor(out=qkT[d][par][:], in0=eti[:], in1=p_qk[:], op=ALU.mult), [eti, p_qk], [qkT[d][par]])
            p_m = PQ()
            K.PE(lambda en: en.transpose(p_m[:], N_[:], idf[:]), [N_, idf], [p_m])
            K.A(lambda en: en.activation(out=M_[:], in_=p_m[:], func=AF.Copy), [p_m], [M_])
            K.G(lambda en: en.tensor_tensor(out=y[:], in0=idf[:], in1=N_[:], op=ALU.subtract), [idf, N_], [y])
            Pc, PTc = N_, M_
            for lev in range(1, 6):
                Pn, PTn = Pm[lev % 2][d][par], PTm[lev % 2][d][par]
                p1 = PQ()
                K.PE(lambda en, p1=p1, Pc=Pc, PTc=PTc: en.matmul(p1[:], lhsT=Pc[:], rhs=PTc[:], start=True, stop=True), [Pc, PTc], [p1])
                K.A(lambda en, p1=p1, PTn=PTn: en.activation(out=PTn[:], in_=p1[:], func=AF.Copy), [p1], [PTn])
                if lev < 5:
                    p2 = PQ()
                    K.PE(lambda en, p2=p2, Pc=Pc, PTc=PTc: en.matmul(p2[:], lhsT=PTc[:], rhs=Pc[:], start=True, stop=True), [Pc, PTc], [p2])
                    K.A(lambda en, p2=p2, Pn=Pn: en.activation(out=Pn[:], in_=p2[:], func=AF.Copy), [p2], [Pn])
                p3 = PQ()
                K.PE(lambda en, p3=p3, PTn=PTn: en.matmul(p3[:], lhsT=PTn[:], rhs=y[:], start=True, stop=True), [PTn, y], [p3])
                K.V(lambda en, p3=p3: en.tensor_tensor(out=y[:], in0=y[:], in1=p3[:], op=ALU.add), [y, p3], [y])
                Pc, PTc = Pn, PTn
            K.A(lambda en: en.activation(out=ybf[:], in_=y[:], func=AF.Copy), [y], [ybf])
            K.V(lambda en: en.tensor_scalar(out=Vb_[d][par][:], in0=p_vt[:], scalar1=bcol, scalar2=None, op0=ALU.mult), [p_vt, cl_], [Vb_[d][par]])
            K.A(lambda en: en.activation(out=cv[:, 0:1], in_=gcol, func=AF.Exp), [cl_], [cv])
            K.V(lambda en: en.tensor_tensor(out=cv[:, 1:2], in0=cv[:, 0:1], in1=bcol, op=ALU.mult), [cv, cl_], [cv])
            K.V(lambda en: en.tensor_scalar(out=Kbe[d][par][:], in0=p_kt[:], scalar1=cv[:, 1:2], scalar2=None, op0=ALU.mult), [p_kt, cv], [Kbe[d][par]])
            K.V(lambda en: en.tensor_scalar(out=kdA[d][par][0:64, :], in0=p_kt[0:64, :], scalar1=cv[0:64, 3:4], scalar2=None, op0=ALU.mult), [p_kt, cv], [kdA[d][par]])
            K.V(lambda en: en.tensor_scalar(out=kdB[d][par][64:128, :], in0=p_kt[64:128, :], scalar1=cv[64:128, 3:4], scalar2=None, op0=ALU.mult), [p_kt, cv], [kdB[d][par]])
            K.V(lambda en: en.tensor_tensor(out=qdT[d][par][:], in0=QN[:, c0:c0 + 128], in1=egB[d][par][:], op=ALU.mult), [QN, egB[d][par]], [qdT[d][par]])
            p_u, p_w = PQ(), PQ()
            K.PE(lambda en: en.matmul(p_u[:], lhsT=ybf[:], rhs=Vb_[d][par][:], start=True, stop=True), [ybf, Vb_[d][par]], [p_u])
            K.PE(lambda en: en.matmul(p_w[:], lhsT=Kbe[d][par][:], rhs=ybf[:], start=True, stop=True), [ybf, Kbe[d][par]], [p_w])
            K.A(lambda en: en.activation(out=Ut[d][par][:], in_=p_u[:], func=AF.Copy), [p_u], [Ut[d][par]])
            K.A(lambda en: en.activation(out=nWT[d][par][:], in_=p_w[:], func=AF.Copy, scale=-1.0), [p_w], [nWT[d][par]])

        def seq(d, h, step):
            par = step % 2
            sc = order[d][step]
            c0 = sc * 128
            halves = ((0, kdA[d][par]), (64, kdB[d][par]))
            if d == 1:
                halves = halves[::-1]
            p_vn, p_o, p_s = qt[d * 4 + 0], qt[d * 4 + 1], qt[d * 4 + 2]
            for (r0, kd) in halves:
                col = lastc[d][0] if r0 == 0 else lastc[d][1]
                K.PE(lambda en: en.matmul(p_vn[:], lhsT=nWT[d][par][:], rhs=Sb[d][:], start=True, stop=True), [nWT[d][par], Sb[d]], [p_vn])
                K.V(lambda en, r0=r0: en.tensor_tensor(out=VNEW[d][r0:r0 + 64, :], in0=p_vn[r0:r0 + 64, :], in1=Ut[d][par][r0:r0 + 64, :], op=ALU.add),
                    [p_vn, Ut[d][par]], [VNEW[d]])
                K.PE(lambda en: en.matmul(p_o[:], lhsT=qdT[d][par][:], rhs=Sb[d][:], start=True, stop=False), [qdT[d][par], Sb[d]], [p_o])
                K.PE(lambda en: en.matmul(p_o[:], lhsT=qkT[d][par][:], rhs=VNEW[d][:], start=False, stop=True), [qkT[d][par], VNEW[d]], [p_o])
                K.A(lambda en, r0=r0: en.activation(out=OSB[d][par][r0:r0 + 64, :], in_=p_o[r0:r0 + 64, :], func=AF.Copy), [p_o], [OSB[d][par]])
                K.PE(lambda en, kd=kd: en.matmul(p_s[:], lhsT=kd[:], rhs=VNEW[d][:], start=True, stop=True), [kd, VNEW[d]], [p_s])
                K.V(lambda en, col=col: en.scalar_tensor_tensor(out=Sf[d][:], in0=Sf[d][:], scalar=egB[d][par][:, col:col + 1], in1=p_s[:],
                                                               op0=ALU.mult, op1=ALU.add), [Sf[d], egB[d][par], p_s], [Sf[d]])
                K.A(lambda en: en.activation(out=Sb[d][:], in_=Sf[d][:], func=AF.Copy), [Sf[d]], [Sb[d]])
            K.store(O[d].t[c0:c0 + 128, h * 128:(h + 1) * 128], OSB[d][par][:], reads=[OSB[d][par]], writes=[O[d].sub(sc)])

        def head_prep(h):
            NRM = NRMs[h % 2]
            for wi in range(3):
                c = wi * 4 + h
                ft = 20 + c
                K.load(X[:], projT.t[ft * 128:(ft + 1) * 128, :], reads=all_regions(projT, [ft]), writes=[X])
                K.V(lambda en, c=c: en.tensor_scalar(out=U[:], in0=X[:], scalar1=cw[:, c, 2:3], scalar2=None, op0=ALU.mult), [X, cw], [U])
                for (s0, s1) in SEGS:
                    for (j, off) in ((0, -2), (1, -1), (3, 1)):
                        if off < 0:
                            oa, ob, ia, ib = s0 - off, s1, s0, s1 + off
                        else:
                            oa, ob, ia, ib = s0, s1 - off, s0 + off, s1
                        K.V(lambda en, c=c, j=j, oa=oa, ob=ob, ia=ia, ib=ib: en.scalar_tensor_tensor(
                            out=U[:, oa:ob], in0=X[:, ia:ib], scalar=cw[:, c, j:j + 1], in1=U[:, oa:ob], op0=ALU.mult, op1=ALU.add), [X, U, cw], [U])
                K.A(lambda en: en.activation(out=U[:], in_=U[:], func=AF.Silu), [U], [U])
                if wi == 2:
                    K.G(lambda en: en.tensor_copy(out=NRM[2][:], in_=U[:]), [U], [NRM[2]])
                    continue
                K.G(lambda en: en.tensor_tensor(out=SQ[:], in0=U[:], in1=U[:], op=ALU.mult), [U], [SQ])
                for ti, (t0, n) in enumerate(TT512):
                    pn, r = psn[1], rs[ti % 2]
                    K.PE(lambda en, pn=pn, t0=t0, n=n: en.matmul(pn[:, 0:n], lhsT=ones[:], rhs=SQ[:, t0:t0 + n], start=True, stop=True), [ones, SQ], [pn])
                    K.V(lambda en, pn=pn, r=r, n=n: en.tensor_scalar(out=r[:, 0:n], in0=pn[:, 0:n], scalar1=EPS, scalar2=None, op0=ALU.add), [pn], [r])
                    K.A(lambda en, r=r, n=n: en.activation(out=r[:, 0:n], in_=r[:, 0:n], func=AF.Sqrt), [r], [r])
                    K.V(lambda en, r=r, n=n: en.reciprocal(out=r[:, 0:n], in_=r[:, 0:n]), [r], [r])
                    K.V(lambda en, r=r, t0=t0, n=n, wi=wi: en.scalar_tensor_tensor(out=NRM[wi][:, t0:t0 + n], in0=U[:, t0:t0 + n],
                                                                                 scalar=(128.0 ** -0.5 if wi == 0 else 1.0), in1=r[:, 0:n],
                                                                                 op0=ALU.mult, op1=ALU.mult), [U, r], [NRM[wi]])

        NH = 4 if DBG > 10 else 1
        head_prep(0)
        for h in range(NH):
            nxt = K.capture(head_prep, h + 1) if h + 1 < NH else []
            stride = (len(nxt) + NSC - 1) // NSC if nxt else 0
            for d in range(2):
                K.V(lambda en, d=d: en.memset(Sf[d][:], 0.0), [], [Sf[d]])
                K.V(lambda en, d=d: en.memset(Sb[d][:], 0.0), [], [Sb[d]])
                K.V(lambda en, d=d: en.memset(VNEW[d][:], 0.0), [], [VNEW[d]])
            if DBG == 1:
                continue
            K.emit_interleaved([K.capture(prep, d, h, 0) for d in range(2)])
            if DBG == 2:
                continue
            for step in range(NSC if DBG > 10 else 1):
                lists = []
                if step + 1 < NSC:
                    lists += [K.capture(prep, d, h, step + 1) for d in range(2)]
                lists += [K.capture(seq, d, h, step) for d in range(2)]
                if nxt:
                    lists.append(nxt[step * stride:(step + 1) * stride])
                K.emit_interleaved(lists)
    if DBG > 10:
        head_norm_epilogue(K, I, O, projT, zT, gate_ft0=32, out_row0=512, center=False)


def host_inputs(inputs, b):
    f = lambda a: np.ascontiguousarray(np.asarray(a, np.float32))
    m = {
        "x_b": f(inputs["x"][b]), "ctx_b": f(inputs["ctx"][b]),
        "c_b": _col(inputs["c"][b], 8), "c_ctx": _col(inputs["c_ctx"], 8),
        "ada_w": f(inputs["ada_w"]),
        "ada_b": np.stack([_col(inputs["ada_b"][l], 48) for l in range(DEPTH)]),
        "norm1_g": np.stack([_col(inputs["norm1_g"][l], 8) for l in range(DEPTH)]),
        "norm2_g": np.stack([_col(inputs["norm2_g"][l], 8) for l in range(DEPTH)]),
        "mix_w_out": f(inputs["mix_w_out"]), "mlp_w1": f(inputs["mlp_w1"]), "mlp_w2": f(inputs["mlp_w2"]),
        "ev_w_in": f(inputs["ev_w_in"]), "od_w_in": f(inputs["od_w_in"]),
        "final_g": _col(inputs["final_g"], 8),
        "ident": np.eye(128, dtype=np.float32), "ones": np.ones((128, 128), np.float32),
    }
    m.update(host_mixer_inputs(inputs))
    return m


def _bd(w):
    out = np.zeros((2, 4, 128, 128), np.float32)
    for d in range(2):
        for ct in range(4):
            out[d, ct, 0:64, 0:64] = w[d, 2 * ct]
            out[d, ct, 64:128, 64:128] = w[d, 2 * ct + 1]
    return out


def host_mixer_inputs(inputs):
    f = lambda a: np.ascontiguousarray(np.asarray(a, np.float32))
    m = {}
    m["lru_cw"] = f(np.asarray(inputs["lru_conv_w"]).reshape(2, 4, 4, 128).transpose(0, 3, 2, 1))
    m["lru_cb"] = f(np.asarray(inputs["lru_conv_b"]).reshape(2, 4, 128).transpose(0, 2, 1))
    for nm, key in (("lru_ba", "lru_ba"), ("lru_bx", "lru_bx"), ("lru_lam", "lru_lambda")):
        m[nm] = f(np.asarray(inputs[key]).reshape(2, 2, 4, 128).transpose(0, 3, 1, 2))
    m["lru_wa_bd"] = np.stack([_bd(np.asarray(inputs["lru_wa"][e])) for e in range(2)])
    m["lru_wx_bd"] = np.stack([_bd(np.asarray(inputs["lru_wx"][e])) for e in range(2)])
    m["ret_lg"] = f(np.broadcast_to(np.asarray(inputs["ret_log_gamma"]).reshape(2, 1, 8), (2, 128, 8)))
    j = np.arange(128, dtype=np.float32)
    m["pos_cols"] = f(np.stack([j + 1.0, 128.0 - j], 1))
    m["tri_f"] = f((j[:, None] <= j[None, :]).astype(np.float32))
    m["tri_b"] = f((j[:, None] >= j[None, :]).astype(np.float32))
    n_freq = 16
    inv = np.power(np.float32(10000.0), -np.arange(n_freq, dtype=np.float32) / n_freq).astype(np.float32)
    rows = NLAT // 64
    r = np.arange(rows, dtype=np.float32)
    c = np.arange(64, dtype=np.float32)
    row_ang = np.broadcast_to(r[:, None, None] * inv, (rows, 64, n_freq))
    col_ang = np.broadcast_to(c[None, :, None] * inv, (rows, 64, n_freq))
    ang = np.concatenate([row_ang, col_ang], -1).reshape(NLAT, 32).astype(np.float32)
    cosT = np.cos(ang).T.astype(np.float32)
    sinT = np.sin(ang).T.astype(np.float32)
    m["rope_cos"] = f(np.concatenate([cosT, cosT, cosT, cosT], 0))
    m["rope_sin"] = f(np.concatenate([-sinT, sinT, -sinT, sinT], 0))
    m["hg_logits"] = f(np.asarray(inputs["hg_lb_logits"]).reshape(2, 2, 4, 128).transpose(3, 0, 1, 2))
    m["gdn_cw"] = f(np.asarray(inputs["gdn_conv_w"]).reshape(2, 4, 12, 128).transpose(0, 3, 2, 1))
    ab = np.zeros((2, 16, 2), np.float32)
    ab[:, 8:16, 0] = np.asarray(inputs["gdn_a_log"]).reshape(2, 8)
    ab[:, 8:16, 1] = np.asarray(inputs["gdn_dt_bias"]).reshape(2, 8)
    m["gdn_ab"] = ab
    sel = np.zeros((16, 16, 128), np.float32)
    for r in range(16):
        sel[r, r, :] = 1.0
    m["gdn_sel"] = sel
    blk = (j[:, None] // 64) == (j[None, :] // 64)
    m["gdn_masks"] = f(np.stack([(j[:, None] < j[None, :]) & blk, (j[:, None] <= j[None, :]) & blk,
                                 (j[:, None] > j[None, :]) & blk, (j[:, None] >= j[None, :]) & blk]).astype(np.float32))
    return m


def build_mixer_test(kind, idx, F):
    nc = bass.Bass("TRN2", target_bir_lowering=False)
    K = KB(nc, debug=True)
    I = {}
    I["ident"] = K.din("ident", [128, 128])
    I["ones"] = K.din("ones", [128, 128])
    mixer_inputs(K, I)
    projT = K.din("projT", [F, T])
    zT = K.dout("zT", [D, T])
    done = K.dout("done", [128, 128])
    S = mixer_scratch(K)
    if kind == "even":
        even_mixer(K, idx, I, S, projT, zT)
    else:
        odd_mixer(K, idx, I, S, projT, zT)
    K.P.barrier()
    fin = K.P.dma("sync", done.t[:, :], I["ident"].t[:, :])
    K.P.emit(final_wait_ops=[fin])
    return nc, K


_PROGRAM_CACHE = {}


def kernel(**inputs):
    n_cores = 8
    if "full" not in _PROGRAM_CACHE:
        _PROGRAM_CACHE["full"] = build_program({"debug": False})[0]
    nc = _PROGRAM_CACHE["full"]
    shared = None
    in_maps = []
    for b in range(n_cores):
        m = host_inputs(inputs, b)
        if shared is None:
            shared = m
        else:
            for k in list(m.keys()):
                if k not in ("x_b", "ctx_b", "c_b"):
                    m[k] = shared[k]
        in_maps.append(m)
    res = run_bass_kernel_spmd(nc, in_maps, core_ids=list(range(n_cores)))
    out = np.stack([np.asarray(res.results[b]["out"], dtype=np.float32) for b in range(n_cores)], axis=0)
    return out
```

```python
import contextlib
import os
import numpy as np
import concourse.bass as bass
import concourse.mybir as mybir
from concourse.bass_utils import run_bass_kernel_spmd

F32 = mybir.dt.float32
BF16 = mybir.dt.bfloat16
AF = mybir.ActivationFunctionType
ALU = mybir.AluOpType
AX = mybir.AxisListType

D = 1024
NCTX = 256
NLAT = 4096
T = NCTX + NLAT
DEPTH = 4
DFF = 4096
EV_IN = 2560
OD_IN = 4624
EPS = 1e-6

ENGS = ("sync", "tensor", "vector", "scalar", "gpsimd")
SEM_WRAP = 20000
DMA_SLOTS = 8
SAME_ENGINE_SYNC = True


class Buf:
    __slots__ = ("w", "r", "x")

    def __init__(self, excl=False):
        self.w = None
        self.r = []
        self.x = excl


class Op:
    __slots__ = ("eng", "fn", "deps", "dma", "signaled", "token")

    def __init__(self, eng, fn, dma):
        self.eng = eng
        self.fn = fn
        self.deps = []
        self.dma = dma
        self.signaled = False
        self.token = None


class TT:
    def __init__(self, t):
        self.t = t
        self.b = Buf()
        self.subs = {}

    def sub(self, key):
        b = self.subs.get(key)
        if b is None:
            b = Buf()
            self.subs[key] = b
        return b

    def __getitem__(self, k):
        return self.t[k]


def _bufs(lst):
    out = []
    for x in lst:
        if isinstance(x, Buf):
            out.append(x)
        elif isinstance(x, TT):
            out.append(x.b)
        else:
            raise TypeError(type(x))
    return out


class Prog:
    def __init__(self, nc):
        self.nc = nc
        self.ops = {e: [] for e in ENGS}
        self.dma_ops = {e: [] for e in ENGS}
        self.nops = 0
        self.barrier_deps = []
        self.barrier_pending = set()

    def barrier(self):
        deps = []
        for e in ENGS:
            for op in reversed(self.ops[e]):
                if not op.dma:
                    deps.append(op)
                    break
            deps += self.dma_ops[e][-DMA_SLOTS:]
        self.barrier_deps = deps
        self.barrier_pending = set(ENGS)

    def add(self, eng, fn, reads=(), writes=(), dma=False):
        op = Op(eng, fn, dma)
        reads = _bufs(reads)
        writes = _bufs(writes)
        xr = [b for b in reads if b.x]
        if xr:
            writes = writes + [b for b in xr if b not in writes]
            reads = [b for b in reads if not b.x]
        deps = []
        for b in reads:
            if b.w is not None:
                deps.append(b.w)
        for b in writes:
            if b.w is not None:
                deps.append(b.w)
            lastc = {}
            for r_ in b.r:
                if r_.dma:
                    deps.append(r_)
                else:
                    lastc[r_.eng] = r_
            deps.extend(lastc.values())
        for b in writes:
            b.w = op
            b.r = []
        for b in reads:
            b.r.append(op)
        if eng in self.barrier_pending:
            self.barrier_pending.discard(eng)
            deps.extend(self.barrier_deps)
        if dma:
            lst = self.dma_ops[eng]
            if len(lst) >= DMA_SLOTS:
                deps.append(lst[len(lst) - DMA_SLOTS])
            lst.append(op)
        seen = set()
        dd = []
        for d in deps:
            if id(d) in seen or d is op:
                continue
            seen.add(id(d))
            if (not d.dma) and (not dma) and d.eng == eng and (eng == "tensor" or not SAME_ENGINE_SYNC):
                continue
            dd.append(d)
            d.signaled = True
        op.deps = dd
        self.ops[eng].append(op)
        self.nops += 1
        return op

    def dma(self, eng, out, in_, reads=(), writes=(), **kw):
        return self.add(eng, lambda e: e.dma_start(out=out, in_=in_, **kw), reads, writes, dma=True)

    def emit(self, final_wait_ops=()):
        nc = self.nc
        nsem = {}
        for e in ENGS:
            cnt = 0
            dcnt = 0
            for op in self.ops[e]:
                if op.dma:
                    op.token = ("d", e, dcnt % DMA_SLOTS, 16 * (dcnt // DMA_SLOTS + 1))
                    dcnt += 1
                elif op.signaled:
                    op.token = ("c", e, cnt // SEM_WRAP, cnt % SEM_WRAP + 1)
                    cnt += 1
            nsem[e] = (cnt + SEM_WRAP - 1) // SEM_WRAP
        with contextlib.ExitStack() as st:
            sems = {}
            for e in ENGS:
                for k in range(nsem[e]):
                    sems[("c", e, k)] = st.enter_context(nc.semaphore(f"c_{e}_{k}"))
                if self.dma_ops[e]:
                    for k in range(DMA_SLOTS):
                        sems[("d", e, k)] = st.enter_context(nc.semaphore(f"d_{e}_{k}"))
            block = st.enter_context(nc.Block())

            def make(e):
                def body(eng):
                    waited = {}
                    for op in self.ops[e]:
                        for d in op.deps:
                            key = d.token[:3]
                            val = d.token[3]
                            if key[0] == "c":
                                kk = (key[0], key[1])
                                cur = waited.get(kk, (-1, 0))
                                if (key[2], val) <= cur:
                                    continue
                                waited[kk] = (key[2], val)
                            else:
                                if waited.get(key, 0) >= val:
                                    continue
                                waited[key] = val
                            eng.wait_ge(sems[key], val)
                        ins = op.fn(eng)
                        if op.token is not None:
                            ins.then_inc(sems[op.token[:3]], 16 if op.dma else 1)
                    if e == "sync":
                        for d in final_wait_ops:
                            eng.wait_ge(sems[d.token[:3]], d.token[3])
                return body

            for e in ENGS:
                if self.ops[e] or (e == "sync" and final_wait_ops):
                    getattr(block, e)(make(e))


class KB:
    def __init__(self, nc, debug=False):
        self.nc = nc
        self.P = Prog(nc)
        self.debug = debug
        self.dram = {}
        self.uid = 0
        self.rr = 0

    def din(self, name, shape, dt=F32):
        t = TT(self.nc.dram_tensor(name, list(shape), dt, kind="ExternalInput").ap())
        self.dram[name] = t
        return t

    def dout(self, name, shape, dt=F32):
        t = TT(self.nc.dram_tensor(name, list(shape), dt, kind="ExternalOutput").ap())
        self.dram[name] = t
        return t

    def dscr(self, name, shape, dt=F32, dbg=False):
        kind = "ExternalOutput" if (self.debug and dbg) else "Internal"
        t = TT(self.nc.dram_tensor(name, list(shape), dt, kind=kind).ap())
        self.dram[name] = t
        return t

    def sb(self, st, shape, dt=F32, name=None):
        self.uid += 1
        return TT(st.enter_context(self.nc.sbuf_tensor(f"{name or 's'}{self.uid}", list(shape), dt)))

    def ps(self, st, shape, dt=F32, name=None):
        self.uid += 1
        nfree = 512 if dt == F32 else 1024
        full = st.enter_context(self.nc.psum_tensor(f"{name or 'p'}{self.uid}", [128, nfree], dt))
        n = 1
        for x in shape[1:]:
            n *= x
        assert n <= nfree
        v = full[0:shape[0], 0:n]
        if len(shape) == 3:
            v = v.rearrange("p (a b) -> p a b", a=shape[1])
        t = TT(v)
        t.b = Buf(excl=True)
        return t

    def psq(self, bank, view):
        t = TT(view)
        t.b = bank.b
        return t

    @contextlib.contextmanager
    def phase(self):
        with contextlib.ExitStack() as st:
            yield st
        self.P.barrier()

    def capture(self, f, *args):
        saved = self.P.add
        lst = []

        def rec(eng, fn, reads=(), writes=(), dma=False):
            lst.append((eng, fn, list(reads), list(writes), dma))
            return None
        self.P.add = rec
        try:
            f(*args)
        finally:
            self.P.add = saved
        return lst

    def emit_interleaved(self, lists):
        idx = [0] * len(lists)
        live = True
        while live:
            live = False
            for k, l in enumerate(lists):
                if idx[k] < len(l):
                    self.P.add(*l[idx[k]])
                    idx[k] += 1
                    live = True

    def op(self, eng, fn, reads=(), writes=()):
        return self.P.add(eng, fn, reads, writes)

    def V(self, fn, reads=(), writes=()):
        return self.P.add("vector", fn, reads, writes)

    def A(self, fn, reads=(), writes=()):
        return self.P.add("scalar", fn, reads, writes)

    def G(self, fn, reads=(), writes=()):
        return self.P.add("gpsimd", fn, reads, writes)

    def PE(self, fn, reads=(), writes=()):
        return self.P.add("tensor", fn, reads, writes)

    def VA(self, fn, reads=(), writes=()):
        self.rr += 1
        return self.P.add("vector" if self.rr % 2 else "scalar", fn, reads, writes)

    def load(self, out, in_, reads=(), writes=(), eng="sync"):
        return self.P.dma(eng, out, in_, reads, writes)

    def store(self, out, in_, reads=(), writes=(), eng="gpsimd"):
        return self.P.dma(eng, out, in_, reads, writes)


def copy_any(out, in_):
    def f(e):
        if hasattr(e, "tensor_copy"):
            return e.tensor_copy(out=out, in_=in_)
        return e.activation(out=out, in_=in_, func=AF.Copy)
    return f


def token_groups(n):
    gs = []
    t = 0
    while t < NCTX:
        m = min(n, NCTX - t)
        gs.append((t, m, 1))
        t += m
    while t < T:
        m = min(n, T - t)
        gs.append((t, m, 0))
        t += m
    return gs


def phase_input_transpose(K, x_b, ctx_b, xT, ident):
    with K.phase() as st:
        idt = K.sb(st, [128, 128], F32, "ident")
        K.load(idt[:], ident[:, :], writes=[idt])
        xin = [K.sb(st, [128, D], F32, "xin") for _ in range(2)]
        stg = [K.sb(st, [128, 8, 128], F32, "xstg") for _ in range(2)]
        pss = [K.ps(st, [128, 4, 128], F32, "ptr") for _ in range(2)]
        for i in range(T // 128):
            xt = xin[i % 2]
            sg = stg[i % 2]
            src = ctx_b.t[i * 128:(i + 1) * 128, :] if i < 2 else x_b.t[(i - 2) * 128:(i - 1) * 128, :]
            K.load(xt[:], src, writes=[xt])
            for kg in range(2):
                ps = pss[kg]
                for j in range(4):
                    k = kg * 4 + j
                    K.PE(lambda e, ps=ps, j=j, k=k, xt=xt: e.transpose(ps[:, j, :], xt[:, k * 128:(k + 1) * 128], idt[:]),
                         [xt, idt], [ps])
                K.VA(copy_any(sg[:, kg * 4:(kg + 1) * 4, :], ps[:]), [ps], [sg])
            K.store(xT.t.rearrange("(k p) t -> p k t", p=128)[:, :, i * 128:(i + 1) * 128], sg[:],
                    reads=[sg], writes=[xT.sub(i // 2 if i < 2 else 1 + (i - 2) // 4)])


def phase_adaln(K, layer, c_b, c_ctx, ada_w, ada_b, norm1_g, norm2_g, mod):
    modT, G1, G2 = mod["modT"], mod["G1"], mod["G2"]
    with K.phase() as st:
        sT = K.sb(st, [128, 2, 8], F32, "sT")
        K.load(sT[:, 0, :], c_b.t[:, :], writes=[sT])
        K.load(sT[:, 1, :], c_ctx.t[:, :], writes=[sT])
        sS = K.sb(st, [128, 2, 8], F32, "sS")
        K.A(lambda e: e.activation(out=sS[:], in_=sT[:], func=AF.Silu), [sT], [sS])
        bT = K.sb(st, [128, 48], F32, "bT")
        K.load(bT[:], ada_b.t[layer], writes=[bT])
        gT = K.sb(st, [128, 2, 8], F32, "gT")
        K.load(gT[:, 0, :], norm1_g.t[layer], writes=[gT])
        K.load(gT[:, 1, :], norm2_g.t[layer], writes=[gT])
        wst = [K.sb(st, [128, 8, 1024], F32, "adaw") for _ in range(2)]
        ps = K.ps(st, [128, 48, 2], F32, "pmod")
        for j in range(6):
            w = wst[j % 2]
            for kh in range(2):
                K.load(w[:, kh * 4:(kh + 1) * 4, :],
                       ada_w.t[layer].rearrange("(k p) n -> p k n", p=128)[:, kh * 4:(kh + 1) * 4, j * 1024:(j + 1) * 1024],
                       writes=[w])
            for f in range(8):
                for k in range(8):
                    K.PE(lambda e, w=w, f=f, k=k, j=j: e.matmul(ps[:, j * 8 + f, :], lhsT=w[:, k, f * 128:(f + 1) * 128],
                                                               rhs=sS[:, :, k], start=(k == 0), stop=(k == 7)),
                         [w, sS], [ps])
        K.V(lambda e: e.tensor_tensor(out=modT[:], in0=ps[:], in1=bT[:].unsqueeze(2).broadcast_to([128, 48, 2]), op=ALU.add),
            [ps, bT], [modT])
        for (G, gi, j) in ((G1, 0, 1), (G2, 1, 4)):
            K.V(lambda e, G=G, gi=gi, j=j: e.scalar_tensor_tensor(
                out=G[:], in0=modT[:, j * 8:(j + 1) * 8, :], scalar=1.0,
                in1=gT[:, gi, :].unsqueeze(2).broadcast_to([128, 8, 2]), op0=ALU.add, op1=ALU.mult),
                [modT, gT], [G])


def load_weight_bf16(K, st, w_ap, R, F, name, stage):
    nk = R // 128
    dst = K.sb(st, [128, nk, F], BF16, name)
    wv = w_ap.rearrange("(k p) n -> p k n", p=128)
    i = 0
    for k0 in range(0, nk, 8):
        kn = min(8, nk - k0)
        SW = stage[0].t.shape[2]
        for c0 in range(0, F, SW):
            cn = min(SW, F - c0)
            sg = stage[i % len(stage)]
            K.load(sg[:, 0:kn, 0:cn], wv[:, k0:k0 + kn, c0:c0 + cn], writes=[sg])
            eng = "gpsimd" if i % 2 == 0 else "vector"
            K.op(eng, lambda e, sg=sg, k0=k0, kn=kn, c0=c0, cn=cn: e.tensor_copy(out=dst[:, k0:k0 + kn, c0:c0 + cn], in_=sg[:, 0:kn, 0:cn]),
                 [sg], [dst])
            i += 1
    return dst


def norm_modulate(K, xg, n, s, G, modT, jshift, ones, sq, psn, rstd, tmp, hT):
    K.A(lambda e: e.activation(out=sq[:, :, 0:n], in_=xg[:, :, 0:n], func=AF.Square), [xg], [sq])
    for k in range(8):
        K.PE(lambda e, k=k: e.matmul(psn[:, 0:n], lhsT=ones[:], rhs=sq[:, k, 0:n], start=(k == 0), stop=(k == 7)), [sq, ones], [psn])
    K.V(lambda e: e.tensor_scalar(out=rstd[:, 0:n], in0=psn[:, 0:n], scalar1=1.0 / D, scalar2=EPS, op0=ALU.mult, op1=ALU.add), [psn], [rstd])
    K.A(lambda e: e.activation(out=rstd[:, 0:n], in_=rstd[:, 0:n], func=AF.Sqrt), [rstd], [rstd])
    K.V(lambda e: e.reciprocal(out=rstd[:, 0:n], in_=rstd[:, 0:n]), [rstd], [rstd])
    for k in range(8):
        K.V(lambda e, k=k: e.scalar_tensor_tensor(out=tmp[:, k, 0:n], in0=xg[:, k, 0:n], scalar=G[:, k, s:s + 1], in1=rstd[:, 0:n],
                                                  op0=ALU.mult, op1=ALU.mult), [xg, G, rstd], [tmp])
        K.A(lambda e, k=k: e.activation(out=hT[:, k, 0:n], in_=tmp[:, k, 0:n], func=AF.Identity,
                                        bias=modT[:, jshift * 8 + k, s:s + 1], scale=1.0), [tmp, modT], [hT])


def phase_norm_inproj(K, xT, w_in_ap, F, projT, mod, ones_d, hdbg=None):
    NG = 512
    with K.phase() as st:
        ones = K.sb(st, [128, 128], F32, "ones")
        K.load(ones[:], ones_d[:, :], writes=[ones])
        stage = [K.sb(st, [128, 8, 256], F32, "wstage") for _ in range(2)]
        W = load_weight_bf16(K, st, w_in_ap, D, F, "win", stage)
        xgs = [K.sb(st, [128, 8, NG], F32, "xg") for _ in range(2)]
        sq = K.sb(st, [128, 8, NG], F32, "sq")
        tmp = K.sb(st, [128, 8, NG], F32, "tmp")
        rstd = K.sb(st, [128, NG], F32, "rstd")
        hTs = [K.sb(st, [128, 8, NG], BF16, "hT") for _ in range(2)]
        outs = [K.sb(st, [128, NG], F32, "pout") for _ in range(3)]
        psn = K.ps(st, [128, NG], F32, "psn")
        pss = [K.ps(st, [128, NG], F32, "psp") for _ in range(4)]
        xv = xT.t.rearrange("(k p) t -> p k t", p=128)
        groups = token_groups(NG)
        nft = (F + 127) // 128
        cntr = {"c": 0}

        def norm_list(gi):
            t0, n, s_ = groups[gi]
            xg, hT = xgs[gi % 2], hTs[gi % 2]
            K.load(xg[:, :, 0:n], xv[:, :, t0:t0 + n], reads=[xT.sub(gi)], writes=[xg])
            norm_modulate(K, xg, n, s_, mod["G1"], mod["modT"], 0, ones, sq, psn, rstd, tmp, hT)
            if hdbg is not None:
                hf = tmp
                K.V(lambda en: en.tensor_copy(out=hf[:, :, 0:n], in_=hT[:, :, 0:n]), [hT], [hf])
                K.store(hdbg.t.rearrange("(k p) t -> p k t", p=128)[:, :, t0:t0 + n], hf[:, :, 0:n], reads=[hf], writes=[hdbg])

        def mm_list(gi):
            t0, n, s_ = groups[gi]
            hT = hTs[gi % 2]
            for f in range(nft):
                m = min(128, F - f * 128)
                ps = pss[cntr["c"] % 4]
                ot = outs[cntr["c"] % 3]
                cntr["c"] += 1
                for k in range(8):
                    K.PE(lambda en, ps=ps, f=f, m=m, k=k: en.matmul(ps[0:m, 0:n], lhsT=W[:, k, f * 128:f * 128 + m], rhs=hT[:, k, 0:n],
                                                                 start=(k == 0), stop=(k == 7)), [W, hT], [ps])
                K.VA(copy_any(ot[0:m, 0:n], ps[0:m, 0:n]), [ps], [ot])
                K.store(projT.t[f * 128:f * 128 + m, t0:t0 + n], ot[0:m, 0:n], reads=[ot], writes=[projT.sub((f, gi))])

        norm_list(0)
        for gi in range(len(groups)):
            lists = [K.capture(mm_list, gi)]
            if gi + 1 < len(groups):
                lists.append(K.capture(norm_list, gi + 1))
            K.emit_interleaved(lists)


NP_GROUPS = token_groups(512)


def region(t0):
    for gi, (a, n, s) in enumerate(NP_GROUPS):
        if a <= t0 < a + n:
            return gi
    raise ValueError


def regions(tt, t0, n):
    return [tt.sub(gi) for gi, (a, m, s) in enumerate(NP_GROUPS) if a < t0 + n and t0 < a + m]


def phase_out_w1(K, xT, zT, aTd, w_out_ap, w1_ap, mod, ones_d, skip_ctx, xmid_dbg=None):
    NG = 256
    modT = mod["modT"]
    with K.phase() as st:
        ones = K.sb(st, [128, 128], F32, "ones")
        K.load(ones[:], ones_d[:, :], writes=[ones])
        stage = [K.sb(st, [128, 8, 512], F32, "wstage") for _ in range(2)]
        Wo = load_weight_bf16(K, st, w_out_ap, D, D, "wo", stage)
        W1 = load_weight_bf16(K, st, w1_ap, D, DFF, "w1", stage)
        zgs = [K.sb(st, [128, 8, NG], F32, "zg") for _ in range(2)]
        xgs = [K.sb(st, [128, 8, NG], F32, "xg") for _ in range(2)]
        zbs = [K.sb(st, [128, 8, NG], BF16, "zb") for _ in range(2)]
        sq = K.sb(st, [128, 8, NG], F32, "sq")
        rstd = K.sb(st, [128, NG], F32, "rstd")
        hTs = [K.sb(st, [128, 8, NG], BF16, "hT") for _ in range(2)]
        rl = [K.sb(st, [128, NG], F32, "rl") for _ in range(2)]
        aTs = [K.sb(st, [128, 4, NG], BF16, "aTs") for _ in range(2)]
        psn = K.ps(st, [128, NG], F32, "psn")
        psA = [K.ps(st, [128, NG], F32, "psA") for _ in range(2)]
        psC = [K.ps(st, [128, NG], F32, "psC") for _ in range(4)]
        xv = xT.t.rearrange("(k p) t -> p k t", p=128)
        zv = zT.t.rearrange("(k p) t -> p k t", p=128)
        av = aTd.t.rearrange("(k p) t -> p k t", p=128)
        groups = [g for g in token_groups(NG) if not (g[2] == 1 and skip_ctx)]
        cA = {"c": 0}
        cC = {"c": 0}

        def stage_ab(i):
            t0, n, s_ = groups[i]
            zg, xg, zb, hT = zgs[i % 2], xgs[i % 2], zbs[i % 2], hTs[i % 2]
            reg = region(t0)
            K.load(zg[:, :, 0:n], zv[:, :, t0:t0 + n], reads=[zT.sub(reg)], writes=[zg])
            K.load(xg[:, :, 0:n], xv[:, :, t0:t0 + n], reads=[xT.sub(reg)], writes=[xg])
            K.G(lambda en: en.tensor_copy(out=zb[:, :, 0:n], in_=zg[:, :, 0:n]), [zg], [zb])
            for f in range(8):
                ps = psA[cA["c"] % 2]
                cA["c"] += 1
                for k in range(8):
                    K.PE(lambda en, ps=ps, f=f, k=k: en.matmul(ps[:, 0:n], lhsT=Wo[:, k, f * 128:(f + 1) * 128], rhs=zb[:, k, 0:n],
                                                            start=(k == 0), stop=(k == 7)), [Wo, zb], [ps])
                K.V(lambda en, ps=ps, f=f: en.scalar_tensor_tensor(
                    out=xg[:, f, 0:n], in0=ps[:, 0:n], scalar=modT[:, 2 * 8 + f, s_:s_ + 1], in1=xg[:, f, 0:n], op0=ALU.mult, op1=ALU.add),
                    [ps, modT, xg], [xg])
            K.store(xv[:, :, t0:t0 + n], xg[:, :, 0:n], reads=[xg], writes=[xT.sub(reg)])
            if xmid_dbg is not None:
                K.store(xmid_dbg.t.rearrange("(k p) t -> p k t", p=128)[:, :, t0:t0 + n], xg[:, :, 0:n], reads=[xg], writes=[xmid_dbg])
            norm_modulate(K, xg, n, s_, mod["G2"], modT, 3, ones, sq, psn, rstd, zg, hT)

        def stage_c(i):
            t0, n, s_ = groups[i]
            hT = hTs[i % 2]
            reg = region(t0)
            for f in range(32):
                ps = psC[cC["c"] % 4]
                r = rl[cC["c"] % 2]
                cC["c"] += 1
                ast = aTs[(f // 4) % 2]
                for k in range(8):
                    K.PE(lambda en, ps=ps, f=f, k=k: en.matmul(ps[:, 0:n], lhsT=W1[:, k, f * 128:(f + 1) * 128], rhs=hT[:, k, 0:n],
                                                            start=(k == 0), stop=(k == 7)), [W1, hT], [ps])
                K.A(lambda en, ps=ps, r=r: en.activation(out=r[:, 0:n], in_=ps[:, 0:n], func=AF.Relu), [ps], [r])
                K.op("gpsimd" if f % 2 else "vector",
                     lambda en, r=r, f=f, ast=ast: en.tensor_tensor(out=ast[:, f % 4, 0:n], in0=r[:, 0:n], in1=r[:, 0:n], op=ALU.mult), [r], [ast])
                if f % 4 == 3:
                    K.store(av[:, f - 3:f + 1, t0:t0 + n], ast[:, :, 0:n], reads=[ast], writes=[aTd.sub(reg)])

        stage_ab(0)
        for i in range(len(groups)):
            lists = [K.capture(stage_c, i)]
            if i + 1 < len(groups):
                lists.append(K.capture(stage_ab, i + 1))
            K.emit_interleaved(lists)


def phase_w2(K, xT, aTd, w2_ap, mod, skip_ctx):
    NG = 512
    modT = mod["modT"]
    with K.phase() as st:
        stage = [K.sb(st, [128, 8, 512], F32, "wstage") for _ in range(2)]
        W2 = load_weight_bf16(K, st, w2_ap, DFF, D, "w2", stage)
        ags = [K.sb(st, [128, 32, NG], BF16, "ag") for _ in range(2)]
        xgs = [K.sb(st, [128, 8, NG], F32, "xg") for _ in range(2)]
        pss = [K.ps(st, [128, NG], F32, "psp") for _ in range(4)]
        xv = xT.t.rearrange("(k p) t -> p k t", p=128)
        av = aTd.t.rearrange("(k p) t -> p k t", p=128)
        cnt = 0
        for gi, (t0, n, s) in enumerate(NP_GROUPS):
            if s == 1 and skip_ctx:
                continue
            ag = ags[gi % 2]
            xg = xgs[gi % 2]
            for q in range(4):
                K.load(ag[:, q * 8:(q + 1) * 8, 0:n], av[:, q * 8:(q + 1) * 8, t0:t0 + n], reads=[aTd.sub(gi)], writes=[ag])
            K.load(xg[:, :, 0:n], xv[:, :, t0:t0 + n], reads=[xT.sub(gi)], writes=[xg])
            for f in range(8):
                ps = pss[cnt % 4]
                cnt += 1
                for k in range(32):
                    K.PE(lambda e, ps=ps, f=f, k=k, n=n, ag=ag: e.matmul(ps[:, 0:n], lhsT=W2[:, k, f * 128:(f + 1) * 128], rhs=ag[:, k, 0:n],
                                                                     start=(k == 0), stop=(k == 31)), [W2, ag], [ps])
                K.V(lambda e, ps=ps, f=f, xg=xg, n=n, s=s: e.scalar_tensor_tensor(
                    out=xg[:, f, 0:n], in0=ps[:, 0:n], scalar=modT[:, 5 * 8 + f, s:s + 1], in1=xg[:, f, 0:n], op0=ALU.mult, op1=ALU.add),
                    [ps, modT, xg], [xg])
            K.store(xv[:, :, t0:t0 + n], xg[:, :, 0:n], reads=[xg], writes=[xT.sub(gi)])


def phase_final(K, xT, final_g, ones_d, ident, out):
    NG = 512
    with K.phase() as st:
        ones = K.sb(st, [128, 128], F32, "ones")
        K.load(ones[:], ones_d[:, :], writes=[ones])
        idt = K.sb(st, [128, 128], F32, "ident")
        K.load(idt[:], ident[:, :], writes=[idt])
        gf = K.sb(st, [128, 8], F32, "gf")
        K.load(gf[:], final_g.t[:, :], writes=[gf])
        xgs = [K.sb(st, [128, 8, NG], F32, "xg") for _ in range(2)]
        sq = K.sb(st, [128, 8, NG], F32, "sq")
        rstd = K.sb(st, [128, NG], F32, "rstd")
        yT = K.sb(st, [128, 8, NG], F32, "yT")
        ots = [K.sb(st, [128, D], F32, "ot") for _ in range(2)]
        psn = K.ps(st, [128, NG], F32, "psn")
        pss = [K.ps(st, [128, 4, 128], F32, "ptr") for _ in range(2)]
        xv = xT.t.rearrange("(k p) t -> p k t", p=128)
        fin = []
        cnt = 0
        for gi, (t0, n, s) in enumerate(token_groups(NG)):
            if s == 1:
                continue
            xg = xgs[gi % 2]
            K.load(xg[:, :, 0:n], xv[:, :, t0:t0 + n], reads=[xT.sub(gi)], writes=[xg])
            K.A(lambda e, xg=xg: e.activation(out=sq[:], in_=xg[:], func=AF.Square), [xg], [sq])
            for k in range(8):
                K.PE(lambda e, k=k: e.matmul(psn[:], lhsT=ones[:], rhs=sq[:, k, :], start=(k == 0), stop=(k == 7)), [sq, ones], [psn])
            K.V(lambda e: e.tensor_scalar(out=rstd[:], in0=psn[:], scalar1=1.0 / D, scalar2=EPS, op0=ALU.mult, op1=ALU.add), [psn], [rstd])
            K.A(lambda e: e.activation(out=rstd[:], in_=rstd[:], func=AF.Sqrt), [rstd], [rstd])
            K.V(lambda e: e.reciprocal(out=rstd[:], in_=rstd[:]), [rstd], [rstd])
            for k in range(8):
                K.V(lambda e, k=k, xg=xg: e.scalar_tensor_tensor(out=yT[:, k, :], in0=xg[:, k, :], scalar=gf[:, k:k + 1], in1=rstd[:],
                                                                op0=ALU.mult, op1=ALU.mult), [xg, gf, rstd], [yT])
            for tt in range(n // 128):
                ot = ots[cnt % 2]
                for kg in range(2):
                    ps = pss[kg]
                    for j in range(4):
                        k = kg * 4 + j
                        K.PE(lambda e, ps=ps, j=j, k=k, tt=tt: e.transpose(ps[:, j, :], yT[:, k, tt * 128:(tt + 1) * 128], idt[:]), [yT, idt], [ps])
                    K.VA(copy_any(ot[:, kg * 512:(kg + 1) * 512], ps[:].rearrange("p a b -> p (a b)")), [ps], [ot])
                r0 = t0 - NCTX + tt * 128
                fin.append(K.store(out.t[r0:r0 + 128, :], ot[:], reads=[ot], writes=[out]))
                cnt += 1
    return fin


def _col(v, nchunk):
    return np.ascontiguousarray(np.asarray(v, np.float32).reshape(nchunk, 128).T)


def build_program(cfg):
    debug = cfg.get("debug", False)
    layers = cfg.get("layers", list(range(DEPTH)))
    nc = bass.Bass("TRN2", target_bir_lowering=False)
    K = KB(nc, debug=debug)
    I = {}
    I["x_b"] = K.din("x_b", [NLAT, D])
    I["ctx_b"] = K.din("ctx_b", [NCTX, D])
    I["c_b"] = K.din("c_b", [128, 8])
    I["c_ctx"] = K.din("c_ctx", [128, 8])
    I["ada_w"] = K.din("ada_w", [DEPTH, D, 6 * D])
    I["ada_b"] = K.din("ada_b", [DEPTH, 128, 48])
    I["norm1_g"] = K.din("norm1_g", [DEPTH, 128, 8])
    I["norm2_g"] = K.din("norm2_g", [DEPTH, 128, 8])
    I["mix_w_out"] = K.din("mix_w_out", [DEPTH, D, D])
    I["mlp_w1"] = K.din("mlp_w1", [DEPTH, D, DFF])
    I["mlp_w2"] = K.din("mlp_w2", [DEPTH, DFF, D])
    I["ev_w_in"] = K.din("ev_w_in", [2, D, EV_IN])
    I["od_w_in"] = K.din("od_w_in", [2, D, OD_IN])
    I["final_g"] = K.din("final_g", [128, 8])
    I["ident"] = K.din("ident", [128, 128])
    I["ones"] = K.din("ones", [128, 128])
    mixer_inputs(K, I)
    out = K.dout("out", [NLAT, D])
    xT = K.dscr("xT", [D, T], F32, dbg=True)
    projT = K.dscr("projT", [OD_IN, T], F32, dbg=True)
    zT = K.dscr("zT", [D, T], F32, dbg=True)
    aTd = K.dscr("aTd", [DFF, T], BF16)
    hdbg = K.dscr("hdbg", [D, T], F32, dbg=True) if debug else None
    xmid = K.dscr("xmid", [D, T], F32, dbg=True) if debug else None
    zin = K.din("zin", [D, T]) if cfg.get("z_from_input") else None
    S = mixer_scratch(K)

    with contextlib.ExitStack() as gst:
        mod = {"modT": K.sb(gst, [128, 48, 2], F32, "modT"), "G1": K.sb(gst, [128, 8, 2], F32, "G1"),
               "G2": K.sb(gst, [128, 8, 2], F32, "G2")}
        phase_input_transpose(K, I["x_b"], I["ctx_b"], xT, I["ident"].t)
        for layer in layers:
            last = (layer == DEPTH - 1)
            phase_adaln(K, layer, I["c_b"], I["c_ctx"], I["ada_w"], I["ada_b"], I["norm1_g"], I["norm2_g"], mod)
            if layer % 2 == 0:
                w_in, F = I["ev_w_in"].t[layer // 2], EV_IN
            else:
                w_in, F = I["od_w_in"].t[layer // 2], OD_IN
            phase_norm_inproj(K, xT, w_in, F, projT, mod, I["ones"].t, hdbg=hdbg if (debug and layer == layers[0]) else None)
            zsrc = zT
            if zin is not None:
                zsrc = zin
            elif layer % 2 == 0:
                even_mixer(K, layer // 2, I, S, projT, zT)
            else:
                odd_mixer(K, layer // 2, I, S, projT, zT)
            phase_out_w1(K, xT, zsrc, aTd, I["mix_w_out"].t[layer], I["mlp_w1"].t[layer], mod, I["ones"].t, skip_ctx=last,
                         xmid_dbg=xmid if (debug and layer == layers[0]) else None)
            phase_w2(K, xT, aTd, I["mlp_w2"].t[layer], mod, skip_ctx=last)
        fin = phase_final(K, xT, I["final_g"], I["ones"].t, I["ident"].t, out)
    K.P.emit(final_wait_ops=fin)
    return nc, K


def mixer_inputs(K, I):
    I["lru_cw"] = K.din("lru_cw", [2, 128, 4, 4])
    I["lru_cb"] = K.din("lru_cb", [2, 128, 4])
    I["lru_ba"] = K.din("lru_ba", [2, 128, 2, 4])
    I["lru_bx"] = K.din("lru_bx", [2, 128, 2, 4])
    I["lru_lam"] = K.din("lru_lam", [2, 128, 2, 4])
    I["lru_wa_bd"] = K.din("lru_wa_bd", [2, 2, 4, 128, 128])
    I["lru_wx_bd"] = K.din("lru_wx_bd", [2, 2, 4, 128, 128])
    I["ret_lg"] = K.din("ret_lg", [2, 128, 8])
    I["pos_cols"] = K.din("pos_cols", [128, 2])
    I["tri_f"] = K.din("tri_f", [128, 128])
    I["tri_b"] = K.din("tri_b", [128, 128])
    I["rope_cos"] = K.din("rope_cos", [128, NLAT])
    I["rope_sin"] = K.din("rope_sin", [128, NLAT])
    I["hg_logits"] = K.din("hg_logits", [128, 2, 2, 4])
    I["gdn_cw"] = K.din("gdn_cw", [2, 128, 12, 4])
    I["gdn_ab"] = K.din("gdn_ab", [2, 16, 2])
    I["gdn_sel"] = K.din("gdn_sel", [16, 16, 128])
    I["gdn_masks"] = K.din("gdn_masks", [4, 128, 128])


def mixer_scratch(K):
    S = {}
    S["O_f"] = K.dscr("O_f", [T, 512], F32, dbg=True)
    S["O_b"] = K.dscr("O_b", [T, 512], F32, dbg=True)
    S["GATES"] = K.dscr("GATES", [3, 16, T], F32, dbg=True)
    return S


def all_regions(tt, rows):
    return [tt.sub((r, gi)) for r in rows for gi in range(len(NP_GROUPS))]


def z_regions(zT):
    return [zT.sub(gi) for gi in range(len(NP_GROUPS))]


SEGS = ((0, NCTX), (NCTX, T))
TT512 = [(t0, min(512, T - t0)) for t0 in range(0, T, 512)]


def lru_phase(K, e, I, projT, zT):
    with K.phase() as st:
        cw = K.sb(st, [128, 4, 4], F32, "cw")
        cb = K.sb(st, [128, 4], F32, "cb")
        ba = K.sb(st, [128, 2, 4], F32, "ba")
        bx = K.sb(st, [128, 2, 4], F32, "bx")
        lam = K.sb(st, [128, 2, 4], F32, "lam")
        cl = K.sb(st, [128, 2, 4], F32, "cl")
        one = K.sb(st, [128, 1], F32, "one")
        K.load(cw[:], I["lru_cw"].t[e], writes=[cw])
        K.load(cb[:], I["lru_cb"].t[e], writes=[cb])
        K.load(ba[:], I["lru_ba"].t[e], writes=[ba])
        K.load(bx[:], I["lru_bx"].t[e], writes=[bx])
        K.load(lam[:], I["lru_lam"].t[e], writes=[lam])
        K.V(lambda en: en.memset(one[:], 1.0), [], [one])
        nba = K.sb(st, [128, 2, 4], F32, "nba")
        nbx = K.sb(st, [128, 2, 4], F32, "nbx")
        K.V(lambda en: en.tensor_scalar(out=nba[:], in0=ba[:], scalar1=-1.0, scalar2=None, op0=ALU.mult), [ba], [nba])
        K.V(lambda en: en.tensor_scalar(out=nbx[:], in0=bx[:], scalar1=-1.0, scalar2=None, op0=ALU.mult), [bx], [nbx])
        K.A(lambda en: en.activation(out=cl[:], in_=lam[:], func=AF.Exp, scale=-1.0), [lam], [cl])
        K.V(lambda en: en.tensor_scalar(out=cl[:], in0=cl[:], scalar1=1.0, scalar2=None, op0=ALU.add), [cl], [cl])
        K.A(lambda en: en.activation(out=cl[:], in_=cl[:], func=AF.Ln), [cl], [cl])
        K.V(lambda en: en.tensor_scalar(out=cl[:], in0=cl[:], scalar1=-8.0, scalar2=None, op0=ALU.mult), [cl], [cl])
        wst = K.sb(st, [128, 128], F32, "bdst")
        BD = {}
        for d in range(2):
            for ct in range(4):
                for nm, key in (("a", "lru_wa_bd"), ("x", "lru_wx_bd")):
                    w = K.sb(st, [128, 128], BF16, "bd")
                    K.load(wst[:], I[key].t[e, d, ct], writes=[wst])
                    K.V(lambda en, w=w: en.tensor_copy(out=w[:], in_=wst[:]), [wst], [w])
                    BD[(nm, d, ct)] = w
        B1 = K.sb(st, [128, T], F32, "B1")
        B2 = K.sb(st, [128, T], F32, "B2")
        B3 = K.sb(st, [128, T], F32, "B3")
        B4 = K.sb(st, [128, T], F32, "B4")
        B5 = K.sb(st, [128, T], F32, "B5")
        B6 = K.sb(st, [128, T], F32, "B6")
        ub = K.sb(st, [128, T], BF16, "ub")
        rt = [K.sb(st, [128, 512], F32, "rt") for _ in range(2)]
        it = [K.sb(st, [128, 512], F32, "it") for _ in range(2)]
        mt = [K.sb(st, [128, 512], F32, "mt") for _ in range(2)]
        psa = [K.ps(st, [128, 512], F32, "psa") for _ in range(2)]
        psx = [K.ps(st, [128, 512], F32, "psx") for _ in range(2)]
        for ct in range(4):
            x, u, Aa, INP, H0, H1 = B1, B2, B3, B4, B5, B6
            K.load(x[:], projT.t[ct * 128:(ct + 1) * 128, :], reads=all_regions(projT, [ct]), writes=[x])
            K.V(lambda en, ct=ct: en.tensor_scalar(out=u[:], in0=x[:], scalar1=cw[:, ct, 2:3], scalar2=cb[:, ct:ct + 1], op0=ALU.mult, op1=ALU.add),
                [x, cw, cb], [u])
            for (s0, s1) in SEGS:
                for (j, off) in ((0, -2), (1, -1), (3, 1)):
                    if off < 0:
                        oa, ob, ia, ib = s0 - off, s1, s0, s1 + off
                    else:
                        oa, ob, ia, ib = s0, s1 - off, s0 + off, s1
                    K.V(lambda en, ct=ct, j=j, oa=oa, ob=ob, ia=ia, ib=ib: en.scalar_tensor_tensor(
                        out=u[:, oa:ob], in0=x[:, ia:ib], scalar=cw[:, ct, j:j + 1], in1=u[:, oa:ob], op0=ALU.mult, op1=ALU.add), [x, u, cw], [u])
            K.A(lambda en: en.activation(out=ub[:], in_=u[:], func=AF.Copy), [u], [ub])
            for d in range(2):
                for ti, (t0, n) in enumerate(TT512):
                    pa, px = psa[ti % 2], psx[ti % 2]
                    r, ii, m = rt[ti % 2], it[ti % 2], mt[ti % 2]
                    K.PE(lambda en, pa=pa, d=d, ct=ct, t0=t0, n=n: en.matmul(pa[:, 0:n], lhsT=BD[("a", d, ct)][:], rhs=ub[:, t0:t0 + n], start=True, stop=True),
                         [BD[("a", d, ct)], ub], [pa])
                    K.PE(lambda en, px=px, d=d, ct=ct, t0=t0, n=n: en.matmul(px[:, 0:n], lhsT=BD[("x", d, ct)][:], rhs=ub[:, t0:t0 + n], start=True, stop=True),
                         [BD[("x", d, ct)], ub], [px])
                    K.A(lambda en, pa=pa, r=r, d=d, ct=ct, n=n: en.activation(out=r[:, 0:n], in_=pa[:, 0:n], func=AF.Exp, bias=nba[:, d, ct:ct + 1], scale=-1.0),
                        [pa, nba], [r])
                    K.A(lambda en, r=r, n=n: en.activation(out=r[:, 0:n], in_=r[:, 0:n], func=AF.Ln, bias=one[:, 0:1], scale=1.0), [r, one], [r])
                    K.A(lambda en, r=r, n=n: en.activation(out=r[:, 0:n], in_=r[:, 0:n], func=AF.Exp, scale=-1.0), [r], [r])
                    K.A(lambda en, r=r, d=d, ct=ct, t0=t0, n=n: en.activation(out=Aa[:, t0:t0 + n], in_=r[:, 0:n], func=AF.Exp, scale=cl[:, d, ct:ct + 1]),
                        [r, cl], [Aa])
                    K.A(lambda en, px=px, ii=ii, d=d, ct=ct, n=n: en.activation(out=ii[:, 0:n], in_=px[:, 0:n], func=AF.Exp, bias=nbx[:, d, ct:ct + 1], scale=-1.0),
                        [px, nbx], [ii])
                    K.A(lambda en, ii=ii, n=n: en.activation(out=ii[:, 0:n], in_=ii[:, 0:n], func=AF.Ln, bias=one[:, 0:1], scale=1.0), [ii, one], [ii])
                    K.V(lambda en, m=m, t0=t0, n=n: en.scalar_tensor_tensor(out=m[:, 0:n], in0=Aa[:, t0:t0 + n], scalar=-1.0, in1=Aa[:, t0:t0 + n],
                                                                            op0=ALU.mult, op1=ALU.mult), [Aa], [m])
                    K.A(lambda en, m=m, n=n: en.activation(out=m[:, 0:n], in_=m[:, 0:n], func=AF.Ln, bias=one[:, 0:1], scale=1.0), [m, one], [m])
                    K.V(lambda en, m=m, ii=ii, n=n: en.scalar_tensor_tensor(out=m[:, 0:n], in0=m[:, 0:n], scalar=0.5, in1=ii[:, 0:n],
                                                                            op0=ALU.mult, op1=ALU.subtract), [m, ii], [m])
                    K.A(lambda en, m=m, n=n: en.activation(out=m[:, 0:n], in_=m[:, 0:n], func=AF.Exp), [m], [m])
                    K.V(lambda en, m=m, t0=t0, n=n: en.tensor_tensor(out=INP[:, t0:t0 + n], in0=m[:, 0:n], in1=u[:, t0:t0 + n], op=ALU.mult), [m, u], [INP])
                if d == 0:
                    K.V(lambda en: en.tensor_tensor_scan(out=H0[:], data0=Aa[:], data1=INP[:], initial=0.0, op0=ALU.mult, op1=ALU.add), [Aa, INP], [H0])
                else:
                    K.V(lambda en: en.tensor_tensor_scan(out=H1[:, 0:NCTX][:, ::-1], data0=Aa[:, 0:NCTX][:, ::-1], data1=INP[:, 0:NCTX][:, ::-1],
                                                         initial=0.0, op0=ALU.mult, op1=ALU.add), [Aa, INP], [H1])
                    K.V(lambda en: en.tensor_tensor_scan(out=H1[:, NCTX:T][:, ::-1], data0=Aa[:, NCTX:T][:, ::-1], data1=INP[:, NCTX:T][:, ::-1],
                                                         initial=H1[:, 0:1], op0=ALU.mult, op1=ALU.add), [Aa, INP, H1], [H1])
            K.G(lambda en: en.tensor_tensor(out=H0[:], in0=H0[:], in1=H1[:], op=ALU.add), [H0, H1], [H0])
            g, tq = B1, B3
            K.load(g[:], projT.t[512 + ct * 128:512 + (ct + 1) * 128, :], reads=all_regions(projT, [4 + ct]), writes=[g])
            K.A(lambda en: en.activation(out=tq[:], in_=g[:], func=AF.Square), [g], [tq])
            K.V(lambda en: en.tensor_scalar(out=tq[:], in0=tq[:], scalar1=0.044715, scalar2=1.0, op0=ALU.mult, op1=ALU.add), [tq], [tq])
            K.G(lambda en: en.tensor_tensor(out=tq[:], in0=tq[:], in1=g[:], op=ALU.mult), [tq, g], [tq])
            K.A(lambda en: en.activation(out=tq[:], in_=tq[:], func=AF.Sigmoid, scale=1.5957691216057308), [tq], [tq])
            K.V(lambda en: en.tensor_tensor(out=tq[:], in0=tq[:], in1=g[:], op=ALU.mult), [tq, g], [tq])
            K.V(lambda en: en.tensor_tensor(out=tq[:], in0=tq[:], in1=H0[:], op=ALU.mult), [tq, H0], [tq])
            K.store(zT.t[ct * 128:(ct + 1) * 128, :], tq[:], reads=[tq], writes=z_regions(zT))


def even_mixer(K, e, I, S, projT, zT):
    lru_phase(K, e, I, projT, zT)
    retention_phase(K, e, I, S, projT, zT)


def retention_phase(K, e, I, S, projT, zT):
    O = [S["O_f"], S["O_b"]]
    NCH = T // 128
    with K.phase() as st:
        idf = K.sb(st, [128, 128], F32, "idf")
        idb = K.sb(st, [128, 128], BF16, "idb")
        K.load(idf[:], I["ident"].t[:, :], writes=[idf])
        K.V(lambda en: en.tensor_copy(out=idb[:], in_=idf[:]), [idf], [idb])
        lg = K.sb(st, [128, 8], F32, "lg")
        pos = K.sb(st, [128, 2], F32, "pos")
        K.load(lg[:], I["ret_lg"].t[e], writes=[lg])
        K.load(pos[:], I["pos_cols"].t[:, :], writes=[pos])
        tri = [K.sb(st, [128, 128], F32, "tri") for _ in range(2)]
        K.load(tri[0][:], I["tri_f"].t[:, :], writes=[tri[0]])
        K.load(tri[1][:], I["tri_b"].t[:, :], writes=[tri[1]])
        t8 = K.sb(st, [128, 8], F32, "t8")
        qd8 = K.sb(st, [128, 8], F32, "qd8")
        gi8 = K.sb(st, [128, 8], F32, "gi8")
        cd8 = K.sb(st, [128, 8], F32, "cd8")
        for d in range(2):
            K.V(lambda en, d=d: en.tensor_scalar(out=t8[:, d * 4:(d + 1) * 4], in0=lg[:, d * 4:(d + 1) * 4], scalar1=pos[:, d:d + 1], scalar2=None, op0=ALU.mult),
                [lg, pos], [t8])
        K.A(lambda en: en.activation(out=qd8[:], in_=t8[:], func=AF.Exp), [t8], [qd8])
        K.A(lambda en: en.activation(out=gi8[:], in_=t8[:], func=AF.Exp, scale=-1.0), [t8], [gi8])
        K.A(lambda en: en.activation(out=cd8[:], in_=lg[:], func=AF.Exp, scale=128.0), [lg], [cd8])
        GINV, QDEC, CD = [], [], []
        for d in range(2):
            for (lst, src) in ((GINV, gi8), (QDEC, qd8), (CD, cd8)):
                x = K.sb(st, [128, 4, 128], F32, "mul")
                K.V(lambda en, x=x, src=src, d=d: en.tensor_copy(out=x[:], in_=src[:, d * 4:(d + 1) * 4].unsqueeze(2).broadcast_to([128, 4, 128])), [src], [x])
                lst.append(x)
        qk = [K.sb(st, [128, 2, T], BF16, "qb"), K.sb(st, [128, 2, T], BF16, "kb")]
        X = K.sb(st, [128, T], F32, "ropex")
        SW = K.sb(st, [128, NLAT], F32, "ropesw")
        COS = K.sb(st, [128, NLAT], F32, "cos")
        SIN = K.sb(st, [128, NLAT], F32, "sin")
        K.load(COS[:], I["rope_cos"].t[:, :], writes=[COS])
        K.load(SIN[:], I["rope_sin"].t[:, :], writes=[SIN])
        for which in range(2):
            for p in range(2):
                ft = 8 + which * 2 + p
                K.load(X[:], projT.t[ft * 128:(ft + 1) * 128, :], reads=all_regions(projT, [ft]), writes=[X])
                for bi, (dst, src) in enumerate(((0, 32), (32, 0), (64, 96), (96, 64))):
                    K.op("vector" if bi % 2 == 0 else "scalar", copy_any(SW[dst:dst + 32, :], X[src:src + 32, NCTX:T]), [X], [SW])
                K.V(lambda en: en.tensor_tensor(out=X[:, NCTX:T], in0=X[:, NCTX:T], in1=COS[:], op=ALU.mult), [X, COS], [X])
                K.G(lambda en: en.tensor_tensor(out=SW[:], in0=SW[:], in1=SIN[:], op=ALU.mult), [SW, SIN], [SW])
                K.V(lambda en: en.tensor_tensor(out=X[:, NCTX:T], in0=X[:, NCTX:T], in1=SW[:], op=ALU.add), [X, SW], [X])
                K.A(lambda en, which=which, p=p: en.activation(out=qk[which][:, p, :], in_=X[:], func=AF.Copy, scale=(0.125 if which else 1.0)),
                    [X], [qk[which]])
        qb, kb = qk
        qz = K.sb(st, [128, 4, T], BF16, "qz")
        K.G(lambda en: en.memset(qz[:], 0.0), [], [qz])
        for h in range(4):
            p, b = h // 2, (h % 2) * 64
            K.op("vector" if h % 2 == 0 else "scalar", copy_any(qz[b:b + 64, h, :], qb[b:b + 64, p, :]), [qb], [qz])
        vin = [K.sb(st, [128, 4, 128], F32, "vin") for _ in range(2)]
        VD = [K.sb(st, [128, 4, 128], BF16, "VD") for _ in range(2)]
        ktok = [K.sb(st, [128, 2, 128], BF16, "ktok") for _ in range(2)]
        PT = [K.sb(st, [128, 4, 128], BF16, "PT") for _ in range(2)]
        osb = [K.sb(st, [128, 4, 128], F32, "osb") for _ in range(2)]
        Sf = [K.sb(st, [128, 4, 128], F32, "Sf") for _ in range(2)]
        Sb = [K.sb(st, [128, 4, 128], BF16, "Sb") for _ in range(2)]
        tS = [K.sb(st, [128, 4, 128], F32, "tS") for _ in range(2)]
        ps_v = K.ps(st, [128, 4, 128], F32, "ps_v")
        ps_k = K.ps(st, [128, 2, 128], BF16, "ps_k")
        ps_st = [K.ps(st, [128, 4, 128], F32, "ps_st") for _ in range(2)]
        ps_o = [K.ps(st, [128, 4, 128], F32, "ps_o") for _ in range(2)]
        ps_s = [K.ps(st, [128, 4, 128], F32, "ps_s") for _ in range(2)]
        for d in range(2):
            K.V(lambda en, d=d: en.memset(Sf[d][:], 0.0), [], [Sf[d]])
            K.V(lambda en, d=d: en.memset(Sb[d][:], 0.0), [], [Sb[d]])
        order = [list(range(NCH)), [1, 0] + list(range(NCH - 1, 1, -1))]
        vrows = projT.t[1536:2048, :].rearrange("(h p) t -> p h t", p=128)
        for step in range(NCH):
            for d in range(2):
                n = order[d][step]
                c0 = n * 128
                reg = region(c0)
                K.load(vin[d][:], vrows[:, :, c0:c0 + 128], reads=[projT.sub((12 + h, reg)) for h in range(4)], writes=[vin[d]])
                for h in range(4):
                    K.PE(lambda en, d=d, h=h: en.transpose(ps_v[:, h, :], vin[d][:, h, :], idf[:]), [vin[d], idf], [ps_v])
                K.V(lambda en, d=d: en.tensor_tensor(out=VD[d][:], in0=ps_v[:], in1=GINV[d][:], op=ALU.mult), [ps_v, GINV[d]], [VD[d]])
                for p in range(2):
                    K.PE(lambda en, p=p, c0=c0: en.transpose(ps_k[:, p, :], kb[:, p, c0:c0 + 128], idb[:]), [kb, idb], [ps_k])
                K.A(lambda en, d=d: en.activation(out=ktok[d][:], in_=ps_k[:], func=AF.Copy), [ps_k], [ktok[d]])
                for h in range(4):
                    p, b = h // 2, (h % 2) * 64
                    K.PE(lambda en, d=d, h=h, p=p, b=b, c0=c0: en.matmul(ps_st[d][:, h, :], lhsT=kb[:, p, c0:c0 + 128], rhs=qz[:, h, c0:c0 + 128],
                                                                     start=True, stop=True), [kb, qz], [ps_st[d]])
                K.V(lambda en, d=d: en.tensor_tensor(out=PT[d][:], in0=ps_st[d][:], in1=tri[d][:].unsqueeze(1).broadcast_to([128, 4, 128]), op=ALU.mult),
                    [ps_st[d], tri[d]], [PT[d]])
                for h in range(4):
                    p, b = h // 2, (h % 2) * 64
                    K.PE(lambda en, d=d, h=h: en.matmul(ps_o[d][:, h, :], lhsT=PT[d][:, h, :], rhs=VD[d][:, h, :], start=True, stop=False),
                         [PT[d], VD[d]], [ps_o[d]])
                    K.PE(lambda en, d=d, h=h, p=p, b=b, c0=c0: en.matmul(ps_o[d][:, h, :], lhsT=qz[:, h, c0:c0 + 128], rhs=Sb[d][:, h, :],
                                                                     start=False, stop=True), [qz, Sb[d]], [ps_o[d]])
                K.V(lambda en, d=d: en.tensor_tensor(out=osb[d][:], in0=ps_o[d][:], in1=QDEC[d][:], op=ALU.mult), [ps_o[d], QDEC[d]], [osb[d]])
                K.store(O[d].t[c0:c0 + 128, :], osb[d][:].rearrange("p h v -> p (h v)"), reads=[osb[d]], writes=[O[d].sub(n)])
                for h in range(4):
                    p = h // 2
                    K.PE(lambda en, d=d, h=h, p=p: en.matmul(ps_s[d][:, h, :], lhsT=ktok[d][:, p, :], rhs=VD[d][:, h, :], start=True, stop=True),
                         [ktok[d], VD[d]], [ps_s[d]])
                K.V(lambda en, d=d: en.tensor_tensor(out=tS[d][:], in0=ps_s[d][:], in1=Sf[d][:], op=ALU.add), [ps_s[d], Sf[d]], [tS[d]])
                K.G(lambda en, d=d: en.tensor_tensor(out=Sf[d][:], in0=tS[d][:], in1=CD[d][:], op=ALU.mult), [tS[d], CD[d]], [Sf[d]])
                K.A(lambda en, d=d: en.activation(out=Sb[d][:], in_=Sf[d][:], func=AF.Copy), [Sf[d]], [Sb[d]])
    head_norm_epilogue(K, I, O, projT, zT, gate_ft0=16, out_row0=512, center=True)


def head_norm_epilogue(K, I, O, projT, zT, gate_ft0, out_row0, center):
    NCH = T // 128
    with K.phase() as st:
        idf = K.sb(st, [128, 128], F32, "idf")
        K.load(idf[:], I["ident"].t[:, :], writes=[idf])
        zrows = projT.t[gate_ft0 * 128:(gate_ft0 + 4) * 128, :].rearrange("(h p) t -> p h t", p=128)
        zout = zT.t[out_row0:out_row0 + 512, :].rearrange("(h p) t -> p h t", p=128)
        ofs = [K.sb(st, [128, 4, 128], F32, "of") for _ in range(2)]
        obs = [K.sb(st, [128, 4, 128], F32, "ob") for _ in range(2)]
        zgs = [K.sb(st, [128, 4, 128], F32, "zg") for _ in range(2)]
        ocs = [K.sb(st, [128, 4, 128], F32, "oc") for _ in range(2)]
        sqs = [K.sb(st, [128, 4, 128], F32, "sq") for _ in range(2)]
        st4s = [K.sb(st, [128, 4], F32, "st4") for _ in range(2)]
        rs4s = [K.sb(st, [128, 4], F32, "rs4") for _ in range(2)]
        yo = [K.sb(st, [128, 4, 128], F32, "yo") for _ in range(2)]
        ps_t = [K.ps(st, [128, 4, 128], F32, "ps_t") for _ in range(2)]

        def chunk(n):
            c0 = n * 128
            reg = region(c0)
            of, ob, zg, y, pt = ofs[n % 2], obs[n % 2], zgs[n % 2], yo[n % 2], ps_t[n % 2]
            oc, sq, st4, rs4 = ocs[n % 2], sqs[n % 2], st4s[n % 2], rs4s[n % 2]
            K.load(of[:].rearrange("p h v -> p (h v)"), O[0].t[c0:c0 + 128, :], reads=[O[0].sub(n)], writes=[of])
            K.load(ob[:].rearrange("p h v -> p (h v)"), O[1].t[c0:c0 + 128, :], reads=[O[1].sub(n)], writes=[ob])
            K.load(zg[:], zrows[:, :, c0:c0 + 128], reads=[projT.sub((gate_ft0 + h, reg)) for h in range(4)], writes=[zg])
            if center:
                K.V(lambda en: en.tensor_tensor(out=of[:], in0=of[:], in1=ob[:], op=ALU.add), [of, ob], [of])
                K.V(lambda en: en.tensor_reduce(out=st4[:], in_=of[:], axis=AX.X, op=ALU.add), [of], [st4])
                K.V(lambda en: en.tensor_scalar(out=st4[:], in0=st4[:], scalar1=-1.0 / 128, scalar2=None, op0=ALU.mult), [st4], [st4])
                K.V(lambda en: en.tensor_tensor(out=oc[:], in0=of[:], in1=st4[:].unsqueeze(2).broadcast_to([128, 4, 128]), op=ALU.add), [of, st4], [oc])
            else:
                K.V(lambda en: en.tensor_tensor(out=oc[:], in0=of[:], in1=ob[:], op=ALU.add), [of, ob], [oc])
            K.G(lambda en: en.tensor_tensor(out=sq[:], in0=oc[:], in1=oc[:], op=ALU.mult), [oc], [sq])
            K.V(lambda en: en.tensor_reduce(out=rs4[:], in_=sq[:], axis=AX.X, op=ALU.add), [sq], [rs4])
            K.V(lambda en: en.tensor_scalar(out=rs4[:], in0=rs4[:], scalar1=1.0 / 128, scalar2=EPS, op0=ALU.mult, op1=ALU.add), [rs4], [rs4])
            K.A(lambda en: en.activation(out=rs4[:], in_=rs4[:], func=AF.Sqrt), [rs4], [rs4])
            K.V(lambda en: en.reciprocal(out=rs4[:], in_=rs4[:]), [rs4], [rs4])
            K.V(lambda en: en.tensor_tensor(out=oc[:], in0=oc[:], in1=rs4[:].unsqueeze(2).broadcast_to([128, 4, 128]), op=ALU.mult), [oc, rs4], [oc])
            for h in range(4):
                K.PE(lambda en, h=h: en.transpose(pt[:, h, :], oc[:, h, :], idf[:]), [oc, idf], [pt])
            K.A(lambda en: en.activation(out=zg[:], in_=zg[:], func=AF.Silu), [zg], [zg])
            K.V(lambda en: en.tensor_tensor(out=y[:], in0=pt[:], in1=zg[:], op=ALU.mult), [pt, zg], [y])
            K.store(zout[:, :, c0:c0 + 128], y[:], reads=[y], writes=[zT.sub(reg)])

        for n in range(0, NCH, 2):
            K.emit_interleaved([K.capture(chunk, n), K.capture(chunk, n + 1)])


GC = 32
NGC = T // GC


def gla_phase(K, o, I, S, projT, zT):
    O = [S["O_f"], S["O_b"]]
    with K.phase() as st:
        idf = K.sb(st, [128, 128], F32, "idf")
        idb = K.sb(st, [128, 128], BF16, "idb")
        K.load(idf[:], I["ident"].t[:, :], writes=[idf])
        K.V(lambda en: en.tensor_copy(out=idb[:], in_=idf[:]), [idf], [idb])
        tri = [K.sb(st, [128, 128], F32, "tri") for _ in range(2)]
        K.load(tri[0][:], I["tri_f"].t[:, :], writes=[tri[0]])
        K.load(tri[1][:], I["tri_b"].t[:, :], writes=[tri[1]])
        lgt = K.sb(st, [128, 2, 2, 4], F32, "lgt")
        lb = K.sb(st, [128, 2, 4], F32, "lb")
        oml = K.sb(st, [128, 2, 4], F32, "oml")
        K.load(lgt[:], I["hg_logits"].t[:, :, :, :], writes=[lgt])
        if o == 0:
            K.V(lambda en: en.memset(lb[:], 0.0), [], [lb])
        else:
            K.V(lambda en: en.tensor_tensor(out=lb[:], in0=lgt[:, :, 1, :], in1=lgt[:, :, 0, :], op=ALU.subtract), [lgt], [lb])
            K.A(lambda en: en.activation(out=lb[:], in_=lb[:], func=AF.Sigmoid), [lb], [lb])
        K.V(lambda en: en.tensor_scalar(out=oml[:], in0=lb[:], scalar1=-1.0, scalar2=1.0, op0=ALU.mult, op1=ALU.add), [lb], [oml])
        MASKX = K.sb(st, [128, T + GC], F32, "mask")
        K.G(lambda en: en.memset(MASKX[:], 1.0), [], [MASKX])
        K.G(lambda en: en.memset(MASKX[:, 0::GC], 0.0), [], [MASKX])
        Q = K.sb(st, [128, T], F32, "Q")
        F1 = K.sb(st, [128, T], F32, "F1")
        F2 = K.sb(st, [128, T], F32, "F2")
        F3 = K.sb(st, [128, T], F32, "F3")
        QTs = [[K.sb(st, [128, T], BF16, "QT") for _ in range(2)] for _ in range(2)]
        KTs = [[K.sb(st, [128, T], BF16, "KT") for _ in range(2)] for _ in range(2)]
        Vbs = [K.sb(st, [128, T], BF16, "Vb") for _ in range(2)]
        GLs = [[K.sb(st, [128, NGC], F32, "GL") for _ in range(2)] for _ in range(2)]
        TR = [[K.sb(st, [GC, 2, 128], BF16, "TR") for _ in range(2)] for _ in range(2)]
        PT = [[K.sb(st, [GC, GC], BF16, "PT") for _ in range(2)] for _ in range(2)]
        OS = [[K.sb(st, [GC, 8, 128], F32, "OS") for _ in range(2)] for _ in range(2)]
        Sf = [K.sb(st, [128, 128], F32, "Sf") for _ in range(2)]
        Sb = [K.sb(st, [128, 128], BF16, "Sb") for _ in range(2)]
        ps_tr = [K.ps(st, [GC, 2, 128], BF16, "ps_tr") for _ in range(2)]
        ps_st = [K.ps(st, [GC, GC], F32, "ps_st") for _ in range(2)]
        ps_o = [K.ps(st, [GC, 128], F32, "ps_o") for _ in range(2)]
        ps_s = [K.ps(st, [128, 128], F32, "ps_s") for _ in range(2)]
        nctx_c = NCTX // GC
        order = [list(range(NGC)), list(range(nctx_c - 1, -1, -1)) + list(range(NGC - 1, nctx_c - 1, -1))]

        def prep_head(h):
            QT, KT, Vb, GL = QTs[h % 2], KTs[h % 2], Vbs[h % 2], GLs[h % 2]
            K.load(Q[:], projT.t[h * 128:(h + 1) * 128, :], reads=all_regions(projT, [h]), writes=[Q])
            K.A(lambda en: en.activation(out=Q[:], in_=Q[:], func=AF.Silu), [Q], [Q])
            for d in range(2):
                ft = 4 + 4 * d + h
                K.load(F1[:], projT.t[ft * 128:(ft + 1) * 128, :], reads=all_regions(projT, [ft]), writes=[F1])
                K.A(lambda en: en.activation(out=F1[:], in_=F1[:], func=AF.Sigmoid), [F1], [F1])
                K.V(lambda en, d=d: en.tensor_scalar(out=F1[:], in0=F1[:], scalar1=oml[:, d, h:h + 1], scalar2=lb[:, d, h:h + 1], op0=ALU.mult, op1=ALU.add),
                    [F1, oml, lb], [F1])
                K.A(lambda en: en.activation(out=F2[:], in_=F1[:], func=AF.Ln), [F1], [F2])
                if d == 0:
                    K.V(lambda en: en.tensor_tensor_scan(out=F3[:], data0=MASKX[:, 0:T], data1=F2[:], initial=0.0, op0=ALU.mult, op1=ALU.add), [MASKX, F2], [F3])
                else:
                    K.V(lambda en: en.tensor_tensor_scan(out=F3[:, ::-1], data0=MASKX[:, 1:T + 1][:, ::-1], data1=F2[:, ::-1], initial=0.0,
                                                         op0=ALU.mult, op1=ALU.add), [MASKX, F2], [F3])
                K.A(lambda en: en.activation(out=F2[:], in_=F3[:], func=AF.Exp), [F3], [F2])
                K.V(lambda en, d=d: en.tensor_tensor(out=QT[d][:], in0=Q[:], in1=F2[:], op=ALU.mult), [Q, F2], [QT[d]])
                K.G(lambda en, d=d: en.tensor_copy(out=GL[d][:], in_=F2[:, (GC - 1 if d == 0 else 0)::GC]), [F2], [GL[d]])
                K.A(lambda en: en.activation(out=F2[:], in_=F3[:], func=AF.Exp, scale=-1.0), [F3], [F2])
                K.G(lambda en: en.tensor_scalar(out=F1[:], in0=F1[:], scalar1=-1.0, scalar2=1.0, op0=ALU.mult, op1=ALU.add), [F1], [F1])
                K.G(lambda en, d=d: en.tensor_tensor(out=KT[d][:], in0=F1[:], in1=F2[:], op=ALU.mult), [F1, F2], [KT[d]])
            ft = 12 + h
            K.load(F3[:], projT.t[ft * 128:(ft + 1) * 128, :], reads=all_regions(projT, [ft]), writes=[F3])
            K.G(lambda en: en.tensor_copy(out=Vb[:], in_=F3[:]), [F3], [Vb])

        def intra(d, step, h):
            QT, KT, Vb = QTs[h % 2], KTs[h % 2], Vbs[h % 2]
            n = order[d][step]
            c0 = n * GC
            tr, pt = TR[d][step % 2], PT[d][step % 2]
            K.PE(lambda en: en.transpose(ps_tr[d][:, 0, :], Vb[:, c0:c0 + GC], idb[:]), [Vb, idb], [ps_tr[d]])
            K.PE(lambda en: en.transpose(ps_tr[d][:, 1, :], KT[d][:, c0:c0 + GC], idb[:]), [KT[d], idb], [ps_tr[d]])
            K.A(lambda en: en.activation(out=tr[:], in_=ps_tr[d][:], func=AF.Copy), [ps_tr[d]], [tr])
            K.PE(lambda en: en.matmul(ps_st[d][:], lhsT=KT[d][:, c0:c0 + GC], rhs=QT[d][:, c0:c0 + GC], start=True, stop=True),
                 [KT[d], QT[d]], [ps_st[d]])
            K.V(lambda en: en.tensor_tensor(out=pt[:], in0=ps_st[d][:], in1=tri[d][0:GC, 0:GC], op=ALU.mult), [ps_st[d], tri[d]], [pt])

        def inter(d, step, h):
            QT, GL = QTs[h % 2], GLs[h % 2]
            n = order[d][step]
            c0 = n * GC
            tr, pt = TR[d][step % 2], PT[d][step % 2]
            os_ = OS[d][(step // 8) % 2]
            K.PE(lambda en: en.matmul(ps_o[d][:], lhsT=pt[:], rhs=tr[:, 0, :], start=True, stop=False), [pt, tr], [ps_o[d]])
            K.PE(lambda en: en.matmul(ps_o[d][:], lhsT=QT[d][:, c0:c0 + GC], rhs=Sb[d][:], start=False, stop=True), [QT[d], Sb[d]], [ps_o[d]])
            K.PE(lambda en: en.matmul(ps_s[d][:], lhsT=tr[:, 1, :], rhs=tr[:, 0, :], start=True, stop=True), [tr], [ps_s[d]])
            K.V(lambda en: en.tensor_copy(out=os_[:, n % 8, :], in_=ps_o[d][:]), [ps_o[d]], [os_])
            if step == 0:
                K.V(lambda en: en.tensor_copy(out=Sf[d][:], in_=ps_s[d][:]), [ps_s[d]], [Sf[d]])
            else:
                np_ = order[d][step - 1]
                K.V(lambda en: en.scalar_tensor_tensor(out=Sf[d][:], in0=Sf[d][:], scalar=GL[d][:, np_:np_ + 1], in1=ps_s[d][:],
                                                       op0=ALU.mult, op1=ALU.add), [Sf[d], GL[d], ps_s[d]], [Sf[d]])
            K.A(lambda en: en.activation(out=Sb[d][:], in_=Sf[d][:], func=AF.Copy, scale=GL[d][:, n:n + 1]), [Sf[d], GL[d]], [Sb[d]])
            if step % 8 == 7:
                n0 = (n // 8) * 8
                K.store(O[d].t[n0 * GC:(n0 + 8) * GC, h * 128:(h + 1) * 128].rearrange("(c p) v -> p c v", p=GC), os_[:],
                        reads=[os_], writes=[O[d].sub(n0 * GC // 128), O[d].sub(n0 * GC // 128 + 1)])

        prep_head(0)
        for h in range(4):
            nxt = K.capture(prep_head, h + 1) if h + 1 < 4 else []
            stride = (len(nxt) + NGC - 2) // (NGC - 1) if nxt else 0
            for d in range(2):
                K.V(lambda en, d=d: en.memset(Sb[d][:], 0.0), [], [Sb[d]])
            K.emit_interleaved([K.capture(intra, d, 0, h) for d in range(2)])
            for step in range(NGC):
                lists = []
                if step + 1 < NGC:
                    lists += [K.capture(intra, d, step + 1, h) for d in range(2)]
                lists += [K.capture(inter, d, step, h) for d in range(2)]
                if nxt:
                    lists.append(nxt[step * stride:(step + 1) * stride])
                K.emit_interleaved(lists)
            if nxt and NGC * stride < len(nxt):
                K.emit_interleaved([nxt[NGC * stride:]])
    head_norm_epilogue(K, I, O, projT, zT, gate_ft0=16, out_row0=0, center=False)


def odd_mixer(K, o, I, S, projT, zT):
    if not os.environ.get("GDN_DBG"):
        gla_phase(K, o, I, S, projT, zT)
    gdn_phase(K, o, I, S, projT, zT)


def _quarter(bank, q):
    v = bank.t
    if len(v.shape) == 3:
        return v[:, q, :]
    return v[:, q * 128:(q + 1) * 128]


def gdn_phase(K, o, I, S, projT, zT):
    O = [S["O_f"], S["O_b"]]
    GATES = S["GATES"]
    NSC = T // 128
    with K.phase() as st:
        G16 = K.sb(st, [16, T], F32, "G16")
        BETA = K.sb(st, [16, T], F32, "BETA")
        GCf = K.sb(st, [16, T], F32, "GCf")
        GCb = K.sb(st, [16, T], F32, "GCb")
        ab = K.sb(st, [16, 2], F32, "ab")
        nea = K.sb(st, [16, 1], F32, "nea")
        one64 = K.sb(st, [16, 64], F32, "one64")
        K.load(G16[:], projT.t[4608:4624, :], reads=all_regions(projT, [36]), writes=[G16])
        K.load(ab[:], I["gdn_ab"].t[o], writes=[ab])
        K.V(lambda en: en.memset(one64[:], 1.0), [], [one64])
        K.A(lambda en: en.activation(out=nea[:], in_=ab[:, 0:1], func=AF.Exp), [ab], [nea])
        K.V(lambda en: en.tensor_scalar(out=nea[:], in0=nea[:], scalar1=-1.0, scalar2=None, op0=ALU.mult), [nea], [nea])
        K.A(lambda en: en.activation(out=BETA[:], in_=G16[:], func=AF.Sigmoid), [G16], [BETA])
        K.A(lambda en: en.activation(out=G16[:], in_=G16[:], func=AF.Exp, bias=ab[:, 1:2], scale=1.0), [G16, ab], [G16])
        K.V(lambda en: en.tensor_scalar(out=G16[:], in0=G16[:], scalar1=1.0, scalar2=None, op0=ALU.add), [G16], [G16])
        K.A(lambda en: en.activation(out=G16[:], in_=G16[:], func=AF.Ln), [G16], [G16])
        K.V(lambda en: en.tensor_scalar(out=G16[:], in0=G16[:], scalar1=nea[:, 0:1], scalar2=None, op0=ALU.mult), [G16, nea], [G16])
        for c in range(T // 64):
            K.V(lambda en, c=c: en.tensor_tensor_scan(out=GCf[:, c * 64:(c + 1) * 64], data0=one64[:], data1=G16[:, c * 64:(c + 1) * 64],
                                                      initial=0.0, op0=ALU.mult, op1=ALU.add), [G16, one64], [GCf])
            K.V(lambda en, c=c: en.tensor_tensor_scan(out=GCb[:, c * 64:(c + 1) * 64][:, ::-1], data0=one64[:], data1=G16[:, c * 64:(c + 1) * 64][:, ::-1],
                                                      initial=0.0, op0=ALU.mult, op1=ALU.add), [G16, one64], [GCb])
        K.store(GATES.t[0], BETA[:], reads=[BETA], writes=[GATES])
        K.store(GATES.t[1], GCf[:], reads=[GCf], writes=[GATES])
        K.store(GATES.t[2], GCb[:], reads=[GCb], writes=[GATES])
    DBG = int(os.environ.get("GDN_DBG", "99"))
    if DBG == 0:
        return
    with K.phase() as st:
        idf = K.sb(st, [128, 128], F32, "idf")
        idb = K.sb(st, [128, 128], BF16, "idb")
        ones = K.sb(st, [128, 128], F32, "ones")
        K.load(idf[:], I["ident"].t[:, :], writes=[idf])
        K.load(ones[:], I["ones"].t[:, :], writes=[ones])
        K.V(lambda en: en.tensor_copy(out=idb[:], in_=idf[:]), [idf], [idb])
        SEL = K.sb(st, [16, 16, 128], F32, "SEL")
        K.load(SEL[:], I["gdn_sel"].t[:, :, :], writes=[SEL])
        MSK = [K.sb(st, [128, 128], F32, "msk") for _ in range(4)]
        for i in range(4):
            K.load(MSK[i][:], I["gdn_masks"].t[i], writes=[MSK[i]])
        cw = K.sb(st, [128, 12, 4], F32, "gcw")
        K.load(cw[:], I["gdn_cw"].t[o], writes=[cw])
        X = K.sb(st, [128, T], F32, "gX")
        U = K.sb(st, [128, T], F32, "gU")
        SQ = K.sb(st, [128, T], F32, "gSQ")
        NRMs = [[K.sb(st, [128, T], BF16, "gN") for _ in range(3)] for _ in range(2)]
        rs = [K.sb(st, [128, 512], F32, "grs") for _ in range(2)]
        psn = [K.ps(st, [128, 512], F32, "gpsn") for _ in range(2)]
        banks = [K.ps(st, [128, 4, 128], F32, "gbank") for _ in range(5)]
        bfbank = K.ps(st, [128, 4, 128], BF16, "gbfbank")
        qt = [K.psq(banks[i // 4], banks[i // 4].t[:, i % 4, :]) for i in range(8)]
        rb = [[banks[2], banks[3]], [banks[4], psn[0]]]
        rot = [[K.psq(rb[d][i % 2], _quarter(rb[d][i % 2], (i // 2) % 4)) for i in range(8)] for d in range(2)]
        qbf = [K.psq(bfbank, bfbank.t[:, i, :]) for i in range(4)]
        pp = {"i": [0, 0], "d": 0}

        def PQ():
            d_ = pp["d"]
            pp["i"][d_] += 1
            return rot[d_][pp["i"][d_] % 8]

        def mk(shape, dt, name):
            return [[K.sb(st, shape, dt, name) for _ in range(2)] for _ in range(2)]

        gate_in = mk([16, 2, 128], F32, "gin")
        cols = mk([128, 2, 16], F32, "gcols")
        DT_ = mk([128, 128], F32, "gDT")
        ETs = mk([128, 128], F32, "gETs")
        ETi = mk([128, 128], F32, "gETi")
        Nm = mk([128, 128], F32, "gN_")
        Mm = mk([128, 128], F32, "gM_")
        Pm = [mk([128, 128], F32, "gP") for _ in range(2)]
        PTm = [mk([128, 128], F32, "gPT") for _ in range(2)]
        Y = mk([128, 128], F32, "gY")
        Ybf = mk([128, 128], BF16, "gYbf")
        qkT = mk([128, 128], BF16, "gqkT")
        Vb_ = mk([128, 128], BF16, "gVb")
        Kbe = mk([128, 128], BF16, "gKbe")
        cvec = mk([128, 4], F32, "gcvec")
        Ut = mk([128, 128], F32, "gUt")
        nWT = mk([128, 128], BF16, "gnWT")
        kdA = mk([128, 128], BF16, "gkdA")
        kdB = mk([128, 128], BF16, "gkdB")
        egB = mk([128, 128], F32, "gegB")
        qdT = mk([128, 128], BF16, "gqdT")
        VNEW = [K.sb(st, [128, 128], BF16, "gVNEW") for _ in range(2)]
        OSB = mk([128, 128], F32, "gOSB")
        Sf = [K.sb(st, [128, 128], F32, "gSf") for _ in range(2)]
        Sb = [K.sb(st, [128, 128], BF16, "gSb") for _ in range(2)]
        for d in range(2):
            for par in range(2):
                K.G(lambda en, d=d, par=par: en.memset(kdA[d][par][:], 0.0), [], [kdA[d][par]])
                K.G(lambda en, d=d, par=par: en.memset(kdB[d][par][:], 0.0), [], [kdB[d][par]])
        order = [list(range(NSC)), [1, 0] + list(range(NSC - 1, 1, -1))]
        lastc = [(63, 127), (0, 64)]
        def prep(d, h, step):
            QN, KN, VN = NRMs[h % 2]
            pp["d"] = d
            par = step % 2
            sc = order[d][step]
            c0 = sc * 128
            rg, rb = 8 + d * 4 + h, d * 4 + h
            gin, cl_, dt_, ets, eti = gate_in[d][par], cols[d][par], DT_[d][par], ETs[d][par], ETi[d][par]
            N_, M_, y, ybf = Nm[d][par], Mm[d][par], Y[d][par], Ybf[d][par]
            cv = cvec[d][par]
            K.load(gin[:, 0, :], GATES.t[0, :, c0:c0 + 128], reads=[GATES], writes=[gin])
            K.load(gin[:, 1, :], GATES.t[1 + d, :, c0:c0 + 128], reads=[GATES], writes=[gin])
            p_gc, p_b, p_c, p_kk, p_qk = PQ(), PQ(), PQ(), PQ(), PQ()
            K.PE(lambda en: en.matmul(p_gc[:], lhsT=SEL[:, rg, :], rhs=gin[:, 1, :], start=True, stop=True), [SEL, gin], [p_gc])
            K.PE(lambda en: en.matmul(p_b[:], lhsT=SEL[:, rb, :], rhs=gin[:, 0, :], start=True, stop=True), [SEL, gin], [p_b])
            K.PE(lambda en: en.transpose(p_c[:, 0:16], gin[:, 0, :], idf[0:16, 0:16]), [gin, idf], [p_c])
            K.PE(lambda en: en.transpose(p_c[:, 16:32], gin[:, 1, :], idf[0:16, 0:16]), [gin, idf], [p_c])
            K.A(lambda en: en.activation(out=cl_[:].rearrange("p a b -> p (a b)"), in_=p_c[:, 0:32], func=AF.Copy), [p_c], [cl_])
            bcol, gcol = cl_[:, 0, rb:rb + 1], cl_[:, 1, rg:rg + 1]
            K.PE(lambda en: en.matmul(p_kk[:], lhsT=KN[:, c0:c0 + 128], rhs=KN[:, c0:c0 + 128], start=True, stop=True), [KN], [p_kk])
            K.PE(lambda en: en.matmul(p_qk[:], lhsT=KN[:, c0:c0 + 128], rhs=QN[:, c0:c0 + 128], start=True, stop=True), [KN, QN], [p_qk])
            p_kt, p_vt = qbf[(2 * d) % 4], qbf[(2 * d + 1) % 4]
            K.PE(lambda en: en.transpose(p_kt[:], KN[:, c0:c0 + 128], idb[:]), [KN, idb], [p_kt])
            K.PE(lambda en: en.transpose(p_vt[:], VN[:, c0:c0 + 128], idb[:]), [VN, idb], [p_vt])
            K.V(lambda en: en.tensor_scalar(out=dt_[:], in0=p_gc[:], scalar1=gcol, scalar2=0.0, op0=ALU.subtract, op1=ALU.min), [p_gc, cl_], [dt_])
            K.A(lambda en: en.activation(out=egB[d][par][:], in_=p_gc[:], func=AF.Exp), [p_gc], [egB[d][par]])
            for (r0, col) in ((0, lastc[d][0]), (64, lastc[d][1])):
                K.V(lambda en, r0=r0, col=col: en.tensor_tensor(out=cv[r0:r0 + 64, 2:3], in0=p_gc[r0:r0 + 64, col:col + 1], in1=cl_[r0:r0 + 64, 1, rg:rg + 1],
                                                               op=ALU.subtract), [p_gc, cl_], [cv])
            K.A(lambda en: en.activation(out=cv[:, 3:4], in_=cv[:, 2:3], func=AF.Exp), [cv], [cv])
            K.A(lambda en: en.activation(out=dt_[:], in_=dt_[:], func=AF.Exp), [dt_], [dt_])
            K.G(lambda en: en.tensor_tensor(out=ets[:], in0=dt_[:], in1=MSK[2 * d][:], op=ALU.mult), [dt_, MSK[2 * d]], [ets])
            K.G(lambda en: en.tensor_tensor(out=eti[:], in0=dt_[:], in1=MSK[2 * d + 1][:], op=ALU.mult), [dt_, MSK[2 * d + 1]], [eti])
            K.V(lambda en: en.tensor_tensor(out=ets[:], in0=ets[:], in1=p_b[:], op=ALU.mult), [ets, p_b], [ets])
            K.V(lambda en: en.tensor_tensor(out=N_[:], in0=ets[:], in1=p_kk[:], op=ALU.mult), [ets, p_kk], [N_])
            K.V(lambda en: en.tensor_tensor(out=qkT[d][par][:], in0=eti[:], in1=p_qk[:], op=ALU.mult), [eti, p_qk], [qkT[d][par]])
            p_m = PQ()
            K.PE(lambda en: en.transpose(p_m[:], N_[:], idf[:]), [N_, idf], [p_m])
            K.A(lambda en: en.activation(out=M_[:], in_=p_m[:], func=AF.Copy), [p_m], [M_])
            K.G(lambda en: en.tensor_tensor(out=y[:], in0=idf[:], in1=N_[:], op=ALU.subtract), [idf, N_], [y])
            Pc, PTc = N_, M_
            for lev in range(1, 6):
                Pn, PTn = Pm[lev % 2][d][par], PTm[lev % 2][d][par]
                p1 = PQ()
                K.PE(lambda en, p1=p1, Pc=Pc, PTc=PTc: en.matmul(p1[:], lhsT=Pc[:], rhs=PTc[:], start=True, stop=True), [Pc, PTc], [p1])
                K.A(lambda en, p1=p1, PTn=PTn: en.activation(out=PTn[:], in_=p1[:], func=AF.Copy), [p1], [PTn])
                if lev < 5:
                    p2 = PQ()
                    K.PE(lambda en, p2=p2, Pc=Pc, PTc=PTc: en.matmul(p2[:], lhsT=PTc[:], rhs=Pc[:], start=True, stop=True), [Pc, PTc], [p2])
                    K.A(lambda en, p2=p2, Pn=Pn: en.activation(out=Pn[:], in_=p2[:], func=AF.Copy), [p2], [Pn])
                p3 = PQ()
                K.PE(lambda en, p3=p3, PTn=PTn: en.matmul(p3[:], lhsT=PTn[:], rhs=y[:], start=True, stop=True), [PTn, y], [p3])
                K.V(lambda en, p3=p3: en.tensor_tensor(out=y[:], in0=y[:], in1=p3[:], op=ALU.add), [y, p3], [y])
                Pc, PTc = Pn, PTn
            K.A(lambda en: en.activation(out=ybf[:], in_=y[:], func=AF.Copy), [y], [ybf])
            K.V(lambda en: en.tensor_scalar(out=Vb_[d][par][:], in0=p_vt[:], scalar1=bcol, scalar2=None, op0=ALU.mult), [p_vt, cl_], [Vb_[d][par]])
            K.A(lambda en: en.activation(out=cv[:, 0:1], in_=gcol, func=AF.Exp), [cl_], [cv])
            K.V(lambda en: en.tensor_tensor(out=cv[:, 1:2], in0=cv[:, 0:1], in1=bcol, op=ALU.mult), [cv, cl_], [cv])
            K.V(lambda en: en.tensor_scalar(out=Kbe[d][par][:], in0=p_kt[:], scalar1=cv[:, 1:2], scalar2=None, op0=ALU.mult), [p_kt, cv], [Kbe[d][par]])
            K.V(lambda en: en.tensor_scalar(out=kdA[d][par][0:64, :], in0=p_kt[0:64, :], scalar1=cv[0:64, 3:4], scalar2=None, op0=ALU.mult), [p_kt, cv], [kdA[d][par]])
            K.V(lambda en: en.tensor_scalar(out=kdB[d][par][64:128, :], in0=p_kt[64:128, :], scalar1=cv[64:128, 3:4], scalar2=None, op0=ALU.mult), [p_kt, cv], [kdB[d][par]])
            K.V(lambda en: en.tensor_tensor(out=qdT[d][par][:], in0=QN[:, c0:c0 + 128], in1=egB[d][par][:], op=ALU.mult), [QN, egB[d][par]], [qdT[d][par]])
            p_u, p_w = PQ(), PQ()
            K.PE(lambda en: en.matmul(p_u[:], lhsT=ybf[:], rhs=Vb_[d][par][:], start=True, stop=True), [ybf, Vb_[d][par]], [p_u])
            K.PE(lambda en: en.matmul(p_w[:], lhsT=Kbe[d][par][:], rhs=ybf[:], start=True, stop=True), [ybf, Kbe[d][par]], [p_w])
            K.A(lambda en: en.activation(out=Ut[d][par][:], in_=p_u[:], func=AF.Copy), [p_u], [Ut[d][par]])
            K.A(lambda en: en.activation(out=nWT[d][par][:], in_=p_w[:], func=AF.Copy, scale=-1.0), [p_w], [nWT[d][par]])

        def seq(d, h, step):
            par = step % 2
            sc = order[d][step]
            c0 = sc * 128
            halves = ((0, kdA[d][par]), (64, kdB[d][par]))
            if d == 1:
                halves = halves[::-1]
            p_vn, p_o, p_s = qt[d * 4 + 0], qt[d * 4 + 1], qt[d * 4 + 2]
            for (r0, kd) in halves:
                col = lastc[d][0] if r0 == 0 else lastc[d][1]
                K.PE(lambda en: en.matmul(p_vn[:], lhsT=nWT[d][par][:], rhs=Sb[d][:], start=True, stop=True), [nWT[d][par], Sb[d]], [p_vn])
                K.V(lambda en, r0=r0: en.tensor_tensor(out=VNEW[d][r0:r0 + 64, :], in0=p_vn[r0:r0 + 64, :], in1=Ut[d][par][r0:r0 + 64, :], op=ALU.add),
                    [p_vn, Ut[d][par]], [VNEW[d]])
                K.PE(lambda en: en.matmul(p_o[:], lhsT=qdT[d][par][:], rhs=Sb[d][:], start=True, stop=False), [qdT[d][par], Sb[d]], [p_o])
                K.PE(lambda en: en.matmul(p_o[:], lhsT=qkT[d][par][:], rhs=VNEW[d][:], start=False, stop=True), [qkT[d][par], VNEW[d]], [p_o])
                K.A(lambda en, r0=r0: en.activation(out=OSB[d][par][r0:r0 + 64, :], in_=p_o[r0:r0 + 64, :], func=AF.Copy), [p_o], [OSB[d][par]])
                K.PE(lambda en, kd=kd: en.matmul(p_s[:], lhsT=kd[:], rhs=VNEW[d][:], start=True, stop=True), [kd, VNEW[d]], [p_s])
                K.V(lambda en, col=col: en.scalar_tensor_tensor(out=Sf[d][:], in0=Sf[d][:], scalar=egB[d][par][:, col:col + 1], in1=p_s[:],
                                                               op0=ALU.mult, op1=ALU.add), [Sf[d], egB[d][par], p_s], [Sf[d]])
                K.A(lambda en: en.activation(out=Sb[d][:], in_=Sf[d][:], func=AF.Copy), [Sf[d]], [Sb[d]])
            K.store(O[d].t[c0:c0 + 128, h * 128:(h + 1) * 128], OSB[d][par][:], reads=[OSB[d][par]], writes=[O[d].sub(sc)])

        def head_prep(h):
            NRM = NRMs[h % 2]
            for wi in range(3):
                c = wi * 4 + h
                ft = 20 + c
                K.load(X[:], projT.t[ft * 128:(ft + 1) * 128, :], reads=all_regions(projT, [ft]), writes=[X])
                K.V(lambda en, c=c: en.tensor_scalar(out=U[:], in0=X[:], scalar1=cw[:, c, 2:3], scalar2=None, op0=ALU.mult), [X, cw], [U])
                for (s0, s1) in SEGS:
                    for (j, off) in ((0, -2), (1, -1), (3, 1)):
                        if off < 0:
                            oa, ob, ia, ib = s0 - off, s1, s0, s1 + off
                        else:
                            oa, ob, ia, ib = s0, s1 - off, s0 + off, s1
                        K.V(lambda en, c=c, j=j, oa=oa, ob=ob, ia=ia, ib=ib: en.scalar_tensor_tensor(
                            out=U[:, oa:ob], in0=X[:, ia:ib], scalar=cw[:, c, j:j + 1], in1=U[:, oa:ob], op0=ALU.mult, op1=ALU.add), [X, U, cw], [U])
                K.A(lambda en: en.activation(out=U[:], in_=U[:], func=AF.Silu), [U], [U])
                if wi == 2:
                    K.G(lambda en: en.tensor_copy(out=NRM[2][:], in_=U[:]), [U], [NRM[2]])
                    continue
                K.G(lambda en: en.tensor_tensor(out=SQ[:], in0=U[:], in1=U[:], op=ALU.mult), [U], [SQ])
                for ti, (t0, n) in enumerate(TT512):
                    pn, r = psn[1], rs[ti % 2]
                    K.PE(lambda en, pn=pn, t0=t0, n=n: en.matmul(pn[:, 0:n], lhsT=ones[:], rhs=SQ[:, t0:t0 + n], start=True, stop=True), [ones, SQ], [pn])
                    K.V(lambda en, pn=pn, r=r, n=n: en.tensor_scalar(out=r[:, 0:n], in0=pn[:, 0:n], scalar1=EPS, scalar2=None, op0=ALU.add), [pn], [r])
                    K.A(lambda en, r=r, n=n: en.activation(out=r[:, 0:n], in_=r[:, 0:n], func=AF.Sqrt), [r], [r])
                    K.V(lambda en, r=r, n=n: en.reciprocal(out=r[:, 0:n], in_=r[:, 0:n]), [r], [r])
                    K.V(lambda en, r=r, t0=t0, n=n, wi=wi: en.scalar_tensor_tensor(out=NRM[wi][:, t0:t0 + n], in0=U[:, t0:t0 + n],
                                                                                 scalar=(128.0 ** -0.5 if wi == 0 else 1.0), in1=r[:, 0:n],
                                                                                 op0=ALU.mult, op1=ALU.mult), [U, r], [NRM[wi]])

        NH = 4 if DBG > 10 else 1
        head_prep(0)
        for h in range(NH):
            nxt = K.capture(head_prep, h + 1) if h + 1 < NH else []
            stride = (len(nxt) + NSC - 1) // NSC if nxt else 0
            for d in range(2):
                K.V(lambda en, d=d: en.memset(Sf[d][:], 0.0), [], [Sf[d]])
                K.V(lambda en, d=d: en.memset(Sb[d][:], 0.0), [], [Sb[d]])
                K.V(lambda en, d=d: en.memset(VNEW[d][:], 0.0), [], [VNEW[d]])
            if DBG == 1:
                continue
            K.emit_interleaved([K.capture(prep, d, h, 0) for d in range(2)])
            if DBG == 2:
                continue
            for step in range(NSC if DBG > 10 else 1):
                lists = []
                if step + 1 < NSC:
                    lists += [K.capture(prep, d, h, step + 1) for d in range(2)]
                lists += [K.capture(seq, d, h, step) for d in range(2)]
                if nxt:
                    lists.append(nxt[step * stride:(step + 1) * stride])
                K.emit_interleaved(lists)
    if DBG > 10:
        head_norm_epilogue(K, I, O, projT, zT, gate_ft0=32, out_row0=512, center=False)


def host_inputs(inputs, b):
    f = lambda a: np.ascontiguousarray(np.asarray(a, np.float32))
    m = {
        "x_b": f(inputs["x"][b]), "ctx_b": f(inputs["ctx"][b]),
        "c_b": _col(inputs["c"][b], 8), "c_ctx": _col(inputs["c_ctx"], 8),
        "ada_w": f(inputs["ada_w"]),
        "ada_b": np.stack([_col(inputs["ada_b"][l], 48) for l in range(DEPTH)]),
        "norm1_g": np.stack([_col(inputs["norm1_g"][l], 8) for l in range(DEPTH)]),
        "norm2_g": np.stack([_col(inputs["norm2_g"][l], 8) for l in range(DEPTH)]),
        "mix_w_out": f(inputs["mix_w_out"]), "mlp_w1": f(inputs["mlp_w1"]), "mlp_w2": f(inputs["mlp_w2"]),
        "ev_w_in": f(inputs["ev_w_in"]), "od_w_in": f(inputs["od_w_in"]),
        "final_g": _col(inputs["final_g"], 8),
        "ident": np.eye(128, dtype=np.float32), "ones": np.ones((128, 128), np.float32),
    }
    m.update(host_mixer_inputs(inputs))
    return m


def _bd(w):
    out = np.zeros((2, 4, 128, 128), np.float32)
    for d in range(2):
        for ct in range(4):
            out[d, ct, 0:64, 0:64] = w[d, 2 * ct]
            out[d, ct, 64:128, 64:128] = w[d, 2 * ct + 1]
    return out


def host_mixer_inputs(inputs):
    f = lambda a: np.ascontiguousarray(np.asarray(a, np.float32))
    m = {}
    m["lru_cw"] = f(np.asarray(inputs["lru_conv_w"]).reshape(2, 4, 4, 128).transpose(0, 3, 2, 1))
    m["lru_cb"] = f(np.asarray(inputs["lru_conv_b"]).reshape(2, 4, 128).transpose(0, 2, 1))
    for nm, key in (("lru_ba", "lru_ba"), ("lru_bx", "lru_bx"), ("lru_lam", "lru_lambda")):
        m[nm] = f(np.asarray(inputs[key]).reshape(2, 2, 4, 128).transpose(0, 3, 1, 2))
    m["lru_wa_bd"] = np.stack([_bd(np.asarray(inputs["lru_wa"][e])) for e in range(2)])
    m["lru_wx_bd"] = np.stack([_bd(np.asarray(inputs["lru_wx"][e])) for e in range(2)])
    m["ret_lg"] = f(np.broadcast_to(np.asarray(inputs["ret_log_gamma"]).reshape(2, 1, 8), (2, 128, 8)))
    j = np.arange(128, dtype=np.float32)
    m["pos_cols"] = f(np.stack([j + 1.0, 128.0 - j], 1))
    m["tri_f"] = f((j[:, None] <= j[None, :]).astype(np.float32))
    m["tri_b"] = f((j[:, None] >= j[None, :]).astype(np.float32))
    n_freq = 16
    inv = np.power(np.float32(10000.0), -np.arange(n_freq, dtype=np.float32) / n_freq).astype(np.float32)
    rows = NLAT // 64
    r = np.arange(rows, dtype=np.float32)
    c = np.arange(64, dtype=np.float32)
    row_ang = np.broadcast_to(r[:, None, None] * inv, (rows, 64, n_freq))
    col_ang = np.broadcast_to(c[None, :, None] * inv, (rows, 64, n_freq))
    ang = np.concatenate([row_ang, col_ang], -1).reshape(NLAT, 32).astype(np.float32)
    cosT = np.cos(ang).T.astype(np.float32)
    sinT = np.sin(ang).T.astype(np.float32)
    m["rope_cos"] = f(np.concatenate([cosT, cosT, cosT, cosT], 0))
    m["rope_sin"] = f(np.concatenate([-sinT, sinT, -sinT, sinT], 0))
    m["hg_logits"] = f(np.asarray(inputs["hg_lb_logits"]).reshape(2, 2, 4, 128).transpose(3, 0, 1, 2))
    m["gdn_cw"] = f(np.asarray(inputs["gdn_conv_w"]).reshape(2, 4, 12, 128).transpose(0, 3, 2, 1))
    ab = np.zeros((2, 16, 2), np.float32)
    ab[:, 8:16, 0] = np.asarray(inputs["gdn_a_log"]).reshape(2, 8)
    ab[:, 8:16, 1] = np.asarray(inputs["gdn_dt_bias"]).reshape(2, 8)
    m["gdn_ab"] = ab
    sel = np.zeros((16, 16, 128), np.float32)
    for r in range(16):
        sel[r, r, :] = 1.0
    m["gdn_sel"] = sel
    blk = (j[:, None] // 64) == (j[None, :] // 64)
    m["gdn_masks"] = f(np.stack([(j[:, None] < j[None, :]) & blk, (j[:, None] <= j[None, :]) & blk,
                                 (j[:, None] > j[None, :]) & blk, (j[:, None] >= j[None, :]) & blk]).astype(np.float32))
    return m


def build_mixer_test(kind, idx, F):
    nc = bass.Bass("TRN2", target_bir_lowering=False)
    K = KB(nc, debug=True)
    I = {}
    I["ident"] = K.din("ident", [128, 128])
    I["ones"] = K.din("ones", [128, 128])
    mixer_inputs(K, I)
    projT = K.din("projT", [F, T])
    zT = K.dout("zT", [D, T])
    done = K.dout("done", [128, 128])
    S = mixer_scratch(K)
    if kind == "even":
        even_mixer(K, idx, I, S, projT, zT)
    else:
        odd_mixer(K, idx, I, S, projT, zT)
    K.P.barrier()
    fin = K.P.dma("sync", done.t[:, :], I["ident"].t[:, :])
    K.P.emit(final_wait_ops=[fin])
    return nc, K


_PROGRAM_CACHE = {}


def kernel(**inputs):
    n_cores = 8
    if "full" not in _PROGRAM_CACHE:
        _PROGRAM_CACHE["full"] = build_program({"debug": False})[0]
    nc = _PROGRAM_CACHE["full"]
    shared = None
    in_maps = []
    for b in range(n_cores):
        m = host_inputs(inputs, b)
        if shared is None:
            shared = m
        else:
            for k in list(m.keys()):
                if k not in ("x_b", "ctx_b", "c_b"):
                    m[k] = shared[k]
        in_maps.append(m)
    res = run_bass_kernel_spmd(nc, in_maps, core_ids=list(range(n_cores)))
    out = np.stack([np.asarray(res.results[b]["out"], dtype=np.float32) for b in range(n_cores)], axis=0)
    return out
```

```python
import contextlib
import os
import numpy as np
import concourse.bass as bass
import concourse.mybir as mybir
from concourse.bass_utils import run_bass_kernel_spmd

F32 = mybir.dt.float32
BF16 = mybir.dt.bfloat16
AF = mybir.ActivationFunctionType
ALU = mybir.AluOpType
AX = mybir.AxisListType

D = 1024
NCTX = 256
NLAT = 4096
T = NCTX + NLAT
DEPTH = 4
DFF = 4096
EV_IN = 2560
OD_IN = 4624
EPS = 1e-6

ENGS = ("sync", "tensor", "vector", "scalar", "gpsimd")
SEM_WRAP = 20000
DMA_SLOTS = 8
SAME_ENGINE_SYNC = True


class Buf:
    __slots__ = ("w", "r", "x")

    def __init__(self, excl=False):
        self.w = None
        self.r = []
        self.x = excl


class Op:
    __slots__ = ("eng", "fn", "deps", "dma", "signaled", "token")

    def __init__(self, eng, fn, dma):
        self.eng = eng
        self.fn = fn
        self.deps = []
        self.dma = dma
        self.signaled = False
        self.token = None


class TT:
    def __init__(self, t):
        self.t = t
        self.b = Buf()
        self.subs = {}

    def sub(self, key):
        b = self.subs.get(key)
        if b is None:
            b = Buf()
            self.subs[key] = b
        return b

    def __getitem__(self, k):
        return self.t[k]


def _bufs(lst):
    out = []
    for x in lst:
        if isinstance(x, Buf):
            out.append(x)
        elif isinstance(x, TT):
            out.append(x.b)
        else:
            raise TypeError(type(x))
    return out


class Prog:
    def __init__(self, nc):
        self.nc = nc
        self.ops = {e: [] for e in ENGS}
        self.dma_ops = {e: [] for e in ENGS}
        self.nops = 0
        self.barrier_deps = []
        self.barrier_pending = set()

    def barrier(self):
        deps = []
        for e in ENGS:
            for op in reversed(self.ops[e]):
                if not op.dma:
                    deps.append(op)
                    break
            deps += self.dma_ops[e][-DMA_SLOTS:]
        self.barrier_deps = deps
        self.barrier_pending = set(ENGS)

    def add(self, eng, fn, reads=(), writes=(), dma=False):
        op = Op(eng, fn, dma)
        reads = _bufs(reads)
        writes = _bufs(writes)
        xr = [b for b in reads if b.x]
        if xr:
            writes = writes + [b for b in xr if b not in writes]
            reads = [b for b in reads if not b.x]
        deps = []
        for b in reads:
            if b.w is not None:
                deps.append(b.w)
        for b in writes:
            if b.w is not None:
                deps.append(b.w)
            lastc = {}
            for r_ in b.r:
                if r_.dma:
                    deps.append(r_)
                else:
                    lastc[r_.eng] = r_
            deps.extend(lastc.values())
        for b in writes:
            b.w = op
            b.r = []
        for b in reads:
            b.r.append(op)
        if eng in self.barrier_pending:
            self.barrier_pending.discard(eng)
            deps.extend(self.barrier_deps)
        if dma:
            lst = self.dma_ops[eng]
            if len(lst) >= DMA_SLOTS:
                deps.append(lst[len(lst) - DMA_SLOTS])
            lst.append(op)
        seen = set()
        dd = []
        for d in deps:
            if id(d) in seen or d is op:
                continue
            seen.add(id(d))
            if (not d.dma) and (not dma) and d.eng == eng and (eng == "tensor" or not SAME_ENGINE_SYNC):
                continue
            dd.append(d)
            d.signaled = True
        op.deps = dd
        self.ops[eng].append(op)
        self.nops += 1
        return op

    def dma(self, eng, out, in_, reads=(), writes=(), **kw):
        return self.add(eng, lambda e: e.dma_start(out=out, in_=in_, **kw), reads, writes, dma=True)

    def emit(self, final_wait_ops=()):
        nc = self.nc
        nsem = {}
        for e in ENGS:
            cnt = 0
            dcnt = 0
            for op in self.ops[e]:
                if op.dma:
                    op.token = ("d", e, dcnt % DMA_SLOTS, 16 * (dcnt // DMA_SLOTS + 1))
                    dcnt += 1
                elif op.signaled:
                    op.token = ("c", e, cnt // SEM_WRAP, cnt % SEM_WRAP + 1)
                    cnt += 1
            nsem[e] = (cnt + SEM_WRAP - 1) // SEM_WRAP
        with contextlib.ExitStack() as st:
            sems = {}
            for e in ENGS:
                for k in range(nsem[e]):
                    sems[("c", e, k)] = st.enter_context(nc.semaphore(f"c_{e}_{k}"))
                if self.dma_ops[e]:
                    for k in range(DMA_SLOTS):
                        sems[("d", e, k)] = st.enter_context(nc.semaphore(f"d_{e}_{k}"))
            block = st.enter_context(nc.Block())

            def make(e):
                def body(eng):
                    waited = {}
                    for op in self.ops[e]:
                        for d in op.deps:
                            key = d.token[:3]
                            val = d.token[3]
                            if key[0] == "c":
                                kk = (key[0], key[1])
                                cur = waited.get(kk, (-1, 0))
                                if (key[2], val) <= cur:
                                    continue
                                waited[kk] = (key[2], val)
                            else:
                                if waited.get(key, 0) >= val:
                                    continue
                                waited[key] = val
                            eng.wait_ge(sems[key], val)
                        ins = op.fn(eng)
                        if op.token is not None:
                            ins.then_inc(sems[op.token[:3]], 16 if op.dma else 1)
                    if e == "sync":
                        for d in final_wait_ops:
                            eng.wait_ge(sems[d.token[:3]], d.token[3])
                return body

            for e in ENGS:
                if self.ops[e] or (e == "sync" and final_wait_ops):
                    getattr(block, e)(make(e))


class KB:
    def __init__(self, nc, debug=False):
        self.nc = nc
        self.P = Prog(nc)
        self.debug = debug
        self.dram = {}
        self.uid = 0
        self.rr = 0

    def din(self, name, shape, dt=F32):
        t = TT(self.nc.dram_tensor(name, list(shape), dt, kind="ExternalInput").ap())
        self.dram[name] = t
        return t

    def dout(self, name, shape, dt=F32):
        t = TT(self.nc.dram_tensor(name, list(shape), dt, kind="ExternalOutput").ap())
        self.dram[name] = t
        return t

    def dscr(self, name, shape, dt=F32, dbg=False):
        kind = "ExternalOutput" if (self.debug and dbg) else "Internal"
        t = TT(self.nc.dram_tensor(name, list(shape), dt, kind=kind).ap())
        self.dram[name] = t
        return t

    def sb(self, st, shape, dt=F32, name=None):
        self.uid += 1
        return TT(st.enter_context(self.nc.sbuf_tensor(f"{name or 's'}{self.uid}", list(shape), dt)))

    def ps(self, st, shape, dt=F32, name=None):
        self.uid += 1
        nfree = 512 if dt == F32 else 1024
        full = st.enter_context(self.nc.psum_tensor(f"{name or 'p'}{self.uid}", [128, nfree], dt))
        n = 1
        for x in shape[1:]:
            n *= x
        assert n <= nfree
        v = full[0:shape[0], 0:n]
        if len(shape) == 3:
            v = v.rearrange("p (a b) -> p a b", a=shape[1])
        t = TT(v)
        t.b = Buf(excl=True)
        return t

    def psq(self, bank, view):
        t = TT(view)
        t.b = bank.b
        return t

    @contextlib.contextmanager
    def phase(self):
        with contextlib.ExitStack() as st:
            yield st
        self.P.barrier()

    def capture(self, f, *args):
        saved = self.P.add
        lst = []

        def rec(eng, fn, reads=(), writes=(), dma=False):
            lst.append((eng, fn, list(reads), list(writes), dma))
            return None
        self.P.add = rec
        try:
            f(*args)
        finally:
            self.P.add = saved
        return lst

    def emit_interleaved(self, lists):
        idx = [0] * len(lists)
        live = True
        while live:
            live = False
            for k, l in enumerate(lists):
                if idx[k] < len(l):
                    self.P.add(*l[idx[k]])
                    idx[k] += 1
                    live = True

    def op(self, eng, fn, reads=(), writes=()):
        return self.P.add(eng, fn, reads, writes)

    def V(self, fn, reads=(), writes=()):
        return self.P.add("vector", fn, reads, writes)

    def A(self, fn, reads=(), writes=()):
        return self.P.add("scalar", fn, reads, writes)

    def G(self, fn, reads=(), writes=()):
        return self.P.add("gpsimd", fn, reads, writes)

    def PE(self, fn, reads=(), writes=()):
        return self.P.add("tensor", fn, reads, writes)

    def VA(self, fn, reads=(), writes=()):
        self.rr += 1
        return self.P.add("vector" if self.rr % 2 else "scalar", fn, reads, writes)

    def load(self, out, in_, reads=(), writes=(), eng="sync"):
        return self.P.dma(eng, out, in_, reads, writes)

    def store(self, out, in_, reads=(), writes=(), eng="gpsimd"):
        return self.P.dma(eng, out, in_, reads, writes)


def copy_any(out, in_):
    def f(e):
        if hasattr(e, "tensor_copy"):
            return e.tensor_copy(out=out, in_=in_)
        return e.activation(out=out, in_=in_, func=AF.Copy)
    return f


def token_groups(n):
    gs = []
    t = 0
    while t < NCTX:
        m = min(n, NCTX - t)
        gs.append((t, m, 1))
        t += m
    while t < T:
        m = min(n, T - t)
        gs.append((t, m, 0))
        t += m
    return gs


def phase_input_transpose(K, x_b, ctx_b, xT, ident):
    with K.phase() as st:
        idt = K.sb(st, [128, 128], F32, "ident")
        K.load(idt[:], ident[:, :], writes=[idt])
        xin = [K.sb(st, [128, D], F32, "xin") for _ in range(2)]
        stg = [K.sb(st, [128, 8, 128], F32, "xstg") for _ in range(2)]
        pss = [K.ps(st, [128, 4, 128], F32, "ptr") for _ in range(2)]
        for i in range(T // 128):
            xt = xin[i % 2]
            sg = stg[i % 2]
            src = ctx_b.t[i * 128:(i + 1) * 128, :] if i < 2 else x_b.t[(i - 2) * 128:(i - 1) * 128, :]
            K.load(xt[:], src, writes=[xt])
            for kg in range(2):
                ps = pss[kg]
                for j in range(4):
                    k = kg * 4 + j
                    K.PE(lambda e, ps=ps, j=j, k=k, xt=xt: e.transpose(ps[:, j, :], xt[:, k * 128:(k + 1) * 128], idt[:]),
                         [xt, idt], [ps])
                K.VA(copy_any(sg[:, kg * 4:(kg + 1) * 4, :], ps[:]), [ps], [sg])
            K.store(xT.t.rearrange("(k p) t -> p k t", p=128)[:, :, i * 128:(i + 1) * 128], sg[:],
                    reads=[sg], writes=[xT.sub(i // 2 if i < 2 else 1 + (i - 2) // 4)])


def phase_adaln(K, layer, c_b, c_ctx, ada_w, ada_b, norm1_g, norm2_g, mod):
    modT, G1, G2 = mod["modT"], mod["G1"], mod["G2"]
    with K.phase() as st:
        sT = K.sb(st, [128, 2, 8], F32, "sT")
        K.load(sT[:, 0, :], c_b.t[:, :], writes=[sT])
        K.load(sT[:, 1, :], c_ctx.t[:, :], writes=[sT])
        sS = K.sb(st, [128, 2, 8], F32, "sS")
        K.A(lambda e: e.activation(out=sS[:], in_=sT[:], func=AF.Silu), [sT], [sS])
        bT = K.sb(st, [128, 48], F32, "bT")
        K.load(bT[:], ada_b.t[layer], writes=[bT])
        gT = K.sb(st, [128, 2, 8], F32, "gT")
        K.load(gT[:, 0, :], norm1_g.t[layer], writes=[gT])
        K.load(gT[:, 1, :], norm2_g.t[layer], writes=[gT])
        wst = [K.sb(st, [128, 8, 1024], F32, "adaw") for _ in range(2)]
        ps = K.ps(st, [128, 48, 2], F32, "pmod")
        for j in range(6):
            w = wst[j % 2]
            for kh in range(2):
                K.load(w[:, kh * 4:(kh + 1) * 4, :],
                       ada_w.t[layer].rearrange("(k p) n -> p k n", p=128)[:, kh * 4:(kh + 1) * 4, j * 1024:(j + 1) * 1024],
                       writes=[w])
            for f in range(8):
                for k in range(8):
                    K.PE(lambda e, w=w, f=f, k=k, j=j: e.matmul(ps[:, j * 8 + f, :], lhsT=w[:, k, f * 128:(f + 1) * 128],
                                                               rhs=sS[:, :, k], start=(k == 0), stop=(k == 7)),
                         [w, sS], [ps])
        K.V(lambda e: e.tensor_tensor(out=modT[:], in0=ps[:], in1=bT[:].unsqueeze(2).broadcast_to([128, 48, 2]), op=ALU.add),
            [ps, bT], [modT])
        for (G, gi, j) in ((G1, 0, 1), (G2, 1, 4)):
            K.V(lambda e, G=G, gi=gi, j=j: e.scalar_tensor_tensor(
                out=G[:], in0=modT[:, j * 8:(j + 1) * 8, :], scalar=1.0,
                in1=gT[:, gi, :].unsqueeze(2).broadcast_to([128, 8, 2]), op0=ALU.add, op1=ALU.mult),
                [modT, gT], [G])


def load_weight_bf16(K, st, w_ap, R, F, name, stage):
    nk = R // 128
    dst = K.sb(st, [128, nk, F], BF16, name)
    wv = w_ap.rearrange("(k p) n -> p k n", p=128)
    i = 0
    for k0 in range(0, nk, 8):
        kn = min(8, nk - k0)
        SW = stage[0].t.shape[2]
        for c0 in range(0, F, SW):
            cn = min(SW, F - c0)
            sg = stage[i % len(stage)]
            K.load(sg[:, 0:kn, 0:cn], wv[:, k0:k0 + kn, c0:c0 + cn], writes=[sg])
            eng = "gpsimd" if i % 2 == 0 else "vector"
            K.op(eng, lambda e, sg=sg, k0=k0, kn=kn, c0=c0, cn=cn: e.tensor_copy(out=dst[:, k0:k0 + kn, c0:c0 + cn], in_=sg[:, 0:kn, 0:cn]),
                 [sg], [dst])
            i += 1
    return dst


def norm_modulate(K, xg, n, s, G, modT, jshift, ones, sq, psn, rstd, tmp, hT):
    K.A(lambda e: e.activation(out=sq[:, :, 0:n], in_=xg[:, :, 0:n], func=AF.Square), [xg], [sq])
    for k in range(8):
        K.PE(lambda e, k=k: e.matmul(psn[:, 0:n], lhsT=ones[:], rhs=sq[:, k, 0:n], start=(k == 0), stop=(k == 7)), [sq, ones], [psn])
    K.V(lambda e: e.tensor_scalar(out=rstd[:, 0:n], in0=psn[:, 0:n], scalar1=1.0 / D, scalar2=EPS, op0=ALU.mult, op1=ALU.add), [psn], [rstd])
    K.A(lambda e: e.activation(out=rstd[:, 0:n], in_=rstd[:, 0:n], func=AF.Sqrt), [rstd], [rstd])
    K.V(lambda e: e.reciprocal(out=rstd[:, 0:n], in_=rstd[:, 0:n]), [rstd], [rstd])
    for k in range(8):
        K.V(lambda e, k=k: e.scalar_tensor_tensor(out=tmp[:, k, 0:n], in0=xg[:, k, 0:n], scalar=G[:, k, s:s + 1], in1=rstd[:, 0:n],
                                                  op0=ALU.mult, op1=ALU.mult), [xg, G, rstd], [tmp])
        K.A(lambda e, k=k: e.activation(out=hT[:, k, 0:n], in_=tmp[:, k, 0:n], func=AF.Identity,
                                        bias=modT[:, jshift * 8 + k, s:s + 1], scale=1.0), [tmp, modT], [hT])


def phase_norm_inproj(K, xT, w_in_ap, F, projT, mod, ones_d, hdbg=None):
    NG = 512
    with K.phase() as st:
        ones = K.sb(st, [128, 128], F32, "ones")
        K.load(ones[:], ones_d[:, :], writes=[ones])
        stage = [K.sb(st, [128, 8, 256], F32, "wstage") for _ in range(2)]
        W = load_weight_bf16(K, st, w_in_ap, D, F, "win", stage)
        xgs = [K.sb(st, [128, 8, NG], F32, "xg") for _ in range(2)]
        sq = K.sb(st, [128, 8, NG], F32, "sq")
        tmp = K.sb(st, [128, 8, NG], F32, "tmp")
        rstd = K.sb(st, [128, NG], F32, "rstd")
        hTs = [K.sb(st, [128, 8, NG], BF16, "hT") for _ in range(2)]
        outs = [K.sb(st, [128, NG], F32, "pout") for _ in range(3)]
        psn = K.ps(st, [128, NG], F32, "psn")
        pss = [K.ps(st, [128, NG], F32, "psp") for _ in range(4)]
        xv = xT.t.rearrange("(k p) t -> p k t", p=128)
        groups = token_groups(NG)
        nft = (F + 127) // 128
        cntr = {"c": 0}

        def norm_list(gi):
            t0, n, s_ = groups[gi]
            xg, hT = xgs[gi % 2], hTs[gi % 2]
            K.load(xg[:, :, 0:n], xv[:, :, t0:t0 + n], reads=[xT.sub(gi)], writes=[xg])
            norm_modulate(K, xg, n, s_, mod["G1"], mod["modT"], 0, ones, sq, psn, rstd, tmp, hT)
            if hdbg is not None:
                hf = tmp
                K.V(lambda en: en.tensor_copy(out=hf[:, :, 0:n], in_=hT[:, :, 0:n]), [hT], [hf])
                K.store(hdbg.t.rearrange("(k p) t -> p k t", p=128)[:, :, t0:t0 + n], hf[:, :, 0:n], reads=[hf], writes=[hdbg])

        def mm_list(gi):
            t0, n, s_ = groups[gi]
            hT = hTs[gi % 2]
            for f in range(nft):
                m = min(128, F - f * 128)
                ps = pss[cntr["c"] % 4]
                ot = outs[cntr["c"] % 3]
                cntr["c"] += 1
                for k in range(8):
                    K.PE(lambda en, ps=ps, f=f, m=m, k=k: en.matmul(ps[0:m, 0:n], lhsT=W[:, k, f * 128:f * 128 + m], rhs=hT[:, k, 0:n],
                                                                 start=(k == 0), stop=(k == 7)), [W, hT], [ps])
                K.VA(copy_any(ot[0:m, 0:n], ps[0:m, 0:n]), [ps], [ot])
                K.store(projT.t[f * 128:f * 128 + m, t0:t0 + n], ot[0:m, 0:n], reads=[ot], writes=[projT.sub((f, gi))])

        norm_list(0)
        for gi in range(len(groups)):
            lists = [K.capture(mm_list, gi)]
            if gi + 1 < len(groups):
                lists.append(K.capture(norm_list, gi + 1))
            K.emit_interleaved(lists)


NP_GROUPS = token_groups(512)


def region(t0):
    for gi, (a, n, s) in enumerate(NP_GROUPS):
        if a <= t0 < a + n:
            return gi
    raise ValueError


def regions(tt, t0, n):
    return [tt.sub(gi) for gi, (a, m, s) in enumerate(NP_GROUPS) if a < t0 + n and t0 < a + m]


def phase_out_w1(K, xT, zT, aTd, w_out_ap, w1_ap, mod, ones_d, skip_ctx, xmid_dbg=None):
    NG = 256
    modT = mod["modT"]
    with K.phase() as st:
        ones = K.sb(st, [128, 128], F32, "ones")
        K.load(ones[:], ones_d[:, :], writes=[ones])
        stage = [K.sb(st, [128, 8, 512], F32, "wstage") for _ in range(2)]
        Wo = load_weight_bf16(K, st, w_out_ap, D, D, "wo", stage)
        W1 = load_weight_bf16(K, st, w1_ap, D, DFF, "w1", stage)
        zgs = [K.sb(st, [128, 8, NG], F32, "zg") for _ in range(2)]
        xgs = [K.sb(st, [128, 8, NG], F32, "xg") for _ in range(2)]
        zbs = [K.sb(st, [128, 8, NG], BF16, "zb") for _ in range(2)]
        sq = K.sb(st, [128, 8, NG], F32, "sq")
        rstd = K.sb(st, [128, NG], F32, "rstd")
        hTs = [K.sb(st, [128, 8, NG], BF16, "hT") for _ in range(2)]
        rl = [K.sb(st, [128, NG], F32, "rl") for _ in range(2)]
        aTs = [K.sb(st, [128, 4, NG], BF16, "aTs") for _ in range(2)]
        psn = K.ps(st, [128, NG], F32, "psn")
        psA = [K.ps(st, [128, NG], F32, "psA") for _ in range(2)]
        psC = [K.ps(st, [128, NG], F32, "psC") for _ in range(4)]
        xv = xT.t.rearrange("(k p) t -> p k t", p=128)
        zv = zT.t.rearrange("(k p) t -> p k t", p=128)
        av = aTd.t.rearrange("(k p) t -> p k t", p=128)
        groups = [g for g in token_groups(NG) if not (g[2] == 1 and skip_ctx)]
        cA = {"c": 0}
        cC = {"c": 0}

        def stage_ab(i):
            t0, n, s_ = groups[i]
            zg, xg, zb, hT = zgs[i % 2], xgs[i % 2], zbs[i % 2], hTs[i % 2]
            reg = region(t0)
            K.load(zg[:, :, 0:n], zv[:, :, t0:t0 + n], reads=[zT.sub(reg)], writes=[zg])
            K.load(xg[:, :, 0:n], xv[:, :, t0:t0 + n], reads=[xT.sub(reg)], writes=[xg])
            K.G(lambda en: en.tensor_copy(out=zb[:, :, 0:n], in_=zg[:, :, 0:n]), [zg], [zb])
            for f in range(8):
                ps = psA[cA["c"] % 2]
                cA["c"] += 1
                for k in range(8):
                    K.PE(lambda en, ps=ps, f=f, k=k: en.matmul(ps[:, 0:n], lhsT=Wo[:, k, f * 128:(f + 1) * 128], rhs=zb[:, k, 0:n],
                                                            start=(k == 0), stop=(k == 7)), [Wo, zb], [ps])
                K.V(lambda en, ps=ps, f=f: en.scalar_tensor_tensor(
                    out=xg[:, f, 0:n], in0=ps[:, 0:n], scalar=modT[:, 2 * 8 + f, s_:s_ + 1], in1=xg[:, f, 0:n], op0=ALU.mult, op1=ALU.add),
                    [ps, modT, xg], [xg])
            K.store(xv[:, :, t0:t0 + n], xg[:, :, 0:n], reads=[xg], writes=[xT.sub(reg)])
            if xmid_dbg is not None:
                K.store(xmid_dbg.t.rearrange("(k p) t -> p k t", p=128)[:, :, t0:t0 + n], xg[:, :, 0:n], reads=[xg], writes=[xmid_dbg])
            norm_modulate(K, xg, n, s_, mod["G2"], modT, 3, ones, sq, psn, rstd, zg, hT)

        def stage_c(i):
            t0, n, s_ = groups[i]
            hT = hTs[i % 2]
            reg = region(t0)
            for f in range(32):
                ps = psC[cC["c"] % 4]
                r = rl[cC["c"] % 2]
                cC["c"] += 1
                ast = aTs[(f // 4) % 2]
                for k in range(8):
                    K.PE(lambda en, ps=ps, f=f, k=k: en.matmul(ps[:, 0:n], lhsT=W1[:, k, f * 128:(f + 1) * 128], rhs=hT[:, k, 0:n],
                                                            start=(k == 0), stop=(k == 7)), [W1, hT], [ps])
                K.A(lambda en, ps=ps, r=r: en.activation(out=r[:, 0:n], in_=ps[:, 0:n], func=AF.Relu), [ps], [r])
                K.op("gpsimd" if f % 2 else "vector",
                     lambda en, r=r, f=f, ast=ast: en.tensor_tensor(out=ast[:, f % 4, 0:n], in0=r[:, 0:n], in1=r[:, 0:n], op=ALU.mult), [r], [ast])
                if f % 4 == 3:
                    K.store(av[:, f - 3:f + 1, t0:t0 + n], ast[:, :, 0:n], reads=[ast], writes=[aTd.sub(reg)])

        stage_ab(0)
        for i in range(len(groups)):
            lists = [K.capture(stage_c, i)]
            if i + 1 < len(groups):
                lists.append(K.capture(stage_ab, i + 1))
            K.emit_interleaved(lists)


def phase_w2(K, xT, aTd, w2_ap, mod, skip_ctx):
    NG = 512
    modT = mod["modT"]
    with K.phase() as st:
        stage = [K.sb(st, [128, 8, 512], F32, "wstage") for _ in range(2)]
        W2 = load_weight_bf16(K, st, w2_ap, DFF, D, "w2", stage)
        ags = [K.sb(st, [128, 32, NG], BF16, "ag") for _ in range(2)]
        xgs = [K.sb(st, [128, 8, NG], F32, "xg") for _ in range(2)]
        pss = [K.ps(st, [128, NG], F32, "psp") for _ in range(4)]
        xv = xT.t.rearrange("(k p) t -> p k t", p=128)
        av = aTd.t.rearrange("(k p) t -> p k t", p=128)
        cnt = 0
        for gi, (t0, n, s) in enumerate(NP_GROUPS):
            if s == 1 and skip_ctx:
                continue
            ag = ags[gi % 2]
            xg = xgs[gi % 2]
            for q in range(4):
                K.load(ag[:, q * 8:(q + 1) * 8, 0:n], av[:, q * 8:(q + 1) * 8, t0:t0 + n], reads=[aTd.sub(gi)], writes=[ag])
            K.load(xg[:, :, 0:n], xv[:, :, t0:t0 + n], reads=[xT.sub(gi)], writes=[xg])
            for f in range(8):
                ps = pss[cnt % 4]
                cnt += 1
                for k in range(32):
                    K.PE(lambda e, ps=ps, f=f, k=k, n=n, ag=ag: e.matmul(ps[:, 0:n], lhsT=W2[:, k, f * 128:(f + 1) * 128], rhs=ag[:, k, 0:n],
                                                                     start=(k == 0), stop=(k == 31)), [W2, ag], [ps])
                K.V(lambda e, ps=ps, f=f, xg=xg, n=n, s=s: e.scalar_tensor_tensor(
                    out=xg[:, f, 0:n], in0=ps[:, 0:n], scalar=modT[:, 5 * 8 + f, s:s + 1], in1=xg[:, f, 0:n], op0=ALU.mult, op1=ALU.add),
                    [ps, modT, xg], [xg])
            K.store(xv[:, :, t0:t0 + n], xg[:, :, 0:n], reads=[xg], writes=[xT.sub(gi)])


def phase_final(K, xT, final_g, ones_d, ident, out):
    NG = 512
    with K.phase() as st:
        ones = K.sb(st, [128, 128], F32, "ones")
        K.load(ones[:], ones_d[:, :], writes=[ones])
        idt = K.sb(st, [128, 128], F32, "ident")
        K.load(idt[:], ident[:, :], writes=[idt])
        gf = K.sb(st, [128, 8], F32, "gf")
        K.load(gf[:], final_g.t[:, :], writes=[gf])
        xgs = [K.sb(st, [128, 8, NG], F32, "xg") for _ in range(2)]
        sq = K.sb(st, [128, 8, NG], F32, "sq")
        rstd = K.sb(st, [128, NG], F32, "rstd")
        yT = K.sb(st, [128, 8, NG], F32, "yT")
        ots = [K.sb(st, [128, D], F32, "ot") for _ in range(2)]
        psn = K.ps(st, [128, NG], F32, "psn")
        pss = [K.ps(st, [128, 4, 128], F32, "ptr") for _ in range(2)]
        xv = xT.t.rearrange("(k p) t -> p k t", p=128)
        fin = []
        cnt = 0
        for gi, (t0, n, s) in enumerate(token_groups(NG)):
            if s == 1:
                continue
            xg = xgs[gi % 2]
            K.load(xg[:, :, 0:n], xv[:, :, t0:t0 + n], reads=[xT.sub(gi)], writes=[xg])
            K.A(lambda e, xg=xg: e.activation(out=sq[:], in_=xg[:], func=AF.Square), [xg], [sq])
            for k in range(8):
                K.PE(lambda e, k=k: e.matmul(psn[:], lhsT=ones[:], rhs=sq[:, k, :], start=(k == 0), stop=(k == 7)), [sq, ones], [psn])
            K.V(lambda e: e.tensor_scalar(out=rstd[:], in0=psn[:], scalar1=1.0 / D, scalar2=EPS, op0=ALU.mult, op1=ALU.add), [psn], [rstd])
            K.A(lambda e: e.activation(out=rstd[:], in_=rstd[:], func=AF.Sqrt), [rstd], [rstd])
            K.V(lambda e: e.reciprocal(out=rstd[:], in_=rstd[:]), [rstd], [rstd])
            for k in range(8):
                K.V(lambda e, k=k, xg=xg: e.scalar_tensor_tensor(out=yT[:, k, :], in0=xg[:, k, :], scalar=gf[:, k:k + 1], in1=rstd[:],
                                                                op0=ALU.mult, op1=ALU.mult), [xg, gf, rstd], [yT])
            for tt in range(n // 128):
                ot = ots[cnt % 2]
                for kg in range(2):
                    ps = pss[kg]
                    for j in range(4):
                        k = kg * 4 + j
                        K.PE(lambda e, ps=ps, j=j, k=k, tt=tt: e.transpose(ps[:, j, :], yT[:, k, tt * 128:(tt + 1) * 128], idt[:]), [yT, idt], [ps])
                    K.VA(copy_any(ot[:, kg * 512:(kg + 1) * 512], ps[:].rearrange("p a b -> p (a b)")), [ps], [ot])
                r0 = t0 - NCTX + tt * 128
                fin.append(K.store(out.t[r0:r0 + 128, :], ot[:], reads=[ot], writes=[out]))
                cnt += 1
    return fin


def _col(v, nchunk):
    return np.ascontiguousarray(np.asarray(v, np.float32).reshape(nchunk, 128).T)


def build_program(cfg):
    debug = cfg.get("debug", False)
    layers = cfg.get("layers", list(range(DEPTH)))
    nc = bass.Bass("TRN2", target_bir_lowering=False)
    K = KB(nc, debug=debug)
    I = {}
    I["x_b"] = K.din("x_b", [NLAT, D])
    I["ctx_b"] = K.din("ctx_b", [NCTX, D])
    I["c_b"] = K.din("c_b", [128, 8])
    I["c_ctx"] = K.din("c_ctx", [128, 8])
    I["ada_w"] = K.din("ada_w", [DEPTH, D, 6 * D])
    I["ada_b"] = K.din("ada_b", [DEPTH, 128, 48])
    I["norm1_g"] = K.din("norm1_g", [DEPTH, 128, 8])
    I["norm2_g"] = K.din("norm2_g", [DEPTH, 128, 8])
    I["mix_w_out"] = K.din("mix_w_out", [DEPTH, D, D])
    I["mlp_w1"] = K.din("mlp_w1", [DEPTH, D, DFF])
    I["mlp_w2"] = K.din("mlp_w2", [DEPTH, DFF, D])
    I["ev_w_in"] = K.din("ev_w_in", [2, D, EV_IN])
    I["od_w_in"] = K.din("od_w_in", [2, D, OD_IN])
    I["final_g"] = K.din("final_g", [128, 8])
    I["ident"] = K.din("ident", [128, 128])
    I["ones"] = K.din("ones", [128, 128])
    mixer_inputs(K, I)
    out = K.dout("out", [NLAT, D])
    xT = K.dscr("xT", [D, T], F32, dbg=True)
    projT = K.dscr("projT", [OD_IN, T], F32, dbg=True)
    zT = K.dscr("zT", [D, T], F32, dbg=True)
    aTd = K.dscr("aTd", [DFF, T], BF16)
    hdbg = K.dscr("hdbg", [D, T], F32, dbg=True) if debug else None
    xmid = K.dscr("xmid", [D, T], F32, dbg=True) if debug else None
    zin = K.din("zin", [D, T]) if cfg.get("z_from_input") else None
    S = mixer_scratch(K)

    with contextlib.ExitStack() as gst:
        mod = {"modT": K.sb(gst, [128, 48, 2], F32, "modT"), "G1": K.sb(gst, [128, 8, 2], F32, "G1"),
               "G2": K.sb(gst, [128, 8, 2], F32, "G2")}
        phase_input_transpose(K, I["x_b"], I["ctx_b"], xT, I["ident"].t)
        for layer in layers:
            last = (layer == DEPTH - 1)
            phase_adaln(K, layer, I["c_b"], I["c_ctx"], I["ada_w"], I["ada_b"], I["norm1_g"], I["norm2_g"], mod)
            if layer % 2 == 0:
                w_in, F = I["ev_w_in"].t[layer // 2], EV_IN
            else:
                w_in, F = I["od_w_in"].t[layer // 2], OD_IN
            phase_norm_inproj(K, xT, w_in, F, projT, mod, I["ones"].t, hdbg=hdbg if (debug and layer == layers[0]) else None)
            zsrc = zT
            if zin is not None:
                zsrc = zin
            elif layer % 2 == 0:
                even_mixer(K, layer // 2, I, S, projT, zT)
            else:
                odd_mixer(K, layer // 2, I, S, projT, zT)
            phase_out_w1(K, xT, zsrc, aTd, I["mix_w_out"].t[layer], I["mlp_w1"].t[layer], mod, I["ones"].t, skip_ctx=last,
                         xmid_dbg=xmid if (debug and layer == layers[0]) else None)
            phase_w2(K, xT, aTd, I["mlp_w2"].t[layer], mod, skip_ctx=last)
        fin = phase_final(K, xT, I["final_g"], I["ones"].t, I["ident"].t, out)
    K.P.emit(final_wait_ops=fin)
    return nc, K


def mixer_inputs(K, I):
    I["lru_cw"] = K.din("lru_cw", [2, 128, 4, 4])
    I["lru_cb"] = K.din("lru_cb", [2, 128, 4])
    I["lru_ba"] = K.din("lru_ba", [2, 128, 2, 4])
    I["lru_bx"] = K.din("lru_bx", [2, 128, 2, 4])
    I["lru_lam"] = K.din("lru_lam", [2, 128, 2, 4])
    I["lru_wa_bd"] = K.din("lru_wa_bd", [2, 2, 4, 128, 128])
    I["lru_wx_bd"] = K.din("lru_wx_bd", [2, 2, 4, 128, 128])
    I["ret_lg"] = K.din("ret_lg", [2, 128, 8])
    I["pos_cols"] = K.din("pos_cols", [128, 2])
    I["tri_f"] = K.din("tri_f", [128, 128])
    I["tri_b"] = K.din("tri_b", [128, 128])
    I["rope_cos"] = K.din("rope_cos", [128, NLAT])
    I["rope_sin"] = K.din("rope_sin", [128, NLAT])
    I["hg_logits"] = K.din("hg_logits", [128, 2, 2, 4])
    I["gdn_cw"] = K.din("gdn_cw", [2, 128, 12, 4])
    I["gdn_ab"] = K.din("gdn_ab", [2, 16, 2])
    I["gdn_sel"] = K.din("gdn_sel", [16, 16, 128])
    I["gdn_masks"] = K.din("gdn_masks", [4, 128, 128])


def mixer_scratch(K):
    S = {}
    S["O_f"] = K.dscr("O_f", [T, 512], F32, dbg=True)
    S["O_b"] = K.dscr("O_b", [T, 512], F32, dbg=True)
    S["GATES"] = K.dscr("GATES", [3, 16, T], F32, dbg=True)
    return S


def all_regions(tt, rows):
    return [tt.sub((r, gi)) for r in rows for gi in range(len(NP_GROUPS))]


def z_regions(zT):
    return [zT.sub(gi) for gi in range(len(NP_GROUPS))]


SEGS = ((0, NCTX), (NCTX, T))
TT512 = [(t0, min(512, T - t0)) for t0 in range(0, T, 512)]


def lru_phase(K, e, I, projT, zT):
    with K.phase() as st:
        cw = K.sb(st, [128, 4, 4], F32, "cw")
        cb = K.sb(st, [128, 4], F32, "cb")
        ba = K.sb(st, [128, 2, 4], F32, "ba")
        bx = K.sb(st, [128, 2, 4], F32, "bx")
        lam = K.sb(st, [128, 2, 4], F32, "lam")
        cl = K.sb(st, [128, 2, 4], F32, "cl")
        one = K.sb(st, [128, 1], F32, "one")
        K.load(cw[:], I["lru_cw"].t[e], writes=[cw])
        K.load(cb[:], I["lru_cb"].t[e], writes=[cb])
        K.load(ba[:], I["lru_ba"].t[e], writes=[ba])
        K.load(bx[:], I["lru_bx"].t[e], writes=[bx])
        K.load(lam[:], I["lru_lam"].t[e], writes=[lam])
        K.V(lambda en: en.memset(one[:], 1.0), [], [one])
        nba = K.sb(st, [128, 2, 4], F32, "nba")
        nbx = K.sb(st, [128, 2, 4], F32, "nbx")
        K.V(lambda en: en.tensor_scalar(out=nba[:], in0=ba[:], scalar1=-1.0, scalar2=None, op0=ALU.mult), [ba], [nba])
        K.V(lambda en: en.tensor_scalar(out=nbx[:], in0=bx[:], scalar1=-1.0, scalar2=None, op0=ALU.mult), [bx], [nbx])
        K.A(lambda en: en.activation(out=cl[:], in_=lam[:], func=AF.Exp, scale=-1.0), [lam], [cl])
        K.V(lambda en: en.tensor_scalar(out=cl[:], in0=cl[:], scalar1=1.0, scalar2=None, op0=ALU.add), [cl], [cl])
        K.A(lambda en: en.activation(out=cl[:], in_=cl[:], func=AF.Ln), [cl], [cl])
        K.V(lambda en: en.tensor_scalar(out=cl[:], in0=cl[:], scalar1=-8.0, scalar2=None, op0=ALU.mult), [cl], [cl])
        wst = K.sb(st, [128, 128], F32, "bdst")
        BD = {}
        for d in range(2):
            for ct in range(4):
                for nm, key in (("a", "lru_wa_bd"), ("x", "lru_wx_bd")):
                    w = K.sb(st, [128, 128], BF16, "bd")
                    K.load(wst[:], I[key].t[e, d, ct], writes=[wst])
                    K.V(lambda en, w=w: en.tensor_copy(out=w[:], in_=wst[:]), [wst], [w])
                    BD[(nm, d, ct)] = w
        B1 = K.sb(st, [128, T], F32, "B1")
        B2 = K.sb(st, [128, T], F32, "B2")
        B3 = K.sb(st, [128, T], F32, "B3")
        B4 = K.sb(st, [128, T], F32, "B4")
        B5 = K.sb(st, [128, T], F32, "B5")
        B6 = K.sb(st, [128, T], F32, "B6")
        ub = K.sb(st, [128, T], BF16, "ub")
        rt = [K.sb(st, [128, 512], F32, "rt") for _ in range(2)]
        it = [K.sb(st, [128, 512], F32, "it") for _ in range(2)]
        mt = [K.sb(st, [128, 512], F32, "mt") for _ in range(2)]
        psa = [K.ps(st, [128, 512], F32, "psa") for _ in range(2)]
        psx = [K.ps(st, [128, 512], F32, "psx") for _ in range(2)]
        for ct in range(4):
            x, u, Aa, INP, H0, H1 = B1, B2, B3, B4, B5, B6
            K.load(x[:], projT.t[ct * 128:(ct + 1) * 128, :], reads=all_regions(projT, [ct]), writes=[x])
            K.V(lambda en, ct=ct: en.tensor_scalar(out=u[:], in0=x[:], scalar1=cw[:, ct, 2:3], scalar2=cb[:, ct:ct + 1], op0=ALU.mult, op1=ALU.add),
                [x, cw, cb], [u])
            for (s0, s1) in SEGS:
                for (j, off) in ((0, -2), (1, -1), (3, 1)):
                    if off < 0:
                        oa, ob, ia, ib = s0 - off, s1, s0, s1 + off
                    else:
                        oa, ob, ia, ib = s0, s1 - off, s0 + off, s1
                    K.V(lambda en, ct=ct, j=j, oa=oa, ob=ob, ia=ia, ib=ib: en.scalar_tensor_tensor(
                        out=u[:, oa:ob], in0=x[:, ia:ib], scalar=cw[:, ct, j:j + 1], in1=u[:, oa:ob], op0=ALU.mult, op1=ALU.add), [x, u, cw], [u])
            K.A(lambda en: en.activation(out=ub[:], in_=u[:], func=AF.Copy), [u], [ub])
            for d in range(2):
                def tile(ti, t0, n, d=d, ct=ct):
                    pa, px = psa[ti % 2], psx[ti % 2]
                    r, ii, m = rt[ti % 2], it[ti % 2], mt[ti % 2]
                    K.PE(lambda en, pa=pa, d=d, ct=ct, t0=t0, n=n: en.matmul(pa[:, 0:n], lhsT=BD[("a", d, ct)][:], rhs=ub[:, t0:t0 + n], start=True, stop=True),
                         [BD[("a", d, ct)], ub], [pa])
                    K.PE(lambda en, px=px, d=d, ct=ct, t0=t0, n=n: en.matmul(px[:, 0:n], lhsT=BD[("x", d, ct)][:], rhs=ub[:, t0:t0 + n], start=True, stop=True),
                         [BD[("x", d, ct)], ub], [px])
                    K.A(lambda en, pa=pa, r=r, d=d, ct=ct, n=n: en.activation(out=r[:, 0:n], in_=pa[:, 0:n], func=AF.Exp, bias=nba[:, d, ct:ct + 1], scale=-1.0),
                        [pa, nba], [r])
                    K.A(lambda en, r=r, n=n: en.activation(out=r[:, 0:n], in_=r[:, 0:n], func=AF.Ln, bias=one[:, 0:1], scale=1.0), [r, one], [r])
                    K.A(lambda en, r=r, n=n: en.activation(out=r[:, 0:n], in_=r[:, 0:n], func=AF.Exp, scale=-1.0), [r], [r])
                    K.A(lambda en, r=r, d=d, ct=ct, t0=t0, n=n: en.activation(out=Aa[:, t0:t0 + n], in_=r[:, 0:n], func=AF.Exp, scale=cl[:, d, ct:ct + 1]),
                        [r, cl], [Aa])
                    K.A(lambda en, px=px, ii=ii, d=d, ct=ct, n=n: en.activation(out=ii[:, 0:n], in_=px[:, 0:n], func=AF.Exp, bias=nbx[:, d, ct:ct + 1], scale=-1.0),
                        [px, nbx], [ii])
                    K.A(lambda en, ii=ii, n=n: en.activation(out=ii[:, 0:n], in_=ii[:, 0:n], func=AF.Ln, bias=one[:, 0:1], scale=1.0), [ii, one], [ii])
                    K.V(lambda en, m=m, t0=t0, n=n: en.scalar_tensor_tensor(out=m[:, 0:n], in0=Aa[:, t0:t0 + n], scalar=-1.0, in1=Aa[:, t0:t0 + n],
                                                                            op0=ALU.mult, op1=ALU.mult), [Aa], [m])
                    K.A(lambda en, m=m, n=n: en.activation(out=m[:, 0:n], in_=m[:, 0:n], func=AF.Ln, bias=one[:, 0:1], scale=1.0), [m, one], [m])
                    K.V(lambda en, m=m, ii=ii, n=n: en.scalar_tensor_tensor(out=m[:, 0:n], in0=m[:, 0:n], scalar=0.5, in1=ii[:, 0:n],
                                                                            op0=ALU.mult, op1=ALU.subtract), [m, ii], [m])
                    K.A(lambda en, m=m, n=n: en.activation(out=m[:, 0:n], in_=m[:, 0:n], func=AF.Exp), [m], [m])
                    K.V(lambda en, m=m, t0=t0, n=n: en.tensor_tensor(out=INP[:, t0:t0 + n], in0=m[:, 0:n], in1=u[:, t0:t0 + n], op=ALU.mult), [m, u], [INP])
                for ti in range(0, len(TT512), 2):
                    K.emit_interleaved([K.capture(tile, tj, *TT512[tj]) for tj in range(ti, min(ti + 2, len(TT512)))])
                if d == 0:
                    K.V(lambda en: en.tensor_tensor_scan(out=H0[:], data0=Aa[:], data1=INP[:], initial=0.0, op0=ALU.mult, op1=ALU.add), [Aa, INP], [H0])
                else:
                    K.V(lambda en: en.tensor_tensor_scan(out=H1[:, 0:NCTX][:, ::-1], data0=Aa[:, 0:NCTX][:, ::-1], data1=INP[:, 0:NCTX][:, ::-1],
                                                         initial=0.0, op0=ALU.mult, op1=ALU.add), [Aa, INP], [H1])
                    K.V(lambda en: en.tensor_tensor_scan(out=H1[:, NCTX:T][:, ::-1], data0=Aa[:, NCTX:T][:, ::-1], data1=INP[:, NCTX:T][:, ::-1],
                                                         initial=H1[:, 0:1], op0=ALU.mult, op1=ALU.add), [Aa, INP, H1], [H1])
            K.G(lambda en: en.tensor_tensor(out=H0[:], in0=H0[:], in1=H1[:], op=ALU.add), [H0, H1], [H0])
            g, tq = B1, B3
            K.load(g[:], projT.t[512 + ct * 128:512 + (ct + 1) * 128, :], reads=all_regions(projT, [4 + ct]), writes=[g])
            K.A(lambda en: en.activation(out=tq[:], in_=g[:], func=AF.Square), [g], [tq])
            K.V(lambda en: en.tensor_scalar(out=tq[:], in0=tq[:], scalar1=0.044715, scalar2=1.0, op0=ALU.mult, op1=ALU.add), [tq], [tq])
            K.G(lambda en: en.tensor_tensor(out=tq[:], in0=tq[:], in1=g[:], op=ALU.mult), [tq, g], [tq])
            K.A(lambda en: en.activation(out=tq[:], in_=tq[:], func=AF.Sigmoid, scale=1.5957691216057308), [tq], [tq])
            K.V(lambda en: en.tensor_tensor(out=tq[:], in0=tq[:], in1=g[:], op=ALU.mult), [tq, g], [tq])
            K.V(lambda en: en.tensor_tensor(out=tq[:], in0=tq[:], in1=H0[:], op=ALU.mult), [tq, H0], [tq])
            K.store(zT.t[ct * 128:(ct + 1) * 128, :], tq[:], reads=[tq], writes=z_regions(zT))


def even_mixer(K, e, I, S, projT, zT):
    lru_phase(K, e, I, projT, zT)
    retention_phase(K, e, I, S, projT, zT)


def retention_phase(K, e, I, S, projT, zT):
    O = [S["O_f"], S["O_b"]]
    NCH = T // 128
    with K.phase() as st:
        idf = K.sb(st, [128, 128], F32, "idf")
        idb = K.sb(st, [128, 128], BF16, "idb")
        K.load(idf[:], I["ident"].t[:, :], writes=[idf])
        K.V(lambda en: en.tensor_copy(out=idb[:], in_=idf[:]), [idf], [idb])
        lg = K.sb(st, [128, 8], F32, "lg")
        pos = K.sb(st, [128, 2], F32, "pos")
        K.load(lg[:], I["ret_lg"].t[e], writes=[lg])
        K.load(pos[:], I["pos_cols"].t[:, :], writes=[pos])
        tri = [K.sb(st, [128, 128], F32, "tri") for _ in range(2)]
        K.load(tri[0][:], I["tri_f"].t[:, :], writes=[tri[0]])
        K.load(tri[1][:], I["tri_b"].t[:, :], writes=[tri[1]])
        t8 = K.sb(st, [128, 8], F32, "t8")
        qd8 = K.sb(st, [128, 8], F32, "qd8")
        gi8 = K.sb(st, [128, 8], F32, "gi8")
        cd8 = K.sb(st, [128, 8], F32, "cd8")
        for d in range(2):
            K.V(lambda en, d=d: en.tensor_scalar(out=t8[:, d * 4:(d + 1) * 4], in0=lg[:, d * 4:(d + 1) * 4], scalar1=pos[:, d:d + 1], scalar2=None, op0=ALU.mult),
                [lg, pos], [t8])
        K.A(lambda en: en.activation(out=qd8[:], in_=t8[:], func=AF.Exp), [t8], [qd8])
        K.A(lambda en: en.activation(out=gi8[:], in_=t8[:], func=AF.Exp, scale=-1.0), [t8], [gi8])
        K.A(lambda en: en.activation(out=cd8[:], in_=lg[:], func=AF.Exp, scale=128.0), [lg], [cd8])
        GINV, QDEC, CD = [], [], []
        for d in range(2):
            for (lst, src) in ((GINV, gi8), (QDEC, qd8), (CD, cd8)):
                x = K.sb(st, [128, 4, 128], F32, "mul")
                K.V(lambda en, x=x, src=src, d=d: en.tensor_copy(out=x[:], in_=src[:, d * 4:(d + 1) * 4].unsqueeze(2).broadcast_to([128, 4, 128])), [src], [x])
                lst.append(x)
        qk = [K.sb(st, [128, 2, T], BF16, "qb"), K.sb(st, [128, 2, T], BF16, "kb")]
        X = K.sb(st, [128, T], F32, "ropex")
        SW = K.sb(st, [128, NLAT], F32, "ropesw")
        COS = K.sb(st, [128, NLAT], F32, "cos")
        SIN = K.sb(st, [128, NLAT], F32, "sin")
        K.load(COS[:], I["rope_cos"].t[:, :], writes=[COS])
        K.load(SIN[:], I["rope_sin"].t[:, :], writes=[SIN])
        for which in range(2):
            for p in range(2):
                ft = 8 + which * 2 + p
                K.load(X[:], projT.t[ft * 128:(ft + 1) * 128, :], reads=all_regions(projT, [ft]), writes=[X])
                for bi, (dst, src) in enumerate(((0, 32), (32, 0), (64, 96), (96, 64))):
                    K.op("vector" if bi % 2 == 0 else "scalar", copy_any(SW[dst:dst + 32, :], X[src:src + 32, NCTX:T]), [X], [SW])
                K.V(lambda en: en.tensor_tensor(out=X[:, NCTX:T], in0=X[:, NCTX:T], in1=COS[:], op=ALU.mult), [X, COS], [X])
                K.G(lambda en: en.tensor_tensor(out=SW[:], in0=SW[:], in1=SIN[:], op=ALU.mult), [SW, SIN], [SW])
                K.V(lambda en: en.tensor_tensor(out=X[:, NCTX:T], in0=X[:, NCTX:T], in1=SW[:], op=ALU.add), [X, SW], [X])
                K.A(lambda en, which=which, p=p: en.activation(out=qk[which][:, p, :], in_=X[:], func=AF.Copy, scale=(0.125 if which else 1.0)),
                    [X], [qk[which]])
        qb, kb = qk
        qz = K.sb(st, [128, 4, T], BF16, "qz")
        K.G(lambda en: en.memset(qz[:], 0.0), [], [qz])
        for h in range(4):
            p, b = h // 2, (h % 2) * 64
            K.op("vector" if h % 2 == 0 else "scalar", copy_any(qz[b:b + 64, h, :], qb[b:b + 64, p, :]), [qb], [qz])
        vin = [K.sb(st, [128, 4, 128], F32, "vin") for _ in range(2)]
        VD = [K.sb(st, [128, 4, 128], BF16, "VD") for _ in range(2)]
        ktok = [K.sb(st, [128, 2, 128], BF16, "ktok") for _ in range(2)]
        PT = [K.sb(st, [128, 4, 128], BF16, "PT") for _ in range(2)]
        osb = [K.sb(st, [128, 4, 128], F32, "osb") for _ in range(2)]
        Sf = [K.sb(st, [128, 4, 128], F32, "Sf") for _ in range(2)]
        Sb = [K.sb(st, [128, 4, 128], BF16, "Sb") for _ in range(2)]
        tS = [K.sb(st, [128, 4, 128], F32, "tS") for _ in range(2)]
        ps_v = K.ps(st, [128, 4, 128], F32, "ps_v")
        ps_k = K.ps(st, [128, 2, 128], BF16, "ps_k")
        ps_st = [K.ps(st, [128, 4, 128], F32, "ps_st") for _ in range(2)]
        ps_o = [K.ps(st, [128, 4, 128], F32, "ps_o") for _ in range(2)]
        ps_s = [K.ps(st, [128, 4, 128], F32, "ps_s") for _ in range(2)]
        for d in range(2):
            K.V(lambda en, d=d: en.memset(Sf[d][:], 0.0), [], [Sf[d]])
            K.V(lambda en, d=d: en.memset(Sb[d][:], 0.0), [], [Sb[d]])
        order = [list(range(NCH)), [1, 0] + list(range(NCH - 1, 1, -1))]
        vrows = projT.t[1536:2048, :].rearrange("(h p) t -> p h t", p=128)
        for step in range(NCH):
            for d in range(2):
                n = order[d][step]
                c0 = n * 128
                reg = region(c0)
                K.load(vin[d][:], vrows[:, :, c0:c0 + 128], reads=[projT.sub((12 + h, reg)) for h in range(4)], writes=[vin[d]])
                for h in range(4):
                    K.PE(lambda en, d=d, h=h: en.transpose(ps_v[:, h, :], vin[d][:, h, :], idf[:]), [vin[d], idf], [ps_v])
                K.V(lambda en, d=d: en.tensor_tensor(out=VD[d][:], in0=ps_v[:], in1=GINV[d][:], op=ALU.mult), [ps_v, GINV[d]], [VD[d]])
                for p in range(2):
                    K.PE(lambda en, p=p, c0=c0: en.transpose(ps_k[:, p, :], kb[:, p, c0:c0 + 128], idb[:]), [kb, idb], [ps_k])
                K.A(lambda en, d=d: en.activation(out=ktok[d][:], in_=ps_k[:], func=AF.Copy), [ps_k], [ktok[d]])
                for h in range(4):
                    p, b = h // 2, (h % 2) * 64
                    K.PE(lambda en, d=d, h=h, p=p, b=b, c0=c0: en.matmul(ps_st[d][:, h, :], lhsT=kb[:, p, c0:c0 + 128], rhs=qz[:, h, c0:c0 + 128],
                                                                     start=True, stop=True), [kb, qz], [ps_st[d]])
                K.V(lambda en, d=d: en.tensor_tensor(out=PT[d][:], in0=ps_st[d][:], in1=tri[d][:].unsqueeze(1).broadcast_to([128, 4, 128]), op=ALU.mult),
                    [ps_st[d], tri[d]], [PT[d]])
                for h in range(4):
                    p, b = h // 2, (h % 2) * 64
                    K.PE(lambda en, d=d, h=h: en.matmul(ps_o[d][:, h, :], lhsT=PT[d][:, h, :], rhs=VD[d][:, h, :], start=True, stop=False),
                         [PT[d], VD[d]], [ps_o[d]])
                    K.PE(lambda en, d=d, h=h, p=p, b=b, c0=c0: en.matmul(ps_o[d][:, h, :], lhsT=qz[:, h, c0:c0 + 128], rhs=Sb[d][:, h, :],
                                                                     start=False, stop=True), [qz, Sb[d]], [ps_o[d]])
                K.V(lambda en, d=d: en.tensor_tensor(out=osb[d][:], in0=ps_o[d][:], in1=QDEC[d][:], op=ALU.mult), [ps_o[d], QDEC[d]], [osb[d]])
                K.store(O[d].t[c0:c0 + 128, :], osb[d][:].rearrange("p h v -> p (h v)"), reads=[osb[d]], writes=[O[d].sub(n)])
                for h in range(4):
                    p = h // 2
                    K.PE(lambda en, d=d, h=h, p=p: en.matmul(ps_s[d][:, h, :], lhsT=ktok[d][:, p, :], rhs=VD[d][:, h, :], start=True, stop=True),
                         [ktok[d], VD[d]], [ps_s[d]])
                K.V(lambda en, d=d: en.tensor_tensor(out=tS[d][:], in0=ps_s[d][:], in1=Sf[d][:], op=ALU.add), [ps_s[d], Sf[d]], [tS[d]])
                K.G(lambda en, d=d: en.tensor_tensor(out=Sf[d][:], in0=tS[d][:], in1=CD[d][:], op=ALU.mult), [tS[d], CD[d]], [Sf[d]])
                K.A(lambda en, d=d: en.activation(out=Sb[d][:], in_=Sf[d][:], func=AF.Copy), [Sf[d]], [Sb[d]])
    head_norm_epilogue(K, I, O, projT, zT, gate_ft0=16, out_row0=512, center=True)


def head_norm_epilogue(K, I, O, projT, zT, gate_ft0, out_row0, center):
    NCH = T // 128
    with K.phase() as st:
        idf = K.sb(st, [128, 128], F32, "idf")
        K.load(idf[:], I["ident"].t[:, :], writes=[idf])
        zrows = projT.t[gate_ft0 * 128:(gate_ft0 + 4) * 128, :].rearrange("(h p) t -> p h t", p=128)
        zout = zT.t[out_row0:out_row0 + 512, :].rearrange("(h p) t -> p h t", p=128)
        ofs = [K.sb(st, [128, 4, 128], F32, "of") for _ in range(2)]
        obs = [K.sb(st, [128, 4, 128], F32, "ob") for _ in range(2)]
        zgs = [K.sb(st, [128, 4, 128], F32, "zg") for _ in range(2)]
        ocs = [K.sb(st, [128, 4, 128], F32, "oc") for _ in range(2)]
        sqs = [K.sb(st, [128, 4, 128], F32, "sq") for _ in range(2)]
        st4s = [K.sb(st, [128, 4], F32, "st4") for _ in range(2)]
        rs4s = [K.sb(st, [128, 4], F32, "rs4") for _ in range(2)]
        yo = [K.sb(st, [128, 4, 128], F32, "yo") for _ in range(2)]
        ps_t = [K.ps(st, [128, 4, 128], F32, "ps_t") for _ in range(2)]

        def chunk(n):
            c0 = n * 128
            reg = region(c0)
            of, ob, zg, y, pt = ofs[n % 2], obs[n % 2], zgs[n % 2], yo[n % 2], ps_t[n % 2]
            oc, sq, st4, rs4 = ocs[n % 2], sqs[n % 2], st4s[n % 2], rs4s[n % 2]
            K.load(of[:].rearrange("p h v -> p (h v)"), O[0].t[c0:c0 + 128, :], reads=[O[0].sub(n)], writes=[of])
            K.load(ob[:].rearrange("p h v -> p (h v)"), O[1].t[c0:c0 + 128, :], reads=[O[1].sub(n)], writes=[ob])
            K.load(zg[:], zrows[:, :, c0:c0 + 128], reads=[projT.sub((gate_ft0 + h, reg)) for h in range(4)], writes=[zg])
            if center:
                K.V(lambda en: en.tensor_tensor(out=of[:], in0=of[:], in1=ob[:], op=ALU.add), [of, ob], [of])
                K.V(lambda en: en.tensor_reduce(out=st4[:], in_=of[:], axis=AX.X, op=ALU.add), [of], [st4])
                K.V(lambda en: en.tensor_scalar(out=st4[:], in0=st4[:], scalar1=-1.0 / 128, scalar2=None, op0=ALU.mult), [st4], [st4])
                K.V(lambda en: en.tensor_tensor(out=oc[:], in0=of[:], in1=st4[:].unsqueeze(2).broadcast_to([128, 4, 128]), op=ALU.add), [of, st4], [oc])
            else:
                K.V(lambda en: en.tensor_tensor(out=oc[:], in0=of[:], in1=ob[:], op=ALU.add), [of, ob], [oc])
            K.G(lambda en: en.tensor_tensor(out=sq[:], in0=oc[:], in1=oc[:], op=ALU.mult), [oc], [sq])
            K.V(lambda en: en.tensor_reduce(out=rs4[:], in_=sq[:], axis=AX.X, op=ALU.add), [sq], [rs4])
            K.V(lambda en: en.tensor_scalar(out=rs4[:], in0=rs4[:], scalar1=1.0 / 128, scalar2=EPS, op0=ALU.mult, op1=ALU.add), [rs4], [rs4])
            K.A(lambda en: en.activation(out=rs4[:], in_=rs4[:], func=AF.Sqrt), [rs4], [rs4])
            K.V(lambda en: en.reciprocal(out=rs4[:], in_=rs4[:]), [rs4], [rs4])
            K.V(lambda en: en.tensor_tensor(out=oc[:], in0=oc[:], in1=rs4[:].unsqueeze(2).broadcast_to([128, 4, 128]), op=ALU.mult), [oc, rs4], [oc])
            for h in range(4):
                K.PE(lambda en, h=h: en.transpose(pt[:, h, :], oc[:, h, :], idf[:]), [oc, idf], [pt])
            K.A(lambda en: en.activation(out=zg[:], in_=zg[:], func=AF.Silu), [zg], [zg])
            K.V(lambda en: en.tensor_tensor(out=y[:], in0=pt[:], in1=zg[:], op=ALU.mult), [pt, zg], [y])
            K.store(zout[:, :, c0:c0 + 128], y[:], reads=[y], writes=[zT.sub(reg)])

        for n in range(0, NCH, 2):
            K.emit_interleaved([K.capture(chunk, n), K.capture(chunk, n + 1)])


GC = 32
NGC = T // GC


def gla_phase(K, o, I, S, projT, zT):
    O = [S["O_f"], S["O_b"]]
    with K.phase() as st:
        idf = K.sb(st, [128, 128], F32, "idf")
        idb = K.sb(st, [128, 128], BF16, "idb")
        K.load(idf[:], I["ident"].t[:, :], writes=[idf])
        K.V(lambda en: en.tensor_copy(out=idb[:], in_=idf[:]), [idf], [idb])
        tri = [K.sb(st, [128, 128], F32, "tri") for _ in range(2)]
        K.load(tri[0][:], I["tri_f"].t[:, :], writes=[tri[0]])
        K.load(tri[1][:], I["tri_b"].t[:, :], writes=[tri[1]])
        lgt = K.sb(st, [128, 2, 2, 4], F32, "lgt")
        lb = K.sb(st, [128, 2, 4], F32, "lb")
        oml = K.sb(st, [128, 2, 4], F32, "oml")
        K.load(lgt[:], I["hg_logits"].t[:, :, :, :], writes=[lgt])
        if o == 0:
            K.V(lambda en: en.memset(lb[:], 0.0), [], [lb])
        else:
            K.V(lambda en: en.tensor_tensor(out=lb[:], in0=lgt[:, :, 1, :], in1=lgt[:, :, 0, :], op=ALU.subtract), [lgt], [lb])
            K.A(lambda en: en.activation(out=lb[:], in_=lb[:], func=AF.Sigmoid), [lb], [lb])
        K.V(lambda en: en.tensor_scalar(out=oml[:], in0=lb[:], scalar1=-1.0, scalar2=1.0, op0=ALU.mult, op1=ALU.add), [lb], [oml])
        MASKX = K.sb(st, [128, T + GC], F32, "mask")
        K.G(lambda en: en.memset(MASKX[:], 1.0), [], [MASKX])
        K.G(lambda en: en.memset(MASKX[:, 0::GC], 0.0), [], [MASKX])
        Q = K.sb(st, [128, T], F32, "Q")
        F1 = K.sb(st, [128, T], F32, "F1")
        F2 = K.sb(st, [128, T], F32, "F2")
        F3 = K.sb(st, [128, T], F32, "F3")
        QTs = [[K.sb(st, [128, T], BF16, "QT") for _ in range(2)] for _ in range(2)]
        KTs = [[K.sb(st, [128, T], BF16, "KT") for _ in range(2)] for _ in range(2)]
        Vbs = [K.sb(st, [128, T], BF16, "Vb") for _ in range(2)]
        GLs = [[K.sb(st, [128, NGC], F32, "GL") for _ in range(2)] for _ in range(2)]
        TR = [[K.sb(st, [GC, 2, 128], BF16, "TR") for _ in range(2)] for _ in range(2)]
        PT = [[K.sb(st, [GC, GC], BF16, "PT") for _ in range(2)] for _ in range(2)]
        OS = [[K.sb(st, [GC, 8, 128], F32, "OS") for _ in range(2)] for _ in range(2)]
        Sf = [K.sb(st, [128, 128], F32, "Sf") for _ in range(2)]
        Sb = [K.sb(st, [128, 128], BF16, "Sb") for _ in range(2)]
        ps_tr = [K.ps(st, [GC, 2, 128], BF16, "ps_tr") for _ in range(2)]
        ps_st = [K.ps(st, [GC, GC], F32, "ps_st") for _ in range(2)]
        ps_o = [K.ps(st, [GC, 128], F32, "ps_o") for _ in range(2)]
        ps_s = [K.ps(st, [128, 128], F32, "ps_s") for _ in range(2)]
        nctx_c = NCTX // GC
        order = [list(range(NGC)), list(range(nctx_c - 1, -1, -1)) + list(range(NGC - 1, nctx_c - 1, -1))]

        def prep_head(h):
            QT, KT, Vb, GL = QTs[h % 2], KTs[h % 2], Vbs[h % 2], GLs[h % 2]
            K.load(Q[:], projT.t[h * 128:(h + 1) * 128, :], reads=all_regions(projT, [h]), writes=[Q])
            K.A(lambda en: en.activation(out=Q[:], in_=Q[:], func=AF.Silu), [Q], [Q])
            for d in range(2):
                ft = 4 + 4 * d + h
                K.load(F1[:], projT.t[ft * 128:(ft + 1) * 128, :], reads=all_regions(projT, [ft]), writes=[F1])
                K.A(lambda en: en.activation(out=F1[:], in_=F1[:], func=AF.Sigmoid), [F1], [F1])
                K.V(lambda en, d=d: en.tensor_scalar(out=F1[:], in0=F1[:], scalar1=oml[:, d, h:h + 1], scalar2=lb[:, d, h:h + 1], op0=ALU.mult, op1=ALU.add),
                    [F1, oml, lb], [F1])
                K.A(lambda en: en.activation(out=F2[:], in_=F1[:], func=AF.Ln), [F1], [F2])
                if d == 0:
                    K.V(lambda en: en.tensor_tensor_scan(out=F3[:], data0=MASKX[:, 0:T], data1=F2[:], initial=0.0, op0=ALU.mult, op1=ALU.add), [MASKX, F2], [F3])
                else:
                    K.V(lambda en: en.tensor_tensor_scan(out=F3[:, ::-1], data0=MASKX[:, 1:T + 1][:, ::-1], data1=F2[:, ::-1], initial=0.0,
                                                         op0=ALU.mult, op1=ALU.add), [MASKX, F2], [F3])
                K.A(lambda en: en.activation(out=F2[:], in_=F3[:], func=AF.Exp), [F3], [F2])
                K.V(lambda en, d=d: en.tensor_tensor(out=QT[d][:], in0=Q[:], in1=F2[:], op=ALU.mult), [Q, F2], [QT[d]])
                K.G(lambda en, d=d: en.tensor_copy(out=GL[d][:], in_=F2[:, (GC - 1 if d == 0 else 0)::GC]), [F2], [GL[d]])
                K.A(lambda en: en.activation(out=F2[:], in_=F3[:], func=AF.Exp, scale=-1.0), [F3], [F2])
                K.G(lambda en: en.tensor_scalar(out=F1[:], in0=F1[:], scalar1=-1.0, scalar2=1.0, op0=ALU.mult, op1=ALU.add), [F1], [F1])
                K.G(lambda en, d=d: en.tensor_tensor(out=KT[d][:], in0=F1[:], in1=F2[:], op=ALU.mult), [F1, F2], [KT[d]])
            ft = 12 + h
            K.load(F3[:], projT.t[ft * 128:(ft + 1) * 128, :], reads=all_regions(projT, [ft]), writes=[F3])
            K.G(lambda en: en.tensor_copy(out=Vb[:], in_=F3[:]), [F3], [Vb])

        def intra(d, step, h):
            QT, KT, Vb = QTs[h % 2], KTs[h % 2], Vbs[h % 2]
            n = order[d][step]
            c0 = n * GC
            tr, pt = TR[d][step % 2], PT[d][step % 2]
            K.PE(lambda en: en.transpose(ps_tr[d][:, 0, :], Vb[:, c0:c0 + GC], idb[:]), [Vb, idb], [ps_tr[d]])
            K.PE(lambda en: en.transpose(ps_tr[d][:, 1, :], KT[d][:, c0:c0 + GC], idb[:]), [KT[d], idb], [ps_tr[d]])
            K.A(lambda en: en.activation(out=tr[:], in_=ps_tr[d][:], func=AF.Copy), [ps_tr[d]], [tr])
            K.PE(lambda en: en.matmul(ps_st[d][:], lhsT=KT[d][:, c0:c0 + GC], rhs=QT[d][:, c0:c0 + GC], start=True, stop=True),
                 [KT[d], QT[d]], [ps_st[d]])
            K.V(lambda en: en.tensor_tensor(out=pt[:], in0=ps_st[d][:], in1=tri[d][0:GC, 0:GC], op=ALU.mult), [ps_st[d], tri[d]], [pt])

        def inter(d, step, h):
            QT, GL = QTs[h % 2], GLs[h % 2]
            n = order[d][step]
            c0 = n * GC
            tr, pt = TR[d][step % 2], PT[d][step % 2]
            os_ = OS[d][(step // 8) % 2]
            K.PE(lambda en: en.matmul(ps_o[d][:], lhsT=pt[:], rhs=tr[:, 0, :], start=True, stop=False), [pt, tr], [ps_o[d]])
            K.PE(lambda en: en.matmul(ps_o[d][:], lhsT=QT[d][:, c0:c0 + GC], rhs=Sb[d][:], start=False, stop=True), [QT[d], Sb[d]], [ps_o[d]])
            K.PE(lambda en: en.matmul(ps_s[d][:], lhsT=tr[:, 1, :], rhs=tr[:, 0, :], start=True, stop=True), [tr], [ps_s[d]])
            K.V(lambda en: en.tensor_copy(out=os_[:, n % 8, :], in_=ps_o[d][:]), [ps_o[d]], [os_])
            if step == 0:
                K.V(lambda en: en.tensor_copy(out=Sf[d][:], in_=ps_s[d][:]), [ps_s[d]], [Sf[d]])
            else:
                np_ = order[d][step - 1]
                K.V(lambda en: en.scalar_tensor_tensor(out=Sf[d][:], in0=Sf[d][:], scalar=GL[d][:, np_:np_ + 1], in1=ps_s[d][:],
                                                       op0=ALU.mult, op1=ALU.add), [Sf[d], GL[d], ps_s[d]], [Sf[d]])
            K.A(lambda en: en.activation(out=Sb[d][:], in_=Sf[d][:], func=AF.Copy, scale=GL[d][:, n:n + 1]), [Sf[d], GL[d]], [Sb[d]])
            if step % 8 == 7:
                n0 = (n // 8) * 8
                K.store(O[d].t[n0 * GC:(n0 + 8) * GC, h * 128:(h + 1) * 128].rearrange("(c p) v -> p c v", p=GC), os_[:],
                        reads=[os_], writes=[O[d].sub(n0 * GC // 128), O[d].sub(n0 * GC // 128 + 1)])

        prep_head(0)
        for h in range(4):
            nxt = K.capture(prep_head, h + 1) if h + 1 < 4 else []
            stride = (len(nxt) + NGC - 2) // (NGC - 1) if nxt else 0
            for d in range(2):
                K.V(lambda en, d=d: en.memset(Sb[d][:], 0.0), [], [Sb[d]])
            K.emit_interleaved([K.capture(intra, d, 0, h) for d in range(2)])
            for step in range(NGC):
                lists = []
                if step + 1 < NGC:
                    lists += [K.capture(intra, d, step + 1, h) for d in range(2)]
                lists += [K.capture(inter, d, step, h) for d in range(2)]
                if nxt:
                    lists.append(nxt[step * stride:(step + 1) * stride])
                K.emit_interleaved(lists)
            if nxt and NGC * stride < len(nxt):
                K.emit_interleaved([nxt[NGC * stride:]])
    head_norm_epilogue(K, I, O, projT, zT, gate_ft0=16, out_row0=0, center=False)


def odd_mixer(K, o, I, S, projT, zT):
    if not os.environ.get("GDN_DBG"):
        gla_phase(K, o, I, S, projT, zT)
    gdn_phase(K, o, I, S, projT, zT)


def _quarter(bank, q):
    v = bank.t
    if len(v.shape) == 3:
        return v[:, q, :]
    return v[:, q * 128:(q + 1) * 128]


def gdn_phase(K, o, I, S, projT, zT):
    O = [S["O_f"], S["O_b"]]
    GATES = S["GATES"]
    NSC = T // 128
    with K.phase() as st:
        G16 = K.sb(st, [16, T], F32, "G16")
        BETA = K.sb(st, [16, T], F32, "BETA")
        GCf = K.sb(st, [16, T], F32, "GCf")
        GCb = K.sb(st, [16, T], F32, "GCb")
        ab = K.sb(st, [16, 2], F32, "ab")
        nea = K.sb(st, [16, 1], F32, "nea")
        one64 = K.sb(st, [16, 64], F32, "one64")
        K.load(G16[:], projT.t[4608:4624, :], reads=all_regions(projT, [36]), writes=[G16])
        K.load(ab[:], I["gdn_ab"].t[o], writes=[ab])
        K.V(lambda en: en.memset(one64[:], 1.0), [], [one64])
        K.A(lambda en: en.activation(out=nea[:], in_=ab[:, 0:1], func=AF.Exp), [ab], [nea])
        K.V(lambda en: en.tensor_scalar(out=nea[:], in0=nea[:], scalar1=-1.0, scalar2=None, op0=ALU.mult), [nea], [nea])
        K.A(lambda en: en.activation(out=BETA[:], in_=G16[:], func=AF.Sigmoid), [G16], [BETA])
        K.A(lambda en: en.activation(out=G16[:], in_=G16[:], func=AF.Exp, bias=ab[:, 1:2], scale=1.0), [G16, ab], [G16])
        K.V(lambda en: en.tensor_scalar(out=G16[:], in0=G16[:], scalar1=1.0, scalar2=None, op0=ALU.add), [G16], [G16])
        K.A(lambda en: en.activation(out=G16[:], in_=G16[:], func=AF.Ln), [G16], [G16])
        K.V(lambda en: en.tensor_scalar(out=G16[:], in0=G16[:], scalar1=nea[:, 0:1], scalar2=None, op0=ALU.mult), [G16, nea], [G16])
        for c in range(T // 64):
            K.V(lambda en, c=c: en.tensor_tensor_scan(out=GCf[:, c * 64:(c + 1) * 64], data0=one64[:], data1=G16[:, c * 64:(c + 1) * 64],
                                                      initial=0.0, op0=ALU.mult, op1=ALU.add), [G16, one64], [GCf])
            K.V(lambda en, c=c: en.tensor_tensor_scan(out=GCb[:, c * 64:(c + 1) * 64][:, ::-1], data0=one64[:], data1=G16[:, c * 64:(c + 1) * 64][:, ::-1],
                                                      initial=0.0, op0=ALU.mult, op1=ALU.add), [G16, one64], [GCb])
        K.store(GATES.t[0], BETA[:], reads=[BETA], writes=[GATES])
        K.store(GATES.t[1], GCf[:], reads=[GCf], writes=[GATES])
        K.store(GATES.t[2], GCb[:], reads=[GCb], writes=[GATES])
    DBG = int(os.environ.get("GDN_DBG", "99"))
    if DBG == 0:
        return
    with K.phase() as st:
        idf = K.sb(st, [128, 128], F32, "idf")
        idb = K.sb(st, [128, 128], BF16, "idb")
        ones = K.sb(st, [128, 128], F32, "ones")
        K.load(idf[:], I["ident"].t[:, :], writes=[idf])
        K.load(ones[:], I["ones"].t[:, :], writes=[ones])
        K.V(lambda en: en.tensor_copy(out=idb[:], in_=idf[:]), [idf], [idb])
        SEL = K.sb(st, [16, 16, 128], F32, "SEL")
        K.load(SEL[:], I["gdn_sel"].t[:, :, :], writes=[SEL])
        MSK = [K.sb(st, [128, 128], F32, "msk") for _ in range(4)]
        for i in range(4):
            K.load(MSK[i][:], I["gdn_masks"].t[i], writes=[MSK[i]])
        cw = K.sb(st, [128, 12, 4], F32, "gcw")
        K.load(cw[:], I["gdn_cw"].t[o], writes=[cw])
        X = K.sb(st, [128, T], F32, "gX")
        U = K.sb(st, [128, T], F32, "gU")
        SQ = K.sb(st, [128, T], F32, "gSQ")
        NRMs = [[K.sb(st, [128, T], BF16, "gN") for _ in range(3)] for _ in range(2)]
        rs = [K.sb(st, [128, 512], F32, "grs") for _ in range(2)]
        psn = [K.ps(st, [128, 512], F32, "gpsn") for _ in range(2)]
        banks = [K.ps(st, [128, 4, 128], F32, "gbank") for _ in range(5)]
        bfbank = K.ps(st, [128, 4, 128], BF16, "gbfbank")
        qt = [K.psq(banks[i // 4], banks[i // 4].t[:, i % 4, :]) for i in range(8)]
        rb = [[banks[2], banks[3]], [banks[4], psn[0]]]
        rot = [[K.psq(rb[d][i % 2], _quarter(rb[d][i % 2], (i // 2) % 4)) for i in range(8)] for d in range(2)]
        qbf = [K.psq(bfbank, bfbank.t[:, i, :]) for i in range(4)]
        pp = {"i": [0, 0], "d": 0}

        def PQ():
            d_ = pp["d"]
            pp["i"][d_] += 1
            return rot[d_][pp["i"][d_] % 8]

        def mk(shape, dt, name):
            return [[K.sb(st, shape, dt, name) for _ in range(2)] for _ in range(2)]

        gate_in = mk([16, 2, 128], F32, "gin")
        cols = mk([128, 2, 16], F32, "gcols")
        DT_ = mk([128, 128], F32, "gDT")
        ETs = mk([128, 128], F32, "gETs")
        ETi = mk([128, 128], F32, "gETi")
        Nm = mk([128, 128], F32, "gN_")
        Mm = mk([128, 128], F32, "gM_")
        Pm = [mk([128, 128], F32, "gP") for _ in range(2)]
        PTm = [mk([128, 128], F32, "gPT") for _ in range(2)]
        Y = mk([128, 128], F32, "gY")
        Ybf = mk([128, 128], BF16, "gYbf")
        qkT = mk([128, 128], BF16, "gqkT")
        Vb_ = mk([128, 128], BF16, "gVb")
        Kbe = mk([128, 128], BF16, "gKbe")
        cvec = mk([128, 4], F32, "gcvec")
        Ut = mk([128, 128], F32, "gUt")
        nWT = mk([128, 128], BF16, "gnWT")
        kdA = mk([128, 128], BF16, "gkdA")
        kdB = mk([128, 128], BF16, "gkdB")
        egB = mk([128, 128], F32, "gegB")
        qdT = mk([128, 128], BF16, "gqdT")
        VNEW = [K.sb(st, [128, 128], BF16, "gVNEW") for _ in range(2)]
        OSB = mk([128, 128], F32, "gOSB")
        Sf = [K.sb(st, [128, 128], F32, "gSf") for _ in range(2)]
        Sb = [K.sb(st, [128, 128], BF16, "gSb") for _ in range(2)]
        for d in range(2):
            for par in range(2):
                K.G(lambda en, d=d, par=par: en.memset(kdA[d][par][:], 0.0), [], [kdA[d][par]])
                K.G(lambda en, d=d, par=par: en.memset(kdB[d][par][:], 0.0), [], [kdB[d][par]])
        order = [list(range(NSC)), [1, 0] + list(range(NSC - 1, 1, -1))]
        lastc = [(63, 127), (0, 64)]
        def prep(d, h, step):
            QN, KN, VN = NRMs[h % 2]
            pp["d"] = d
            par = step % 2
            sc = order[d][step]
            c0 = sc * 128
            rg, rb = 8 + d * 4 + h, d * 4 + h
            gin, cl_, dt_, ets, eti = gate_in[d][par], cols[d][par], DT_[d][par], ETs[d][par], ETi[d][par]
            N_, M_, y, ybf = Nm[d][par], Mm[d][par], Y[d][par], Ybf[d][par]
            cv = cvec[d][par]
            K.load(gin[:, 0, :], GATES.t[0, :, c0:c0 + 128], reads=[GATES], writes=[gin])
            K.load(gin[:, 1, :], GATES.t[1 + d, :, c0:c0 + 128], reads=[GATES], writes=[gin])
            p_gc, p_b, p_c, p_kk, p_qk = PQ(), PQ(), PQ(), PQ(), PQ()
            K.PE(lambda en: en.matmul(p_gc[:], lhsT=SEL[:, rg, :], rhs=gin[:, 1, :], start=True, stop=True), [SEL, gin], [p_gc])
            K.PE(lambda en: en.matmul(p_b[:], lhsT=SEL[:, rb, :], rhs=gin[:, 0, :], start=True, stop=True), [SEL, gin], [p_b])
            K.PE(lambda en: en.transpose(p_c[:, 0:16], gin[:, 0, :], idf[0:16, 0:16]), [gin, idf], [p_c])
            K.PE(lambda en: en.transpose(p_c[:, 16:32], gin[:, 1, :], idf[0:16, 0:16]), [gin, idf], [p_c])
            K.A(lambda en: en.activation(out=cl_[:].rearrange("p a b -> p (a b)"), in_=p_c[:, 0:32], func=AF.Copy), [p_c], [cl_])
            bcol, gcol = cl_[:, 0, rb:rb + 1], cl_[:, 1, rg:rg + 1]
            K.PE(lambda en: en.matmul(p_kk[:], lhsT=KN[:, c0:c0 + 128], rhs=KN[:, c0:c0 + 128], start=True, stop=True), [KN], [p_kk])
            K.PE(lambda en: en.matmul(p_qk[:], lhsT=KN[:, c0:c0 + 128], rhs=QN[:, c0:c0 + 128], start=True, stop=True), [KN, QN], [p_qk])
            p_kt, p_vt = qbf[(2 * d) % 4], qbf[(2 * d + 1) % 4]
            K.PE(lambda en: en.transpose(p_kt[:], KN[:, c0:c0 + 128], idb[:]), [KN, idb], [p_kt])
            K.PE(lambda en: en.transpose(p_vt[:], VN[:, c0:c0 + 128], idb[:]), [VN, idb], [p_vt])
            K.V(lambda en: en.tensor_scalar(out=dt_[:], in0=p_gc[:], scalar1=gcol, scalar2=0.0, op0=ALU.subtract, op1=ALU.min), [p_gc, cl_], [dt_])
            K.A(lambda en: en.activation(out=egB[d][par][:], in_=p_gc[:], func=AF.Exp), [p_gc], [egB[d][par]])
            for (r0, col) in ((0, lastc[d][0]), (64, lastc[d][1])):
                K.V(lambda en, r0=r0, col=col: en.tensor_tensor(out=cv[r0:r0 + 64, 2:3], in0=p_gc[r0:r0 + 64, col:col + 1], in1=cl_[r0:r0 + 64, 1, rg:rg + 1],
                                                               op=ALU.subtract), [p_gc, cl_], [cv])
            K.A(lambda en: en.activation(out=cv[:, 3:4], in_=cv[:, 2:3], func=AF.Exp), [cv], [cv])
            K.A(lambda en: en.activation(out=dt_[:], in_=dt_[:], func=AF.Exp), [dt_], [dt_])
            K.G(lambda en: en.tensor_tensor(out=ets[:], in0=dt_[:], in1=MSK[2 * d][:], op=ALU.mult), [dt_, MSK[2 * d]], [ets])
            K.G(lambda en: en.tensor_tensor(out=eti[:], in0=dt_[:], in1=MSK[2 * d + 1][:], op=ALU.mult), [dt_, MSK[2 * d + 1]], [eti])
            K.V(lambda en: en.tensor_tensor(out=ets[:], in0=ets[:], in1=p_b[:], op=ALU.mult), [ets, p_b], [ets])
            K.V(lambda en: en.tensor_tensor(out=N_[:], in0=ets[:], in1=p_kk[:], op=ALU.mult), [ets, p_kk], [N_])
            K.V(lambda en: en.tensor_tensor(out=qkT[d][par][:], in0=eti[:], in1=p_qk[:], op=ALU.mult), [eti, p_qk], [qkT[d][par]])
            p_m = PQ()
            K.PE(lambda en: en.transpose(p_m[:], N_[:], idf[:]), [N_, idf], [p_m])
            K.A(lambda en: en.activation(out=M_[:], in_=p_m[:], func=AF.Copy), [p_m], [M_])
            K.G(lambda en: en.tensor_tensor(out=y[:], in0=idf[:], in1=N_[:], op=ALU.subtract), [idf, N_], [y])
            Pc, PTc = N_, M_
            for lev in range(1, 6):
                Pn, PTn = Pm[lev % 2][d][par], PTm[lev % 2][d][par]
                p1 = PQ()
                K.PE(lambda en, p1=p1, Pc=Pc, PTc=PTc: en.matmul(p1[:], lhsT=Pc[:], rhs=PTc[:], start=True, stop=True), [Pc, PTc], [p1])
                K.A(lambda en, p1=p1, PTn=PTn: en.activation(out=PTn[:], in_=p1[:], func=AF.Copy), [p1], [PTn])
                if lev < 5:
                    p2 = PQ()
                    K.PE(lambda en, p2=p2, Pc=Pc, PTc=PTc: en.matmul(p2[:], lhsT=PTc[:], rhs=Pc[:], start=True, stop=True), [Pc, PTc], [p2])
                    K.A(lambda en, p2=p2, Pn=Pn: en.activation(out=Pn[:], in_=p2[:], func=AF.Copy), [p2], [Pn])
                p3 = PQ()
                K.PE(lambda en, p3=p3, PTn=PTn: en.matmul(p3[:], lhsT=PTn[:], rhs=y[:], start=True, stop=True), [PTn, y], [p3])
                K.V(lambda en, p3=p3: en.tensor_tensor(out=y[:], in0=y[:], in1=p3[:], op=ALU.add), [y, p3], [y])
                Pc, PTc = Pn, PTn
            K.A(lambda en: en.activation(out=ybf[:], in_=y[:], func=AF.Copy), [y], [ybf])
            K.V(lambda en: en.tensor_scalar(out=Vb_[d][par][:], in0=p_vt[:], scalar1=bcol, scalar2=None, op0=ALU.mult), [p_vt, cl_], [Vb_[d][par]])
            K.A(lambda en: en.activation(out=cv[:, 0:1], in_=gcol, func=AF.Exp), [cl_], [cv])
            K.V(lambda en: en.tensor_tensor(out=cv[:, 1:2], in0=cv[:, 0:1], in1=bcol, op=ALU.mult), [cv, cl_], [cv])
            K.V(lambda en: en.tensor_scalar(out=Kbe[d][par][:], in0=p_kt[:], scalar1=cv[:, 1:2], scalar2=None, op0=ALU.mult), [p_kt, cv], [Kbe[d][par]])
            K.V(lambda en: en.tensor_scalar(out=kdA[d][par][0:64, :], in0=p_kt[0:64, :], scalar1=cv[0:64, 3:4], scalar2=None, op0=ALU.mult), [p_kt, cv], [kdA[d][par]])
            K.V(lambda en: en.tensor_scalar(out=kdB[d][par][64:128, :], in0=p_kt[64:128, :], scalar1=cv[64:128, 3:4], scalar2=None, op0=ALU.mult), [p_kt, cv], [kdB[d][par]])
            K.V(lambda en: en.tensor_tensor(out=qdT[d][par][:], in0=QN[:, c0:c0 + 128], in1=egB[d][par][:], op=ALU.mult), [QN, egB[d][par]], [qdT[d][par]])
            p_u, p_w = PQ(), PQ()
            K.PE(lambda en: en.matmul(p_u[:], lhsT=ybf[:], rhs=Vb_[d][par][:], start=True, stop=True), [ybf, Vb_[d][par]], [p_u])
            K.PE(lambda en: en.matmul(p_w[:], lhsT=Kbe[d][par][:], rhs=ybf[:], start=True, stop=True), [ybf, Kbe[d][par]], [p_w])
            K.A(lambda en: en.activation(out=Ut[d][par][:], in_=p_u[:], func=AF.Copy), [p_u], [Ut[d][par]])
            K.A(lambda en: en.activation(out=nWT[d][par][:], in_=p_w[:], func=AF.Copy, scale=-1.0), [p_w], [nWT[d][par]])

        def seq(d, h, step):
            par = step % 2
            sc = order[d][step]
            c0 = sc * 128
            halves = ((0, kdA[d][par]), (64, kdB[d][par]))
            if d == 1:
                halves = halves[::-1]
            p_vn, p_o, p_s = qt[d * 4 + 0], qt[d * 4 + 1], qt[d * 4 + 2]
            for (r0, kd) in halves:
                col = lastc[d][0] if r0 == 0 else lastc[d][1]
                K.PE(lambda en: en.matmul(p_vn[:], lhsT=nWT[d][par][:], rhs=Sb[d][:], start=True, stop=True), [nWT[d][par], Sb[d]], [p_vn])
                K.V(lambda en, r0=r0: en.tensor_tensor(out=VNEW[d][r0:r0 + 64, :], in0=p_vn[r0:r0 + 64, :], in1=Ut[d][par][r0:r0 + 64, :], op=ALU.add),
                    [p_vn, Ut[d][par]], [VNEW[d]])
                K.PE(lambda en: en.matmul(p_o[:], lhsT=qdT[d][par][:], rhs=Sb[d][:], start=True, stop=False), [qdT[d][par], Sb[d]], [p_o])
                K.PE(lambda en: en.matmul(p_o[:], lhsT=qkT[d][par][:], rhs=VNEW[d][:], start=False, stop=True), [qkT[d][par], VNEW[d]], [p_o])
                K.A(lambda en, r0=r0: en.activation(out=OSB[d][par][r0:r0 + 64, :], in_=p_o[r0:r0 + 64, :], func=AF.Copy), [p_o], [OSB[d][par]])
                K.PE(lambda en, kd=kd: en.matmul(p_s[:], lhsT=kd[:], rhs=VNEW[d][:], start=True, stop=True), [kd, VNEW[d]], [p_s])
                K.V(lambda en, col=col: en.scalar_tensor_tensor(out=Sf[d][:], in0=Sf[d][:], scalar=egB[d][par][:, col:col + 1], in1=p_s[:],
                                                               op0=ALU.mult, op1=ALU.add), [Sf[d], egB[d][par], p_s], [Sf[d]])
                K.A(lambda en: en.activation(out=Sb[d][:], in_=Sf[d][:], func=AF.Copy), [Sf[d]], [Sb[d]])
            K.store(O[d].t[c0:c0 + 128, h * 128:(h + 1) * 128], OSB[d][par][:], reads=[OSB[d][par]], writes=[O[d].sub(sc)])

        def head_prep(h):
            NRM = NRMs[h % 2]
            for wi in range(3):
                c = wi * 4 + h
                ft = 20 + c
                K.load(X[:], projT.t[ft * 128:(ft + 1) * 128, :], reads=all_regions(projT, [ft]), writes=[X])
                K.V(lambda en, c=c: en.tensor_scalar(out=U[:], in0=X[:], scalar1=cw[:, c, 2:3], scalar2=None, op0=ALU.mult), [X, cw], [U])
                for (s0, s1) in SEGS:
                    for (j, off) in ((0, -2), (1, -1), (3, 1)):
                        if off < 0:
                            oa, ob, ia, ib = s0 - off, s1, s0, s1 + off
                        else:
                            oa, ob, ia, ib = s0, s1 - off, s0 + off, s1
                        K.V(lambda en, c=c, j=j, oa=oa, ob=ob, ia=ia, ib=ib: en.scalar_tensor_tensor(
                            out=U[:, oa:ob], in0=X[:, ia:ib], scalar=cw[:, c, j:j + 1], in1=U[:, oa:ob], op0=ALU.mult, op1=ALU.add), [X, U, cw], [U])
                K.A(lambda en: en.activation(out=U[:], in_=U[:], func=AF.Silu), [U], [U])
                if wi == 2:
                    K.G(lambda en: en.tensor_copy(out=NRM[2][:], in_=U[:]), [U], [NRM[2]])
                    continue
                K.G(lambda en: en.tensor_tensor(out=SQ[:], in0=U[:], in1=U[:], op=ALU.mult), [U], [SQ])
                for ti, (t0, n) in enumerate(TT512):
                    pn, r = psn[1], rs[ti % 2]
                    K.PE(lambda en, pn=pn, t0=t0, n=n: en.matmul(pn[:, 0:n], lhsT=ones[:], rhs=SQ[:, t0:t0 + n], start=True, stop=True), [ones, SQ], [pn])
                    K.V(lambda en, pn=pn, r=r, n=n: en.tensor_scalar(out=r[:, 0:n], in0=pn[:, 0:n], scalar1=EPS, scalar2=None, op0=ALU.add), [pn], [r])
                    K.A(lambda en, r=r, n=n: en.activation(out=r[:, 0:n], in_=r[:, 0:n], func=AF.Sqrt), [r], [r])
                    K.V(lambda en, r=r, n=n: en.reciprocal(out=r[:, 0:n], in_=r[:, 0:n]), [r], [r])
                    K.V(lambda en, r=r, t0=t0, n=n, wi=wi: en.scalar_tensor_tensor(out=NRM[wi][:, t0:t0 + n], in0=U[:, t0:t0 + n],
                                                                                 scalar=(128.0 ** -0.5 if wi == 0 else 1.0), in1=r[:, 0:n],
                                                                                 op0=ALU.mult, op1=ALU.mult), [U, r], [NRM[wi]])

        NH = 4 if DBG > 10 else 1
        head_prep(0)
        for h in range(NH):
            nxt = K.capture(head_prep, h + 1) if h + 1 < NH else []
            stride = (len(nxt) + NSC - 1) // NSC if nxt else 0
            for d in range(2):
                K.V(lambda en, d=d: en.memset(Sf[d][:], 0.0), [], [Sf[d]])
                K.V(lambda en, d=d: en.memset(Sb[d][:], 0.0), [], [Sb[d]])
                K.V(lambda en, d=d: en.memset(VNEW[d][:], 0.0), [], [VNEW[d]])
            if DBG == 1:
                continue
            K.emit_interleaved([K.capture(prep, d, h, 0) for d in range(2)])
            if DBG == 2:
                continue
            for step in range(NSC if DBG > 10 else 1):
                lists = []
                if step + 1 < NSC:
                    lists += [K.capture(prep, d, h, step + 1) for d in range(2)]
                lists += [K.capture(seq, d, h, step) for d in range(2)]
                if nxt:
                    lists.append(nxt[step * stride:(step + 1) * stride])
                K.emit_interleaved(lists)
    if DBG > 10:
        head_norm_epilogue(K, I, O, projT, zT, gate_ft0=32, out_row0=512, center=False)


def host_inputs(inputs, b):
    f = lambda a: np.ascontiguousarray(np.asarray(a, np.float32))
    m = {
        "x_b": f(inputs["x"][b]), "ctx_b": f(inputs["ctx"][b]),
        "c_b": _col(inputs["c"][b], 8), "c_ctx": _col(inputs["c_ctx"], 8),
        "ada_w": f(inputs["ada_w"]),
        "ada_b": np.stack([_col(inputs["ada_b"][l], 48) for l in range(DEPTH)]),
        "norm1_g": np.stack([_col(inputs["norm1_g"][l], 8) for l in range(DEPTH)]),
        "norm2_g": np.stack([_col(inputs["norm2_g"][l], 8) for l in range(DEPTH)]),
        "mix_w_out": f(inputs["mix_w_out"]), "mlp_w1": f(inputs["mlp_w1"]), "mlp_w2": f(inputs["mlp_w2"]),
        "ev_w_in": f(inputs["ev_w_in"]), "od_w_in": f(inputs["od_w_in"]),
        "final_g": _col(inputs["final_g"], 8),
        "ident": np.eye(128, dtype=np.float32), "ones": np.ones((128, 128), np.float32),
    }
    m.update(host_mixer_inputs(inputs))
    return m


def _bd(w):
    out = np.zeros((2, 4, 128, 128), np.float32)
    for d in range(2):
        for ct in range(4):
            out[d, ct, 0:64, 0:64] = w[d, 2 * ct]
            out[d, ct, 64:128, 64:128] = w[d, 2 * ct + 1]
    return out


def host_mixer_inputs(inputs):
    f = lambda a: np.ascontiguousarray(np.asarray(a, np.float32))
    m = {}
    m["lru_cw"] = f(np.asarray(inputs["lru_conv_w"]).reshape(2, 4, 4, 128).transpose(0, 3, 2, 1))
    m["lru_cb"] = f(np.asarray(inputs["lru_conv_b"]).reshape(2, 4, 128).transpose(0, 2, 1))
    for nm, key in (("lru_ba", "lru_ba"), ("lru_bx", "lru_bx"), ("lru_lam", "lru_lambda")):
        m[nm] = f(np.asarray(inputs[key]).reshape(2, 2, 4, 128).transpose(0, 3, 1, 2))
    m["lru_wa_bd"] = np.stack([_bd(np.asarray(inputs["lru_wa"][e])) for e in range(2)])
    m["lru_wx_bd"] = np.stack([_bd(np.asarray(inputs["lru_wx"][e])) for e in range(2)])
    m["ret_lg"] = f(np.broadcast_to(np.asarray(inputs["ret_log_gamma"]).reshape(2, 1, 8), (2, 128, 8)))
    j = np.arange(128, dtype=np.float32)
    m["pos_cols"] = f(np.stack([j + 1.0, 128.0 - j], 1))
    m["tri_f"] = f((j[:, None] <= j[None, :]).astype(np.float32))
    m["tri_b"] = f((j[:, None] >= j[None, :]).astype(np.float32))
    n_freq = 16
    inv = np.power(np.float32(10000.0), -np.arange(n_freq, dtype=np.float32) / n_freq).astype(np.float32)
    rows = NLAT // 64
    r = np.arange(rows, dtype=np.float32)
    c = np.arange(64, dtype=np.float32)
    row_ang = np.broadcast_to(r[:, None, None] * inv, (rows, 64, n_freq))
    col_ang = np.broadcast_to(c[None, :, None] * inv, (rows, 64, n_freq))
    ang = np.concatenate([row_ang, col_ang], -1).reshape(NLAT, 32).astype(np.float32)
    cosT = np.cos(ang).T.astype(np.float32)
    sinT = np.sin(ang).T.astype(np.float32)
    m["rope_cos"] = f(np.concatenate([cosT, cosT, cosT, cosT], 0))
    m["rope_sin"] = f(np.concatenate([-sinT, sinT, -sinT, sinT], 0))
    m["hg_logits"] = f(np.asarray(inputs["hg_lb_logits"]).reshape(2, 2, 4, 128).transpose(3, 0, 1, 2))
    m["gdn_cw"] = f(np.asarray(inputs["gdn_conv_w"]).reshape(2, 4, 12, 128).transpose(0, 3, 2, 1))
    ab = np.zeros((2, 16, 2), np.float32)
    ab[:, 8:16, 0] = np.asarray(inputs["gdn_a_log"]).reshape(2, 8)
    ab[:, 8:16, 1] = np.asarray(inputs["gdn_dt_bias"]).reshape(2, 8)
    m["gdn_ab"] = ab
    sel = np.zeros((16, 16, 128), np.float32)
    for r in range(16):
        sel[r, r, :] = 1.0
    m["gdn_sel"] = sel
    blk = (j[:, None] // 64) == (j[None, :] // 64)
    m["gdn_masks"] = f(np.stack([(j[:, None] < j[None, :]) & blk, (j[:, None] <= j[None, :]) & blk,
                                 (j[:, None] > j[None, :]) & blk, (j[:, None] >= j[None, :]) & blk]).astype(np.float32))
    return m


def build_mixer_test(kind, idx, F):
    nc = bass.Bass("TRN2", target_bir_lowering=False)
    K = KB(nc, debug=True)
    I = {}
    I["ident"] = K.din("ident", [128, 128])
    I["ones"] = K.din("ones", [128, 128])
    mixer_inputs(K, I)
    projT = K.din("projT", [F, T])
    zT = K.dout("zT", [D, T])
    done = K.dout("done", [128, 128])
    S = mixer_scratch(K)
    if kind == "even":
        even_mixer(K, idx, I, S, projT, zT)
    else:
        odd_mixer(K, idx, I, S, projT, zT)
    K.P.barrier()
    fin = K.P.dma("sync", done.t[:, :], I["ident"].t[:, :])
    K.P.emit(final_wait_ops=[fin])
    return nc, K


_PROGRAM_CACHE = {}


def kernel(**inputs):
    n_cores = 8
    if "full" not in _PROGRAM_CACHE:
        _PROGRAM_CACHE["full"] = build_program({"debug": False})[0]
    nc = _PROGRAM_CACHE["full"]
    shared = None
    in_maps = []
    for b in range(n_cores):
        m = host_inputs(inputs, b)
        if shared is None:
            shared = m
        else:
            for k in list(m.keys()):
                if k not in ("x_b", "ctx_b", "c_b"):
                    m[k] = shared[k]
        in_maps.append(m)
    res = run_bass_kernel_spmd(nc, in_maps, core_ids=list(range(n_cores)))
    out = np.stack([np.asarray(res.results[b]["out"], dtype=np.float32) for b in range(n_cores)], axis=0)
    return out
```
